# Optimizing a Trainium2 kernel written in Bass

```python
import math
import jax
import jax.numpy as jnp
from jax import lax
import numpy as np

D_MODEL = 1024
BATCH = 2
SEQ = 8192
DEPTH = 4

N_MIXERS = 3
Q_BLK = 128
EPS = 1e-6
NEG = -1e30
FORCE = 1e4
NSA_HEADS = 16
NSA_DK = 64
NSA_KV = 4
NSA_REP = NSA_HEADS // NSA_KV
CMP_LEN = 32
CMP_STRIDE = 16
CMP_HID = 256
SEL_LEN = 64
SEL_TOPK = 16
WIN = 512
NSA_QW = NSA_HEADS * NSA_DK
NSA_KVW = NSA_KV * NSA_DK
NSA_IN = NSA_QW + 6 * NSA_KVW + 3 * NSA_HEADS
SB_HEADS = 16
SB_DH = D_MODEL // SB_HEADS
DIFF_HEADS = 8
DIFF_DH = D_MODEL // (2 * DIFF_HEADS)
D_FF = 2816
N_A = (DEPTH + 2) // 3
N_B = (DEPTH + 1) // 3
N_C = DEPTH // 3

kernel_name = "hybrid_nsa_stickbreak_diffattn_macaron"


def rms_norm(x, g):
    xf = x.astype(jnp.float32)
    y = xf * lax.rsqrt(jnp.mean(xf * xf, axis=-1, keepdims=True) + EPS)
    return (y * g.astype(jnp.float32)).astype(x.dtype)


def alibi_slopes(n):
    return jnp.exp2(-8.0 * jnp.arange(1, n + 1, dtype=jnp.float32) / n)


def swiglu(h, w1, w2):
    gate, up = jnp.split(h @ w1, 2, axis=-1)
    return (jax.nn.silu(gate) * up) @ w2


def sweep_blocks(fn, seq):
    out = lax.map(fn, jnp.arange(seq // Q_BLK, dtype=jnp.int32) * Q_BLK)
    out = jnp.moveaxis(out, 0, 1)
    return out.reshape(out.shape[0], seq, *out.shape[3:])


def nsa_mixer(h, w_in, cmp_pe, cmp_w1, cmp_w2, w_out):
    B, T, _ = h.shape
    f32 = jnp.float32
    bounds = [int(b) for b in np.cumsum([NSA_QW] + [NSA_KVW] * 6)]
    parts = jnp.split(h @ w_in, bounds, axis=-1)
    q = parts[0].reshape(B, T, NSA_KV, NSA_REP, NSA_DK)
    kc, vc, ks, vs, kw, vw = [p.reshape(B, T, NSA_KV, NSA_DK) for p in parts[1:7]]
    gates = jax.nn.sigmoid(parts[7].astype(f32)).reshape(B, T, NSA_KV, NSA_REP, 3)
    scale = NSA_DK ** -0.5
    slopes = alibi_slopes(NSA_HEADS).reshape(NSA_KV, NSA_REP)

    n_cmp = (T - CMP_LEN) // CMP_STRIDE + 1
    cmp_start = jnp.arange(n_cmp) * CMP_STRIDE
    tok_idx = cmp_start[:, None] + jnp.arange(CMP_LEN)[None, :]

    def compress(a, j):
        blk = a[:, tok_idx] + cmp_pe[j][None, None, :, None, :]
        blk = jnp.moveaxis(blk, 3, 2).reshape(B, n_cmp, NSA_KV, CMP_LEN * NSA_DK)
        return jax.nn.gelu(blk @ cmp_w1[j]) @ cmp_w2[j]

    k_cmp = compress(kc, 0)
    v_cmp = compress(vc, 1)
    cmp_end = cmp_start + CMP_LEN - 1

    n_sel = T // SEL_LEN
    topk = min(SEL_TOPK, n_sel)
    sel_start = jnp.arange(n_sel) * SEL_LEN
    overlap = ((cmp_start[:, None] < sel_start[None, :] + SEL_LEN)
               & (cmp_start[:, None] + CMP_LEN > sel_start[None, :])).astype(f32)
    ks_blk = jnp.moveaxis(ks.reshape(B, n_sel, SEL_LEN, NSA_KV, NSA_DK), 3, 1)
    vs_blk = jnp.moveaxis(vs.reshape(B, n_sel, SEL_LEN, NSA_KV, NSA_DK), 3, 1)
    b_i = jnp.arange(B)[:, None, None, None]
    g_i = jnp.arange(NSA_KV)[None, :, None, None]
    sel_ids = jnp.arange(n_sel)

    kw_pad = jnp.pad(kw, ((0, 0), (WIN, 0), (0, 0), (0, 0)))
    vw_pad = jnp.pad(vw, ((0, 0), (WIN, 0), (0, 0), (0, 0)))

    def block(q0):
        t = q0 + jnp.arange(Q_BLK)
        qb = lax.dynamic_slice_in_dim(q, q0, Q_BLK, axis=1)
        s = jnp.einsum('bqgrd,bngd->bgrqn', qb, k_cmp).astype(f32) * scale
        dist = (t[:, None] - cmp_end[None, :]).astype(f32)
        valid = dist >= 0
        s = jnp.where(valid, s - slopes[None, :, :, None, None] * dist, NEG)
        p_cmp = jax.nn.softmax(s, axis=-1) * valid
        o_cmp = jnp.einsum('bgrqn,bngd->bqgrd', p_cmp.astype(v_cmp.dtype), v_cmp)
        imp = jnp.einsum('bgrqn,ns->bgqs', p_cmp, overlap)
        cur = t // SEL_LEN
        forced = ((sel_ids[None, :] == 0) | (sel_ids[None, :] == cur[:, None])
                  | (sel_ids[None, :] == cur[:, None] - 1))
        imp = jnp.where(forced, imp + FORCE, imp)
        imp = jnp.where(sel_ids[None, :] <= cur[:, None], imp, NEG)
        _, idx = lax.top_k(imp, topk)
        k_g = ks_blk[b_i, g_i, idx]
        v_g = vs_blk[b_i, g_i, idx]
        pos = idx[..., None] * SEL_LEN + jnp.arange(SEL_LEN)
        dist = (t[None, None, :, None, None] - pos).astype(f32)[:, :, None]
        s = jnp.einsum('bqgrd,bgqkld->bgrqkl', qb, k_g).astype(f32) * scale
        s = jnp.where(dist >= 0, s - slopes[None, :, :, None, None, None] * dist, NEG)
        p_sel = jax.nn.softmax(s, axis=(-2, -1))
        o_sel = jnp.einsum('bgrqkl,bgqkld->bqgrd', p_sel.astype(v_g.dtype), v_g)
        kwb = lax.dynamic_slice_in_dim(kw_pad, q0, WIN + Q_BLK, axis=1)
        vwb = lax.dynamic_slice_in_dim(vw_pad, q0, WIN + Q_BLK, axis=1)
        src = q0 - WIN + jnp.arange(WIN + Q_BLK)
        d_int = t[:, None] - src[None, :]
        valid = (d_int >= 0) & (d_int < WIN) & (src[None, :] >= 0)
        s = jnp.einsum('bqgrd,bsgd->bgrqs', qb, kwb).astype(f32) * scale
        s = jnp.where(valid, s - slopes[None, :, :, None, None] * d_int.astype(f32), NEG)
        p_win = jax.nn.softmax(s, axis=-1)
        o_win = jnp.einsum('bgrqs,bsgd->bqgrd', p_win.astype(vwb.dtype), vwb)
        g = lax.dynamic_slice_in_dim(gates, q0, Q_BLK, axis=1)
        o = g[..., 0:1] * o_cmp + g[..., 1:2] * o_sel + g[..., 2:3] * o_win
        return o.reshape(B, Q_BLK, NSA_QW)

    return sweep_blocks(block, T) @ w_out


def stick_breaking_mixer(h, w_in, w_out):
    B, T, _ = h.shape
    f32 = jnp.float32
    q, k, v = [a.reshape(B, T, SB_HEADS, SB_DH) for a in jnp.split(h @ w_in, 3, axis=-1)]
    scale = SB_DH ** -0.5
    src = jnp.arange(T)

    def block(q0):
        t = q0 + jnp.arange(Q_BLK)
        qb = lax.dynamic_slice_in_dim(q, q0, Q_BLK, axis=1)
        z = jnp.einsum('bqhd,bkhd->bhqk', qb, k).astype(f32) * scale
        before = src[None, :] < t[:, None]
        log_1m = jnp.where(before, jax.nn.log_sigmoid(-z), 0.0)
        log_rem = lax.cumsum(log_1m, axis=3, reverse=True) - log_1m
        a = jnp.where(before, jnp.exp(jax.nn.log_sigmoid(z) + log_rem), 0.0)
        o = jnp.einsum('bhqk,bkhd->bqhd', a.astype(v.dtype), v)
        return o.reshape(B, Q_BLK, D_MODEL)

    return sweep_blocks(block, T) @ w_out


def diff_attention_mixer(h, w_in, lam, subln_g, w_out, layer_idx):
    B, T, _ = h.shape
    f32 = jnp.float32
    lam_init = 0.8 - 0.6 * math.exp(-0.3 * layer_idx)
    lam_f = lam.astype(f32)
    lam_full = jnp.exp(jnp.sum(lam_f[0] * lam_f[1])) - jnp.exp(jnp.sum(lam_f[2] * lam_f[3])) + lam_init
    qa, ka, va = jnp.split(h @ w_in, 3, axis=-1)
    q = qa.reshape(B, T, DIFF_HEADS, 2, DIFF_DH)
    k = ka.reshape(B, T, DIFF_HEADS, 2, DIFF_DH)
    v = va.reshape(B, T, DIFF_HEADS, 2 * DIFF_DH)
    scale = DIFF_DH ** -0.5
    slopes = alibi_slopes(DIFF_HEADS)
    src = jnp.arange(T)

    def block(q0):
        t = q0 + jnp.arange(Q_BLK)
        qb = lax.dynamic_slice_in_dim(q, q0, Q_BLK, axis=1)
        dist = (t[:, None] - src[None, :]).astype(f32)
        s = jnp.einsum('bqhmd,bkhmd->bhmqk', qb, k).astype(f32) * scale
        s = jnp.where(dist >= 0, s - slopes[None, :, None, None, None] * dist, NEG)
        p = jax.nn.softmax(s, axis=-1)
        a = p[:, :, 0] - lam_full * p[:, :, 1]
        o = jnp.einsum('bhqk,bkhe->bqhe', a.astype(v.dtype), v)
        o = rms_norm(o, subln_g) * (1.0 - lam_init)
        return o.reshape(B, Q_BLK, D_MODEL)

    return sweep_blocks(block, T) @ w_out


def setup_inputs(seed: int = 0) -> dict:
    key = jax.random.key(seed)
    ks = jax.random.split(key, 18)

    def nrm(k, shape, s):
        return s * jax.random.normal(k, shape, jnp.float32)

    return {
        'x': nrm(ks[0], (BATCH, SEQ, D_MODEL), 1.0),
        'c': nrm(ks[1], (BATCH, D_MODEL), 1.0),
        'ada_w': nrm(ks[2], (DEPTH, D_MODEL, 9 * D_MODEL), 0.5 * D_MODEL ** -0.5),
        'ada_b': nrm(ks[3], (DEPTH, 9 * D_MODEL), 0.02),
        'norm_g': 1.0 + nrm(ks[4], (DEPTH, 6, D_MODEL), 0.05),
        'ffn_w1': nrm(ks[5], (DEPTH, 2, D_MODEL, 2 * D_FF), D_MODEL ** -0.5),
        'ffn_w2': nrm(ks[6], (DEPTH, 2, D_FF, D_MODEL), D_FF ** -0.5),
        'nsa_w_in': nrm(ks[7], (N_A, D_MODEL, NSA_IN), D_MODEL ** -0.5),
        'nsa_cmp_pe': nrm(ks[8], (N_A, 2, CMP_LEN, NSA_DK), 0.1),
        'nsa_cmp_w1': nrm(ks[9], (N_A, 2, CMP_LEN * NSA_DK, CMP_HID), (CMP_LEN * NSA_DK) ** -0.5),
        'nsa_cmp_w2': nrm(ks[10], (N_A, 2, CMP_HID, NSA_DK), CMP_HID ** -0.5),
        'nsa_w_out': nrm(ks[11], (N_A, NSA_QW, D_MODEL), NSA_QW ** -0.5),
        'sb_w_in': nrm(ks[12], (N_B, D_MODEL, 3 * D_MODEL), D_MODEL ** -0.5),
        'sb_w_out': nrm(ks[13], (N_B, D_MODEL, D_MODEL), D_MODEL ** -0.5),
        'diff_w_in': nrm(ks[14], (N_C, D_MODEL, 3 * D_MODEL), D_MODEL ** -0.5),
        'diff_lam': nrm(ks[15], (N_C, 4, DIFF_DH), 0.1),
        'diff_subln_g': 1.0 + nrm(ks[16], (N_C, 2 * DIFF_DH), 0.05),
        'diff_w_out': nrm(ks[17], (N_C, D_MODEL, D_MODEL), D_MODEL ** -0.5),
    }


def reference(x, c, ada_w, ada_b, norm_g, ffn_w1, ffn_w2, nsa_w_in, nsa_cmp_pe, nsa_cmp_w1, nsa_cmp_w2,
              nsa_w_out, sb_w_in, sb_w_out, diff_w_in, diff_lam, diff_subln_g, diff_w_out):
    B, T, D = x.shape
    cond = jax.nn.silu(c)
    for i in range(DEPTH):
        mod = (cond @ ada_w[i] + ada_b[i]).reshape(B, 3, 3, D)

        def sublayer(x, sidx, fn, res_w):
            shift = mod[:, sidx, 0][:, None, :]
            scl = mod[:, sidx, 1][:, None, :]
            gate = mod[:, sidx, 2][:, None, :]
            hh = rms_norm(x, norm_g[i, 2 * sidx]) * (1.0 + scl) + shift
            return x + res_w * gate * rms_norm(fn(hh), norm_g[i, 2 * sidx + 1])

        x = sublayer(x, 0, lambda hh: swiglu(hh, ffn_w1[i, 0], ffn_w2[i, 0]), 0.5)
        kind, j = i % N_MIXERS, i // N_MIXERS
        if kind == 0:
            mixer = lambda hh: nsa_mixer(hh, nsa_w_in[j], nsa_cmp_pe[j], nsa_cmp_w1[j], nsa_cmp_w2[j], nsa_w_out[j])
        elif kind == 1:
            mixer = lambda hh: stick_breaking_mixer(hh, sb_w_in[j], sb_w_out[j])
        else:
            mixer = lambda hh: diff_attention_mixer(hh, diff_w_in[j], diff_lam[j], diff_subln_g[j], diff_w_out[j], i)
        x = sublayer(x, 1, mixer, 1.0)
        x = sublayer(x, 2, lambda hh: swiglu(hh, ffn_w1[i, 1], ffn_w2[i, 1]), 0.5)
    return x
```

```python
import contextlib
import math
import numpy as np
import ml_dtypes
import concourse.bass as bass
import concourse.mybir as mybir
from concourse.bass_utils import run_bass_kernel_spmd

F32 = mybir.dt.float32
BF16 = mybir.dt.bfloat16
AF = mybir.ActivationFunctionType
ALU = mybir.AluOpType
AX = mybir.AxisListType

D = 1024
DFF = 2816
NFF = DFF // 128
B = 2
T = 8192
NCORE = 8
TOK = B * T // NCORE
NT = TOK // 128
EPS = 1e-6
NDMA = 24


def free_sems(nc, handles):
    nc.all_engine_barrier()
    nc.clear_and_free_semaphores(handles)
    nc.all_engine_barrier()


class Sched:
    def __init__(self, nc, es, pfx=""):
        self.nc = nc
        self.pfx = pfx
        self.engs = {}
        for name in ["tensor", "vector", "scalar", "gpsimd", "sync"]:
            sem = nc.alloc_semaphore(name=pfx + "sem_" + name)
            self.engs[name] = dict(obj=getattr(nc, name), sem=sem, cnt=0, waited={})
        self.dma_slots = [dict(sem=nc.alloc_semaphore(name=pfx + "dsem%d" % i), cnt=0) for i in range(NDMA)]
        self.dma_rr = 0
        self.last_write = {}
        self.reads = {}
        self.out_tokens = []
        self.psum_keys = set()

    def _wait(self, engname, tok):
        if tok is None:
            return
        semid, sem, val = tok
        if semid == engname and engname == "tensor":
            return
        e = self.engs[engname]
        if e["waited"].get(semid, 0) >= val:
            return
        e["obj"].wait_ge(sem, val)
        e["waited"][semid] = val

    def _norm(self, keys):
        p = self.pfx
        return [k[len(p):] if (p and k.startswith(p)) else k for k in keys]

    def _deps(self, engname, reads, writes):
        reads, writes = self._norm(reads), self._norm(writes)
        for k in reads:
            self._wait(engname, self.last_write.get(k))
            if k in self.psum_keys:
                for t in self.reads.get(k, []):
                    if t[0] != engname:
                        self._wait(engname, t)
        for k in writes:
            self._wait(engname, self.last_write.get(k))
            for t in self.reads.get(k, []):
                self._wait(engname, t)

    def _commit(self, tok, reads, writes):
        reads, writes = self._norm(reads), self._norm(writes)
        for k in writes:
            self.last_write[k] = tok
            self.reads[k] = []
        for k in reads:
            self.reads.setdefault(k, []).append(tok)

    def op(self, engname, fn, reads=(), writes=()):
        self._deps(engname, reads, writes)
        e = self.engs[engname]
        ins = fn(e["obj"])
        e["cnt"] += 1
        ins.then_inc(e["sem"], 1)
        tok = (engname, e["sem"], e["cnt"])
        self._commit(tok, reads, writes)
        return tok

    def dma(self, queue, out, in_, reads=(), writes=(), is_output=False):
        self._deps(queue, reads, writes)
        idx = self.dma_rr
        slot = self.dma_slots[idx]
        self.dma_rr = (self.dma_rr + 1) % NDMA
        if slot["cnt"] > 0:
            self._wait(queue, ("d%d" % idx, slot["sem"], slot["cnt"] * 16))
        ins = self.engs[queue]["obj"].dma_start(out=out, in_=in_)
        slot["cnt"] += 1
        ins.then_inc(slot["sem"], 16)
        tok = ("d%d" % idx, slot["sem"], slot["cnt"] * 16)
        self._commit(tok, reads, writes)
        if is_output:
            self.out_tokens.append(tok)
        return tok

    def barrier(self):
        toks = []
        for idx, slot in enumerate(self.dma_slots):
            if slot["cnt"] > 0:
                toks.append(("d%d" % idx, slot["sem"], slot["cnt"] * 16))
        for name, e in self.engs.items():
            if e["cnt"] > 0:
                toks.append((name, e["sem"], e["cnt"]))
        for name in self.engs:
            for tk in toks:
                if tk[0] != name:
                    self._wait(name, tk)

    def close(self):
        self.barrier()
        handles = [e["sem"] for e in self.engs.values()] + [sl["sem"] for sl in self.dma_slots]
        free_sems(self.nc, handles)

    def finish(self):
        for idx, slot in enumerate(self.dma_slots):
            if slot["cnt"] > 0:
                self._wait("sync", ("d%d" % idx, slot["sem"], slot["cnt"] * 16))
        for name, e in self.engs.items():
            if name != "sync" and e["cnt"] > 0:
                self._wait("sync", (name, e["sem"], e["cnt"]))


def emit_k1(nc, pfx, mix, n_ffn, pre, binds):
    nv = (1 if mix else 0) + 3 * n_ffn + (2 if pre else 0)
    ng = (1 if mix else 0) + 2 * n_ffn + (1 if pre else 0)
    dr = {}

    def din(name, shape, dt=F32, kind="ExternalInput"):
        if name in binds:
            dr[name] = binds[name]
        else:
            dr[name] = nc.dram_tensor(pfx + name, list(shape), dt, kind=kind).ap()

    din("x", [TOK, D])
    din("c", [128, 8])
    din("adaw", [nv, 8, 128, 8, 128])
    din("adab", [nv, D])
    din("gsel", [ng, D])
    din("ident", [128, 128])
    if mix:
        din("oT", [D, TOK], BF16)
        din("wout", [D, D])
    for f in range(n_ffn):
        din("w1_%d" % f, [NFF, 128, 8, 256])
        din("w2_%d" % f, [DFF, D])
    din("xo", [TOK, D], F32, "ExternalOutput")
    if pre:
        din("hTo", [D, TOK], BF16, "ExternalOutput")

    es = contextlib.ExitStack()
    with es:
        S = Sched(nc, es, pfx)

        def sb(name, shape, dt):
            return es.enter_context(nc.sbuf_tensor(pfx + name, shape, dt))

        def ps(name, shape, dt):
            return es.enter_context(nc.psum_tensor(pfx + name, shape, dt))

        xs = sb("xs", [128, NT, D], F32)
        hT = sb("hT", [128, 8, 1024], BF16)
        actT = sb("actT", [128, NFF, 1024], BF16)
        w1c = [sb("w1c%d" % i, [128, 8, 256], BF16) for i in range(3)]
        w2c = [sb("w2c%d" % i, [128, 1024], BF16) for i in range(3)]
        adas = [sb("adas%d" % i, [128, 8, 128], F32) for i in range(2)]
        prm = [sb("prm%d" % i, [128, D], F32) for i in range(3)]
        scr = [sb("scr%d" % i, [128, D], F32) for i in range(2)]
        hb = [sb("hb%d" % i, [128, D], BF16) for i in range(2)]
        junk = sb("junk", [128, D], BF16)
        sil = [sb("sil%d" % i, [128, 512], F32) for i in range(2)]
        cst = sb("cst", [128, 8], F32)
        cond = sb("cond", [128, 8], F32)
        condbc = sb("condbc", [128, 8, 128], F32)
        ones128 = sb("ones128", [128, 128], F32)
        rows = sb("rows", [1, 2, D], F32)
        identf = sb("identf", [128, 128], F32)
        identb = sb("identb", [128, 128], BF16)
        st = sb("st", [128, 8], F32)
        epsb = sb("epsb", [128, 1], F32)
        PA = ps("PA", [128, 1024], F32)
        PB = ps("PB", [128, 1024], F32)
        PC = ps("PC", [128, 1024], F32)
        PT = ps("PT", [128, 2048], BF16)
        S.psum_keys.update(["PA0", "PA1", "PB0", "PB1", "PC", "PT0", "PT1"])

        xin = dr["x"].rearrange("(n p) d -> p n d", p=128)
        for q4 in range(4):
            S.dma("sync", xs[:, q4 * 4:(q4 + 1) * 4, :], xin[:, q4 * 4:(q4 + 1) * 4, :],
                  writes=["xs%d" % n for n in range(q4 * 4, q4 * 4 + 4)])
        S.dma("sync", cst[:], dr["c"][:, :], writes=["cst"])
        S.dma("sync", identf[:], dr["ident"][:, :], writes=["identf"])
        S.op("vector", lambda e: e.tensor_copy(out=identb[:], in_=identf[:]), reads=["identf"], writes=["identb"])
        S.op("vector", lambda e: e.memset(ones128[:], 1.0), writes=["ones128"])
        S.op("vector", lambda e: e.memset(epsb[:], EPS), writes=["epsb"])
        S.op("scalar", lambda e: e.activation(out=cond[:], in_=cst[:], func=AF.Silu), reads=["cst"], writes=["cond"])
        for k in range(8):
            S.op("vector", lambda e, k=k: e.tensor_scalar(out=condbc[:, k, :], in0=ones128[:], scalar1=cond[:, k:k + 1],
                                                          scalar2=None, op0=ALU.mult),
                 reads=["cond", "ones128"], writes=["condbc"])

        ada_ctr = [0]

        def mod_vec(v, dst_ps):
            S.dma("sync", rows[:, 0, :], dr["adab"][v:v + 1, :], writes=["rows0"])
            for nq in range(8):
                i = ada_ctr[0] % 2
                ada_ctr[0] += 1
                S.dma("sync", adas[i][:], dr["adaw"][v, nq],
                      writes=["adas%d" % i])
                for k in range(8):
                    S.op("tensor", lambda e, k=k, i=i, nq=nq: e.matmul(dst_ps[:, nq * 128:(nq + 1) * 128], lhsT=condbc[:, k, :],
                                                                       rhs=adas[i][:, k, :], start=(k == 0), stop=False),
                         reads=["condbc", "adas%d" % i], writes=[dst_ps.name])
                S.op("tensor", lambda e, nq=nq: e.matmul(dst_ps[:, nq * 128:(nq + 1) * 128], lhsT=ones128[0:1, :],
                                                         rhs=rows[0:1, 0, nq * 128:(nq + 1) * 128], start=False, stop=True),
                     reads=["ones128", "rows0"], writes=[dst_ps.name])

        def g_bcast(gi, dst_ps):
            S.dma("sync", rows[:, 1, :], dr["gsel"][gi:gi + 1, :], writes=["rows1"])
            for h2 in range(2):
                S.op("tensor", lambda e, h2=h2: e.matmul(dst_ps[:, h2 * 512:(h2 + 1) * 512], lhsT=ones128[0:1, :],
                                                         rhs=rows[0:1, 1, h2 * 512:(h2 + 1) * 512], start=True, stop=True),
                     reads=["ones128", "rows1"], writes=[dst_ps.name])

        def make_A(v_scale, gi, dst):
            mod_vec(v_scale, PC)
            S.op("vector", lambda e: e.tensor_scalar(out=scr[0][:], in0=PC[:], scalar1=1.0, scalar2=None, op0=ALU.add),
                 reads=["PC"], writes=["scr0"])
            g_bcast(gi, PC)
            S.op("vector", lambda e: e.tensor_tensor(out=dst[:], in0=PC[:], in1=scr[0][:], op=ALU.mult),
                 reads=["PC", "scr0"], writes=[dst.name])

        def make_B(v_shift, dst):
            mod_vec(v_shift, PC)
            S.op("vector", lambda e: e.tensor_copy(out=dst[:], in_=PC[:]), reads=["PC"], writes=[dst.name])

        def make_G(v_gate, gi, res_w, dst):
            mod_vec(v_gate, PC)
            S.op("vector", lambda e: e.tensor_scalar(out=scr[0][:], in0=PC[:], scalar1=float(res_w), scalar2=None, op0=ALU.mult),
                 reads=["PC"], writes=["scr0"])
            g_bcast(gi, PC)
            S.op("vector", lambda e: e.tensor_tensor(out=dst[:], in0=PC[:], in1=scr[0][:], op=ALU.mult),
                 reads=["PC", "scr0"], writes=[dst.name])

        stc = [0]

        def rstd_of(src_ap, src_keys, col):
            S.op("scalar", lambda e: e.activation(out=junk[:], in_=src_ap, func=AF.Square, accum_out=st[:, col:col + 1]),
                 reads=src_keys, writes=["junk", "st%d" % col])
            S.op("scalar", lambda e: e.activation(out=st[:, col:col + 1], in_=st[:, col:col + 1], func=AF.Sqrt,
                                                  bias=epsb[:], scale=1.0 / D),
                 reads=["st%d" % col, "epsb"], writes=["st%d" % col])
            S.op("vector", lambda e: e.reciprocal(out=st[:, col:col + 1], in_=st[:, col:col + 1]),
                 reads=["st%d" % col], writes=["st%d" % col])

        def prenorm_tile(n, A, Bv, i):
            col = stc[0] % 4
            stc[0] += 1
            rstd_of(xs[:, n, :], ["xs%d" % n], col)
            S.op("vector", lambda e: e.scalar_tensor_tensor(out=scr[1][:], in0=xs[:, n, :], scalar=st[:, col:col + 1], in1=A[:],
                                                            op0=ALU.mult, op1=ALU.mult),
                 reads=["xs%d" % n, "st%d" % col, A.name], writes=["scr1"])
            S.op("gpsimd", lambda e: e.tensor_tensor(out=hb[i][:], in0=scr[1][:], in1=Bv[:], op=ALU.add),
                 reads=["scr1", Bv.name], writes=["hb%d" % i])

        def transpose_tile(i, half, dst_fn, dst_keys):
            pk = "PT%d" % half
            for k in range(8):
                S.op("tensor", lambda e, k=k: e.transpose(out=PT[:, half * 1024 + k * 128: half * 1024 + (k + 1) * 128],
                                                          in_=hb[i][:, k * 128:(k + 1) * 128], identity=identb[:]),
                     reads=["hb%d" % i, "identb"], writes=[pk])
            S.op("scalar", lambda e: e.activation(out=dst_fn(), in_=PT[:, half * 1024:(half + 1) * 1024].rearrange("p (k t) -> p k t", k=8),
                                                  func=AF.Copy),
                 reads=[pk], writes=dst_keys)

        def epilogue(Y, n, G):
            col = 4 + stc[0] % 4
            stc[0] += 1
            rstd_of(Y[:], [Y.name + "0", Y.name + "1"], col)
            S.op("vector", lambda e: e.scalar_tensor_tensor(out=scr[0][:], in0=Y[:], scalar=st[:, col:col + 1], in1=G[:],
                                                            op0=ALU.mult, op1=ALU.mult),
                 reads=[Y.name + "0", Y.name + "1", "st%d" % col, G.name], writes=["scr0"])
            S.op("gpsimd", lambda e: e.tensor_tensor(out=xs[:, n, :], in0=xs[:, n, :], in1=scr[0][:], op=ALU.add),
                 reads=["scr0", "xs%d" % n], writes=["xs%d" % n])

        vi = 0
        gi = 0
        Yb = [PA, PB]
        if mix:
            make_G(vi, gi, 1.0, prm[2])
            vi += 1
            gi += 1
            wo = actT[:, 0:8, :]
            S.dma("gpsimd", wo, dr["wout"].rearrange("(k p) n -> p k n", p=128), writes=["actT"])
            for half in range(2):
                oTs = actT[:, 8:16, :]
                S.dma("sync", oTs, dr["oT"][:, half * 1024:(half + 1) * 1024].rearrange("(k p) t -> p k t", p=128),
                      writes=["actT_o"])
                for tt in range(8):
                    n = half * 8 + tt
                    Y = Yb[n % 2]
                    for h2 in range(2):
                        for k in range(8):
                            S.op("tensor", lambda e, k=k, h2=h2, tt=tt, Y=Y: e.matmul(
                                Y[:, h2 * 512:(h2 + 1) * 512], lhsT=actT[:, 8 + k, tt * 128:(tt + 1) * 128],
                                rhs=actT[:, k, h2 * 512:(h2 + 1) * 512], start=(k == 0), stop=(k == 7)),
                                reads=["actT", "actT_o"], writes=[Y.name + str(h2)])
                    epilogue(Y, n, prm[2])

        wctr = [0, 0]
        for f in range(n_ffn):
            make_B(vi, prm[1])
            make_A(vi + 1, gi, prm[0])
            make_G(vi + 2, gi + 1, 0.5, prm[2])
            vi += 3
            gi += 2
            w1d = dr["w1_%d" % f]
            w2d = dr["w2_%d" % f]
            for grp in range(2):
                for tt in range(8):
                    n = grp * 8 + tt
                    i = n % 2
                    prenorm_tile(n, prm[0], prm[1], i)
                    transpose_tile(i, n % 2, lambda tt=tt: hT[:, :, tt * 128:(tt + 1) * 128], ["hT"])
                for j in range(NFF):
                    wi = wctr[0] % 3
                    wctr[0] += 1
                    S.dma("gpsimd", w1c[wi][:], w1d[j], writes=["w1c%da" % wi, "w1c%db" % wi])
                    for th in range(2):
                        Pg = PA if th == 0 else PB
                        for part, (c0, key) in enumerate([(0, "a"), (128, "b")]):
                            for k in range(8):
                                S.op("tensor", lambda e, k=k, th=th, c0=c0, part=part, Pg=Pg, wi=wi: e.matmul(
                                    Pg[:, part * 512:(part + 1) * 512], lhsT=w1c[wi][:, k, c0:c0 + 128],
                                    rhs=hT[:, k, th * 512:(th + 1) * 512], start=(k == 0), stop=(k == 7)),
                                    reads=["hT", "w1c%d%s" % (wi, key)], writes=[Pg.name + str(part)])
                        si = th
                        S.op("scalar", lambda e, Pg=Pg, si=si: e.activation(out=sil[si][:], in_=Pg[:, 0:512], func=AF.Silu),
                             reads=[Pg.name + "0"], writes=["sil%d" % si])
                        S.op("vector", lambda e, Pg=Pg, si=si, j=j, th=th: e.tensor_tensor(
                            out=actT[:, j, th * 512:(th + 1) * 512], in0=Pg[:, 512:1024], in1=sil[si][:], op=ALU.mult),
                            reads=[Pg.name + "1", "sil%d" % si], writes=["actT%d" % j])
                for tp in range(4):
                    for j in range(NFF):
                        wi = wctr[1] % 3
                        wctr[1] += 1
                        S.dma("gpsimd", w2c[wi][:], w2d[j * 128:(j + 1) * 128, :], writes=["w2c%d" % wi])
                        for q in range(2):
                            tt = tp * 2 + q
                            Y = Yb[q]
                            for h2 in range(2):
                                S.op("tensor", lambda e, j=j, tt=tt, h2=h2, Y=Y, wi=wi: e.matmul(
                                    Y[:, h2 * 512:(h2 + 1) * 512], lhsT=actT[:, j, tt * 128:(tt + 1) * 128],
                                    rhs=w2c[wi][:, h2 * 512:(h2 + 1) * 512], start=(j == 0), stop=(j == NFF - 1)),
                                    reads=["actT%d" % j, "w2c%d" % wi], writes=[Y.name + str(h2)])
                    for q in range(2):
                        epilogue(Yb[q], grp * 8 + tp * 2 + q, prm[2])

        if pre:
            make_B(vi, prm[1])
            make_A(vi + 1, gi, prm[0])
            if binds.get("hTo_chunked"):
                def hTo_dst(n):
                    return dr["hTo"][n // 4, :, (n % 4) * 128:(n % 4 + 1) * 128].rearrange("(k p) t -> p k t", p=128)
            else:
                def hTo_dst(n):
                    return dr["hTo"].rearrange("(k p) t -> p k t", p=128)[:, :, n * 128:(n + 1) * 128]
            for n in range(NT):
                i = n % 2
                prenorm_tile(n, prm[0], prm[1], i)
                transpose_tile(i, n % 2, lambda n=n: hT[:, :, (n % 8) * 128:(n % 8 + 1) * 128], ["hTo%d" % (n % 8)])
                S.dma("sync", hTo_dst(n), hT[:, :, (n % 8) * 128:(n % 8 + 1) * 128],
                      reads=["hTo%d" % (n % 8)], is_output=True)
        xout = dr["xo"].rearrange("(n p) d -> p n d", p=128)
        for q4 in range(4):
            S.dma("sync", xout[:, q4 * 4:(q4 + 1) * 4, :], xs[:, q4 * 4:(q4 + 1) * 4, :],
                  reads=["xs%d" % n for n in range(q4 * 4, q4 * 4 + 4)], is_output=True)
        S.close()


def arrange_w1(w1):
    g = w1[:, :DFF].reshape(8, 128, NFF, 128)
    u = w1[:, DFF:].reshape(8, 128, NFF, 128)
    cat = np.concatenate([g, u], axis=3)
    return np.ascontiguousarray(cat.transpose(2, 1, 0, 3))


def arrange_adaw(cols):
    return np.ascontiguousarray(cols.reshape(8, 128, 8, 128).transpose(2, 1, 0, 3))


_IDENT = np.eye(128, dtype=np.float32)
_NC_CACHE = {}


def get_nc(key, builder):
    if key not in _NC_CACHE:
        _NC_CACHE[key] = builder()
    return _NC_CACHE[key]


def k1_inputs(pfx, c, ada_w, ada_b, norm_g, ffn_w1, ffn_w2, mix, ffns, pre, wout=None):
    vecs, gs = [], []
    if mix is not None:
        vecs += [(mix, 3 + 2)]
        gs += [(mix, 3)]
    for (l, w) in ffns:
        s_ = 0 if w == 0 else 2
        vecs += [(l, 3 * s_ + 0), (l, 3 * s_ + 1), (l, 3 * s_ + 2)]
        gs += [(l, 2 * s_), (l, 2 * s_ + 1)]
    if pre is not None:
        vecs += [(pre, 3), (pre, 4)]
        gs += [(pre, 2)]
    common = {
        pfx + "adaw": np.stack([arrange_adaw(ada_w[l][:, v * D:(v + 1) * D]) for (l, v) in vecs]),
        pfx + "adab": np.stack([ada_b[l][v * D:(v + 1) * D] for (l, v) in vecs]),
        pfx + "gsel": np.stack([norm_g[l][r] for (l, r) in gs]),
        pfx + "ident": _IDENT,
    }
    for f, (l, w) in enumerate(ffns):
        common[pfx + "w1_%d" % f] = arrange_w1(ffn_w1[l][w])
        common[pfx + "w2_%d" % f] = np.ascontiguousarray(ffn_w2[l][w])
    if mix is not None:
        common[pfx + "wout"] = np.ascontiguousarray(wout)
    per_b = [{pfx + "c": np.ascontiguousarray(c[b].reshape(8, 128).T)} for b in range(B)]
    return common, per_b


BIGNEG = 240000.0


class Ctx:
    def __init__(self, nc, pfx, binds):
        self.nc = nc
        self.pfx = pfx
        self.binds = binds
        self.es = contextlib.ExitStack()
        self.dr = {}
        self.psum_names = []

    def din(self, name, shape, dt=F32):
        if name in self.binds:
            self.dr[name] = self.binds[name]
        else:
            self.dr[name] = self.nc.dram_tensor(self.pfx + name, list(shape), dt, kind="ExternalInput").ap()
        return self.dr[name]

    def dout(self, name, shape, dt=F32):
        if name in self.binds:
            self.dr[name] = self.binds[name]
        else:
            self.dr[name] = self.nc.dram_tensor(self.pfx + name, list(shape), dt, kind="ExternalOutput").ap()
        return self.dr[name]

    def sb(self, name, shape, dt):
        return self.es.enter_context(self.nc.sbuf_tensor(self.pfx + "s_" + name, list(shape), dt))

    def ps(self, name, shape, dt):
        self.psum_names.append(name)
        return self.es.enter_context(self.nc.psum_tensor(self.pfx + name, list(shape), dt))


def o_dest(oTd, gathered):
    if gathered:
        def f(g):
            return oTd[g // 4, :, (g % 4) * 512:(g % 4 + 1) * 512].rearrange("(c p) t -> p c t", p=128)
    else:
        def f(g):
            return oTd[:, g * 512:(g + 1) * 512].rearrange("(c p) t -> p c t", p=128)
    return f


def hT_source(hTd, gathered):
    if gathered:
        def f(tg):
            r, tc = tg // 4, tg % 4
            return hTd[tc, r * D:(r + 1) * D, :].rearrange("(k p) t -> p k t", p=128)
    else:
        def f(tg):
            return hTd[:, tg * 512:(tg + 1) * 512].rearrange("(k p) t -> p k t", p=128)
    return f


def flash_pipeline(S, tiles, emit_s, emit_mid, emit_av, lookahead=2):
    pend = []
    for tl in tiles:
        emit_s(tl)
        emit_mid(tl)
        pend.append(tl)
        if len(pend) > lookahead:
            emit_av(pend.pop(0))
    for tl in pend:
        emit_av(tl)


NREL = 67


def emit_diff(nc, pfx, binds, lam_init, gathered):
    C = Ctx(nc, pfx, binds)
    nc, es = C.nc, C.es
    hTd = C.din("hT", [D, T], BF16)
    hsrc = hT_source(hTd, gathered)
    odst = None
    wqd = C.din("wq", [128, 8, 256])
    wkd = C.din("wk", [128, 8, 256])
    wvd = C.din("wv", [128, 8, 256])
    based = C.din("base", [128, 2, 512])
    cbd = C.din("cb", [128, 2, NREL])
    dmaskd = C.din("dmask", [128, 4, 512], BF16)
    lamd = C.din("lam", [128, 256])
    subgd = C.din("subg", [128, 128])
    identd = C.din("ident", [128, 128])
    oTd = C.dout("oT", [256, T], BF16)
    odst = o_dest(oTd, gathered)
    with es:
        S = Sched(nc, es, pfx)
        QT = [C.sb("QT%d" % h, [128, T], BF16) for h in range(2)]
        KT = [C.sb("KT%d" % h, [128, T], BF16) for h in range(2)]
        Vaug = C.sb("Vaug", [128, 64, 2, 129], BF16)
        wq = C.sb("wq", [128, 8, 256], BF16)
        wk = C.sb("wk", [128, 8, 256], BF16)
        wv = C.sb("wv", [128, 8, 256], BF16)
        hTs = [C.sb("hTs%d" % i, [128, 8, 512], BF16) for i in range(2)]
        base = C.sb("base", [128, 2, 512], F32)
        cbt = C.sb("cbt", [128, 2, NREL], F32)
        dmask = C.sb("dmask", [128, 4, 512], BF16)
        tb = [C.sb("tb%d" % i, [128, 512], F32) for i in range(2)]
        Pb = [C.sb("Pb%d" % i, [128, 512], BF16) for i in range(4)]
        lam = C.sb("lam", [128, 256], F32)
        subg = C.sb("subg", [128, 128], F32)
        identf = C.sb("identf", [128, 128], F32)
        identb = C.sb("identb", [128, 128], BF16)
        sm = C.sb("sm", [128, 16], F32)
        lamneg = C.sb("lamneg", [128, 1], F32)
        epsb = C.sb("epsb", [128, 1], F32)
        o0s = C.sb("o0s", [128, 128], F32)
        od = C.sb("od", [128, 128], F32)
        junk = C.sb("junk", [128, 256], F32)
        ob = C.sb("ob", [128, 4, 256], BF16)
        oTs = [C.sb("oTs%d" % i, [128, 2, 512], BF16) for i in range(2)]
        SB_ = [C.ps("Sb%d" % i, [128, 512], F32) for i in range(3)]
        OB = [[C.ps("O%d%d" % (m, p), [128, 512], F32) for p in range(2)] for m in range(2)]
        PT = C.ps("PT", [128, 1024], BF16)
        S.psum_keys.update(C.psum_names)

        S.dma("sync", identf[:], identd[:, :], writes=["identf"])
        S.op("vector", lambda e: e.tensor_copy(out=identb[:], in_=identf[:]), reads=["identf"], writes=["identb"])
        S.dma("gpsimd", wq[:], wqd[:, :, :], writes=["wq"])
        S.dma("gpsimd", wk[:], wkd[:, :, :], writes=["wk"])
        S.dma("gpsimd", wv[:], wvd[:, :, :], writes=["wv"])
        S.dma("sync", base[:], based[:, :, :], writes=["base"])
        S.dma("sync", cbt[:], cbd[:, :, :], writes=["cbt"])
        S.dma("sync", dmask[:], dmaskd[:, :, :], writes=["dmask"])
        S.dma("sync", lam[:], lamd[:, :], writes=["lam"])
        S.dma("sync", subg[:], subgd[:, :], writes=["subg"])
        S.op("vector", lambda e: e.memset(epsb[:], EPS), writes=["epsb"])
        S.op("vector", lambda e: e.memset(Vaug[:, :, :, 128:129], 1.0), writes=["Vones"])
        S.op("vector", lambda e: e.scalar_tensor_tensor(out=junk[:, 0:64], in0=lam[:, 0:64], scalar=1.0, in1=lam[:, 64:128],
                                                        op0=ALU.mult, op1=ALU.mult, accum_out=sm[:, 0:1]),
             reads=["lam"], writes=["junk", "sm0"])
        S.op("vector", lambda e: e.scalar_tensor_tensor(out=junk[:, 0:64], in0=lam[:, 128:192], scalar=1.0, in1=lam[:, 192:256],
                                                        op0=ALU.mult, op1=ALU.mult, accum_out=sm[:, 1:2]),
             reads=["lam", "junk"], writes=["junk", "sm1"])
        S.op("scalar", lambda e: e.activation(out=sm[:, 0:2], in_=sm[:, 0:2], func=AF.Exp), reads=["sm0", "sm1"],
             writes=["sm0", "sm1"])
        S.op("vector", lambda e: e.tensor_tensor(out=sm[:, 2:3], in0=sm[:, 1:2], in1=sm[:, 0:1], op=ALU.subtract),
             reads=["sm0", "sm1"], writes=["sm2"])
        S.op("vector", lambda e: e.tensor_scalar(out=lamneg[:], in0=sm[:, 2:3], scalar1=-float(lam_init), scalar2=None, op0=ALU.add),
             reads=["sm2"], writes=["lamneg"])
        S.op("vector", lambda e: e.tensor_scalar(out=subg[:], in0=subg[:], scalar1=float(1.0 - lam_init), scalar2=None, op0=ALU.mult),
             reads=["subg"], writes=["subg"])

        cp = [0]

        def evac(dst, src, rk, wk_):
            eng = "scalar" if cp[0] % 2 == 0 else "vector"
            cp[0] += 1
            if eng == "scalar":
                S.op("scalar", lambda e: e.activation(out=dst, in_=src, func=AF.Copy), reads=rk, writes=wk_)
            else:
                S.op("vector", lambda e: e.tensor_copy(out=dst, in_=src), reads=rk, writes=wk_)

        bank = [0]
        for tg in range(16):
            hs = hTs[tg % 2]
            hk = "hTs%d" % (tg % 2)
            S.dma("sync", hs[:], hsrc(tg), writes=[hk])
            for (w, wname, dstl, dname) in [(wq, "wq", QT, "QT"), (wk, "wk", KT, "KT")]:
                for h in range(2):
                    Pp = SB_[bank[0] % 3]
                    bank[0] += 1
                    for k in range(8):
                        S.op("tensor", lambda e: e.matmul(Pp[:], lhsT=w[:, k, h * 128:(h + 1) * 128], rhs=hs[:, k, :],
                                                          start=(k == 0), stop=(k == 7)),
                             reads=[wname, hk], writes=[Pp.name])
                    evac(dstl[h][:, tg * 512:(tg + 1) * 512], Pp[:], [Pp.name], ["%s%d_%d" % (dname, h, tg)])
            for tt in range(4):
                Pp = SB_[bank[0] % 3]
                bank[0] += 1
                for k in range(8):
                    S.op("tensor", lambda e: e.matmul(Pp[:, 0:256], lhsT=hs[:, k, tt * 128:(tt + 1) * 128], rhs=wv[:, k, :],
                                                      start=(k == 0), stop=(k == 7)),
                         reads=["wv", hk], writes=[Pp.name])
                blk = tg * 4 + tt
                evac(Vaug[:, blk, :, 0:128], Pp[:, 0:256].rearrange("p (h e) -> p h e", h=2), [Pp.name], ["V_%d" % blk])

        scale = 64 ** -0.5
        ctr = {"s": 0, "t": 0, "p": 0, "o": 0}
        for g in range(16):
            for h in range(2):
                tiles = [dict(kb=kb, m=m) for kb in range(4 * g + 4) for m in range(2)]
                first_av = {}

                def emit_s(tl):
                    kb, m = tl["kb"], tl["m"]
                    Sp = SB_[ctr["s"] % 3]
                    ctr["s"] += 1
                    tl["Sp"] = Sp
                    r = kb - 4 * g
                    S.op("tensor", lambda e: e.matmul(Sp[:], lhsT=KT[h][m * 64:(m + 1) * 64, kb * 128:(kb + 1) * 128],
                                                      rhs=QT[h][m * 64:(m + 1) * 64, g * 512:(g + 1) * 512], start=True, stop=(r < 0)),
                         reads=["KT%d_%d" % (h, kb // 4), "QT%d_%d" % (h, g)], writes=[Sp.name])
                    if r >= 0:
                        S.op("tensor", lambda e: e.matmul(Sp[:], lhsT=identb[:], rhs=dmask[:, r, :], start=False, stop=True),
                             reads=["identb", "dmask"], writes=[Sp.name])

                def emit_mid(tl):
                    kb, m, Sp = tl["kb"], tl["m"], tl["Sp"]
                    tt_ = tb[ctr["t"] % 2]
                    ctr["t"] += 1
                    Pt = Pb[ctr["p"] % 4]
                    ctr["p"] += 1
                    tl["P"] = Pt
                    S.op("vector", lambda e: e.scalar_tensor_tensor(out=tt_[:], in0=Sp[:], scalar=scale, in1=base[:, h, :],
                                                                    op0=ALU.mult, op1=ALU.add),
                         reads=[Sp.name, "base"], writes=[tt_.name])
                    rel = 4 * g - kb + 3
                    S.op("scalar", lambda e: e.activation(out=Pt[:], in_=tt_[:], func=AF.Exp, bias=cbt[:, h, rel:rel + 1], scale=1.0),
                         reads=[tt_.name, "cbt"], writes=[Pt.name])

                def emit_av(tl):
                    kb, m, Pt = tl["kb"], tl["m"], tl["P"]
                    r = kb - 4 * g
                    for qb in range(4):
                        if qb < r:
                            continue
                        p = qb // 2
                        O = OB[m][p]
                        st_ = (m, p) not in first_av
                        first_av[(m, p)] = True
                        c0 = (qb % 2) * 129
                        S.op("tensor", lambda e: e.matmul(O[:, c0:c0 + 129], lhsT=Pt[:, qb * 128:(qb + 1) * 128], rhs=Vaug[:, kb, h, :],
                                                          start=st_, stop=(kb == 4 * g + qb), skip_group_check=True),
                             reads=[Pt.name, "V_%d" % kb, "Vones"], writes=[O.name])

                flash_pipeline(S, tiles, emit_s, emit_mid, emit_av)
                for qb in range(4):
                    p = qb // 2
                    c0 = (qb % 2) * 129
                    O0, O1 = OB[0][p], OB[1][p]
                    S.op("vector", lambda e: e.reciprocal(out=sm[:, 4:5], in_=O0[:, c0 + 128:c0 + 129]), reads=[O0.name], writes=["sm4"])
                    S.op("vector", lambda e: e.reciprocal(out=sm[:, 5:6], in_=O1[:, c0 + 128:c0 + 129]), reads=[O1.name], writes=["sm5"])
                    S.op("vector", lambda e: e.tensor_tensor(out=sm[:, 5:6], in0=sm[:, 5:6], in1=lamneg[:], op=ALU.mult),
                         reads=["sm5", "lamneg"], writes=["sm5"])
                    S.op("vector", lambda e: e.tensor_scalar(out=o0s[:], in0=O0[:, c0:c0 + 128], scalar1=sm[:, 4:5], scalar2=None, op0=ALU.mult),
                         reads=[O0.name, "sm4"], writes=["o0s"])
                    S.op("vector", lambda e: e.scalar_tensor_tensor(out=od[:], in0=O1[:, c0:c0 + 128], scalar=sm[:, 5:6], in1=o0s[:],
                                                                    op0=ALU.mult, op1=ALU.add),
                         reads=[O1.name, "sm5", "o0s"], writes=["od"])
                    S.op("scalar", lambda e: e.activation(out=junk[:, 0:128], in_=od[:], func=AF.Square, accum_out=sm[:, 6:7]),
                         reads=["od"], writes=["junk", "sm6"])
                    S.op("scalar", lambda e: e.activation(out=sm[:, 6:7], in_=sm[:, 6:7], func=AF.Sqrt, bias=epsb[:], scale=1.0 / 128),
                         reads=["sm6", "epsb"], writes=["sm6"])
                    S.op("vector", lambda e: e.reciprocal(out=sm[:, 6:7], in_=sm[:, 6:7]), reads=["sm6"], writes=["sm6"])
                    S.op("vector", lambda e: e.scalar_tensor_tensor(out=ob[:, qb, h * 128:(h + 1) * 128], in0=od[:], scalar=sm[:, 6:7],
                                                                    in1=subg[:], op0=ALU.mult, op1=ALU.mult),
                         reads=["od", "sm6", "subg"], writes=["ob"])
            ot = oTs[ctr["o"] % 2]
            otk = "oTs%d" % (ctr["o"] % 2)
            ctr["o"] += 1
            for qb in range(4):
                for c in range(2):
                    S.op("tensor", lambda e: e.transpose(out=PT[:, c * 512 + qb * 128:c * 512 + (qb + 1) * 128],
                                                         in_=ob[:, qb, c * 128:(c + 1) * 128], identity=identb[:]),
                         reads=["ob", "identb"], writes=["PT"])
            S.op("vector", lambda e: e.tensor_copy(out=ot[:], in_=PT[:].rearrange("p (c t) -> p c t", c=2)), reads=["PT"], writes=[otk])
            S.dma("sync", odst(g), ot[:], reads=[otk], is_output=True)
        S.close()


def alibi_slopes_np(n):
    return np.exp2(-8.0 * np.arange(1, n + 1, dtype=np.float64) / n)


def diag_mask_tiles(strict):
    jj = np.arange(128)[:, None, None]
    r = np.arange(4)[None, :, None]
    q = np.arange(512)[None, None, :]
    d = q - jj - 128 * r
    ok = d >= (1 if strict else 0)
    return np.where(ok, 0.0, -BIGNEG).astype(np.float32)


def base_tile(slope):
    jj = np.arange(128)[:, None]
    q = np.arange(512)[None, :]
    return (-slope * (q - jj)).astype(np.float32)


def cb_table(slope):
    rel = np.arange(NREL) - 3
    return np.ascontiguousarray(np.broadcast_to((-slope * 128.0 * rel)[None, :], (128, NREL))).astype(np.float32)


def arrange_w(wcols):
    n = wcols.shape[1]
    return np.ascontiguousarray(wcols.reshape(8, 128, n).transpose(1, 0, 2))


def diff_inputs(pfx, w_in, lam, subln_g):
    slopes = alibi_slopes_np(8)
    dm = diag_mask_tiles(False).astype(ml_dtypes.bfloat16)
    lamb = np.ascontiguousarray(np.broadcast_to(lam.reshape(1, 256), (128, 256)))
    sgb = np.ascontiguousarray(np.broadcast_to(subln_g.reshape(1, 128), (128, 128)))
    out = []
    for hg in range(4):
        hs = [2 * hg, 2 * hg + 1]
        cols = np.concatenate([np.arange(h * 128, (h + 1) * 128) for h in hs])
        out.append({
            pfx + "wq": arrange_w(w_in[:, cols]),
            pfx + "wk": arrange_w(w_in[:, 1024 + cols]),
            pfx + "wv": arrange_w(w_in[:, 2048 + cols]),
            pfx + "base": np.ascontiguousarray(np.stack([base_tile(slopes[h]) for h in hs], axis=1)),
            pfx + "cb": np.ascontiguousarray(np.stack([cb_table(slopes[h]) for h in hs], axis=1)),
            pfx + "dmask": dm, pfx + "lam": lamb, pfx + "subg": sgb, pfx + "ident": _IDENT,
        })
    return out


def emit_sb(nc, pfx, binds, gathered):
    C = Ctx(nc, pfx, binds)
    nc, es = C.nc, C.es
    hTd = C.din("hT", [D, T], BF16)
    hsrc = hT_source(hTd, gathered)
    odst = None
    wqd = C.din("wq", [128, 8, 256])
    wkd = C.din("wk", [128, 8, 256])
    wvd = C.din("wv", [128, 8, 256])
    m01d = C.din("m01", [128, 4, 512], BF16)
    trid = C.din("tri", [128, 2, 128], BF16)
    identd = C.din("ident", [128, 128])
    oTd = C.dout("oT", [256, T], BF16)
    odst = o_dest(oTd, gathered)
    with es:
        S = Sched(nc, es, pfx)
        QT = [C.sb("QT%d" % h, [128, T], BF16) for h in range(2)]
        KT = [C.sb("KT%d" % h, [128, T], BF16) for h in range(2)]
        V = C.sb("V", [128, 64, 256], BF16)
        wq = C.sb("wq", [128, 8, 256], BF16)
        wk = C.sb("wk", [128, 8, 256], BF16)
        wv = C.sb("wv", [128, 8, 256], BF16)
        hTs = [C.sb("hTs%d" % i, [128, 8, 512], BF16) for i in range(2)]
        m01 = C.sb("m01", [128, 4, 512], BF16)
        tri = C.sb("tri", [128, 2, 128], BF16)
        eb = [C.sb("eb%d" % i, [128, 512], F32) for i in range(3)]
        spb = [C.sb("spb%d" % i, [128, 512], BF16) for i in range(3)]
        wb = [C.sb("wb%d" % i, [128, 512], F32) for i in range(2)]
        ab = [C.sb("ab%d" % i, [128, 512], BF16) for i in range(3)]
        identf = C.sb("identf", [128, 128], F32)
        identb = C.sb("identb", [128, 128], BF16)
        ob = C.sb("ob", [128, 4, 256], BF16)
        oTs = [C.sb("oTs%d" % i, [128, 2, 512], BF16) for i in range(2)]
        ZB = [C.ps("Zb%d" % i, [128, 512], F32) for i in range(3)]
        XB = [C.ps("Xb%d" % i, [128, 512], F32) for i in range(2)]
        OBk = [C.ps("Ob%d" % i, [128, 512], F32) for i in range(2)]
        PT = C.ps("PT", [128, 1024], BF16)
        S.psum_keys.update(C.psum_names)

        S.dma("sync", identf[:], identd[:, :], writes=["identf"])
        S.op("vector", lambda e: e.tensor_copy(out=identb[:], in_=identf[:]), reads=["identf"], writes=["identb"])
        S.dma("gpsimd", wq[:], wqd[:, :, :], writes=["wq"])
        S.dma("gpsimd", wk[:], wkd[:, :, :], writes=["wk"])
        S.dma("gpsimd", wv[:], wvd[:, :, :], writes=["wv"])
        S.dma("sync", m01[:], m01d[:, :, :], writes=["m01"])
        S.dma("sync", tri[:], trid[:, :, :], writes=["tri"])

        cp = [0]

        def evac(dst, src, rk, wk_):
            eng = "scalar" if cp[0] % 2 == 0 else "vector"
            cp[0] += 1
            if eng == "scalar":
                S.op("scalar", lambda e: e.activation(out=dst, in_=src, func=AF.Copy), reads=rk, writes=wk_)
            else:
                S.op("vector", lambda e: e.tensor_copy(out=dst, in_=src), reads=rk, writes=wk_)

        bank = [0]
        for tg in range(16):
            hs = hTs[tg % 2]
            hk = "hTs%d" % (tg % 2)
            S.dma("sync", hs[:], hsrc(tg), writes=[hk])
            for (w, wname, dstl, dname) in [(wq, "wq", QT, "QT"), (wk, "wk", KT, "KT")]:
                for h in range(2):
                    Pp = ZB[bank[0] % 3]
                    bank[0] += 1
                    for k in range(8):
                        S.op("tensor", lambda e: e.matmul(Pp[:], lhsT=w[:, k, h * 128:(h + 1) * 128], rhs=hs[:, k, :],
                                                          start=(k == 0), stop=(k == 7)),
                             reads=[wname, hk], writes=[Pp.name])
                    evac(dstl[h][:, tg * 512:(tg + 1) * 512], Pp[:], [Pp.name], ["%s%d_%d" % (dname, h, tg)])
            for tt in range(4):
                Pp = ZB[bank[0] % 3]
                bank[0] += 1
                for k in range(8):
                    S.op("tensor", lambda e: e.matmul(Pp[:, 0:256], lhsT=hs[:, k, tt * 128:(tt + 1) * 128], rhs=wv[:, k, :],
                                                      start=(k == 0), stop=(k == 7)),
                         reads=["wv", hk], writes=[Pp.name])
                blk = tg * 4 + tt
                evac(V[:, blk, :], Pp[:, 0:256], [Pp.name], ["V_%d" % blk])

        scale = 64 ** -0.5
        ctr = {"z": 0, "e": 0, "w": 0, "a": 0, "o": 0, "chain": 0}
        for g in range(16):
            tiles = []
            for hh in range(4):
                ch = ctr["chain"]
                ctr["chain"] += 1
                kbs = list(range(4 * g + 3, -1, -1))
                for i, kb in enumerate(kbs):
                    tiles.append(dict(hh=hh, kb=kb, first=(i == 0), last=(i == len(kbs) - 1), X=XB[ch % 2], O=OBk[ch % 2], av0=[True]))
            for i in range(1, len(tiles)):
                if tiles[i]["hh"] == tiles[i - 1]["hh"]:
                    tiles[i]["av0"] = tiles[i - 1]["av0"]

            def stage0(tl):
                hh, kb = tl["hh"], tl["kb"]
                p, half = hh // 2, hh % 2
                Zp = ZB[ctr["z"] % 3]
                ctr["z"] += 1
                i = ctr["e"] % 3
                ctr["e"] += 1
                tl["e"], tl["sp"] = eb[i], spb[i]
                r = kb - 4 * g
                S.op("tensor", lambda e: e.matmul(Zp[:], lhsT=KT[p][half * 64:(half + 1) * 64, kb * 128:(kb + 1) * 128],
                                                  rhs=QT[p][half * 64:(half + 1) * 64, g * 512:(g + 1) * 512], start=True, stop=True),
                     reads=["KT%d_%d" % (p, kb // 4), "QT%d_%d" % (p, g)], writes=[Zp.name])
                S.op("scalar", lambda e: e.activation(out=eb[i][:], in_=Zp[:], func=AF.Exp, scale=scale), reads=[Zp.name], writes=[eb[i].name])
                S.op("scalar", lambda e: e.activation(out=spb[i][:], in_=eb[i][:], func=AF.Ln, bias=1.0, scale=1.0),
                     reads=[eb[i].name], writes=[spb[i].name])
                if r >= 0:
                    S.op("gpsimd", lambda e: e.tensor_tensor(out=spb[i][:], in0=spb[i][:], in1=m01[:, r, :], op=ALU.mult),
                         reads=[spb[i].name, "m01"], writes=[spb[i].name])
                    S.op("gpsimd", lambda e: e.tensor_tensor(out=eb[i][:], in0=eb[i][:], in1=m01[:, r, :], op=ALU.mult),
                         reads=[eb[i].name, "m01"], writes=[eb[i].name])

            def stage1(tl):
                X = tl["X"]
                sp, ee = tl["sp"], tl["e"]
                wi = wb[ctr["w"] % 2]
                ctr["w"] += 1
                ai = ab[ctr["a"] % 3]
                ctr["a"] += 1
                tl["a"] = ai
                S.op("tensor", lambda e: e.matmul(X[:], lhsT=tri[:, 0, :], rhs=sp[:], start=tl["first"], stop=False, skip_group_check=True),
                     reads=["tri", sp.name], writes=[X.name])
                S.op("scalar", lambda e: e.activation(out=wi[:], in_=X[:], func=AF.Exp, scale=-1.0), reads=[X.name], writes=[wi.name])
                S.op("tensor", lambda e: e.matmul(X[:], lhsT=tri[:, 1, :], rhs=sp[:], start=False, stop=tl["last"], skip_group_check=True),
                     reads=["tri", sp.name], writes=[X.name])
                S.op("vector", lambda e: e.tensor_tensor(out=ai[:], in0=ee[:], in1=wi[:], op=ALU.mult),
                     reads=[ee.name, wi.name], writes=[ai.name])

            def stage2(tl):
                hh, kb, O, ai = tl["hh"], tl["kb"], tl["O"], tl["a"]
                r = kb - 4 * g
                for qb in range(4):
                    if qb < r:
                        continue
                    st_ = tl["av0"][0]
                    tl["av0"][0] = False
                    S.op("tensor", lambda e: e.matmul(O[:, qb * 64:(qb + 1) * 64], lhsT=ai[:, qb * 128:(qb + 1) * 128],
                                                      rhs=V[:, kb, hh * 64:(hh + 1) * 64], start=st_, stop=(kb == 0),
                                                      skip_group_check=True),
                         reads=[ai.name, "V_%d" % kb], writes=[O.name])
                if tl["last"]:
                    S.op("vector", lambda e: e.tensor_copy(out=ob[:, :, hh * 64:(hh + 1) * 64],
                                                           in_=O[:, 0:256].rearrange("p (q d) -> p q d", q=4)),
                         reads=[O.name], writes=["ob"])

            n = len(tiles)
            for i in range(n + 2):
                if i < n:
                    stage0(tiles[i])
                if 1 <= i <= n:
                    stage1(tiles[i - 1])
                if 2 <= i:
                    stage2(tiles[i - 2])
            ot = oTs[ctr["o"] % 2]
            otk = ot.name
            ctr["o"] += 1
            for qb in range(4):
                for c in range(2):
                    S.op("tensor", lambda e: e.transpose(out=PT[:, c * 512 + qb * 128:c * 512 + (qb + 1) * 128],
                                                         in_=ob[:, qb, c * 128:(c + 1) * 128], identity=identb[:]),
                         reads=["ob", "identb"], writes=["PT"])
            S.op("vector", lambda e: e.tensor_copy(out=ot[:], in_=PT[:].rearrange("p (c t) -> p c t", c=2)), reads=["PT"], writes=[otk])
            S.dma("sync", odst(g), ot[:], reads=[otk], is_output=True)
        S.close()


def sb_inputs(pfx, w_in):
    jj = np.arange(128)[:, None, None]
    r = np.arange(4)[None, :, None]
    q = np.arange(512)[None, None, :]
    m01 = ((q - jj - 128 * r) >= 1).astype(np.float32).astype(ml_dtypes.bfloat16)
    mm = np.arange(128)[:, None]
    j2 = np.arange(128)[None, :]
    tri = np.ascontiguousarray(np.stack([(mm >= j2), (mm < j2)], axis=1).astype(np.float32).astype(ml_dtypes.bfloat16))
    out = []
    for hg in range(4):
        cols = np.arange(hg * 256, (hg + 1) * 256)
        out.append({
            pfx + "wq": arrange_w(w_in[:, cols]),
            pfx + "wk": arrange_w(w_in[:, 1024 + cols]),
            pfx + "wv": arrange_w(w_in[:, 2048 + cols]),
            pfx + "m01": m01, pfx + "tri": tri, pfx + "ident": _IDENT,
        })
    return out


NSA_FORCE = 1e4
NSA_NEG = -1e30


class _Stop(Exception):
    pass


def emit_nsa(nc, pfx, binds, gathered, dbg=None):
    C = Ctx(nc, pfx, binds)
    nc, es = C.nc, C.es
    hTd = C.din("hT", [D, T], BF16)
    hsrc = hT_source(hTd, gathered)
    odst = None
    wfmd = C.din("wfm", [128, 8, 640])
    wtmd = C.din("wtm", [128, 8, 140])
    cw1d = C.din("cw1", [128, 32, 256])
    cped = C.din("cpe", [128, 32])
    cw2kd = C.din("cw2k", [128, 2, 128])
    cw2vd = C.din("cw2v", [128, 2, 64])
    ovld = C.din("ovl", [128, 4, 128], BF16)
    slpd = C.din("slp", [128, 4])
    cbd = C.din("cb", [128, 4, NREL])
    cbcd = C.din("cbc", [128, 4, 16])
    base0d = C.din("base0", [128, 512])
    basec0d = C.din("basec0", [128, 512])
    cmaskd = C.din("cmask", [128, 5, 512], BF16)
    dmaskd = C.din("dmask", [128, 4, 512], BF16)
    wmaskd = C.din("wmask", [128, 8, 512], BF16)
    indd = C.din("ind", [128, T], BF16)
    adjd = C.din("adj", [64, 128, 128])
    identd = C.din("ident", [128, 128])
    oTd = C.dout("oT", [256, T], BF16)
    odst = o_dest(oTd, gathered)
    with es:
        S = Sched(nc, es, pfx)
        try:
            QT = [C.sb("QT%d" % h, [128, T], BF16) for h in range(2)]
            ksT = C.sb("ksT", [128, T], BF16)
            kwT = C.sb("kwT", [128, T], BF16)
            kcvT = C.sb("kcvT", [128, T], BF16)
            vsA = C.sb("vsA", [128, 64, 65], BF16)
            vwA = C.sb("vwA", [128, 64, 65], BF16)
            gates = C.sb("gates", [128, 64, 12], F32)
            PBUF = C.sb("PBUF", [128, 14464], BF16)
            hTs = [PBUF[:, i * 4096:(i + 1) * 4096].rearrange("p (k t) -> p k t", k=8) for i in range(2)]
            wfm = PBUF[:, 8192:8192 + 5120].rearrange("p (k n) -> p k n", k=8)
            wtm = PBUF[:, 13312:13312 + 1120].rearrange("p (k n) -> p k n", k=8)
            cw1 = PBUF[:, 0:8192].rearrange("p (l f) -> p l f", l=32)
            ind = PBUF[:, 0:8192]
            cpe = C.sb("cpe", [128, 32], BF16)
            cw2k = C.sb("cw2k", [128, 2, 128], BF16)
            cw2v = C.sb("cw2v", [128, 2, 64], BF16)
            slp = C.sb("slp", [128, 4], F32)
            cbt = C.sb("cbt", [128, 4, NREL], F32)
            cbct = C.sb("cbct", [128, 4, 16], F32)
            base0 = C.sb("base0", [128, 512], F32)
            basec0 = C.sb("basec0", [128, 512], F32)
            cmask = C.sb("cmask", [128, 5, 512], BF16)
            dmask = C.sb("dmask", [128, 4, 512], BF16)
            wmask = C.sb("wmask", [128, 8, 512], BF16)
            tb = [C.sb("tb%d" % i, [128, 512], F32) for i in range(3)]
            Pb = [C.sb("Pb%d" % i, [128, 512], BF16) for i in range(4)]
            kcmpT = C.sb("kcmpT", [128, 512], BF16)
            vcA = C.sb("vcA", [128, 4, 193], BF16)
            glb = [C.sb("glb%d" % i, [128, 512], BF16) for i in range(4)]
            peb = C.sb("peb", [128, 4], F32)
            imp = C.sb("imp", [128, 4, 128], F32)
            adjt = [C.sb("adjt%d" % i, [128, 128], F32) for i in range(2)]
            impa = C.sb("impa", [128, 128], F32)
            impb = C.sb("impb", [128, 128], F32)
            m8 = C.sb("m8", [128, 16], F32)
            selb = C.sb("selb", [128, 128], BF16)
            MBT = [C.sb("MBT%d" % i, [128, 512], BF16) for i in range(2)]
            acco = C.sb("acco", [128, 4, 256], F32)
            ob = C.sb("ob", [128, 4, 256], BF16)
            oTs = [C.sb("oTs%d" % i, [128, 2, 512], BF16) for i in range(2)]
            sm = C.sb("sm", [128, 8], F32)
            identf = C.sb("identf", [128, 128], F32)
            identb = C.sb("identb", [128, 128], BF16)
            SB_ = [C.ps("Sb%d" % i, [128, 512], F32) for i in range(3)]
            AC = [C.ps("Ac%d" % i, [128, 512], F32) for i in range(4)]
            PT = C.ps("PT", [128, 1024], BF16)
            S.psum_keys.update(C.psum_names)

            S.dma("sync", identf[:], identd[:, :], writes=["identf"])
            S.op("vector", lambda e: e.tensor_copy(out=identb[:], in_=identf[:]), reads=["identf"], writes=["identb"])
            S.dma("gpsimd", wfm, wfmd[:, :, :], writes=["wfm"])
            S.dma("gpsimd", wtm, wtmd[:, :, :], writes=["wtm"])
            S.dma("gpsimd", cpe[:], cped[:, :], writes=["cpe"])
            S.dma("gpsimd", cw2k[:], cw2kd[:, :, :], writes=["cw2k"])
            S.dma("gpsimd", cw2v[:], cw2vd[:, :, :], writes=["cw2v"])
            for (dst, src, key) in [(slp, slpd, "slp"), (cbt, cbd, "cbt"), (cbct, cbcd, "cbct"), (base0, base0d, "base0"),
                                    (basec0, basec0d, "basec0"), (cmask, cmaskd, "cmask"), (dmask, dmaskd, "dmask"),
                                    (wmask, wmaskd, "wmask")]:
                S.dma("sync", dst[:], src, writes=[key])
            S.op("vector", lambda e: e.memset(vsA[:, :, 64:65], 1.0), writes=["vsones"])
            S.op("vector", lambda e: e.memset(vwA[:, :, 64:65], 1.0), writes=["vwones"])
            S.op("vector", lambda e: e.memset(vcA[:], 0.0), writes=["vcA"])
            S.op("vector", lambda e: e.memset(kcmpT[:], 0.0), writes=["kcmpT"])
            S.op("vector", lambda e: e.memset(vcA[:, :, 64:65], 1.0), reads=["vcA"], writes=["vcA"])
            S.dma("sync", vcA[:, :, 65:193], ovld[:, :, :], reads=["vcA"], writes=["vcA"])

            if dbg == 'const':
                raise _Stop
            cp = [0]

            def evac(dst, src, rk, wk_, scale=None):
                eng = "scalar" if cp[0] % 2 == 0 else "vector"
                cp[0] += 1
                if eng == "scalar" and scale is None:
                    S.op("scalar", lambda e: e.activation(out=dst, in_=src, func=AF.Copy), reads=rk, writes=wk_)
                else:
                    if scale is None:
                        S.op("vector", lambda e: e.tensor_copy(out=dst, in_=src), reads=rk, writes=wk_)
                    else:
                        S.op("vector", lambda e: e.tensor_scalar(out=dst, in0=src, scalar1=float(scale), scalar2=None, op0=ALU.mult),
                             reads=rk, writes=wk_)

            bank = [0]
            fm_dst = [(QT[0], "QT0", 0.125), (QT[1], "QT1", 0.125), (ksT, "ksT", None), (kwT, "kwT", None), (kcvT, "kcvT", None)]
            for tg in range(1 if dbg in ('proj1', 'proj1ns') else 16):
                hs = hTs[tg % 2]
                hk = "hTs%d" % (tg % 2)
                S.dma("sync", hs, hsrc(tg), writes=[hk])
                for fi, (dst, dname, sc) in enumerate(fm_dst):
                    Pp = SB_[bank[0] % 3]
                    bank[0] += 1
                    for k in range(8):
                        S.op("tensor", lambda e: e.matmul(Pp[:], lhsT=wfm[:, k, fi * 128:(fi + 1) * 128], rhs=hs[:, k, :],
                                                          start=(k == 0), stop=(k == 7)),
                             reads=["wfm", hk], writes=[Pp.name])
                    evac(dst[:, tg * 512:(tg + 1) * 512], Pp[:], [Pp.name], ["%s_%d" % (dname, tg)], scale=sc)
                for tt in range(4):
                    Pp = SB_[bank[0] % 3]
                    bank[0] += 1
                    for k in range(8):
                        S.op("tensor", lambda e: e.matmul(Pp[:, 0:140], lhsT=hs[:, k, tt * 128:(tt + 1) * 128], rhs=wtm[:, k, :],
                                                          start=(k == 0), stop=(k == 7)),
                             reads=["wtm", hk], writes=[Pp.name])
                    blk = tg * 4 + tt
                    S.op("vector", lambda e: e.tensor_copy(out=vsA[:, blk, 0:64], in_=Pp[:, 0:64]), reads=[Pp.name], writes=["vs_%d" % blk])
                    S.op("vector", lambda e: e.tensor_copy(out=vwA[:, blk, 0:64], in_=Pp[:, 64:128]), reads=[Pp.name], writes=["vw_%d" % blk])
                    S.op("scalar", lambda e: e.activation(out=gates[:, blk, :], in_=Pp[:, 128:140], func=AF.Exp, scale=-1.0),
                         reads=[Pp.name], writes=["gates_%d" % blk])
                    S.op("vector", lambda e: e.tensor_scalar(out=gates[:, blk, :], in0=gates[:, blk, :], scalar1=1.0, scalar2=None, op0=ALU.add),
                         reads=["gates_%d" % blk], writes=["gates_%d" % blk])
                    S.op("vector", lambda e: e.reciprocal(out=gates[:, blk, :], in_=gates[:, blk, :]),
                         reads=["gates_%d" % blk], writes=["gates_%d" % blk])
            if dbg in ('proj', 'proj1', 'proj1ns'):
                raise _Stop
            S.barrier()

            S.dma("gpsimd", cw1, cw1d[:, :, :], writes=["cw1"])
            kcv = kcvT[:, :].rearrange("p (n s) -> p n s", s=16)
            for j in range(2):
                lo, hi = j * 64, (j + 1) * 64
                for c in range(2):
                    Pp = SB_[bank[0] % 3]
                    bank[0] += 1
                    Pq = AC[0]
                    for l in range(32):
                        S.op("tensor", lambda e: e.matmul(Pq[:, 0:1], lhsT=cw1[lo:hi, l, c * 128:(c + 1) * 128], rhs=cpe[lo:hi, l:l + 1],
                                                          start=(l == 0), stop=(l == 31)),
                             reads=["cw1", "cpe"], writes=[Pq.name])
                    col = j * 2 + c
                    S.op("vector", lambda e: e.tensor_copy(out=peb[:, col:col + 1], in_=Pq[:, 0:1]), reads=[Pq.name], writes=["peb%d" % col])
                    for l in range(32):
                        S.op("tensor", lambda e: e.matmul(Pp[:, 0:511], lhsT=cw1[lo:hi, l, c * 128:(c + 1) * 128],
                                                          rhs=kcv[lo:hi, (l // 16):(l // 16) + 511, l % 16],
                                                          start=(l == 0), stop=(l == 31)),
                             reads=["cw1"] + ["kcvT_%d" % t_ for t_ in range(16)], writes=[Pp.name])
                    xg, x2, ug = tb[0], tb[1], tb[2]
                    S.op("scalar", lambda e: e.activation(out=xg[:, 0:511], in_=Pp[:, 0:511], func=AF.Identity, bias=peb[:, col:col + 1], scale=1.0),
                         reads=[Pp.name, "peb%d" % col], writes=[xg.name])
                    S.op("vector", lambda e: e.tensor_tensor(out=x2[:, 0:511], in0=xg[:, 0:511], in1=xg[:, 0:511], op=ALU.mult),
                         reads=[xg.name], writes=[x2.name])
                    S.op("vector", lambda e: e.tensor_scalar(out=x2[:, 0:511], in0=x2[:, 0:511], scalar1=0.044715, scalar2=1.0,
                                                             op0=ALU.mult, op1=ALU.add), reads=[x2.name], writes=[x2.name])
                    S.op("vector", lambda e: e.tensor_tensor(out=ug[:, 0:511], in0=x2[:, 0:511], in1=xg[:, 0:511], op=ALU.mult),
                         reads=[x2.name, xg.name], writes=[ug.name])
                    S.op("scalar", lambda e: e.activation(out=ug[:, 0:511], in_=ug[:, 0:511], func=AF.Exp, scale=-1.5957691216057308),
                         reads=[ug.name], writes=[ug.name])
                    S.op("vector", lambda e: e.tensor_scalar(out=ug[:, 0:511], in0=ug[:, 0:511], scalar1=1.0, scalar2=None, op0=ALU.add),
                         reads=[ug.name], writes=[ug.name])
                    S.op("vector", lambda e: e.reciprocal(out=ug[:, 0:511], in_=ug[:, 0:511]), reads=[ug.name], writes=[ug.name])
                    gl = glb[j * 2 + c]
                    S.op("vector", lambda e: e.memset(gl[:, 511:512], 0.0), writes=[gl.name])
                    S.op("vector", lambda e: e.tensor_tensor(out=gl[:, 0:511], in0=ug[:, 0:511], in1=xg[:, 0:511], op=ALU.mult),
                         reads=[ug.name, xg.name, gl.name], writes=[gl.name])
            if dbg == 'cmp1':
                raise _Stop
            Pp = SB_[bank[0] % 3]
            bank[0] += 1
            for c in range(2):
                S.op("tensor", lambda e: e.matmul(Pp[:, 0:511], lhsT=cw2k[:, c, :], rhs=glb[c][:, 0:511], start=(c == 0), stop=(c == 1)),
                     reads=["cw2k", glb[c].name], writes=[Pp.name])
            S.op("vector", lambda e: e.tensor_copy(out=kcmpT[:, 0:511], in_=Pp[:, 0:511]), reads=[Pp.name, "kcmpT"], writes=["kcmpT"])
            for nt in range(4):
                nn = 128 if nt < 3 else 127
                Pp = SB_[bank[0] % 3]
                bank[0] += 1
                for c in range(2):
                    S.op("tensor", lambda e: e.matmul(Pp[0:nn, 0:64], lhsT=glb[2 + c][:, nt * 128:nt * 128 + nn], rhs=cw2v[:, c, :],
                                                      start=(c == 0), stop=(c == 1)),
                         reads=["cw2v", glb[2 + c].name], writes=[Pp.name])
                S.op("vector", lambda e: e.tensor_copy(out=vcA[0:nn, nt, 0:64], in_=Pp[0:nn, 0:64]), reads=[Pp.name, "vcA"], writes=["vcA"])
            if dbg == 'cmp2':
                raise _Stop
            S.barrier()
            S.dma("sync", ind, indd[:, :], writes=["ind"])

            ctr = {"s": 0, "t": 0, "p": 0, "o": 0, "ac": 0, "adj": 0, "mbt": 0}

            def run_branch(g, hh, tiles, kT, vA, vkey, ncol, accs, acc_cols, basetile, bkey, cbtab, cbkey):
                p, half = hh // 2, hh % 2
                firsts = {}

                def emit_s(tl):
                    Sp = SB_[ctr["s"] % 3]
                    ctr["s"] += 1
                    tl["Sp"] = Sp
                    kb = tl["kb"]
                    mms = [(kT[half * 64:(half + 1) * 64, kb * 128:(kb + 1) * 128], QT[p][half * 64:(half + 1) * 64, g * 512:(g + 1) * 512],
                            tl["kkeys"] + ["QT%d_%d" % (p, g)])]
                    if tl.get("extra") is not None:
                        mms.append(tl["extra"])
                    if tl.get("mask") is not None:
                        mms.append((identb[:], tl["mask"], ["identb", "cmask", "dmask", "wmask"]))
                    for i, (l_, r_, keys) in enumerate(mms):
                        S.op("tensor", lambda e: e.matmul(Sp[:], lhsT=l_, rhs=r_, start=(i == 0), stop=(i == len(mms) - 1)),
                             reads=keys, writes=[Sp.name])

                def emit_mid(tl):
                    Sp = tl["Sp"]
                    tt_ = tb[ctr["t"] % 3]
                    ctr["t"] += 1
                    Pt = Pb[ctr["p"] % 4]
                    ctr["p"] += 1
                    tl["P"] = Pt
                    S.op("vector", lambda e: e.scalar_tensor_tensor(out=tt_[:], in0=basetile[:], scalar=slp[:, hh:hh + 1], in1=Sp[:],
                                                                    op0=ALU.mult, op1=ALU.add),
                         reads=[Sp.name, bkey, "slp"], writes=[tt_.name])
                    ci = tl["cbi"]
                    S.op("scalar", lambda e: e.activation(out=Pt[:], in_=tt_[:], func=AF.Exp, bias=cbtab[:, hh, ci:ci + 1], scale=1.0),
                         reads=[tt_.name, cbkey], writes=[Pt.name])

                def emit_av(tl):
                    Pt, kb = tl["P"], tl["kb"]
                    for qb in tl["qbs"]:
                        acc, c0 = accs[qb], acc_cols[qb]
                        st_ = acc.name not in firsts
                        firsts[acc.name] = True
                        S.op("tensor", lambda e: e.matmul(acc[:, c0:c0 + ncol], lhsT=Pt[:, qb * 128:(qb + 1) * 128], rhs=vA[:, kb, :],
                                                          start=st_, stop=False, skip_group_check=True),
                             reads=[Pt.name] + tl["vkeys"], writes=[acc.name])

                flash_pipeline(S, tiles, emit_s, emit_mid, emit_av)

            for g in range(16):
                ntmax = (512 * g + 480) // 2048
                for hh in range(4):
                    a0 = AC[(ctr["ac"] % 2) * 2]
                    a1 = AC[(ctr["ac"] % 2) * 2 + 1]
                    ctr["ac"] += 1
                    accs = [a0, a0, a1, a1]
                    cols = [0, 193, 0, 193]
                    tiles = []
                    for nt in range(ntmax + 1):
                        rel2 = g - 4 * nt
                        tiles.append(dict(kb=nt, kkeys=["kcmpT"], vkeys=["vcA"], mask=(cmask[:, rel2, :] if rel2 <= 4 else None),
                                          cbi=rel2, qbs=[0, 1, 2, 3]))
                    run_branch(g, hh, tiles, kcmpT, vcA, "vcA", 193, accs, cols, basec0, "basec0", cbct, "cbct")
                    for qb in range(4):
                        acc, c0 = accs[qb], cols[qb]
                        blk = g * 4 + qb
                        S.op("vector", lambda e: e.tensor_scalar(out=sm[:, 0:1], in0=acc[:, c0 + 64:c0 + 65], scalar1=1e-30, scalar2=None, op0=ALU.max),
                             reads=[acc.name], writes=["sm0"])
                        S.op("vector", lambda e: e.reciprocal(out=sm[:, 0:1], in_=sm[:, 0:1]), reads=["sm0"], writes=["sm0"])
                        S.op("vector", lambda e: e.tensor_tensor(out=sm[:, 1:2], in0=sm[:, 0:1], in1=gates[:, blk, hh * 3:hh * 3 + 1], op=ALU.mult),
                             reads=["sm0", "gates_%d" % blk], writes=["sm1"])
                        S.op("vector", lambda e: e.tensor_scalar(out=acco[:, qb, hh * 64:(hh + 1) * 64], in0=acc[:, c0:c0 + 64],
                                                                 scalar1=sm[:, 1:2], scalar2=None, op0=ALU.mult),
                             reads=[acc.name, "sm1"], writes=["acco%d" % qb])
                        if hh == 0:
                            S.op("vector", lambda e: e.tensor_scalar(out=imp[:, qb, :], in0=acc[:, c0 + 65:c0 + 193], scalar1=sm[:, 0:1],
                                                                     scalar2=None, op0=ALU.mult),
                                 reads=[acc.name, "sm0"], writes=["imp%d" % qb])
                        else:
                            S.op("vector", lambda e: e.scalar_tensor_tensor(out=imp[:, qb, :], in0=acc[:, c0 + 65:c0 + 193], scalar=sm[:, 0:1],
                                                                            in1=imp[:, qb, :], op0=ALU.mult, op1=ALU.add),
                                 reads=[acc.name, "sm0", "imp%d" % qb], writes=["imp%d" % qb])
                if dbg == 'g0c':
                    raise _Stop
                mbt = MBT[ctr["mbt"] % 2]
                ctr["mbt"] += 1
                for qb in range(4):
                    blk = g * 4 + qb
                    at = adjt[ctr["adj"] % 2]
                    ctr["adj"] += 1
                    S.dma("sync", at[:], adjd[blk], writes=[at.name])
                    S.op("vector", lambda e: e.tensor_tensor(out=impa[:], in0=imp[:, qb, :], in1=at[:], op=ALU.add),
                         reads=["imp%d" % qb, at.name], writes=["impa"])
                    S.op("vector", lambda e: e.max(out=m8[:, 0:8], in_=impa[:]), reads=["impa"], writes=["m8a"])
                    S.op("vector", lambda e: e.match_replace(out=impb[:], in_to_replace=m8[:, 0:8], in_values=impa[:], imm_value=-3.0e38),
                         reads=["impa", "m8a"], writes=["impb"])
                    S.op("vector", lambda e: e.max(out=m8[:, 8:16], in_=impb[:]), reads=["impb"], writes=["m8b"])
                    S.op("vector", lambda e: e.tensor_scalar(out=selb[:], in0=impa[:], scalar1=m8[:, 15:16], scalar2=1.0,
                                                             op0=ALU.is_ge, op1=ALU.subtract),
                         reads=["impa", "m8b"], writes=["selb"])
                    S.op("tensor", lambda e: e.transpose(out=PT[:, qb * 128:(qb + 1) * 128], in_=selb[:], identity=identb[:]),
                         reads=["selb", "identb"], writes=["PT"])
                S.op("vector", lambda e: e.tensor_copy(out=mbt[:], in_=PT[:, 0:512]), reads=["PT"], writes=[mbt.name])
                if dbg == 'g0k':
                    raise _Stop
                for hh in range(4):
                    a0 = AC[ctr["ac"] % 4]
                    ctr["ac"] += 1
                    accs = [a0] * 4
                    cols = [0, 65, 130, 195]
                    tiles = []
                    for kb in range(4 * g + 4):
                        r = kb - 4 * g
                        tiles.append(dict(kb=kb, kkeys=["ksT_%d" % (kb // 4)], vkeys=["vs_%d" % kb, "vsones"],
                                          extra=(ind[:, kb * 128:(kb + 1) * 128], mbt[:], ["ind", mbt.name]),
                                          mask=(dmask[:, r, :] if r >= 0 else None), cbi=4 * g - kb + 3,
                                          qbs=[qb for qb in range(4) if qb >= r]))
                    run_branch(g, hh, tiles, ksT, vsA, "vs", 65, accs, cols, base0, "base0", cbt, "cbt")
                    for qb in range(4):
                        c0 = cols[qb]
                        blk = g * 4 + qb
                        S.op("vector", lambda e: e.reciprocal(out=sm[:, 2:3], in_=a0[:, c0 + 64:c0 + 65]), reads=[a0.name], writes=["sm2"])
                        S.op("vector", lambda e: e.tensor_tensor(out=sm[:, 3:4], in0=sm[:, 2:3], in1=gates[:, blk, hh * 3 + 1:hh * 3 + 2], op=ALU.mult),
                             reads=["sm2", "gates_%d" % blk], writes=["sm3"])
                        S.op("vector", lambda e: e.scalar_tensor_tensor(out=acco[:, qb, hh * 64:(hh + 1) * 64], in0=a0[:, c0:c0 + 64],
                                                                        scalar=sm[:, 3:4], in1=acco[:, qb, hh * 64:(hh + 1) * 64],
                                                                        op0=ALU.mult, op1=ALU.add),
                             reads=[a0.name, "sm3", "acco%d" % qb], writes=["acco%d" % qb])
                if dbg == 'g0s':
                    raise _Stop
                for hh in range(4):
                    a0 = AC[ctr["ac"] % 4]
                    ctr["ac"] += 1
                    accs = [a0] * 4
                    cols = [0, 65, 130, 195]
                    tiles = []
                    for kb in range(max(0, 4 * g - 4), 4 * g + 4):
                        r = kb - 4 * g
                        tiles.append(dict(kb=kb, kkeys=["kwT_%d" % (kb // 4)], vkeys=["vw_%d" % kb, "vwones"],
                                          mask=wmask[:, r + 4, :], cbi=4 * g - kb + 3,
                                          qbs=[qb for qb in range(4) if qb >= r and qb - r < 5]))
                    run_branch(g, hh, tiles, kwT, vwA, "vw", 65, accs, cols, base0, "base0", cbt, "cbt")
                    for qb in range(4):
                        c0 = cols[qb]
                        blk = g * 4 + qb
                        S.op("vector", lambda e: e.reciprocal(out=sm[:, 4:5], in_=a0[:, c0 + 64:c0 + 65]), reads=[a0.name], writes=["sm4"])
                        S.op("vector", lambda e: e.tensor_tensor(out=sm[:, 5:6], in0=sm[:, 4:5], in1=gates[:, blk, hh * 3 + 2:hh * 3 + 3], op=ALU.mult),
                             reads=["sm4", "gates_%d" % blk], writes=["sm5"])
                        S.op("vector", lambda e: e.scalar_tensor_tensor(out=acco[:, qb, hh * 64:(hh + 1) * 64], in0=a0[:, c0:c0 + 64],
                                                                        scalar=sm[:, 5:6], in1=acco[:, qb, hh * 64:(hh + 1) * 64],
                                                                        op0=ALU.mult, op1=ALU.add),
                             reads=[a0.name, "sm5", "acco%d" % qb], writes=["acco%d" % qb])
                if dbg == 'g0w':
                    raise _Stop
                S.op("vector", lambda e: e.tensor_copy(out=ob[:], in_=acco[:]), reads=["acco%d" % q_ for q_ in range(4)], writes=["ob"])
                ot = oTs[ctr["o"] % 2]
                ctr["o"] += 1
                for qb in range(4):
                    for c in range(2):
                        S.op("tensor", lambda e: e.transpose(out=PT[:, c * 512 + qb * 128:c * 512 + (qb + 1) * 128],
                                                             in_=ob[:, qb, c * 128:(c + 1) * 128], identity=identb[:]),
                             reads=["ob", "identb"], writes=["PT"])
                S.op("vector", lambda e: e.tensor_copy(out=ot[:], in_=PT[:].rearrange("p (c t) -> p c t", c=2)), reads=["PT"], writes=[ot.name])
                S.dma("sync", odst(g), ot[:], reads=[ot.name], is_output=True)
                if dbg == 'g0':
                    raise _Stop
        except _Stop:
            pass
        S.close()


def nsa_consts():
    c = {}
    jj = np.arange(128)[:, None]
    q = np.arange(512)[None, :]
    c["base0"] = (q - jj).astype(np.float32) * -1.0
    c["basec0"] = -(q - 16 * jj - 31).astype(np.float32)
    rel2 = np.arange(5)[None, :, None]
    okc = (512 * rel2 + q[:, None, :].transpose(1, 0, 2) * 0 + q[None, :, :] * 1 - 16 * jj[:, :, None] - 31) >= 0
    c["cmask"] = np.where(okc, 0.0, -BIGNEG).astype(np.float32).astype(ml_dtypes.bfloat16)
    c["dmask"] = diag_mask_tiles(False).astype(ml_dtypes.bfloat16)
    r = (np.arange(8) - 4)[None, :, None]
    dd = q[None, :, :] - jj[:, :, None] - 128 * r
    c["wmask"] = np.where((dd >= 0) & (dd < 512), 0.0, -BIGNEG).astype(np.float32).astype(ml_dtypes.bfloat16)
    s_ = np.arange(128)[:, None]
    key = np.arange(T)[None, :]
    c["ind"] = np.where(key // 64 == s_, BIGNEG, 0.0).astype(np.float32).astype(ml_dtypes.bfloat16)
    n = np.arange(512)
    cs = n * 16
    ss = np.arange(128) * 64
    ov = ((cs[:, None] < ss[None, :] + 64) & (cs[:, None] + 32 > ss[None, :])).astype(np.float32)
    ov[511, :] = 0.0
    c["ovl"] = np.ascontiguousarray(ov.reshape(4, 128, 128).transpose(1, 0, 2)).astype(ml_dtypes.bfloat16)
    tt = np.arange(T)
    cur = tt // 64
    sid = np.arange(128)[None, :]
    forced = (sid == 0) | (sid == cur[:, None]) | (sid == cur[:, None] - 1)
    adj = np.where(forced, NSA_FORCE, 0.0)
    adj = np.where(sid <= cur[:, None], adj, NSA_NEG).astype(np.float32)
    c["adj"] = np.ascontiguousarray(adj.reshape(64, 128, 128))
    return c


_NSA_CONSTS = {}


def nsa_inputs(pfx, w_in, cmp_pe, cmp_w1, cmp_w2):
    if not _NSA_CONSTS:
        _NSA_CONSTS.update(nsa_consts())
    cst = _NSA_CONSTS
    slopes = alibi_slopes_np(16)
    cw1 = np.ascontiguousarray(np.concatenate([cmp_w1[j].reshape(32, 64, 256).transpose(1, 0, 2) for j in range(2)], axis=0))
    cpe = np.ascontiguousarray(np.concatenate([cmp_pe[j].T for j in range(2)], axis=0))
    cw2k = np.ascontiguousarray(np.concatenate([cmp_w2[0], cmp_w2[0]], axis=1).reshape(2, 128, 128).transpose(1, 0, 2))
    cw2v = np.ascontiguousarray(cmp_w2[1].reshape(2, 128, 64).transpose(1, 0, 2))
    rel = np.arange(NREL) - 3
    out = []
    for grp in range(4):
        hs = [4 * grp + r_ for r_ in range(4)]
        qc = np.arange(grp * 256, (grp + 1) * 256)
        kc = 1024 + grp * 64 + np.arange(64)
        vc, ks, vs, kw, vw = kc + 256, kc + 512, kc + 768, kc + 1024, kc + 1280
        gc = 2560 + grp * 12 + np.arange(12)
        fm_cols = np.concatenate([qc, ks, ks, kw, kw, kc, vc])
        tm_cols = np.concatenate([vs, vw, gc])
        sl = np.array([slopes[h] for h in hs])
        out.append({
            pfx + "wfm": arrange_w(w_in[:, fm_cols]),
            pfx + "wtm": arrange_w(w_in[:, tm_cols]),
            pfx + "cw1": cw1, pfx + "cpe": cpe, pfx + "cw2k": cw2k, pfx + "cw2v": cw2v,
            pfx + "ovl": cst["ovl"],
            pfx + "slp": np.ascontiguousarray(np.broadcast_to(sl[None, :], (128, 4))).astype(np.float32),
            pfx + "cb": np.ascontiguousarray(np.broadcast_to((-sl[:, None] * 128.0 * rel[None, :])[None], (128, 4, NREL))).astype(np.float32),
            pfx + "cbc": np.ascontiguousarray(np.broadcast_to((-sl[:, None] * 512.0 * np.arange(16)[None, :])[None], (128, 4, 16))).astype(np.float32),
            pfx + "base0": cst["base0"], pfx + "basec0": cst["basec0"], pfx + "cmask": cst["cmask"], pfx + "dmask": cst["dmask"],
            pfx + "wmask": cst["wmask"], pfx + "ind": cst["ind"], pfx + "adj": cst["adj"], pfx + "ident": _IDENT,
        })
    return out


DEPTH = 4
CC_GROUPS = [[0, 1, 2, 3], [4, 5, 6, 7]]


def build_fused(nphase=99):
    nc = bass.Bass("TRN2", target_bir_lowering=False)
    x_in = nc.dram_tensor("x", [TOK, D], F32, kind="ExternalInput").ap()
    out = nc.dram_tensor("out", [TOK, D], F32, kind="ExternalOutput").ap()
    x_scr = nc.dram_tensor("x_scr", [TOK, D], F32, kind="Internal").ap()
    hT_loc = [nc.dram_tensor("hT_loc%d" % i, [4, D, 512], BF16, kind="Internal").ap() for i in range(DEPTH)]
    hT_all = [nc.dram_tensor("hT_all%d" % i, [4, 4 * D, 512], BF16, kind="Internal").ap() for i in range(DEPTH)]
    o_loc = [nc.dram_tensor("o_loc%d" % i, [4, 256, 2048], BF16, kind="Internal").ap() for i in range(DEPTH)]
    o_all = [nc.dram_tensor("o_all%d" % i, [4, 4 * 256, 2048], BF16, kind="Internal").ap() for i in range(DEPTH)]
    rank = nc.sync.partition_id() % 4

    def allgather(name, src, dst):
        cs = nc.alloc_semaphore(name=name)
        for ch in range(4):
            nc.gpsimd.collective_compute("AllGather", ALU.bypass, replica_groups=CC_GROUPS, ins=[src[ch]], outs=[dst[ch]]).then_inc(cs, 1)
        for eng in (nc.gpsimd, nc.sync, nc.tensor, nc.vector, nc.scalar):
            eng.wait_ge(cs, 4)
        free_sems(nc, [cs])

    ph = [0]

    def go():
        ph[0] += 1
        return ph[0] <= nphase

    if go():
        emit_k1(nc, "k0_", False, 1, True, {"x": x_in, "xo": x_scr, "hTo": hT_loc[0], "hTo_chunked": True})
    for i in range(DEPTH):
        if go():
            allgather("ccA%d" % i, hT_loc[i], hT_all[i])
        binds = {"hT": hT_all[i], "oT": o_loc[i]}
        kind = i % 3
        if go():
            if kind == 0:
                emit_nsa(nc, "m%d_" % i, binds, True)
            elif kind == 1:
                emit_sb(nc, "m%d_" % i, binds, True)
            else:
                emit_diff(nc, "m%d_" % i, binds, 0.8 - 0.6 * math.exp(-0.3 * i), True)
        if go():
            allgather("ccB%d" % i, o_loc[i], o_all[i])
        oT_ap = o_all[i][rank]
        if go():
            if i < DEPTH - 1:
                emit_k1(nc, "k%d_" % (i + 1), True, 2, True, {"x": x_scr, "xo": x_scr, "hTo": hT_loc[i + 1], "hTo_chunked": True, "oT": oT_ap})
            else:
                emit_k1(nc, "k%d_" % (i + 1), True, 1, False, {"x": x_scr, "xo": out, "oT": oT_ap})
    return nc


_NPHASE = [99]


def kernel(x, c, ada_w, ada_b, norm_g, ffn_w1, ffn_w2, nsa_w_in, nsa_cmp_pe, nsa_cmp_w1, nsa_cmp_w2,
           nsa_w_out, sb_w_in, sb_w_out, diff_w_in, diff_lam, diff_subln_g, diff_w_out):
    f = lambda a: np.asarray(a, dtype=np.float32)
    x, c, ada_w, ada_b, norm_g, ffn_w1, ffn_w2 = map(f, (x, c, ada_w, ada_b, norm_g, ffn_w1, ffn_w2))
    nsa_w_in, nsa_cmp_pe, nsa_cmp_w1, nsa_cmp_w2, nsa_w_out = map(f, (nsa_w_in, nsa_cmp_pe, nsa_cmp_w1, nsa_cmp_w2, nsa_w_out))
    sb_w_in, sb_w_out, diff_w_in, diff_lam, diff_subln_g, diff_w_out = map(
        f, (sb_w_in, sb_w_out, diff_w_in, diff_lam, diff_subln_g, diff_w_out))
    nc = get_nc(("fused", _NPHASE[0]), lambda: build_fused(_NPHASE[0]))
    xt = x.reshape(B * T, D)
    common = {}
    per_b = [dict() for _ in range(B)]
    per_g = [dict() for _ in range(4)]

    def add_k1(pfx, mix, ffns, pre, wout=None):
        cm, pb = k1_inputs(pfx, c, ada_w, ada_b, norm_g, ffn_w1, ffn_w2, mix, ffns, pre, wout)
        common.update(cm)
        for b in range(B):
            per_b[b].update(pb[b])

    add_k1("k0_", None, [(0, 0)], 0)
    for i in range(DEPTH):
        kind, j = i % 3, i // 3
        pfx = "m%d_" % i
        if kind == 0:
            pg = nsa_inputs(pfx, nsa_w_in[j], nsa_cmp_pe[j], nsa_cmp_w1[j], nsa_cmp_w2[j])
            wout = nsa_w_out[j]
        elif kind == 1:
            pg = sb_inputs(pfx, sb_w_in[j])
            wout = sb_w_out[j]
        else:
            pg = diff_inputs(pfx, diff_w_in[j], diff_lam[j], diff_subln_g[j])
            wout = diff_w_out[j]
        for g in range(4):
            per_g[g].update(pg[g])
        if i < DEPTH - 1:
            add_k1("k%d_" % (i + 1), i, [(i, 1), (i + 1, 0)], i + 1, wout)
        else:
            add_k1("k%d_" % (i + 1), i, [(i, 1)], None, wout)
    in_maps = []
    for core in range(NCORE):
        b, g = core // 4, core % 4
        m = dict(common)
        m.update(per_b[b])
        m.update(per_g[g])
        m["x"] = np.ascontiguousarray(xt[core * TOK:(core + 1) * TOK])
        in_maps.append(m)
    if _NPHASE[0] < 99:
        npfx = ["k0_"]
        for i in range(DEPTH):
            npfx += [None, "m%d_" % i, None, "k%d_" % (i + 1)]
        keep = set(p for p in npfx[:_NPHASE[0]] if p)
        in_maps = [{k: v for k, v in m.items() if k == "x" or k[:3] in keep} for m in in_maps]
    res = run_bass_kernel_spmd(nc, in_maps, core_ids=list(range(NCORE)))
    xo = np.concatenate([res.results[i]["out"] for i in range(NCORE)], axis=0)
    return xo.reshape(B, T, D).astype(np.float32)
```

```python
import contextlib
import math
import numpy as np
import ml_dtypes
import concourse.bass as bass
import concourse.mybir as mybir
from concourse.bass_utils import run_bass_kernel_spmd

F32 = mybir.dt.float32
BF16 = mybir.dt.bfloat16
AF = mybir.ActivationFunctionType
ALU = mybir.AluOpType
AX = mybir.AxisListType

D = 1024
DFF = 2816
NFF = DFF // 128
B = 2
T = 8192
NCORE = 8
TOK = B * T // NCORE
NT = TOK // 128
EPS = 1e-6
NDMA = 24


def free_sems(nc, handles):
    nc.all_engine_barrier()
    nc.clear_and_free_semaphores(handles)
    nc.all_engine_barrier()


class Sched:
    def __init__(self, nc, es, pfx=""):
        self.nc = nc
        self.pfx = pfx
        self.engs = {}
        for name in ["tensor", "vector", "scalar", "gpsimd", "sync"]:
            sem = nc.alloc_semaphore(name=pfx + "sem_" + name)
            self.engs[name] = dict(obj=getattr(nc, name), sem=sem, cnt=0, waited={})
        self.dma_slots = [dict(sem=nc.alloc_semaphore(name=pfx + "dsem%d" % i), cnt=0) for i in range(NDMA)]
        self.dma_rr = 0
        self.last_write = {}
        self.reads = {}
        self.out_tokens = []
        self.psum_keys = set()

    def _wait(self, engname, tok):
        if tok is None:
            return
        semid, sem, val = tok
        if semid == engname and engname == "tensor":
            return
        e = self.engs[engname]
        if e["waited"].get(semid, 0) >= val:
            return
        e["obj"].wait_ge(sem, val)
        e["waited"][semid] = val

    def _norm(self, keys):
        p = self.pfx
        return [k[len(p):] if (p and k.startswith(p)) else k for k in keys]

    def _deps(self, engname, reads, writes):
        reads, writes = self._norm(reads), self._norm(writes)
        for k in reads:
            self._wait(engname, self.last_write.get(k))
            if k in self.psum_keys:
                for t in self.reads.get(k, []):
                    if t[0] != engname:
                        self._wait(engname, t)
        for k in writes:
            self._wait(engname, self.last_write.get(k))
            for t in self.reads.get(k, []):
                self._wait(engname, t)

    def _commit(self, tok, reads, writes):
        reads, writes = self._norm(reads), self._norm(writes)
        for k in writes:
            self.last_write[k] = tok
            self.reads[k] = []
        for k in reads:
            self.reads.setdefault(k, []).append(tok)

    def op(self, engname, fn, reads=(), writes=()):
        self._deps(engname, reads, writes)
        e = self.engs[engname]
        ins = fn(e["obj"])
        e["cnt"] += 1
        ins.then_inc(e["sem"], 1)
        tok = (engname, e["sem"], e["cnt"])
        self._commit(tok, reads, writes)
        return tok

    def dma(self, queue, out, in_, reads=(), writes=(), is_output=False):
        self._deps(queue, reads, writes)
        idx = self.dma_rr
        slot = self.dma_slots[idx]
        self.dma_rr = (self.dma_rr + 1) % NDMA
        if slot["cnt"] > 0:
            self._wait(queue, ("d%d" % idx, slot["sem"], slot["cnt"] * 16))
        ins = self.engs[queue]["obj"].dma_start(out=out, in_=in_)
        slot["cnt"] += 1
        ins.then_inc(slot["sem"], 16)
        tok = ("d%d" % idx, slot["sem"], slot["cnt"] * 16)
        self._commit(tok, reads, writes)
        if is_output:
            self.out_tokens.append(tok)
        return tok

    def barrier(self):
        toks = []
        for idx, slot in enumerate(self.dma_slots):
            if slot["cnt"] > 0:
                toks.append(("d%d" % idx, slot["sem"], slot["cnt"] * 16))
        for name, e in self.engs.items():
            if e["cnt"] > 0:
                toks.append((name, e["sem"], e["cnt"]))
        for name in self.engs:
            for tk in toks:
                if tk[0] != name:
                    self._wait(name, tk)

    def close(self):
        self.barrier()
        handles = [e["sem"] for e in self.engs.values()] + [sl["sem"] for sl in self.dma_slots]
        free_sems(self.nc, handles)

    def finish(self):
        for idx, slot in enumerate(self.dma_slots):
            if slot["cnt"] > 0:
                self._wait("sync", ("d%d" % idx, slot["sem"], slot["cnt"] * 16))
        for name, e in self.engs.items():
            if name != "sync" and e["cnt"] > 0:
                self._wait("sync", (name, e["sem"], e["cnt"]))


def emit_k1(nc, pfx, mix, n_ffn, pre, binds, rows):
    nv = (1 if mix else 0) + 3 * n_ffn + (2 if pre else 0)
    ng = (1 if mix else 0) + 2 * n_ffn + (1 if pre else 0)
    dr = {}

    def din(name, shape, dt=F32, kind="ExternalInput"):
        if name in binds:
            dr[name] = binds[name]
        else:
            dr[name] = nc.dram_tensor(pfx + name, list(shape), dt, kind=kind).ap()

    din("x", [TOK, D])
    din("ident", [128, 128])
    modrows = binds["modrows"]
    if mix:
        din("oT", [D, TOK], BF16)
        din("wout", [D, D])
    for f in range(n_ffn):
        din("w1_%d" % f, [NFF, 128, 8, 256])
        din("w2_%d" % f, [DFF, D])
    din("xo", [TOK, D], F32, "ExternalOutput")
    if pre:
        din("hTo", [D, TOK], BF16, "ExternalOutput")

    es = contextlib.ExitStack()
    with es:
        S = Sched(nc, es, pfx)

        def sb(name, shape, dt):
            return es.enter_context(nc.sbuf_tensor(pfx + name, shape, dt))

        def ps(name, shape, dt):
            return es.enter_context(nc.psum_tensor(pfx + name, shape, dt))

        xs = sb("xs", [128, NT, D], F32)
        hT = sb("hT", [128, 8, 1024], BF16)
        actT = sb("actT", [128, NFF, 1024], BF16)
        w1c = [sb("w1c%d" % i, [128, 8, 256], BF16) for i in range(3)]
        w2c = [sb("w2c%d" % i, [128, 1024], BF16) for i in range(3)]
        prmsets = [[sb("prm%d_%d" % (j, i), [128, D], F32) for i in range(3)] for j in range(2)]
        scr = [sb("scr%d" % i, [128, D], F32) for i in range(2)]
        hb = [sb("hb%d" % i, [128, D], BF16) for i in range(2)]
        junk = sb("junk", [128, D], BF16)
        sil = [sb("sil%d" % i, [128, 512], F32) for i in range(2)]
        identf = sb("identf", [128, 128], F32)
        identb = sb("identb", [128, 128], BF16)
        st = sb("st", [128, 8], F32)
        epsb = sb("epsb", [128, 1], F32)
        PA = ps("PA", [128, 1024], F32)
        PB = ps("PB", [128, 1024], F32)
        PC = ps("PC", [128, 1024], F32)
        PT = ps("PT", [128, 2048], BF16)
        S.psum_keys.update(["PA0", "PA1", "PB0", "PB1", "PC", "PT0", "PT1"])

        xin = dr["x"].rearrange("(n p) d -> p n d", p=128)
        for q4 in range(4):
            S.dma("sync", xs[:, q4 * 4:(q4 + 1) * 4, :], xin[:, q4 * 4:(q4 + 1) * 4, :],
                  writes=["xs%d" % n for n in range(q4 * 4, q4 * 4 + 4)])
        S.dma("sync", identf[:], dr["ident"][:, :], writes=["identf"])
        S.op("vector", lambda e: e.tensor_copy(out=identb[:], in_=identf[:]), reads=["identf"], writes=["identb"])
        S.op("vector", lambda e: e.memset(epsb[:], EPS), writes=["epsb"])

        def load_row(row, dst):
            S.dma("sync", dst[:], modrows[row:row + 1, :].partition_broadcast(128), writes=[dst.name])

        pset = [0]

        def next_prm():
            pset[0] += 1
            return prmsets[pset[0] % 2]

        stc = [0]

        def rstd_of(src_ap, src_keys, col):
            S.op("scalar", lambda e: e.activation(out=junk[:], in_=src_ap, func=AF.Square, accum_out=st[:, col:col + 1]),
                 reads=src_keys, writes=["junk", "st%d" % col])
            S.op("scalar", lambda e: e.activation(out=st[:, col:col + 1], in_=st[:, col:col + 1], func=AF.Sqrt,
                                                  bias=epsb[:], scale=1.0 / D),
                 reads=["st%d" % col, "epsb"], writes=["st%d" % col])
            S.op("vector", lambda e: e.reciprocal(out=st[:, col:col + 1], in_=st[:, col:col + 1]),
                 reads=["st%d" % col], writes=["st%d" % col])

        def prenorm_tile(n, A, Bv, i):
            col = stc[0] % 4
            stc[0] += 1
            rstd_of(xs[:, n, :], ["xs%d" % n], col)
            S.op("vector", lambda e: e.scalar_tensor_tensor(out=scr[1][:], in0=xs[:, n, :], scalar=st[:, col:col + 1], in1=A[:],
                                                            op0=ALU.mult, op1=ALU.mult),
                 reads=["xs%d" % n, "st%d" % col, A.name], writes=["scr1"])
            S.op("gpsimd", lambda e: e.tensor_tensor(out=hb[i][:], in0=scr[1][:], in1=Bv[:], op=ALU.add),
                 reads=["scr1", Bv.name], writes=["hb%d" % i])

        def transpose_tile(i, half, dst_fn, dst_keys):
            pk = "PT%d" % half
            for k in range(8):
                S.op("tensor", lambda e, k=k: e.transpose(out=PT[:, half * 1024 + k * 128: half * 1024 + (k + 1) * 128],
                                                          in_=hb[i][:, k * 128:(k + 1) * 128], identity=identb[:]),
                     reads=["hb%d" % i, "identb"], writes=[pk])
            S.op("scalar", lambda e: e.activation(out=dst_fn(), in_=PT[:, half * 1024:(half + 1) * 1024].rearrange("p (k t) -> p k t", k=8),
                                                  func=AF.Copy),
                 reads=[pk], writes=dst_keys)

        def epilogue(Y, n, G):
            col = 4 + stc[0] % 4
            stc[0] += 1
            rstd_of(Y[:], [Y.name + "0", Y.name + "1"], col)
            S.op("vector", lambda e: e.scalar_tensor_tensor(out=scr[0][:], in0=Y[:], scalar=st[:, col:col + 1], in1=G[:],
                                                            op0=ALU.mult, op1=ALU.mult),
                 reads=[Y.name + "0", Y.name + "1", "st%d" % col, G.name], writes=["scr0"])
            S.op("gpsimd", lambda e: e.tensor_tensor(out=xs[:, n, :], in0=xs[:, n, :], in1=scr[0][:], op=ALU.add),
                 reads=["scr0", "xs%d" % n], writes=["xs%d" % n])

        vi = 0
        gi = 0
        Yb = [PA, PB]
        if mix:
            prm = next_prm()
            load_row(rows["mix"], prm[2])
            wo = actT[:, 0:8, :]
            S.dma("gpsimd", wo, dr["wout"].rearrange("(k p) n -> p k n", p=128), writes=["actT"])
            for half in range(2):
                oTs = actT[:, 8:16, :]
                S.dma("sync", oTs, dr["oT"][:, half * 1024:(half + 1) * 1024].rearrange("(k p) t -> p k t", p=128),
                      writes=["actT_o"])
                for tt in range(8):
                    n = half * 8 + tt
                    Y = Yb[n % 2]
                    for h2 in range(2):
                        for k in range(8):
                            S.op("tensor", lambda e, k=k, h2=h2, tt=tt, Y=Y: e.matmul(
                                Y[:, h2 * 512:(h2 + 1) * 512], lhsT=actT[:, 8 + k, tt * 128:(tt + 1) * 128],
                                rhs=actT[:, k, h2 * 512:(h2 + 1) * 512], start=(k == 0), stop=(k == 7)),
                                reads=["actT", "actT_o"], writes=[Y.name + str(h2)])
                    epilogue(Y, n, prm[2])

        wctr = [0, 0]
        for f in range(n_ffn):
            prm = next_prm()
            load_row(rows["ffn"][f][0], prm[0])
            load_row(rows["ffn"][f][1], prm[1])
            load_row(rows["ffn"][f][2], prm[2])
            w1d = dr["w1_%d" % f]
            w2d = dr["w2_%d" % f]
            for grp in range(2):
                for tt in range(8):
                    n = grp * 8 + tt
                    i = n % 2
                    prenorm_tile(n, prm[0], prm[1], i)
                    transpose_tile(i, n % 2, lambda tt=tt: hT[:, :, tt * 128:(tt + 1) * 128], ["hT"])
                for j in range(NFF):
                    wi = wctr[0] % 3
                    wctr[0] += 1
                    S.dma("gpsimd", w1c[wi][:], w1d[j], writes=["w1c%da" % wi, "w1c%db" % wi])
                    for th in range(2):
                        Pg = PA if th == 0 else PB
                        for part, (c0, key) in enumerate([(0, "a"), (128, "b")]):
                            for k in range(8):
                                S.op("tensor", lambda e, k=k, th=th, c0=c0, part=part, Pg=Pg, wi=wi: e.matmul(
                                    Pg[:, part * 512:(part + 1) * 512], lhsT=w1c[wi][:, k, c0:c0 + 128],
                                    rhs=hT[:, k, th * 512:(th + 1) * 512], start=(k == 0), stop=(k == 7)),
                                    reads=["hT", "w1c%d%s" % (wi, key)], writes=[Pg.name + str(part)])
                        si = th
                        S.op("scalar", lambda e, Pg=Pg, si=si: e.activation(out=sil[si][:], in_=Pg[:, 0:512], func=AF.Silu),
                             reads=[Pg.name + "0"], writes=["sil%d" % si])
                        S.op("vector", lambda e, Pg=Pg, si=si, j=j, th=th: e.tensor_tensor(
                            out=actT[:, j, th * 512:(th + 1) * 512], in0=Pg[:, 512:1024], in1=sil[si][:], op=ALU.mult),
                            reads=[Pg.name + "1", "sil%d" % si], writes=["actT%d" % j])
                for tp in range(4):
                    for j in range(NFF):
                        wi = wctr[1] % 3
                        wctr[1] += 1
                        S.dma("gpsimd", w2c[wi][:], w2d[j * 128:(j + 1) * 128, :], writes=["w2c%d" % wi])
                        for q in range(2):
                            tt = tp * 2 + q
                            Y = Yb[q]
                            for h2 in range(2):
                                S.op("tensor", lambda e, j=j, tt=tt, h2=h2, Y=Y, wi=wi: e.matmul(
                                    Y[:, h2 * 512:(h2 + 1) * 512], lhsT=actT[:, j, tt * 128:(tt + 1) * 128],
                                    rhs=w2c[wi][:, h2 * 512:(h2 + 1) * 512], start=(j == 0), stop=(j == NFF - 1)),
                                    reads=["actT%d" % j, "w2c%d" % wi], writes=[Y.name + str(h2)])
                    for q in range(2):
                        epilogue(Yb[q], grp * 8 + tp * 2 + q, prm[2])

        if pre:
            prm = next_prm()
            load_row(rows["pre"][0], prm[0])
            load_row(rows["pre"][1], prm[1])
            if binds.get("hTo_chunked"):
                def hTo_dst(n):
                    return dr["hTo"][n // 4, :, (n % 4) * 128:(n % 4 + 1) * 128].rearrange("(k p) t -> p k t", p=128)
            else:
                def hTo_dst(n):
                    return dr["hTo"].rearrange("(k p) t -> p k t", p=128)[:, :, n * 128:(n + 1) * 128]
            for n in range(NT):
                i = n % 2
                prenorm_tile(n, prm[0], prm[1], i)
                transpose_tile(i, n % 2, lambda n=n: hT[:, :, (n % 8) * 128:(n % 8 + 1) * 128], ["hTo%d" % (n % 8)])
                S.dma("sync", hTo_dst(n), hT[:, :, (n % 8) * 128:(n % 8 + 1) * 128],
                      reads=["hTo%d" % (n % 8)], is_output=True)
        xout = dr["xo"].rearrange("(n p) d -> p n d", p=128)
        for q4 in range(4):
            S.dma("sync", xout[:, q4 * 4:(q4 + 1) * 4, :], xs[:, q4 * 4:(q4 + 1) * 4, :],
                  reads=["xs%d" % n for n in range(q4 * 4, q4 * 4 + 4)], is_output=True)
        S.close()


def arrange_w1(w1):
    g = w1[:, :DFF].reshape(8, 128, NFF, 128)
    u = w1[:, DFF:].reshape(8, 128, NFF, 128)
    cat = np.concatenate([g, u], axis=3)
    return np.ascontiguousarray(cat.transpose(2, 1, 0, 3))


def arrange_adaw(cols):
    return np.ascontiguousarray(cols.reshape(8, 128, 8, 128).transpose(2, 1, 0, 3))


_IDENT = np.eye(128, dtype=np.float32)
_NC_CACHE = {}


def get_nc(key, builder):
    if key not in _NC_CACHE:
        _NC_CACHE[key] = builder()
    return _NC_CACHE[key]


def k1_inputs(pfx, ffn_w1, ffn_w2, mix, ffns, wout=None):
    common = {pfx + "ident": _IDENT}
    for f, (l, w) in enumerate(ffns):
        common[pfx + "w1_%d" % f] = arrange_w1(ffn_w1[l][w])
        common[pfx + "w2_%d" % f] = np.ascontiguousarray(ffn_w2[l][w])
    if mix is not None:
        common[pfx + "wout"] = np.ascontiguousarray(wout)
    return common


def k1_rows(mix, ffns, pre):
    rows = {"ffn": []}
    if mix is not None:
        rows["mix"] = mix * 9 + 3 + 2
    for (l, w) in ffns:
        base = l * 9 + (0 if w == 0 else 2) * 3
        rows["ffn"].append((base, base + 1, base + 2))
    if pre is not None:
        rows["pre"] = (pre * 9 + 3, pre * 9 + 4)
    return rows


def mods_inputs(pfx, c, ada_w, ada_b, norm_g):
    per_r = []
    for r in range(4):
        adaw = np.ascontiguousarray(ada_w[r].reshape(8, 128, 9, D).transpose(2, 1, 0, 3))
        per_r.append({pfx + "adaw": adaw, pfx + "adab": np.ascontiguousarray(ada_b[r].reshape(1, 9, D)),
                      pfx + "g": np.ascontiguousarray(norm_g[r].reshape(1, 6, D))})
    per_b = [{pfx + "c": np.ascontiguousarray(c[b].reshape(8, 128).T)} for b in range(B)]
    return per_r, per_b


BIGNEG = 240000.0


class Ctx:
    def __init__(self, nc, pfx, binds):
        self.nc = nc
        self.pfx = pfx
        self.binds = binds
        self.es = contextlib.ExitStack()
        self.dr = {}
        self.psum_names = []

    def din(self, name, shape, dt=F32):
        if name in self.binds:
            self.dr[name] = self.binds[name]
        else:
            self.dr[name] = self.nc.dram_tensor(self.pfx + name, list(shape), dt, kind="ExternalInput").ap()
        return self.dr[name]

    def dout(self, name, shape, dt=F32):
        if name in self.binds:
            self.dr[name] = self.binds[name]
        else:
            self.dr[name] = self.nc.dram_tensor(self.pfx + name, list(shape), dt, kind="ExternalOutput").ap()
        return self.dr[name]

    def sb(self, name, shape, dt):
        return self.es.enter_context(self.nc.sbuf_tensor(self.pfx + "s_" + name, list(shape), dt))

    def ps(self, name, shape, dt):
        self.psum_names.append(name)
        return self.es.enter_context(self.nc.psum_tensor(self.pfx + name, list(shape), dt))


def o_dest(oTd, gathered):
    if gathered:
        def f(g):
            return oTd[g // 4, :, (g % 4) * 512:(g % 4 + 1) * 512].rearrange("(c p) t -> p c t", p=128)
    else:
        def f(g):
            return oTd[:, g * 512:(g + 1) * 512].rearrange("(c p) t -> p c t", p=128)
    return f


def hT_source(hTd, gathered):
    if gathered:
        def f(tg):
            r, tc = tg // 4, tg % 4
            return hTd[tc, r * D:(r + 1) * D, :].rearrange("(k p) t -> p k t", p=128)
    else:
        def f(tg):
            return hTd[:, tg * 512:(tg + 1) * 512].rearrange("(k p) t -> p k t", p=128)
    return f


def flash_pipeline(S, tiles, emit_s, emit_mid, emit_av, lookahead=2):
    pend = []
    for tl in tiles:
        emit_s(tl)
        emit_mid(tl)
        pend.append(tl)
        if len(pend) > lookahead:
            emit_av(pend.pop(0))
    for tl in pend:
        emit_av(tl)


def emit_mods(nc, pfx, binds):
    C = Ctx(nc, pfx, binds)
    es = C.es
    adawd = C.din("adaw", [9, 128, 8, D])
    adabd = C.din("adab", [1, 9, D])
    gd = C.din("g", [1, 6, D])
    cd = C.din("c", [128, 8])
    outd = binds["rows_out"]
    with es:
        S = Sched(nc, es, pfx)
        adas = [C.sb("adas%d" % i, [128, 8, D], F32) for i in range(2)]
        adab = C.sb("adab", [1, 9, D], F32)
        gl = C.sb("gl", [1, 6, D], F32)
        cst = C.sb("cst", [128, 8], F32)
        cond = C.sb("cond", [128, 8], F32)
        mod = C.sb("mod", [1, 9, D], F32)
        R = C.sb("R", [1, 9, D], F32)
        tmp = C.sb("tmp", [1, D], F32)
        PP = [C.ps("PP%d" % i, [128, 512], F32) for i in range(4)]
        S.psum_keys.update(C.psum_names)
        S.dma("sync", cst[:], cd, writes=["cst"])
        S.dma("sync", adab[:], adabd, writes=["adab"])
        S.dma("sync", gl[:], gd, writes=["gl"])
        S.op("scalar", lambda e: e.activation(out=cond[:], in_=cst[:], func=AF.Silu), reads=["cst"], writes=["cond"])
        pc = [0]
        for v in range(9):
            ad = adas[v % 2]
            S.dma("sync" if v % 2 == 0 else "scalar", ad[:], adawd[v], writes=[ad.name])
            for h2 in range(2):
                Pp = PP[pc[0] % 4]
                pc[0] += 1
                for k in range(8):
                    S.op("tensor", lambda e: e.matmul(Pp[0:1, :], lhsT=cond[:, k:k + 1], rhs=ad[:, k, h2 * 512:(h2 + 1) * 512],
                                                      start=(k == 0), stop=(k == 7)),
                         reads=["cond", ad.name], writes=[Pp.name])
                S.op("vector", lambda e: e.tensor_tensor(out=mod[0:1, v, h2 * 512:(h2 + 1) * 512], in0=Pp[0:1, :],
                                                         in1=adab[0:1, v, h2 * 512:(h2 + 1) * 512], op=ALU.add),
                     reads=[Pp.name, "adab"], writes=["mod%d" % v])
        for s_ in range(3):
            res_w = 1.0 if s_ == 1 else 0.5
            sh, sc, gt = 3 * s_, 3 * s_ + 1, 3 * s_ + 2
            S.op("vector", lambda e: e.tensor_scalar(out=tmp[:], in0=mod[0:1, sc, :], scalar1=1.0, scalar2=None, op0=ALU.add),
                 reads=["mod%d" % sc], writes=["tmp"])
            S.op("vector", lambda e: e.tensor_tensor(out=R[0:1, 3 * s_ + 0, :], in0=tmp[:], in1=gl[0:1, 2 * s_, :], op=ALU.mult),
                 reads=["tmp", "gl"], writes=["R"])
            S.op("vector", lambda e: e.tensor_copy(out=R[0:1, 3 * s_ + 1, :], in_=mod[0:1, sh, :]), reads=["mod%d" % sh, "R"], writes=["R"])
            S.op("vector", lambda e: e.tensor_scalar(out=tmp[:], in0=mod[0:1, gt, :], scalar1=float(res_w), scalar2=None, op0=ALU.mult),
                 reads=["mod%d" % gt, "R"], writes=["tmp"])
            S.op("vector", lambda e: e.tensor_tensor(out=R[0:1, 3 * s_ + 2, :], in0=tmp[:], in1=gl[0:1, 2 * s_ + 1, :], op=ALU.mult),
                 reads=["tmp", "gl", "R"], writes=["R"])
        S.dma("sync", outd.rearrange("(o r) d -> o r d", o=1), R[:], reads=["R"], is_output=True)
        S.close()


NREL = 67


def emit_diff(nc, pfx, binds, lam_init, gathered):
    C = Ctx(nc, pfx, binds)
    nc, es = C.nc, C.es
    hTd = C.din("hT", [D, T], BF16)
    hsrc = hT_source(hTd, gathered)
    odst = None
    wqd = C.din("wq", [128, 8, 256])
    wkd = C.din("wk", [128, 8, 256])
    wvd = C.din("wv", [128, 8, 256])
    based = C.din("base", [128, 2, 512])
    cbd = C.din("cb", [128, 2, NREL])
    dmaskd = C.din("dmask", [128, 4, 512], BF16)
    lamd = C.din("lam", [128, 256])
    subgd = C.din("subg", [128, 128])
    identd = C.din("ident", [128, 128])
    oTd = C.dout("oT", [256, T], BF16)
    odst = o_dest(oTd, gathered)
    with es:
        S = Sched(nc, es, pfx)
        QT = [C.sb("QT%d" % h, [128, T], BF16) for h in range(2)]
        KT = [C.sb("KT%d" % h, [128, T], BF16) for h in range(2)]
        Vaug = C.sb("Vaug", [128, 64, 2, 129], BF16)
        wq = C.sb("wq", [128, 8, 256], BF16)
        wk = C.sb("wk", [128, 8, 256], BF16)
        wv = C.sb("wv", [128, 8, 256], BF16)
        hTs = [C.sb("hTs%d" % i, [128, 8, 512], BF16) for i in range(2)]
        base = C.sb("base", [128, 2, 512], F32)
        cbt = C.sb("cbt", [128, 2, NREL], F32)
        dmask = C.sb("dmask", [128, 4, 512], BF16)
        tb = [C.sb("tb%d" % i, [128, 512], F32) for i in range(2)]
        Pb = [C.sb("Pb%d" % i, [128, 512], BF16) for i in range(4)]
        lam = C.sb("lam", [128, 256], F32)
        subg = C.sb("subg", [128, 128], F32)
        identf = C.sb("identf", [128, 128], F32)
        identb = C.sb("identb", [128, 128], BF16)
        sm = C.sb("sm", [128, 16], F32)
        lamneg = C.sb("lamneg", [128, 1], F32)
        epsb = C.sb("epsb", [128, 1], F32)
        o0s = C.sb("o0s", [128, 128], F32)
        od = C.sb("od", [128, 128], F32)
        junk = C.sb("junk", [128, 256], F32)
        ob = C.sb("ob", [128, 4, 256], BF16)
        oTs = [C.sb("oTs%d" % i, [128, 2, 512], BF16) for i in range(2)]
        SB_ = [C.ps("Sb%d" % i, [128, 512], F32) for i in range(3)]
        OB = [[C.ps("O%d%d" % (m, p), [128, 512], F32) for p in range(2)] for m in range(2)]
        PT = C.ps("PT", [128, 1024], BF16)
        S.psum_keys.update(C.psum_names)

        S.dma("sync", identf[:], identd[:, :], writes=["identf"])
        S.op("vector", lambda e: e.tensor_copy(out=identb[:], in_=identf[:]), reads=["identf"], writes=["identb"])
        S.dma("gpsimd", wq[:], wqd[:, :, :], writes=["wq"])
        S.dma("gpsimd", wk[:], wkd[:, :, :], writes=["wk"])
        S.dma("gpsimd", wv[:], wvd[:, :, :], writes=["wv"])
        S.dma("sync", base[:], based[:, :, :], writes=["base"])
        S.dma("sync", cbt[:], cbd[:, :, :], writes=["cbt"])
        S.dma("sync", dmask[:], dmaskd[:, :, :], writes=["dmask"])
        S.dma("sync", lam[:], lamd[:, :], writes=["lam"])
        S.dma("sync", subg[:], subgd[:, :], writes=["subg"])
        S.op("vector", lambda e: e.memset(epsb[:], EPS), writes=["epsb"])
        S.op("vector", lambda e: e.memset(Vaug[:, :, :, 128:129], 1.0), writes=["Vones"])
        S.op("vector", lambda e: e.scalar_tensor_tensor(out=junk[:, 0:64], in0=lam[:, 0:64], scalar=1.0, in1=lam[:, 64:128],
                                                        op0=ALU.mult, op1=ALU.mult, accum_out=sm[:, 0:1]),
             reads=["lam"], writes=["junk", "sm0"])
        S.op("vector", lambda e: e.scalar_tensor_tensor(out=junk[:, 0:64], in0=lam[:, 128:192], scalar=1.0, in1=lam[:, 192:256],
                                                        op0=ALU.mult, op1=ALU.mult, accum_out=sm[:, 1:2]),
             reads=["lam", "junk"], writes=["junk", "sm1"])
        S.op("scalar", lambda e: e.activation(out=sm[:, 0:2], in_=sm[:, 0:2], func=AF.Exp), reads=["sm0", "sm1"],
             writes=["sm0", "sm1"])
        S.op("vector", lambda e: e.tensor_tensor(out=sm[:, 2:3], in0=sm[:, 1:2], in1=sm[:, 0:1], op=ALU.subtract),
             reads=["sm0", "sm1"], writes=["sm2"])
        S.op("vector", lambda e: e.tensor_scalar(out=lamneg[:], in0=sm[:, 2:3], scalar1=-float(lam_init), scalar2=None, op0=ALU.add),
             reads=["sm2"], writes=["lamneg"])
        S.op("vector", lambda e: e.tensor_scalar(out=subg[:], in0=subg[:], scalar1=float(1.0 - lam_init), scalar2=None, op0=ALU.mult),
             reads=["subg"], writes=["subg"])

        cp = [0]

        def evac(dst, src, rk, wk_):
            eng = "scalar" if cp[0] % 2 == 0 else "vector"
            cp[0] += 1
            if eng == "scalar":
                S.op("scalar", lambda e: e.activation(out=dst, in_=src, func=AF.Copy), reads=rk, writes=wk_)
            else:
                S.op("vector", lambda e: e.tensor_copy(out=dst, in_=src), reads=rk, writes=wk_)

        bank = [0]
        for tg in range(16):
            hs = hTs[tg % 2]
            hk = "hTs%d" % (tg % 2)
            S.dma("sync", hs[:], hsrc(tg), writes=[hk])
            for (w, wname, dstl, dname) in [(wq, "wq", QT, "QT"), (wk, "wk", KT, "KT")]:
                for h in range(2):
                    Pp = SB_[bank[0] % 3]
                    bank[0] += 1
                    for k in range(8):
                        S.op("tensor", lambda e: e.matmul(Pp[:], lhsT=w[:, k, h * 128:(h + 1) * 128], rhs=hs[:, k, :],
                                                          start=(k == 0), stop=(k == 7)),
                             reads=[wname, hk], writes=[Pp.name])
                    evac(dstl[h][:, tg * 512:(tg + 1) * 512], Pp[:], [Pp.name], ["%s%d_%d" % (dname, h, tg)])
            for tt in range(4):
                Pp = SB_[bank[0] % 3]
                bank[0] += 1
                for k in range(8):
                    S.op("tensor", lambda e: e.matmul(Pp[:, 0:256], lhsT=hs[:, k, tt * 128:(tt + 1) * 128], rhs=wv[:, k, :],
                                                      start=(k == 0), stop=(k == 7)),
                         reads=["wv", hk], writes=[Pp.name])
                blk = tg * 4 + tt
                evac(Vaug[:, blk, :, 0:128], Pp[:, 0:256].rearrange("p (h e) -> p h e", h=2), [Pp.name], ["V_%d" % blk])

        scale = 64 ** -0.5
        ctr = {"s": 0, "t": 0, "p": 0, "o": 0}
        for g in range(16):
            for h in range(2):
                tiles = [dict(kb=kb, m=m) for kb in range(4 * g + 4) for m in range(2)]
                first_av = {}

                def emit_s(tl):
                    kb, m = tl["kb"], tl["m"]
                    Sp = SB_[ctr["s"] % 3]
                    ctr["s"] += 1
                    tl["Sp"] = Sp
                    r = kb - 4 * g
                    S.op("tensor", lambda e: e.matmul(Sp[:], lhsT=KT[h][m * 64:(m + 1) * 64, kb * 128:(kb + 1) * 128],
                                                      rhs=QT[h][m * 64:(m + 1) * 64, g * 512:(g + 1) * 512], start=True, stop=(r < 0)),
                         reads=["KT%d_%d" % (h, kb // 4), "QT%d_%d" % (h, g)], writes=[Sp.name])
                    if r >= 0:
                        S.op("tensor", lambda e: e.matmul(Sp[:], lhsT=identb[:], rhs=dmask[:, r, :], start=False, stop=True),
                             reads=["identb", "dmask"], writes=[Sp.name])

                def emit_mid(tl):
                    kb, m, Sp = tl["kb"], tl["m"], tl["Sp"]
                    tt_ = tb[ctr["t"] % 2]
                    ctr["t"] += 1
                    Pt = Pb[ctr["p"] % 4]
                    ctr["p"] += 1
                    tl["P"] = Pt
                    S.op("vector", lambda e: e.scalar_tensor_tensor(out=tt_[:], in0=Sp[:], scalar=scale, in1=base[:, h, :],
                                                                    op0=ALU.mult, op1=ALU.add),
                         reads=[Sp.name, "base"], writes=[tt_.name])
                    rel = 4 * g - kb + 3
                    S.op("scalar", lambda e: e.activation(out=Pt[:], in_=tt_[:], func=AF.Exp, bias=cbt[:, h, rel:rel + 1], scale=1.0),
                         reads=[tt_.name, "cbt"], writes=[Pt.name])

                def emit_av(tl):
                    kb, m, Pt = tl["kb"], tl["m"], tl["P"]
                    r = kb - 4 * g
                    for qb in range(4):
                        if qb < r:
                            continue
                        p = qb // 2
                        O = OB[m][p]
                        st_ = (m, p) not in first_av
                        first_av[(m, p)] = True
                        c0 = (qb % 2) * 129
                        S.op("tensor", lambda e: e.matmul(O[:, c0:c0 + 129], lhsT=Pt[:, qb * 128:(qb + 1) * 128], rhs=Vaug[:, kb, h, :],
                                                          start=st_, stop=(kb == 4 * g + qb), skip_group_check=True),
                             reads=[Pt.name, "V_%d" % kb, "Vones"], writes=[O.name])

                flash_pipeline(S, tiles, emit_s, emit_mid, emit_av)
                for qb in range(4):
                    p = qb // 2
                    c0 = (qb % 2) * 129
                    O0, O1 = OB[0][p], OB[1][p]
                    S.op("vector", lambda e: e.reciprocal(out=sm[:, 4:5], in_=O0[:, c0 + 128:c0 + 129]), reads=[O0.name], writes=["sm4"])
                    S.op("vector", lambda e: e.reciprocal(out=sm[:, 5:6], in_=O1[:, c0 + 128:c0 + 129]), reads=[O1.name], writes=["sm5"])
                    S.op("vector", lambda e: e.tensor_tensor(out=sm[:, 5:6], in0=sm[:, 5:6], in1=lamneg[:], op=ALU.mult),
                         reads=["sm5", "lamneg"], writes=["sm5"])
                    S.op("vector", lambda e: e.tensor_scalar(out=o0s[:], in0=O0[:, c0:c0 + 128], scalar1=sm[:, 4:5], scalar2=None, op0=ALU.mult),
                         reads=[O0.name, "sm4"], writes=["o0s"])
                    S.op("vector", lambda e: e.scalar_tensor_tensor(out=od[:], in0=O1[:, c0:c0 + 128], scalar=sm[:, 5:6], in1=o0s[:],
                                                                    op0=ALU.mult, op1=ALU.add),
                         reads=[O1.name, "sm5", "o0s"], writes=["od"])
                    S.op("scalar", lambda e: e.activation(out=junk[:, 0:128], in_=od[:], func=AF.Square, accum_out=sm[:, 6:7]),
                         reads=["od"], writes=["junk", "sm6"])
                    S.op("scalar", lambda e: e.activation(out=sm[:, 6:7], in_=sm[:, 6:7], func=AF.Sqrt, bias=epsb[:], scale=1.0 / 128),
                         reads=["sm6", "epsb"], writes=["sm6"])
                    S.op("vector", lambda e: e.reciprocal(out=sm[:, 6:7], in_=sm[:, 6:7]), reads=["sm6"], writes=["sm6"])
                    S.op("vector", lambda e: e.scalar_tensor_tensor(out=ob[:, qb, h * 128:(h + 1) * 128], in0=od[:], scalar=sm[:, 6:7],
                                                                    in1=subg[:], op0=ALU.mult, op1=ALU.mult),
                         reads=["od", "sm6", "subg"], writes=["ob"])
            ot = oTs[ctr["o"] % 2]
            otk = "oTs%d" % (ctr["o"] % 2)
            ctr["o"] += 1
            for qb in range(4):
                for c in range(2):
                    S.op("tensor", lambda e: e.transpose(out=PT[:, c * 512 + qb * 128:c * 512 + (qb + 1) * 128],
                                                         in_=ob[:, qb, c * 128:(c + 1) * 128], identity=identb[:]),
                         reads=["ob", "identb"], writes=["PT"])
            S.op("vector", lambda e: e.tensor_copy(out=ot[:], in_=PT[:].rearrange("p (c t) -> p c t", c=2)), reads=["PT"], writes=[otk])
            S.dma("sync", odst(g), ot[:], reads=[otk], is_output=True)
        S.close()


def alibi_slopes_np(n):
    return np.exp2(-8.0 * np.arange(1, n + 1, dtype=np.float64) / n)


def diag_mask_tiles(strict):
    jj = np.arange(128)[:, None, None]
    r = np.arange(4)[None, :, None]
    q = np.arange(512)[None, None, :]
    d = q - jj - 128 * r
    ok = d >= (1 if strict else 0)
    return np.where(ok, 0.0, -BIGNEG).astype(np.float32)


def base_tile(slope):
    jj = np.arange(128)[:, None]
    q = np.arange(512)[None, :]
    return (-slope * (q - jj)).astype(np.float32)


def cb_table(slope):
    rel = np.arange(NREL) - 3
    return np.ascontiguousarray(np.broadcast_to((-slope * 128.0 * rel)[None, :], (128, NREL))).astype(np.float32)


def arrange_w(wcols):
    n = wcols.shape[1]
    return np.ascontiguousarray(wcols.reshape(8, 128, n).transpose(1, 0, 2))


def diff_inputs(pfx, w_in, lam, subln_g):
    slopes = alibi_slopes_np(8)
    dm = diag_mask_tiles(False).astype(ml_dtypes.bfloat16)
    lamb = np.ascontiguousarray(np.broadcast_to(lam.reshape(1, 256), (128, 256)))
    sgb = np.ascontiguousarray(np.broadcast_to(subln_g.reshape(1, 128), (128, 128)))
    out = []
    for hg in range(4):
        hs = [2 * hg, 2 * hg + 1]
        cols = np.concatenate([np.arange(h * 128, (h + 1) * 128) for h in hs])
        out.append({
            pfx + "wq": arrange_w(w_in[:, cols]),
            pfx + "wk": arrange_w(w_in[:, 1024 + cols]),
            pfx + "wv": arrange_w(w_in[:, 2048 + cols]),
            pfx + "base": np.ascontiguousarray(np.stack([base_tile(slopes[h]) for h in hs], axis=1)),
            pfx + "cb": np.ascontiguousarray(np.stack([cb_table(slopes[h]) for h in hs], axis=1)),
            pfx + "dmask": dm, pfx + "lam": lamb, pfx + "subg": sgb, pfx + "ident": _IDENT,
        })
    return out


def emit_sb(nc, pfx, binds, gathered):
    C = Ctx(nc, pfx, binds)
    nc, es = C.nc, C.es
    hTd = C.din("hT", [D, T], BF16)
    hsrc = hT_source(hTd, gathered)
    odst = None
    wqd = C.din("wq", [128, 8, 256])
    wkd = C.din("wk", [128, 8, 256])
    wvd = C.din("wv", [128, 8, 256])
    m01d = C.din("m01", [128, 4, 512], BF16)
    trid = C.din("tri", [128, 2, 128], BF16)
    identd = C.din("ident", [128, 128])
    oTd = C.dout("oT", [256, T], BF16)
    odst = o_dest(oTd, gathered)
    with es:
        S = Sched(nc, es, pfx)
        QT = [C.sb("QT%d" % h, [128, T], BF16) for h in range(2)]
        KT = [C.sb("KT%d" % h, [128, T], BF16) for h in range(2)]
        V = C.sb("V", [128, 64, 256], BF16)
        wq = C.sb("wq", [128, 8, 256], BF16)
        wk = C.sb("wk", [128, 8, 256], BF16)
        wv = C.sb("wv", [128, 8, 256], BF16)
        hTs = [C.sb("hTs%d" % i, [128, 8, 512], BF16) for i in range(2)]
        m01 = C.sb("m01", [128, 4, 512], BF16)
        tri = C.sb("tri", [128, 2, 128], BF16)
        eb = [C.sb("eb%d" % i, [128, 512], F32) for i in range(3)]
        spb = [C.sb("spb%d" % i, [128, 512], BF16) for i in range(3)]
        wb = [C.sb("wb%d" % i, [128, 512], F32) for i in range(2)]
        ab = [C.sb("ab%d" % i, [128, 512], BF16) for i in range(3)]
        identf = C.sb("identf", [128, 128], F32)
        identb = C.sb("identb", [128, 128], BF16)
        ob = C.sb("ob", [128, 4, 256], BF16)
        oTs = [C.sb("oTs%d" % i, [128, 2, 512], BF16) for i in range(2)]
        ZB = [C.ps("Zb%d" % i, [128, 512], F32) for i in range(3)]
        XB = [C.ps("Xb%d" % i, [128, 512], F32) for i in range(2)]
        OBk = [C.ps("Ob%d" % i, [128, 512], F32) for i in range(2)]
        PT = C.ps("PT", [128, 1024], BF16)
        S.psum_keys.update(C.psum_names)

        S.dma("sync", identf[:], identd[:, :], writes=["identf"])
        S.op("vector", lambda e: e.tensor_copy(out=identb[:], in_=identf[:]), reads=["identf"], writes=["identb"])
        S.dma("gpsimd", wq[:], wqd[:, :, :], writes=["wq"])
        S.dma("gpsimd", wk[:], wkd[:, :, :], writes=["wk"])
        S.dma("gpsimd", wv[:], wvd[:, :, :], writes=["wv"])
        S.dma("sync", m01[:], m01d[:, :, :], writes=["m01"])
        S.dma("sync", tri[:], trid[:, :, :], writes=["tri"])

        cp = [0]

        def evac(dst, src, rk, wk_):
            eng = "scalar" if cp[0] % 2 == 0 else "vector"
            cp[0] += 1
            if eng == "scalar":
                S.op("scalar", lambda e: e.activation(out=dst, in_=src, func=AF.Copy), reads=rk, writes=wk_)
            else:
                S.op("vector", lambda e: e.tensor_copy(out=dst, in_=src), reads=rk, writes=wk_)

        bank = [0]
        for tg in range(16):
            hs = hTs[tg % 2]
            hk = "hTs%d" % (tg % 2)
            S.dma("sync", hs[:], hsrc(tg), writes=[hk])
            for (w, wname, dstl, dname) in [(wq, "wq", QT, "QT"), (wk, "wk", KT, "KT")]:
                for h in range(2):
                    Pp = ZB[bank[0] % 3]
                    bank[0] += 1
                    for k in range(8):
                        S.op("tensor", lambda e: e.matmul(Pp[:], lhsT=w[:, k, h * 128:(h + 1) * 128], rhs=hs[:, k, :],
                                                          start=(k == 0), stop=(k == 7)),
                             reads=[wname, hk], writes=[Pp.name])
                    evac(dstl[h][:, tg * 512:(tg + 1) * 512], Pp[:], [Pp.name], ["%s%d_%d" % (dname, h, tg)])
            for tt in range(4):
                Pp = ZB[bank[0] % 3]
                bank[0] += 1
                for k in range(8):
                    S.op("tensor", lambda e: e.matmul(Pp[:, 0:256], lhsT=hs[:, k, tt * 128:(tt + 1) * 128], rhs=wv[:, k, :],
                                                      start=(k == 0), stop=(k == 7)),
                         reads=["wv", hk], writes=[Pp.name])
                blk = tg * 4 + tt
                evac(V[:, blk, :], Pp[:, 0:256], [Pp.name], ["V_%d" % blk])

        scale = 64 ** -0.5
        ctr = {"z": 0, "e": 0, "w": 0, "a": 0, "o": 0, "chain": 0}
        for g in range(16):
            chains = []
            for hh in range(4):
                ch = ctr["chain"]
                ctr["chain"] += 1
                kbs = list(range(4 * g + 3, -1, -1))
                av0 = [True]
                chains.append([dict(hh=hh, kb=kb, first=(i == 0), last=(i == len(kbs) - 1), X=XB[ch % 2], O=OBk[ch % 2], av0=av0)
                               for i, kb in enumerate(kbs)])
            tiles = []
            for pr in range(2):
                for ta, tb_ in zip(chains[2 * pr], chains[2 * pr + 1]):
                    tiles += [ta, tb_]

            def emit_Z(tl):
                hh, kb = tl["hh"], tl["kb"]
                p, half = hh // 2, hh % 2
                Zp = ZB[ctr["z"] % 3]
                ctr["z"] += 1
                tl["Z"] = Zp
                S.op("tensor", lambda e: e.matmul(Zp[:], lhsT=KT[p][half * 64:(half + 1) * 64, kb * 128:(kb + 1) * 128],
                                                  rhs=QT[p][half * 64:(half + 1) * 64, g * 512:(g + 1) * 512], start=True, stop=True),
                     reads=["KT%d_%d" % (p, kb // 4), "QT%d_%d" % (p, g)], writes=[Zp.name])

            def emit_esp(tl):
                kb, Zp = tl["kb"], tl["Z"]
                i = ctr["e"] % 3
                ctr["e"] += 1
                tl["e"], tl["sp"] = eb[i], spb[i]
                r = kb - 4 * g
                S.op("scalar", lambda e: e.activation(out=eb[i][:], in_=Zp[:], func=AF.Exp, scale=scale), reads=[Zp.name], writes=[eb[i].name])
                S.op("scalar", lambda e: e.activation(out=spb[i][:], in_=eb[i][:], func=AF.Ln, bias=1.0, scale=1.0),
                     reads=[eb[i].name], writes=[spb[i].name])
                if r >= 0:
                    S.op("gpsimd", lambda e: e.tensor_tensor(out=spb[i][:], in0=spb[i][:], in1=m01[:, r, :], op=ALU.mult),
                         reads=[spb[i].name, "m01"], writes=[spb[i].name])
                    S.op("gpsimd", lambda e: e.tensor_tensor(out=eb[i][:], in0=eb[i][:], in1=m01[:, r, :], op=ALU.mult),
                         reads=[eb[i].name, "m01"], writes=[eb[i].name])

            def emit_L(tl):
                X, sp = tl["X"], tl["sp"]
                S.op("tensor", lambda e: e.matmul(X[:], lhsT=tri[:, 0, :], rhs=sp[:], start=tl["first"], stop=False, skip_group_check=True),
                     reads=["tri", sp.name], writes=[X.name])

            def emit_w(tl):
                X = tl["X"]
                wi = wb[ctr["w"] % 2]
                ctr["w"] += 1
                tl["w"] = wi
                S.op("scalar", lambda e: e.activation(out=wi[:], in_=X[:], func=AF.Exp, scale=-1.0), reads=[X.name], writes=[wi.name])

            def emit_U(tl):
                X, sp, ee, wi = tl["X"], tl["sp"], tl["e"], tl["w"]
                ai = ab[ctr["a"] % 3]
                ctr["a"] += 1
                tl["a"] = ai
                S.op("tensor", lambda e: e.matmul(X[:], lhsT=tri[:, 1, :], rhs=sp[:], start=False, stop=tl["last"], skip_group_check=True),
                     reads=["tri", sp.name], writes=[X.name])
                S.op("vector", lambda e: e.tensor_tensor(out=ai[:], in0=ee[:], in1=wi[:], op=ALU.mult),
                     reads=[ee.name, wi.name], writes=[ai.name])

            def stage2(tl):
                hh, kb, O, ai = tl["hh"], tl["kb"], tl["O"], tl["a"]
                r = kb - 4 * g
                for qb in range(4):
                    if qb < r:
                        continue
                    st_ = tl["av0"][0]
                    tl["av0"][0] = False
                    S.op("tensor", lambda e: e.matmul(O[:, qb * 64:(qb + 1) * 64], lhsT=ai[:, qb * 128:(qb + 1) * 128],
                                                      rhs=V[:, kb, hh * 64:(hh + 1) * 64], start=st_, stop=(kb == 0),
                                                      skip_group_check=True),
                         reads=[ai.name, "V_%d" % kb], writes=[O.name])
                if tl["last"]:
                    S.op("vector", lambda e: e.tensor_copy(out=ob[:, :, hh * 64:(hh + 1) * 64],
                                                           in_=O[:, 0:256].rearrange("p (q d) -> p q d", q=4)),
                         reads=[O.name], writes=["ob"])

            n = len(tiles)
            for i in range(n + 2):
                if 1 <= i <= n:
                    emit_L(tiles[i - 1])
                if i < n:
                    emit_Z(tiles[i])
                if 1 <= i <= n:
                    emit_w(tiles[i - 1])
                if 2 <= i:
                    stage2(tiles[i - 2])
                if 1 <= i <= n:
                    emit_U(tiles[i - 1])
                if i < n:
                    emit_esp(tiles[i])
            ot = oTs[ctr["o"] % 2]
            otk = ot.name
            ctr["o"] += 1
            for qb in range(4):
                for c in range(2):
                    S.op("tensor", lambda e: e.transpose(out=PT[:, c * 512 + qb * 128:c * 512 + (qb + 1) * 128],
                                                         in_=ob[:, qb, c * 128:(c + 1) * 128], identity=identb[:]),
                         reads=["ob", "identb"], writes=["PT"])
            S.op("vector", lambda e: e.tensor_copy(out=ot[:], in_=PT[:].rearrange("p (c t) -> p c t", c=2)), reads=["PT"], writes=[otk])
            S.dma("sync", odst(g), ot[:], reads=[otk], is_output=True)
        S.close()


def sb_inputs(pfx, w_in):
    jj = np.arange(128)[:, None, None]
    r = np.arange(4)[None, :, None]
    q = np.arange(512)[None, None, :]
    m01 = ((q - jj - 128 * r) >= 1).astype(np.float32).astype(ml_dtypes.bfloat16)
    mm = np.arange(128)[:, None]
    j2 = np.arange(128)[None, :]
    tri = np.ascontiguousarray(np.stack([(mm >= j2), (mm < j2)], axis=1).astype(np.float32).astype(ml_dtypes.bfloat16))
    out = []
    for hg in range(4):
        cols = np.arange(hg * 256, (hg + 1) * 256)
        out.append({
            pfx + "wq": arrange_w(w_in[:, cols]),
            pfx + "wk": arrange_w(w_in[:, 1024 + cols]),
            pfx + "wv": arrange_w(w_in[:, 2048 + cols]),
            pfx + "m01": m01, pfx + "tri": tri, pfx + "ident": _IDENT,
        })
    return out


NSA_FORCE = 1e4
NSA_NEG = -1e30


class _Stop(Exception):
    pass


def emit_nsa(nc, pfx, binds, gathered, dbg=None):
    C = Ctx(nc, pfx, binds)
    nc, es = C.nc, C.es
    hTd = C.din("hT", [D, T], BF16)
    hsrc = hT_source(hTd, gathered)
    odst = None
    wfmd = C.din("wfm", [128, 8, 640])
    wtmd = C.din("wtm", [128, 8, 140])
    cw1d = C.din("cw1", [128, 32, 256])
    cped = C.din("cpe", [128, 32])
    cw2kd = C.din("cw2k", [128, 2, 128])
    cw2vd = C.din("cw2v", [128, 2, 64])
    ovld = C.din("ovl", [128, 4, 128], BF16)
    slpd = C.din("slp", [128, 4])
    cbd = C.din("cb", [128, 4, NREL])
    cbcd = C.din("cbc", [128, 4, 16])
    base0d = C.din("base0", [128, 512])
    basec0d = C.din("basec0", [128, 512])
    cmaskd = C.din("cmask", [128, 5, 512], BF16)
    dmaskd = C.din("dmask", [128, 4, 512], BF16)
    wmaskd = C.din("wmask", [128, 8, 512], BF16)
    indd = C.din("ind", [128, T], BF16)
    adjd = C.din("adj", [64, 128, 128])
    identd = C.din("ident", [128, 128])
    oTd = C.dout("oT", [256, T], BF16)
    odst = o_dest(oTd, gathered)
    with es:
        S = Sched(nc, es, pfx)
        try:
            QT = [C.sb("QT%d" % h, [128, T], BF16) for h in range(2)]
            ksT = C.sb("ksT", [128, T], BF16)
            kwT = C.sb("kwT", [128, T], BF16)
            kcvT = C.sb("kcvT", [128, T], BF16)
            vsA = C.sb("vsA", [128, 64, 65], BF16)
            vwA = C.sb("vwA", [128, 64, 65], BF16)
            gates = C.sb("gates", [128, 64, 12], F32)
            PBUF = C.sb("PBUF", [128, 14464], BF16)
            hTs = [PBUF[:, i * 4096:(i + 1) * 4096].rearrange("p (k t) -> p k t", k=8) for i in range(2)]
            wfm = PBUF[:, 8192:8192 + 5120].rearrange("p (k n) -> p k n", k=8)
            wtm = PBUF[:, 13312:13312 + 1120].rearrange("p (k n) -> p k n", k=8)
            cw1 = PBUF[:, 0:8192].rearrange("p (l f) -> p l f", l=32)
            ind = PBUF[:, 0:8192]
            cpe = C.sb("cpe", [128, 32], BF16)
            cw2k = C.sb("cw2k", [128, 2, 128], BF16)
            cw2v = C.sb("cw2v", [128, 2, 64], BF16)
            slp = C.sb("slp", [128, 4], F32)
            cbt = C.sb("cbt", [128, 4, NREL], F32)
            cbct = C.sb("cbct", [128, 4, 16], F32)
            base0 = C.sb("base0", [128, 512], F32)
            basec0 = C.sb("basec0", [128, 512], F32)
            cmask = C.sb("cmask", [128, 5, 512], BF16)
            dmask = C.sb("dmask", [128, 4, 512], BF16)
            wmask = C.sb("wmask", [128, 8, 512], BF16)
            tb = [C.sb("tb%d" % i, [128, 512], F32) for i in range(3)]
            Pb = [C.sb("Pb%d" % i, [128, 512], BF16) for i in range(4)]
            kcmpT = C.sb("kcmpT", [128, 512], BF16)
            vcA = C.sb("vcA", [128, 4, 193], BF16)
            glb = [C.sb("glb%d" % i, [128, 512], BF16) for i in range(4)]
            peb = C.sb("peb", [128, 4], F32)
            imp = C.sb("imp", [128, 4, 128], F32)
            adjt = [C.sb("adjt%d" % i, [128, 128], F32) for i in range(2)]
            impa = C.sb("impa", [128, 128], F32)
            impb = C.sb("impb", [128, 128], F32)
            m8 = C.sb("m8", [128, 16], F32)
            selb = C.sb("selb", [128, 128], BF16)
            MBT = [C.sb("MBT%d" % i, [128, 512], BF16) for i in range(2)]
            acco = C.sb("acco", [128, 4, 256], F32)
            ob = C.sb("ob", [128, 4, 256], BF16)
            oTs = [C.sb("oTs%d" % i, [128, 2, 512], BF16) for i in range(2)]
            sm = C.sb("sm", [128, 8], F32)
            identf = C.sb("identf", [128, 128], F32)
            identb = C.sb("identb", [128, 128], BF16)
            SB_ = [C.ps("Sb%d" % i, [128, 512], F32) for i in range(3)]
            AC = [C.ps("Ac%d" % i, [128, 512], F32) for i in range(4)]
            PT = C.ps("PT", [128, 1024], BF16)
            S.psum_keys.update(C.psum_names)

            S.dma("sync", identf[:], identd[:, :], writes=["identf"])
            S.op("vector", lambda e: e.tensor_copy(out=identb[:], in_=identf[:]), reads=["identf"], writes=["identb"])
            S.dma("gpsimd", wfm, wfmd[:, :, :], writes=["wfm"])
            S.dma("gpsimd", wtm, wtmd[:, :, :], writes=["wtm"])
            S.dma("gpsimd", cpe[:], cped[:, :], writes=["cpe"])
            S.dma("gpsimd", cw2k[:], cw2kd[:, :, :], writes=["cw2k"])
            S.dma("gpsimd", cw2v[:], cw2vd[:, :, :], writes=["cw2v"])
            for (dst, src, key) in [(slp, slpd, "slp"), (cbt, cbd, "cbt"), (cbct, cbcd, "cbct"), (base0, base0d, "base0"),
                                    (basec0, basec0d, "basec0"), (cmask, cmaskd, "cmask"), (dmask, dmaskd, "dmask"),
                                    (wmask, wmaskd, "wmask")]:
                S.dma("sync", dst[:], src, writes=[key])
            S.op("vector", lambda e: e.memset(vsA[:, :, 64:65], 1.0), writes=["vsones"])
            S.op("vector", lambda e: e.memset(vwA[:, :, 64:65], 1.0), writes=["vwones"])
            S.op("vector", lambda e: e.memset(vcA[:], 0.0), writes=["vcA"])
            S.op("vector", lambda e: e.memset(kcmpT[:], 0.0), writes=["kcmpT"])
            S.op("vector", lambda e: e.memset(vcA[:, :, 64:65], 1.0), reads=["vcA"], writes=["vcA"])
            S.dma("sync", vcA[:, :, 65:193], ovld[:, :, :], reads=["vcA"], writes=["vcA"])

            if dbg == 'const':
                raise _Stop
            cp = [0]

            def evac(dst, src, rk, wk_, scale=None):
                eng = "scalar" if cp[0] % 2 == 0 else "vector"
                cp[0] += 1
                if eng == "scalar" and scale is None:
                    S.op("scalar", lambda e: e.activation(out=dst, in_=src, func=AF.Copy), reads=rk, writes=wk_)
                else:
                    if scale is None:
                        S.op("vector", lambda e: e.tensor_copy(out=dst, in_=src), reads=rk, writes=wk_)
                    else:
                        S.op("vector", lambda e: e.tensor_scalar(out=dst, in0=src, scalar1=float(scale), scalar2=None, op0=ALU.mult),
                             reads=rk, writes=wk_)

            bank = [0]
            fm_dst = [(QT[0], "QT0", 0.125), (QT[1], "QT1", 0.125), (ksT, "ksT", None), (kwT, "kwT", None), (kcvT, "kcvT", None)]
            for tg in range(1 if dbg in ('proj1', 'proj1ns') else 16):
                hs = hTs[tg % 2]
                hk = "hTs%d" % (tg % 2)
                S.dma("sync", hs, hsrc(tg), writes=[hk])
                for fi, (dst, dname, sc) in enumerate(fm_dst):
                    Pp = SB_[bank[0] % 3]
                    bank[0] += 1
                    for k in range(8):
                        S.op("tensor", lambda e: e.matmul(Pp[:], lhsT=wfm[:, k, fi * 128:(fi + 1) * 128], rhs=hs[:, k, :],
                                                          start=(k == 0), stop=(k == 7)),
                             reads=["wfm", hk], writes=[Pp.name])
                    evac(dst[:, tg * 512:(tg + 1) * 512], Pp[:], [Pp.name], ["%s_%d" % (dname, tg)], scale=sc)
                for tt in range(4):
                    Pp = SB_[bank[0] % 3]
                    bank[0] += 1
                    for k in range(8):
                        S.op("tensor", lambda e: e.matmul(Pp[:, 0:140], lhsT=hs[:, k, tt * 128:(tt + 1) * 128], rhs=wtm[:, k, :],
                                                          start=(k == 0), stop=(k == 7)),
                             reads=["wtm", hk], writes=[Pp.name])
                    blk = tg * 4 + tt
                    S.op("vector", lambda e: e.tensor_copy(out=vsA[:, blk, 0:64], in_=Pp[:, 0:64]), reads=[Pp.name], writes=["vs_%d" % blk])
                    S.op("vector", lambda e: e.tensor_copy(out=vwA[:, blk, 0:64], in_=Pp[:, 64:128]), reads=[Pp.name], writes=["vw_%d" % blk])
                    S.op("scalar", lambda e: e.activation(out=gates[:, blk, :], in_=Pp[:, 128:140], func=AF.Exp, scale=-1.0),
                         reads=[Pp.name], writes=["gates_%d" % blk])
                    S.op("vector", lambda e: e.tensor_scalar(out=gates[:, blk, :], in0=gates[:, blk, :], scalar1=1.0, scalar2=None, op0=ALU.add),
                         reads=["gates_%d" % blk], writes=["gates_%d" % blk])
                    S.op("vector", lambda e: e.reciprocal(out=gates[:, blk, :], in_=gates[:, blk, :]),
                         reads=["gates_%d" % blk], writes=["gates_%d" % blk])
            if dbg in ('proj', 'proj1', 'proj1ns'):
                raise _Stop
            S.barrier()

            S.dma("gpsimd", cw1, cw1d[:, :, :], writes=["cw1"])
            kcv = kcvT[:, :].rearrange("p (n s) -> p n s", s=16)
            for j in range(2):
                lo, hi = j * 64, (j + 1) * 64
                for c in range(2):
                    Pp = SB_[bank[0] % 3]
                    bank[0] += 1
                    Pq = AC[0]
                    for l in range(32):
                        S.op("tensor", lambda e: e.matmul(Pq[:, 0:1], lhsT=cw1[lo:hi, l, c * 128:(c + 1) * 128], rhs=cpe[lo:hi, l:l + 1],
                                                          start=(l == 0), stop=(l == 31)),
                             reads=["cw1", "cpe"], writes=[Pq.name])
                    col = j * 2 + c
                    S.op("vector", lambda e: e.tensor_copy(out=peb[:, col:col + 1], in_=Pq[:, 0:1]), reads=[Pq.name], writes=["peb%d" % col])
                    for l in range(32):
                        S.op("tensor", lambda e: e.matmul(Pp[:, 0:511], lhsT=cw1[lo:hi, l, c * 128:(c + 1) * 128],
                                                          rhs=kcv[lo:hi, (l // 16):(l // 16) + 511, l % 16],
                                                          start=(l == 0), stop=(l == 31)),
                             reads=["cw1"] + ["kcvT_%d" % t_ for t_ in range(16)], writes=[Pp.name])
                    xg, x2, ug = tb[0], tb[1], tb[2]
                    S.op("scalar", lambda e: e.activation(out=xg[:, 0:511], in_=Pp[:, 0:511], func=AF.Identity, bias=peb[:, col:col + 1], scale=1.0),
                         reads=[Pp.name, "peb%d" % col], writes=[xg.name])
                    S.op("vector", lambda e: e.tensor_tensor(out=x2[:, 0:511], in0=xg[:, 0:511], in1=xg[:, 0:511], op=ALU.mult),
                         reads=[xg.name], writes=[x2.name])
                    S.op("vector", lambda e: e.tensor_scalar(out=x2[:, 0:511], in0=x2[:, 0:511], scalar1=0.044715, scalar2=1.0,
                                                             op0=ALU.mult, op1=ALU.add), reads=[x2.name], writes=[x2.name])
                    S.op("vector", lambda e: e.tensor_tensor(out=ug[:, 0:511], in0=x2[:, 0:511], in1=xg[:, 0:511], op=ALU.mult),
                         reads=[x2.name, xg.name], writes=[ug.name])
                    S.op("scalar", lambda e: e.activation(out=ug[:, 0:511], in_=ug[:, 0:511], func=AF.Exp, scale=-1.5957691216057308),
                         reads=[ug.name], writes=[ug.name])
                    S.op("vector", lambda e: e.tensor_scalar(out=ug[:, 0:511], in0=ug[:, 0:511], scalar1=1.0, scalar2=None, op0=ALU.add),
                         reads=[ug.name], writes=[ug.name])
                    S.op("vector", lambda e: e.reciprocal(out=ug[:, 0:511], in_=ug[:, 0:511]), reads=[ug.name], writes=[ug.name])
                    gl = glb[j * 2 + c]
                    S.op("vector", lambda e: e.memset(gl[:, 511:512], 0.0), writes=[gl.name])
                    S.op("vector", lambda e: e.tensor_tensor(out=gl[:, 0:511], in0=ug[:, 0:511], in1=xg[:, 0:511], op=ALU.mult),
                         reads=[ug.name, xg.name, gl.name], writes=[gl.name])
            if dbg == 'cmp1':
                raise _Stop
            Pp = SB_[bank[0] % 3]
            bank[0] += 1
            for c in range(2):
                S.op("tensor", lambda e: e.matmul(Pp[:, 0:511], lhsT=cw2k[:, c, :], rhs=glb[c][:, 0:511], start=(c == 0), stop=(c == 1)),
                     reads=["cw2k", glb[c].name], writes=[Pp.name])
            S.op("vector", lambda e: e.tensor_copy(out=kcmpT[:, 0:511], in_=Pp[:, 0:511]), reads=[Pp.name, "kcmpT"], writes=["kcmpT"])
            for nt in range(4):
                nn = 128 if nt < 3 else 127
                Pp = SB_[bank[0] % 3]
                bank[0] += 1
                for c in range(2):
                    S.op("tensor", lambda e: e.matmul(Pp[0:nn, 0:64], lhsT=glb[2 + c][:, nt * 128:nt * 128 + nn], rhs=cw2v[:, c, :],
                                                      start=(c == 0), stop=(c == 1)),
                         reads=["cw2v", glb[2 + c].name], writes=[Pp.name])
                S.op("vector", lambda e: e.tensor_copy(out=vcA[0:nn, nt, 0:64], in_=Pp[0:nn, 0:64]), reads=[Pp.name, "vcA"], writes=["vcA"])
            if dbg == 'cmp2':
                raise _Stop
            S.barrier()
            S.dma("sync", ind, indd[:, :], writes=["ind"])

            ctr = {"s": 0, "t": 0, "p": 0, "o": 0, "ac": 0, "adj": 0, "mbt": 0}

            def run_branch(g, hh, tiles, kT, vA, vkey, ncol, accs, acc_cols, basetile, bkey, cbtab, cbkey):
                p, half = hh // 2, hh % 2
                firsts = {}

                def emit_s(tl):
                    Sp = SB_[ctr["s"] % 3]
                    ctr["s"] += 1
                    tl["Sp"] = Sp
                    kb = tl["kb"]
                    mms = [(kT[half * 64:(half + 1) * 64, kb * 128:(kb + 1) * 128], QT[p][half * 64:(half + 1) * 64, g * 512:(g + 1) * 512],
                            tl["kkeys"] + ["QT%d_%d" % (p, g)])]
                    if tl.get("extra") is not None:
                        mms.append(tl["extra"])
                    if tl.get("mask") is not None:
                        mms.append((identb[:], tl["mask"], ["identb", "cmask", "dmask", "wmask"]))
                    for i, (l_, r_, keys) in enumerate(mms):
                        S.op("tensor", lambda e: e.matmul(Sp[:], lhsT=l_, rhs=r_, start=(i == 0), stop=(i == len(mms) - 1)),
                             reads=keys, writes=[Sp.name])

                def emit_mid(tl):
                    Sp = tl["Sp"]
                    tt_ = tb[ctr["t"] % 3]
                    ctr["t"] += 1
                    Pt = Pb[ctr["p"] % 4]
                    ctr["p"] += 1
                    tl["P"] = Pt
                    S.op("vector", lambda e: e.scalar_tensor_tensor(out=tt_[:], in0=basetile[:], scalar=slp[:, hh:hh + 1], in1=Sp[:],
                                                                    op0=ALU.mult, op1=ALU.add),
                         reads=[Sp.name, bkey, "slp"], writes=[tt_.name])
                    ci = tl["cbi"]
                    S.op("scalar", lambda e: e.activation(out=Pt[:], in_=tt_[:], func=AF.Exp, bias=cbtab[:, hh, ci:ci + 1], scale=1.0),
                         reads=[tt_.name, cbkey], writes=[Pt.name])

                def emit_av(tl):
                    Pt, kb = tl["P"], tl["kb"]
                    for qb in tl["qbs"]:
                        acc, c0 = accs[qb], acc_cols[qb]
                        st_ = acc.name not in firsts
                        firsts[acc.name] = True
                        S.op("tensor", lambda e: e.matmul(acc[:, c0:c0 + ncol], lhsT=Pt[:, qb * 128:(qb + 1) * 128], rhs=vA[:, kb, :],
                                                          start=st_, stop=False, skip_group_check=True),
                             reads=[Pt.name] + tl["vkeys"], writes=[acc.name])

                flash_pipeline(S, tiles, emit_s, emit_mid, emit_av)

            for g in range(16):
                ntmax = (512 * g + 480) // 2048
                for hh in range(4):
                    a0 = AC[(ctr["ac"] % 2) * 2]
                    a1 = AC[(ctr["ac"] % 2) * 2 + 1]
                    ctr["ac"] += 1
                    accs = [a0, a0, a1, a1]
                    cols = [0, 193, 0, 193]
                    tiles = []
                    for nt in range(ntmax + 1):
                        rel2 = g - 4 * nt
                        tiles.append(dict(kb=nt, kkeys=["kcmpT"], vkeys=["vcA"], mask=(cmask[:, rel2, :] if rel2 <= 4 else None),
                                          cbi=rel2, qbs=[0, 1, 2, 3]))
                    run_branch(g, hh, tiles, kcmpT, vcA, "vcA", 193, accs, cols, basec0, "basec0", cbct, "cbct")
                    for qb in range(4):
                        acc, c0 = accs[qb], cols[qb]
                        blk = g * 4 + qb
                        S.op("vector", lambda e: e.tensor_scalar(out=sm[:, 0:1], in0=acc[:, c0 + 64:c0 + 65], scalar1=1e-30, scalar2=None, op0=ALU.max),
                             reads=[acc.name], writes=["sm0"])
                        S.op("vector", lambda e: e.reciprocal(out=sm[:, 0:1], in_=sm[:, 0:1]), reads=["sm0"], writes=["sm0"])
                        S.op("vector", lambda e: e.tensor_tensor(out=sm[:, 1:2], in0=sm[:, 0:1], in1=gates[:, blk, hh * 3:hh * 3 + 1], op=ALU.mult),
                             reads=["sm0", "gates_%d" % blk], writes=["sm1"])
                        S.op("vector", lambda e: e.tensor_scalar(out=acco[:, qb, hh * 64:(hh + 1) * 64], in0=acc[:, c0:c0 + 64],
                                                                 scalar1=sm[:, 1:2], scalar2=None, op0=ALU.mult),
                             reads=[acc.name, "sm1"], writes=["acco%d" % qb])
                        if hh == 0:
                            S.op("vector", lambda e: e.tensor_scalar(out=imp[:, qb, :], in0=acc[:, c0 + 65:c0 + 193], scalar1=sm[:, 0:1],
                                                                     scalar2=None, op0=ALU.mult),
                                 reads=[acc.name, "sm0"], writes=["imp%d" % qb])
                        else:
                            S.op("vector", lambda e: e.scalar_tensor_tensor(out=imp[:, qb, :], in0=acc[:, c0 + 65:c0 + 193], scalar=sm[:, 0:1],
                                                                            in1=imp[:, qb, :], op0=ALU.mult, op1=ALU.add),
                                 reads=[acc.name, "sm0", "imp%d" % qb], writes=["imp%d" % qb])
                if dbg == 'g0c':
                    raise _Stop
                mbt = MBT[ctr["mbt"] % 2]
                ctr["mbt"] += 1
                for qb in range(4):
                    blk = g * 4 + qb
                    at = adjt[ctr["adj"] % 2]
                    ctr["adj"] += 1
                    S.dma("sync", at[:], adjd[blk], writes=[at.name])
                    S.op("vector", lambda e: e.tensor_tensor(out=impa[:], in0=imp[:, qb, :], in1=at[:], op=ALU.add),
                         reads=["imp%d" % qb, at.name], writes=["impa"])
                    S.op("vector", lambda e: e.max(out=m8[:, 0:8], in_=impa[:]), reads=["impa"], writes=["m8a"])
                    S.op("vector", lambda e: e.match_replace(out=impb[:], in_to_replace=m8[:, 0:8], in_values=impa[:], imm_value=-3.0e38),
                         reads=["impa", "m8a"], writes=["impb"])
                    S.op("vector", lambda e: e.max(out=m8[:, 8:16], in_=impb[:]), reads=["impb"], writes=["m8b"])
                    S.op("vector", lambda e: e.tensor_scalar(out=selb[:], in0=impa[:], scalar1=m8[:, 15:16], scalar2=1.0,
                                                             op0=ALU.is_ge, op1=ALU.subtract),
                         reads=["impa", "m8b"], writes=["selb"])
                    S.op("tensor", lambda e: e.transpose(out=PT[:, qb * 128:(qb + 1) * 128], in_=selb[:], identity=identb[:]),
                         reads=["selb", "identb"], writes=["PT"])
                S.op("vector", lambda e: e.tensor_copy(out=mbt[:], in_=PT[:, 0:512]), reads=["PT"], writes=[mbt.name])
                if dbg == 'g0k':
                    raise _Stop
                for hh in range(4):
                    a0 = AC[ctr["ac"] % 4]
                    ctr["ac"] += 1
                    accs = [a0] * 4
                    cols = [0, 65, 130, 195]
                    tiles = []
                    for kb in range(4 * g + 4):
                        r = kb - 4 * g
                        tiles.append(dict(kb=kb, kkeys=["ksT_%d" % (kb // 4)], vkeys=["vs_%d" % kb, "vsones"],
                                          extra=(ind[:, kb * 128:(kb + 1) * 128], mbt[:], ["ind", mbt.name]),
                                          mask=(dmask[:, r, :] if r >= 0 else None), cbi=4 * g - kb + 3,
                                          qbs=[qb for qb in range(4) if qb >= r]))
                    run_branch(g, hh, tiles, ksT, vsA, "vs", 65, accs, cols, base0, "base0", cbt, "cbt")
                    for qb in range(4):
                        c0 = cols[qb]
                        blk = g * 4 + qb
                        S.op("vector", lambda e: e.reciprocal(out=sm[:, 2:3], in_=a0[:, c0 + 64:c0 + 65]), reads=[a0.name], writes=["sm2"])
                        S.op("vector", lambda e: e.tensor_tensor(out=sm[:, 3:4], in0=sm[:, 2:3], in1=gates[:, blk, hh * 3 + 1:hh * 3 + 2], op=ALU.mult),
                             reads=["sm2", "gates_%d" % blk], writes=["sm3"])
                        S.op("vector", lambda e: e.scalar_tensor_tensor(out=acco[:, qb, hh * 64:(hh + 1) * 64], in0=a0[:, c0:c0 + 64],
                                                                        scalar=sm[:, 3:4], in1=acco[:, qb, hh * 64:(hh + 1) * 64],
                                                                        op0=ALU.mult, op1=ALU.add),
                             reads=[a0.name, "sm3", "acco%d" % qb], writes=["acco%d" % qb])
                if dbg == 'g0s':
                    raise _Stop
                for hh in range(4):
                    a0 = AC[ctr["ac"] % 4]
                    ctr["ac"] += 1
                    accs = [a0] * 4
                    cols = [0, 65, 130, 195]
                    tiles = []
                    for kb in range(max(0, 4 * g - 4), 4 * g + 4):
                        r = kb - 4 * g
                        tiles.append(dict(kb=kb, kkeys=["kwT_%d" % (kb // 4)], vkeys=["vw_%d" % kb, "vwones"],
                                          mask=wmask[:, r + 4, :], cbi=4 * g - kb + 3,
                                          qbs=[qb for qb in range(4) if qb >= r and qb - r < 5]))
                    run_branch(g, hh, tiles, kwT, vwA, "vw", 65, accs, cols, base0, "base0", cbt, "cbt")
                    for qb in range(4):
                        c0 = cols[qb]
                        blk = g * 4 + qb
                        S.op("vector", lambda e: e.reciprocal(out=sm[:, 4:5], in_=a0[:, c0 + 64:c0 + 65]), reads=[a0.name], writes=["sm4"])
                        S.op("vector", lambda e: e.tensor_tensor(out=sm[:, 5:6], in0=sm[:, 4:5], in1=gates[:, blk, hh * 3 + 2:hh * 3 + 3], op=ALU.mult),
                             reads=["sm4", "gates_%d" % blk], writes=["sm5"])
                        S.op("vector", lambda e: e.scalar_tensor_tensor(out=acco[:, qb, hh * 64:(hh + 1) * 64], in0=a0[:, c0:c0 + 64],
                                                                        scalar=sm[:, 5:6], in1=acco[:, qb, hh * 64:(hh + 1) * 64],
                                                                        op0=ALU.mult, op1=ALU.add),
                             reads=[a0.name, "sm5", "acco%d" % qb], writes=["acco%d" % qb])
                if dbg == 'g0w':
                    raise _Stop
                S.op("vector", lambda e: e.tensor_copy(out=ob[:], in_=acco[:]), reads=["acco%d" % q_ for q_ in range(4)], writes=["ob"])
                ot = oTs[ctr["o"] % 2]
                ctr["o"] += 1
                for qb in range(4):
                    for c in range(2):
                        S.op("tensor", lambda e: e.transpose(out=PT[:, c * 512 + qb * 128:c * 512 + (qb + 1) * 128],
                                                             in_=ob[:, qb, c * 128:(c + 1) * 128], identity=identb[:]),
                             reads=["ob", "identb"], writes=["PT"])
                S.op("vector", lambda e: e.tensor_copy(out=ot[:], in_=PT[:].rearrange("p (c t) -> p c t", c=2)), reads=["PT"], writes=[ot.name])
                S.dma("sync", odst(g), ot[:], reads=[ot.name], is_output=True)
                if dbg == 'g0':
                    raise _Stop
        except _Stop:
            pass
        S.close()


def nsa_consts():
    c = {}
    jj = np.arange(128)[:, None]
    q = np.arange(512)[None, :]
    c["base0"] = (q - jj).astype(np.float32) * -1.0
    c["basec0"] = -(q - 16 * jj - 31).astype(np.float32)
    rel2 = np.arange(5)[None, :, None]
    okc = (512 * rel2 + q[:, None, :].transpose(1, 0, 2) * 0 + q[None, :, :] * 1 - 16 * jj[:, :, None] - 31) >= 0
    c["cmask"] = np.where(okc, 0.0, -BIGNEG).astype(np.float32).astype(ml_dtypes.bfloat16)
    c["dmask"] = diag_mask_tiles(False).astype(ml_dtypes.bfloat16)
    r = (np.arange(8) - 4)[None, :, None]
    dd = q[None, :, :] - jj[:, :, None] - 128 * r
    c["wmask"] = np.where((dd >= 0) & (dd < 512), 0.0, -BIGNEG).astype(np.float32).astype(ml_dtypes.bfloat16)
    s_ = np.arange(128)[:, None]
    key = np.arange(T)[None, :]
    c["ind"] = np.where(key // 64 == s_, BIGNEG, 0.0).astype(np.float32).astype(ml_dtypes.bfloat16)
    n = np.arange(512)
    cs = n * 16
    ss = np.arange(128) * 64
    ov = ((cs[:, None] < ss[None, :] + 64) & (cs[:, None] + 32 > ss[None, :])).astype(np.float32)
    ov[511, :] = 0.0
    c["ovl"] = np.ascontiguousarray(ov.reshape(4, 128, 128).transpose(1, 0, 2)).astype(ml_dtypes.bfloat16)
    tt = np.arange(T)
    cur = tt // 64
    sid = np.arange(128)[None, :]
    forced = (sid == 0) | (sid == cur[:, None]) | (sid == cur[:, None] - 1)
    adj = np.where(forced, NSA_FORCE, 0.0)
    adj = np.where(sid <= cur[:, None], adj, NSA_NEG).astype(np.float32)
    c["adj"] = np.ascontiguousarray(adj.reshape(64, 128, 128))
    return c


_NSA_CONSTS = {}


def nsa_inputs(pfx, w_in, cmp_pe, cmp_w1, cmp_w2):
    if not _NSA_CONSTS:
        _NSA_CONSTS.update(nsa_consts())
    cst = _NSA_CONSTS
    slopes = alibi_slopes_np(16)
    cw1 = np.ascontiguousarray(np.concatenate([cmp_w1[j].reshape(32, 64, 256).transpose(1, 0, 2) for j in range(2)], axis=0))
    cpe = np.ascontiguousarray(np.concatenate([cmp_pe[j].T for j in range(2)], axis=0))
    cw2k = np.ascontiguousarray(np.concatenate([cmp_w2[0], cmp_w2[0]], axis=1).reshape(2, 128, 128).transpose(1, 0, 2))
    cw2v = np.ascontiguousarray(cmp_w2[1].reshape(2, 128, 64).transpose(1, 0, 2))
    rel = np.arange(NREL) - 3
    out = []
    for grp in range(4):
        hs = [4 * grp + r_ for r_ in range(4)]
        qc = np.arange(grp * 256, (grp + 1) * 256)
        kc = 1024 + grp * 64 + np.arange(64)
        vc, ks, vs, kw, vw = kc + 256, kc + 512, kc + 768, kc + 1024, kc + 1280
        gc = 2560 + grp * 12 + np.arange(12)
        fm_cols = np.concatenate([qc, ks, ks, kw, kw, kc, vc])
        tm_cols = np.concatenate([vs, vw, gc])
        sl = np.array([slopes[h] for h in hs])
        out.append({
            pfx + "wfm": arrange_w(w_in[:, fm_cols]),
            pfx + "wtm": arrange_w(w_in[:, tm_cols]),
            pfx + "cw1": cw1, pfx + "cpe": cpe, pfx + "cw2k": cw2k, pfx + "cw2v": cw2v,
            pfx + "ovl": cst["ovl"],
            pfx + "slp": np.ascontiguousarray(np.broadcast_to(sl[None, :], (128, 4))).astype(np.float32),
            pfx + "cb": np.ascontiguousarray(np.broadcast_to((-sl[:, None] * 128.0 * rel[None, :])[None], (128, 4, NREL))).astype(np.float32),
            pfx + "cbc": np.ascontiguousarray(np.broadcast_to((-sl[:, None] * 512.0 * np.arange(16)[None, :])[None], (128, 4, 16))).astype(np.float32),
            pfx + "base0": cst["base0"], pfx + "basec0": cst["basec0"], pfx + "cmask": cst["cmask"], pfx + "dmask": cst["dmask"],
            pfx + "wmask": cst["wmask"], pfx + "ind": cst["ind"], pfx + "adj": cst["adj"], pfx + "ident": _IDENT,
        })
    return out


DEPTH = 4
CC_GROUPS = [[0, 1, 2, 3], [4, 5, 6, 7]]


def build_fused(nphase=99):
    nc = bass.Bass("TRN2", target_bir_lowering=False)
    x_in = nc.dram_tensor("x", [TOK, D], F32, kind="ExternalInput").ap()
    out = nc.dram_tensor("out", [TOK, D], F32, kind="ExternalOutput").ap()
    x_scr = nc.dram_tensor("x_scr", [TOK, D], F32, kind="Internal").ap()
    hT_loc = [nc.dram_tensor("hT_loc%d" % i, [4, D, 512], BF16, kind="Internal").ap() for i in range(DEPTH)]
    hT_all = [nc.dram_tensor("hT_all%d" % i, [4, 4 * D, 512], BF16, kind="Internal").ap() for i in range(DEPTH)]
    o_loc = [nc.dram_tensor("o_loc%d" % i, [4, 256, 2048], BF16, kind="Internal").ap() for i in range(DEPTH)]
    o_all = [nc.dram_tensor("o_all%d" % i, [4, 4 * 256, 2048], BF16, kind="Internal").ap() for i in range(DEPTH)]
    rank = nc.sync.partition_id() % 4
    mod_loc = nc.dram_tensor("mod_loc", [9, D], F32, kind="Internal").ap()
    mod_all = nc.dram_tensor("mod_all", [36, D], F32, kind="Internal").ap()

    def allgather(name, src, dst, nch=4):
        cs = nc.alloc_semaphore(name=name)
        for ch in range(nch):
            nc.gpsimd.collective_compute("AllGather", ALU.bypass, replica_groups=CC_GROUPS,
                                         ins=[src[ch] if nch > 1 else src], outs=[dst[ch] if nch > 1 else dst]).then_inc(cs, 1)
        for eng in (nc.gpsimd, nc.sync, nc.tensor, nc.vector, nc.scalar):
            eng.wait_ge(cs, nch)
        free_sems(nc, [cs])

    ph = [0]

    def go():
        ph[0] += 1
        return ph[0] <= nphase

    emit_mods(nc, "pro_", {"rows_out": mod_loc})
    allgather("ccM", mod_loc, mod_all, nch=1)
    if go():
        emit_k1(nc, "k0_", False, 1, True, {"x": x_in, "xo": x_scr, "hTo": hT_loc[0], "hTo_chunked": True, "modrows": mod_all},
                k1_rows(None, [(0, 0)], 0))
    for i in range(DEPTH):
        if go():
            allgather("ccA%d" % i, hT_loc[i], hT_all[i])
        binds = {"hT": hT_all[i], "oT": o_loc[i]}
        kind = i % 3
        if go():
            if kind == 0:
                emit_nsa(nc, "m%d_" % i, binds, True)
            elif kind == 1:
                emit_sb(nc, "m%d_" % i, binds, True)
            else:
                emit_diff(nc, "m%d_" % i, binds, 0.8 - 0.6 * math.exp(-0.3 * i), True)
        if go():
            allgather("ccB%d" % i, o_loc[i], o_all[i])
        oT_ap = o_all[i][rank]
        if go():
            if i < DEPTH - 1:
                emit_k1(nc, "k%d_" % (i + 1), True, 2, True,
                        {"x": x_scr, "xo": x_scr, "hTo": hT_loc[i + 1], "hTo_chunked": True, "oT": oT_ap, "modrows": mod_all},
                        k1_rows(i, [(i, 1), (i + 1, 0)], i + 1))
            else:
                emit_k1(nc, "k%d_" % (i + 1), True, 1, False, {"x": x_scr, "xo": out, "oT": oT_ap, "modrows": mod_all},
                        k1_rows(i, [(i, 1)], None))
    return nc


_NPHASE = [99]


def kernel(x, c, ada_w, ada_b, norm_g, ffn_w1, ffn_w2, nsa_w_in, nsa_cmp_pe, nsa_cmp_w1, nsa_cmp_w2,
           nsa_w_out, sb_w_in, sb_w_out, diff_w_in, diff_lam, diff_subln_g, diff_w_out):
    f = lambda a: np.asarray(a, dtype=np.float32)
    x, c, ada_w, ada_b, norm_g, ffn_w1, ffn_w2 = map(f, (x, c, ada_w, ada_b, norm_g, ffn_w1, ffn_w2))
    nsa_w_in, nsa_cmp_pe, nsa_cmp_w1, nsa_cmp_w2, nsa_w_out = map(f, (nsa_w_in, nsa_cmp_pe, nsa_cmp_w1, nsa_cmp_w2, nsa_w_out))
    sb_w_in, sb_w_out, diff_w_in, diff_lam, diff_subln_g, diff_w_out = map(
        f, (sb_w_in, sb_w_out, diff_w_in, diff_lam, diff_subln_g, diff_w_out))
    nc = get_nc(("fused", _NPHASE[0]), lambda: build_fused(_NPHASE[0]))
    xt = x.reshape(B * T, D)
    common = {}
    per_b = [dict() for _ in range(B)]
    per_g = [dict() for _ in range(4)]

    def add_k1(pfx, mix, ffns, pre, wout=None):
        common.update(k1_inputs(pfx, ffn_w1, ffn_w2, mix, ffns, wout))

    pr, pb = mods_inputs("pro_", c, ada_w, ada_b, norm_g)
    for b in range(B):
        per_b[b].update(pb[b])
    for g in range(4):
        per_g[g].update(pr[g])

    add_k1("k0_", None, [(0, 0)], 0)
    for i in range(DEPTH):
        kind, j = i % 3, i // 3
        pfx = "m%d_" % i
        if kind == 0:
            pg = nsa_inputs(pfx, nsa_w_in[j], nsa_cmp_pe[j], nsa_cmp_w1[j], nsa_cmp_w2[j])
            wout = nsa_w_out[j]
        elif kind == 1:
            pg = sb_inputs(pfx, sb_w_in[j])
            wout = sb_w_out[j]
        else:
            pg = diff_inputs(pfx, diff_w_in[j], diff_lam[j], diff_subln_g[j])
            wout = diff_w_out[j]
        for g in range(4):
            per_g[g].update(pg[g])
        if i < DEPTH - 1:
            add_k1("k%d_" % (i + 1), i, [(i, 1), (i + 1, 0)], i + 1, wout)
        else:
            add_k1("k%d_" % (i + 1), i, [(i, 1)], None, wout)
    in_maps = []
    for core in range(NCORE):
        b, g = core // 4, core % 4
        m = dict(common)
        m.update(per_b[b])
        m.update(per_g[g])
        m["x"] = np.ascontiguousarray(xt[core * TOK:(core + 1) * TOK])
        in_maps.append(m)
    if _NPHASE[0] < 99:
        npfx = ["k0_"]
        for i in range(DEPTH):
            npfx += [None, "m%d_" % i, None, "k%d_" % (i + 1)]
        keep = set(p for p in npfx[:_NPHASE[0]] if p)
        in_maps = [{k: v for k, v in m.items() if k == "x" or k[:3] in keep or k.startswith("pro_")} for m in in_maps]
    res = run_bass_kernel_spmd(nc, in_maps, core_ids=list(range(NCORE)))
    xo = np.concatenate([res.results[i]["out"] for i in range(NCORE)], axis=0)
    return xo.reshape(B, T, D).astype(np.float32)
```

```python
import contextlib
import math
import numpy as np
import ml_dtypes
import concourse.bass as bass
import concourse.mybir as mybir
from concourse.bass_utils import run_bass_kernel_spmd

F32 = mybir.dt.float32
BF16 = mybir.dt.bfloat16
AF = mybir.ActivationFunctionType
ALU = mybir.AluOpType
AX = mybir.AxisListType

D = 1024
DFF = 2816
NFF = DFF // 128
B = 2
T = 8192
NCORE = 8
TOK = B * T // NCORE
NT = TOK // 128
EPS = 1e-6
NDMA = 24


def free_sems(nc, handles):
    nc.all_engine_barrier()
    nc.clear_and_free_semaphores(handles)
    nc.all_engine_barrier()


class Sched:
    def __init__(self, nc, es, pfx=""):
        self.nc = nc
        self.pfx = pfx
        self.engs = {}
        for name in ["tensor", "vector", "scalar", "gpsimd", "sync"]:
            sem = nc.alloc_semaphore(name=pfx + "sem_" + name)
            self.engs[name] = dict(obj=getattr(nc, name), sem=sem, cnt=0, waited={})
        self.dma_slots = [dict(sem=nc.alloc_semaphore(name=pfx + "dsem%d" % i), cnt=0) for i in range(NDMA)]
        self.dma_rr = 0
        self.last_write = {}
        self.reads = {}
        self.out_tokens = []
        self.psum_keys = set()

    def _wait(self, engname, tok):
        if tok is None:
            return
        semid, sem, val = tok
        if semid == engname and engname == "tensor":
            return
        e = self.engs[engname]
        if e["waited"].get(semid, 0) >= val:
            return
        e["obj"].wait_ge(sem, val)
        e["waited"][semid] = val

    def _norm(self, keys):
        p = self.pfx
        return [k[len(p):] if (p and k.startswith(p)) else k for k in keys]

    def _deps(self, engname, reads, writes):
        reads, writes = self._norm(reads), self._norm(writes)
        for k in reads:
            self._wait(engname, self.last_write.get(k))
            if k in self.psum_keys:
                for t in self.reads.get(k, []):
                    if t[0] != engname:
                        self._wait(engname, t)
        for k in writes:
            self._wait(engname, self.last_write.get(k))
            for t in self.reads.get(k, []):
                self._wait(engname, t)

    def _commit(self, tok, reads, writes):
        reads, writes = self._norm(reads), self._norm(writes)
        for k in writes:
            self.last_write[k] = tok
            self.reads[k] = []
        for k in reads:
            self.reads.setdefault(k, []).append(tok)

    def op(self, engname, fn, reads=(), writes=()):
        self._deps(engname, reads, writes)
        e = self.engs[engname]
        ins = fn(e["obj"])
        e["cnt"] += 1
        ins.then_inc(e["sem"], 1)
        tok = (engname, e["sem"], e["cnt"])
        self._commit(tok, reads, writes)
        return tok

    def dma(self, queue, out, in_, reads=(), writes=(), is_output=False):
        self._deps(queue, reads, writes)
        idx = self.dma_rr
        slot = self.dma_slots[idx]
        self.dma_rr = (self.dma_rr + 1) % NDMA
        if slot["cnt"] > 0:
            self._wait(queue, ("d%d" % idx, slot["sem"], slot["cnt"] * 16))
        ins = self.engs[queue]["obj"].dma_start(out=out, in_=in_)
        slot["cnt"] += 1
        ins.then_inc(slot["sem"], 16)
        tok = ("d%d" % idx, slot["sem"], slot["cnt"] * 16)
        self._commit(tok, reads, writes)
        if is_output:
            self.out_tokens.append(tok)
        return tok

    def barrier(self):
        toks = []
        for idx, slot in enumerate(self.dma_slots):
            if slot["cnt"] > 0:
                toks.append(("d%d" % idx, slot["sem"], slot["cnt"] * 16))
        for name, e in self.engs.items():
            if e["cnt"] > 0:
                toks.append((name, e["sem"], e["cnt"]))
        for name in self.engs:
            for tk in toks:
                if tk[0] != name:
                    self._wait(name, tk)

    def close(self):
        self.barrier()
        handles = [e["sem"] for e in self.engs.values()] + [sl["sem"] for sl in self.dma_slots]
        free_sems(self.nc, handles)

    def finish(self):
        for idx, slot in enumerate(self.dma_slots):
            if slot["cnt"] > 0:
                self._wait("sync", ("d%d" % idx, slot["sem"], slot["cnt"] * 16))
        for name, e in self.engs.items():
            if name != "sync" and e["cnt"] > 0:
                self._wait("sync", (name, e["sem"], e["cnt"]))


def emit_k1(nc, pfx, mix, n_ffn, pre, binds, rows):
    nv = (1 if mix else 0) + 3 * n_ffn + (2 if pre else 0)
    ng = (1 if mix else 0) + 2 * n_ffn + (1 if pre else 0)
    dr = {}

    def din(name, shape, dt=F32, kind="ExternalInput"):
        if name in binds:
            dr[name] = binds[name]
        else:
            dr[name] = nc.dram_tensor(pfx + name, list(shape), dt, kind=kind).ap()

    din("x", [TOK, D])
    din("ident", [128, 128])
    modrows = binds["modrows"]
    if mix:
        din("oT", [D, TOK], BF16)
        din("wout", [D, D])
    for f in range(n_ffn):
        din("w1_%d" % f, [NFF, 128, 8, 256])
        din("w2_%d" % f, [DFF, D])
    din("xo", [TOK, D], F32, "ExternalOutput")
    if pre:
        din("hTo", [D, TOK], BF16, "ExternalOutput")

    es = contextlib.ExitStack()
    with es:
        S = Sched(nc, es, pfx)

        def sb(name, shape, dt):
            return es.enter_context(nc.sbuf_tensor(pfx + name, shape, dt))

        def ps(name, shape, dt):
            return es.enter_context(nc.psum_tensor(pfx + name, shape, dt))

        xs = sb("xs", [128, NT, D], F32)
        hT = sb("hT", [128, 8, 1024], BF16)
        actT = sb("actT", [128, NFF, 1024], BF16)
        w1c = [sb("w1c%d" % i, [128, 8, 256], BF16) for i in range(3)]
        w2c = [sb("w2c%d" % i, [128, 1024], BF16) for i in range(3)]
        prmsets = [[sb("prm%d_%d" % (j, i), [128, D], F32) for i in range(3)] for j in range(2)]
        scr = [sb("scr%d" % i, [128, D], F32) for i in range(2)]
        hb = [sb("hb%d" % i, [128, D], BF16) for i in range(2)]
        junk = sb("junk", [128, D], BF16)
        sil = [sb("sil%d" % i, [128, 512], F32) for i in range(2)]
        identf = sb("identf", [128, 128], F32)
        identb = sb("identb", [128, 128], BF16)
        st = sb("st", [128, 8], F32)
        epsb = sb("epsb", [128, 1], F32)
        PA = ps("PA", [128, 1024], F32)
        PB = ps("PB", [128, 1024], F32)
        PC = ps("PC", [128, 1024], F32)
        PT = ps("PT", [128, 1024], F32)
        PTb = PT[:].bitcast(BF16)
        S.psum_keys.update(["PA0", "PA1", "PB0", "PB1", "PC0", "PC1", "PT0", "PT1"])

        xin = dr["x"].rearrange("(n p) d -> p n d", p=128)
        for q4 in range(4):
            S.dma("sync", xs[:, q4 * 4:(q4 + 1) * 4, :], xin[:, q4 * 4:(q4 + 1) * 4, :],
                  writes=["xs%d" % n for n in range(q4 * 4, q4 * 4 + 4)])
        S.dma("sync", identf[:], dr["ident"][:, :], writes=["identf"])
        S.op("vector", lambda e: e.tensor_copy(out=identb[:], in_=identf[:]), reads=["identf"], writes=["identb"])
        S.op("vector", lambda e: e.memset(epsb[:], EPS), writes=["epsb"])

        def load_row(row, dst):
            S.dma("sync", dst[:], modrows[row:row + 1, :].partition_broadcast(128), writes=[dst.name])

        pset = [0]

        def next_prm():
            pset[0] += 1
            return prmsets[pset[0] % 2]

        stc = [0]

        def rstd_of(src_ap, src_keys, col):
            S.op("scalar", lambda e: e.activation(out=junk[:], in_=src_ap, func=AF.Square, accum_out=st[:, col:col + 1]),
                 reads=src_keys, writes=["junk", "st%d" % col])
            S.op("scalar", lambda e: e.activation(out=st[:, col:col + 1], in_=st[:, col:col + 1], func=AF.Sqrt,
                                                  bias=epsb[:], scale=1.0 / D),
                 reads=["st%d" % col, "epsb"], writes=["st%d" % col])
            S.op("vector", lambda e: e.reciprocal(out=st[:, col:col + 1], in_=st[:, col:col + 1]),
                 reads=["st%d" % col], writes=["st%d" % col])

        def prenorm_tile(n, A, Bv, i):
            col = stc[0] % 4
            stc[0] += 1
            rstd_of(xs[:, n, :], ["xs%d" % n], col)
            S.op("vector", lambda e: e.scalar_tensor_tensor(out=scr[1][:], in0=xs[:, n, :], scalar=st[:, col:col + 1], in1=A[:],
                                                            op0=ALU.mult, op1=ALU.mult),
                 reads=["xs%d" % n, "st%d" % col, A.name], writes=["scr1"])
            S.op("gpsimd", lambda e: e.tensor_tensor(out=hb[i][:], in0=scr[1][:], in1=Bv[:], op=ALU.add),
                 reads=["scr1", Bv.name], writes=["hb%d" % i])

        def transpose_tile(i, half, dst_fn, dst_keys):
            pk = "PT%d" % half
            for k in range(8):
                S.op("tensor", lambda e, k=k: e.transpose(out=PTb[:, half * 1024 + k * 128: half * 1024 + (k + 1) * 128],
                                                          in_=hb[i][:, k * 128:(k + 1) * 128], identity=identb[:]),
                     reads=["hb%d" % i, "identb"], writes=[pk])
            S.op("scalar", lambda e: e.activation(out=dst_fn(), in_=PTb[:, half * 1024:(half + 1) * 1024].rearrange("p (k t) -> p k t", k=8),
                                                  func=AF.Copy),
                 reads=[pk], writes=dst_keys)

        def epilogue(Y, n, G):
            col = 4 + stc[0] % 4
            stc[0] += 1
            rstd_of(Y[:], [Y.name + "0", Y.name + "1"], col)
            S.op("vector", lambda e: e.scalar_tensor_tensor(out=scr[0][:], in0=Y[:], scalar=st[:, col:col + 1], in1=G[:],
                                                            op0=ALU.mult, op1=ALU.mult),
                 reads=[Y.name + "0", Y.name + "1", "st%d" % col, G.name], writes=["scr0"])
            S.op("gpsimd", lambda e: e.tensor_tensor(out=xs[:, n, :], in0=xs[:, n, :], in1=scr[0][:], op=ALU.add),
                 reads=["scr0", "xs%d" % n], writes=["xs%d" % n])

        vi = 0
        gi = 0
        Yb = [PA, PB]
        if mix:
            prm = next_prm()
            load_row(rows["mix"], prm[2])
            wo = actT[:, 0:8, :]
            S.dma("gpsimd", wo, dr["wout"].rearrange("(k p) n -> p k n", p=128), writes=["actT"])
            for half in range(2):
                oTs = actT[:, 8:16, :]
                S.dma("sync", oTs, dr["oT"][:, half * 1024:(half + 1) * 1024].rearrange("(k p) t -> p k t", p=128),
                      writes=["actT_o"])
                for tt in range(8):
                    n = half * 8 + tt
                    Y = Yb[n % 2]
                    for h2 in range(2):
                        for k in range(8):
                            S.op("tensor", lambda e, k=k, h2=h2, tt=tt, Y=Y: e.matmul(
                                Y[:, h2 * 512:(h2 + 1) * 512], lhsT=actT[:, 8 + k, tt * 128:(tt + 1) * 128],
                                rhs=actT[:, k, h2 * 512:(h2 + 1) * 512], start=(k == 0), stop=(k == 7)),
                                reads=["actT", "actT_o"], writes=[Y.name + str(h2)])
                    epilogue(Y, n, prm[2])

        wctr = [0, 0]
        for f in range(n_ffn):
            prm = next_prm()
            load_row(rows["ffn"][f][0], prm[0])
            load_row(rows["ffn"][f][1], prm[1])
            load_row(rows["ffn"][f][2], prm[2])
            w1d = dr["w1_%d" % f]
            w2d = dr["w2_%d" % f]
            for grp in range(2):
                for tt in range(8):
                    n = grp * 8 + tt
                    i = n % 2
                    prenorm_tile(n, prm[0], prm[1], i)
                    transpose_tile(i, n % 2, lambda tt=tt: hT[:, :, tt * 128:(tt + 1) * 128], ["hT"])
                for j in range(NFF):
                    wi = wctr[0] % 3
                    wctr[0] += 1
                    S.dma("gpsimd", w1c[wi][:], w1d[j], writes=["w1c%da" % wi, "w1c%db" % wi])
                    for th in range(2):
                        Pg = PA if th == 0 else PB
                        for part, (c0, key) in enumerate([(0, "a"), (128, "b")]):
                            for k in range(8):
                                S.op("tensor", lambda e, k=k, th=th, c0=c0, part=part, Pg=Pg, wi=wi: e.matmul(
                                    Pg[:, part * 512:(part + 1) * 512], lhsT=w1c[wi][:, k, c0:c0 + 128],
                                    rhs=hT[:, k, th * 512:(th + 1) * 512], start=(k == 0), stop=(k == 7)),
                                    reads=["hT", "w1c%d%s" % (wi, key)], writes=[Pg.name + str(part)])
                        si = th
                        S.op("scalar", lambda e, Pg=Pg, si=si: e.activation(out=sil[si][:], in_=Pg[:, 0:512], func=AF.Silu),
                             reads=[Pg.name + "0"], writes=["sil%d" % si])
                        S.op("vector", lambda e, Pg=Pg, si=si, j=j, th=th: e.tensor_tensor(
                            out=actT[:, j, th * 512:(th + 1) * 512], in0=Pg[:, 512:1024], in1=sil[si][:], op=ALU.mult),
                            reads=[Pg.name + "1", "sil%d" % si], writes=["actT%d" % j])
                Y4 = [PA, PB, PC, PT]
                for tp in range(2):
                    for j in range(NFF):
                        wi = wctr[1] % 3
                        wctr[1] += 1
                        S.dma("gpsimd", w2c[wi][:], w2d[j * 128:(j + 1) * 128, :], writes=["w2c%d" % wi])
                        for q in range(4):
                            tt = tp * 4 + q
                            Y = Y4[q]
                            for h2 in range(2):
                                S.op("tensor", lambda e, j=j, tt=tt, h2=h2, Y=Y, wi=wi: e.matmul(
                                    Y[:, h2 * 512:(h2 + 1) * 512], lhsT=actT[:, j, tt * 128:(tt + 1) * 128],
                                    rhs=w2c[wi][:, h2 * 512:(h2 + 1) * 512], start=(j == 0), stop=(j == NFF - 1)),
                                    reads=["actT%d" % j, "w2c%d" % wi], writes=[Y.name + str(h2)])
                    for q in range(4):
                        epilogue(Y4[q], grp * 8 + tp * 4 + q, prm[2])

        if pre:
            prm = next_prm()
            load_row(rows["pre"][0], prm[0])
            load_row(rows["pre"][1], prm[1])
            if binds.get("hTo_chunked"):
                def hTo_dst(n):
                    return dr["hTo"][n // 4, :, (n % 4) * 128:(n % 4 + 1) * 128].rearrange("(k p) t -> p k t", p=128)
            else:
                def hTo_dst(n):
                    return dr["hTo"].rearrange("(k p) t -> p k t", p=128)[:, :, n * 128:(n + 1) * 128]
            for n in range(NT):
                i = n % 2
                prenorm_tile(n, prm[0], prm[1], i)
                transpose_tile(i, n % 2, lambda n=n: hT[:, :, (n % 8) * 128:(n % 8 + 1) * 128], ["hTo%d" % (n % 8)])
                S.dma("sync", hTo_dst(n), hT[:, :, (n % 8) * 128:(n % 8 + 1) * 128],
                      reads=["hTo%d" % (n % 8)], is_output=True)
        xout = dr["xo"].rearrange("(n p) d -> p n d", p=128)
        for q4 in range(4):
            S.dma("sync", xout[:, q4 * 4:(q4 + 1) * 4, :], xs[:, q4 * 4:(q4 + 1) * 4, :],
                  reads=["xs%d" % n for n in range(q4 * 4, q4 * 4 + 4)], is_output=True)
        S.close()


def arrange_w1(w1):
    g = w1[:, :DFF].reshape(8, 128, NFF, 128)
    u = w1[:, DFF:].reshape(8, 128, NFF, 128)
    cat = np.concatenate([g, u], axis=3)
    return np.ascontiguousarray(cat.transpose(2, 1, 0, 3))


def arrange_adaw(cols):
    return np.ascontiguousarray(cols.reshape(8, 128, 8, 128).transpose(2, 1, 0, 3))


_IDENT = np.eye(128, dtype=np.float32)
_NC_CACHE = {}


def get_nc(key, builder):
    if key not in _NC_CACHE:
        _NC_CACHE[key] = builder()
    return _NC_CACHE[key]


def k1_inputs(pfx, ffn_w1, ffn_w2, mix, ffns, wout=None):
    common = {pfx + "ident": _IDENT}
    for f, (l, w) in enumerate(ffns):
        common[pfx + "w1_%d" % f] = arrange_w1(ffn_w1[l][w])
        common[pfx + "w2_%d" % f] = np.ascontiguousarray(ffn_w2[l][w])
    if mix is not None:
        common[pfx + "wout"] = np.ascontiguousarray(wout)
    return common


def k1_rows(mix, ffns, pre):
    rows = {"ffn": []}
    if mix is not None:
        rows["mix"] = mix * 9 + 3 + 2
    for (l, w) in ffns:
        base = l * 9 + (0 if w == 0 else 2) * 3
        rows["ffn"].append((base, base + 1, base + 2))
    if pre is not None:
        rows["pre"] = (pre * 9 + 3, pre * 9 + 4)
    return rows


def mods_inputs(pfx, c, ada_w, ada_b, norm_g):
    per_r = []
    for r in range(4):
        adaw = np.ascontiguousarray(ada_w[r].reshape(8, 128, 9, D).transpose(2, 1, 0, 3))
        per_r.append({pfx + "adaw": adaw, pfx + "adab": np.ascontiguousarray(ada_b[r].reshape(1, 9, D)),
                      pfx + "g": np.ascontiguousarray(norm_g[r].reshape(1, 6, D))})
    per_b = [{pfx + "c": np.ascontiguousarray(c[b].reshape(8, 128).T)} for b in range(B)]
    return per_r, per_b


BIGNEG = 240000.0


class Ctx:
    def __init__(self, nc, pfx, binds):
        self.nc = nc
        self.pfx = pfx
        self.binds = binds
        self.es = contextlib.ExitStack()
        self.dr = {}
        self.psum_names = []

    def din(self, name, shape, dt=F32):
        if name in self.binds:
            self.dr[name] = self.binds[name]
        else:
            self.dr[name] = self.nc.dram_tensor(self.pfx + name, list(shape), dt, kind="ExternalInput").ap()
        return self.dr[name]

    def dout(self, name, shape, dt=F32):
        if name in self.binds:
            self.dr[name] = self.binds[name]
        else:
            self.dr[name] = self.nc.dram_tensor(self.pfx + name, list(shape), dt, kind="ExternalOutput").ap()
        return self.dr[name]

    def sb(self, name, shape, dt):
        return self.es.enter_context(self.nc.sbuf_tensor(self.pfx + "s_" + name, list(shape), dt))

    def ps(self, name, shape, dt):
        self.psum_names.append(name)
        return self.es.enter_context(self.nc.psum_tensor(self.pfx + name, list(shape), dt))


def o_dest(oTd, gathered):
    if gathered:
        def f(g):
            return oTd[g // 4, :, (g % 4) * 512:(g % 4 + 1) * 512].rearrange("(c p) t -> p c t", p=128)
    else:
        def f(g):
            return oTd[:, g * 512:(g + 1) * 512].rearrange("(c p) t -> p c t", p=128)
    return f


def hT_source(hTd, gathered):
    if gathered:
        def f(tg):
            r, tc = tg // 4, tg % 4
            return hTd[tc, r * D:(r + 1) * D, :].rearrange("(k p) t -> p k t", p=128)
    else:
        def f(tg):
            return hTd[:, tg * 512:(tg + 1) * 512].rearrange("(k p) t -> p k t", p=128)
    return f


def flash_pipeline(S, tiles, emit_s, emit_mid, emit_av, lookahead=2):
    pend = []
    for tl in tiles:
        emit_s(tl)
        emit_mid(tl)
        pend.append(tl)
        if len(pend) > lookahead:
            emit_av(pend.pop(0))
    for tl in pend:
        emit_av(tl)


def emit_mods(nc, pfx, binds):
    C = Ctx(nc, pfx, binds)
    es = C.es
    adawd = C.din("adaw", [9, 128, 8, D])
    adabd = C.din("adab", [1, 9, D])
    gd = C.din("g", [1, 6, D])
    cd = C.din("c", [128, 8])
    outd = binds["rows_out"]
    with es:
        S = Sched(nc, es, pfx)
        adas = [C.sb("adas%d" % i, [128, 8, D], F32) for i in range(2)]
        adab = C.sb("adab", [1, 9, D], F32)
        gl = C.sb("gl", [1, 6, D], F32)
        cst = C.sb("cst", [128, 8], F32)
        cond = C.sb("cond", [128, 8], F32)
        mod = C.sb("mod", [1, 9, D], F32)
        R = C.sb("R", [1, 9, D], F32)
        tmp = C.sb("tmp", [1, D], F32)
        PP = [C.ps("PP%d" % i, [128, 512], F32) for i in range(4)]
        S.psum_keys.update(C.psum_names)
        S.dma("sync", cst[:], cd, writes=["cst"])
        S.dma("sync", adab[:], adabd, writes=["adab"])
        S.dma("sync", gl[:], gd, writes=["gl"])
        S.op("scalar", lambda e: e.activation(out=cond[:], in_=cst[:], func=AF.Silu), reads=["cst"], writes=["cond"])
        pc = [0]
        for v in range(9):
            ad = adas[v % 2]
            S.dma("sync" if v % 2 == 0 else "scalar", ad[:], adawd[v], writes=[ad.name])
            for h2 in range(2):
                Pp = PP[pc[0] % 4]
                pc[0] += 1
                for k in range(8):
                    S.op("tensor", lambda e: e.matmul(Pp[0:1, :], lhsT=cond[:, k:k + 1], rhs=ad[:, k, h2 * 512:(h2 + 1) * 512],
                                                      start=(k == 0), stop=(k == 7)),
                         reads=["cond", ad.name], writes=[Pp.name])
                S.op("vector", lambda e: e.tensor_tensor(out=mod[0:1, v, h2 * 512:(h2 + 1) * 512], in0=Pp[0:1, :],
                                                         in1=adab[0:1, v, h2 * 512:(h2 + 1) * 512], op=ALU.add),
                     reads=[Pp.name, "adab"], writes=["mod%d" % v])
        for s_ in range(3):
            res_w = 1.0 if s_ == 1 else 0.5
            sh, sc, gt = 3 * s_, 3 * s_ + 1, 3 * s_ + 2
            S.op("vector", lambda e: e.tensor_scalar(out=tmp[:], in0=mod[0:1, sc, :], scalar1=1.0, scalar2=None, op0=ALU.add),
                 reads=["mod%d" % sc], writes=["tmp"])
            S.op("vector", lambda e: e.tensor_tensor(out=R[0:1, 3 * s_ + 0, :], in0=tmp[:], in1=gl[0:1, 2 * s_, :], op=ALU.mult),
                 reads=["tmp", "gl"], writes=["R"])
            S.op("vector", lambda e: e.tensor_copy(out=R[0:1, 3 * s_ + 1, :], in_=mod[0:1, sh, :]), reads=["mod%d" % sh, "R"], writes=["R"])
            S.op("vector", lambda e: e.tensor_scalar(out=tmp[:], in0=mod[0:1, gt, :], scalar1=float(res_w), scalar2=None, op0=ALU.mult),
                 reads=["mod%d" % gt, "R"], writes=["tmp"])
            S.op("vector", lambda e: e.tensor_tensor(out=R[0:1, 3 * s_ + 2, :], in0=tmp[:], in1=gl[0:1, 2 * s_ + 1, :], op=ALU.mult),
                 reads=["tmp", "gl", "R"], writes=["R"])
        S.dma("sync", outd.rearrange("(o r) d -> o r d", o=1), R[:], reads=["R"], is_output=True)
        S.close()


NREL = 67


def emit_diff(nc, pfx, binds, lam_init, gathered):
    C = Ctx(nc, pfx, binds)
    nc, es = C.nc, C.es
    hTd = C.din("hT", [D, T], BF16)
    hsrc = hT_source(hTd, gathered)
    odst = None
    wqd = C.din("wq", [128, 8, 256])
    wkd = C.din("wk", [128, 8, 256])
    wvd = C.din("wv", [128, 8, 256])
    based = C.din("base", [128, 2, 512])
    cbd = C.din("cb", [128, 2, NREL])
    dmaskd = C.din("dmask", [128, 4, 512], BF16)
    lamd = C.din("lam", [128, 256])
    subgd = C.din("subg", [128, 128])
    identd = C.din("ident", [128, 128])
    oTd = C.dout("oT", [256, T], BF16)
    odst = o_dest(oTd, gathered)
    with es:
        S = Sched(nc, es, pfx)
        QT = [C.sb("QT%d" % h, [128, T], BF16) for h in range(2)]
        KT = [C.sb("KT%d" % h, [128, T], BF16) for h in range(2)]
        Vaug = C.sb("Vaug", [128, 64, 2, 129], BF16)
        wq = C.sb("wq", [128, 8, 256], BF16)
        wk = C.sb("wk", [128, 8, 256], BF16)
        wv = C.sb("wv", [128, 8, 256], BF16)
        hTs = [C.sb("hTs%d" % i, [128, 8, 512], BF16) for i in range(2)]
        base = C.sb("base", [128, 2, 512], F32)
        cbt = C.sb("cbt", [128, 2, NREL], F32)
        dmask = C.sb("dmask", [128, 4, 512], BF16)
        tb = [C.sb("tb%d" % i, [128, 512], F32) for i in range(2)]
        Pb = [C.sb("Pb%d" % i, [128, 512], BF16) for i in range(4)]
        lam = C.sb("lam", [128, 256], F32)
        subg = C.sb("subg", [128, 128], F32)
        identf = C.sb("identf", [128, 128], F32)
        identb = C.sb("identb", [128, 128], BF16)
        sm = C.sb("sm", [128, 16], F32)
        lamneg = C.sb("lamneg", [128, 1], F32)
        epsb = C.sb("epsb", [128, 1], F32)
        o0s = C.sb("o0s", [128, 128], F32)
        od = C.sb("od", [128, 128], F32)
        junk = C.sb("junk", [128, 256], F32)
        ob = C.sb("ob", [128, 4, 256], BF16)
        oTs = [C.sb("oTs%d" % i, [128, 2, 512], BF16) for i in range(2)]
        SB_ = [C.ps("Sb%d" % i, [128, 512], F32) for i in range(3)]
        OB = [[C.ps("O%d%d" % (m, p), [128, 512], F32) for p in range(2)] for m in range(2)]
        PT = C.ps("PT", [128, 1024], BF16)
        S.psum_keys.update(C.psum_names)

        S.dma("sync", identf[:], identd[:, :], writes=["identf"])
        S.op("vector", lambda e: e.tensor_copy(out=identb[:], in_=identf[:]), reads=["identf"], writes=["identb"])
        S.dma("gpsimd", wq[:], wqd[:, :, :], writes=["wq"])
        S.dma("gpsimd", wk[:], wkd[:, :, :], writes=["wk"])
        S.dma("gpsimd", wv[:], wvd[:, :, :], writes=["wv"])
        S.dma("sync", base[:], based[:, :, :], writes=["base"])
        S.dma("sync", cbt[:], cbd[:, :, :], writes=["cbt"])
        S.dma("sync", dmask[:], dmaskd[:, :, :], writes=["dmask"])
        S.dma("sync", lam[:], lamd[:, :], writes=["lam"])
        S.dma("sync", subg[:], subgd[:, :], writes=["subg"])
        S.op("vector", lambda e: e.memset(epsb[:], EPS), writes=["epsb"])
        S.op("vector", lambda e: e.memset(Vaug[:, :, :, 128:129], 1.0), writes=["Vones"])
        S.op("vector", lambda e: e.scalar_tensor_tensor(out=junk[:, 0:64], in0=lam[:, 0:64], scalar=1.0, in1=lam[:, 64:128],
                                                        op0=ALU.mult, op1=ALU.mult, accum_out=sm[:, 0:1]),
             reads=["lam"], writes=["junk", "sm0"])
        S.op("vector", lambda e: e.scalar_tensor_tensor(out=junk[:, 0:64], in0=lam[:, 128:192], scalar=1.0, in1=lam[:, 192:256],
                                                        op0=ALU.mult, op1=ALU.mult, accum_out=sm[:, 1:2]),
             reads=["lam", "junk"], writes=["junk", "sm1"])
        S.op("scalar", lambda e: e.activation(out=sm[:, 0:2], in_=sm[:, 0:2], func=AF.Exp), reads=["sm0", "sm1"],
             writes=["sm0", "sm1"])
        S.op("vector", lambda e: e.tensor_tensor(out=sm[:, 2:3], in0=sm[:, 1:2], in1=sm[:, 0:1], op=ALU.subtract),
             reads=["sm0", "sm1"], writes=["sm2"])
        S.op("vector", lambda e: e.tensor_scalar(out=lamneg[:], in0=sm[:, 2:3], scalar1=-float(lam_init), scalar2=None, op0=ALU.add),
             reads=["sm2"], writes=["lamneg"])
        S.op("vector", lambda e: e.tensor_scalar(out=subg[:], in0=subg[:], scalar1=float(1.0 - lam_init), scalar2=None, op0=ALU.mult),
             reads=["subg"], writes=["subg"])

        cp = [0]

        def evac(dst, src, rk, wk_):
            eng = "scalar" if cp[0] % 2 == 0 else "vector"
            cp[0] += 1
            if eng == "scalar":
                S.op("scalar", lambda e: e.activation(out=dst, in_=src, func=AF.Copy), reads=rk, writes=wk_)
            else:
                S.op("vector", lambda e: e.tensor_copy(out=dst, in_=src), reads=rk, writes=wk_)

        bank = [0]
        for tg in range(16):
            hs = hTs[tg % 2]
            hk = "hTs%d" % (tg % 2)
            S.dma("sync", hs[:], hsrc(tg), writes=[hk])
            for (w, wname, dstl, dname) in [(wq, "wq", QT, "QT"), (wk, "wk", KT, "KT")]:
                for h in range(2):
                    Pp = SB_[bank[0] % 3]
                    bank[0] += 1
                    for k in range(8):
                        S.op("tensor", lambda e: e.matmul(Pp[:], lhsT=w[:, k, h * 128:(h + 1) * 128], rhs=hs[:, k, :],
                                                          start=(k == 0), stop=(k == 7)),
                             reads=[wname, hk], writes=[Pp.name])
                    evac(dstl[h][:, tg * 512:(tg + 1) * 512], Pp[:], [Pp.name], ["%s%d_%d" % (dname, h, tg)])
            for tt in range(4):
                Pp = SB_[bank[0] % 3]
                bank[0] += 1
                for k in range(8):
                    S.op("tensor", lambda e: e.matmul(Pp[:, 0:256], lhsT=hs[:, k, tt * 128:(tt + 1) * 128], rhs=wv[:, k, :],
                                                      start=(k == 0), stop=(k == 7)),
                         reads=["wv", hk], writes=[Pp.name])
                blk = tg * 4 + tt
                evac(Vaug[:, blk, :, 0:128], Pp[:, 0:256].rearrange("p (h e) -> p h e", h=2), [Pp.name], ["V_%d" % blk])

        scale = 64 ** -0.5
        ctr = {"s": 0, "t": 0, "p": 0, "o": 0}
        for g in range(16):
            for h in range(2):
                tiles = [dict(kb=kb, m=m) for kb in range(4 * g + 4) for m in range(2)]
                first_av = {}

                def emit_s(tl):
                    kb, m = tl["kb"], tl["m"]
                    Sp = SB_[ctr["s"] % 3]
                    ctr["s"] += 1
                    tl["Sp"] = Sp
                    r = kb - 4 * g
                    S.op("tensor", lambda e: e.matmul(Sp[:], lhsT=KT[h][m * 64:(m + 1) * 64, kb * 128:(kb + 1) * 128],
                                                      rhs=QT[h][m * 64:(m + 1) * 64, g * 512:(g + 1) * 512], start=True, stop=(r < 0)),
                         reads=["KT%d_%d" % (h, kb // 4), "QT%d_%d" % (h, g)], writes=[Sp.name])
                    if r >= 0:
                        S.op("tensor", lambda e: e.matmul(Sp[:], lhsT=identb[:], rhs=dmask[:, r, :], start=False, stop=True),
                             reads=["identb", "dmask"], writes=[Sp.name])

                def emit_mid(tl):
                    kb, m, Sp = tl["kb"], tl["m"], tl["Sp"]
                    tt_ = tb[ctr["t"] % 2]
                    ctr["t"] += 1
                    Pt = Pb[ctr["p"] % 4]
                    ctr["p"] += 1
                    tl["P"] = Pt
                    S.op("vector", lambda e: e.scalar_tensor_tensor(out=tt_[:], in0=Sp[:], scalar=scale, in1=base[:, h, :],
                                                                    op0=ALU.mult, op1=ALU.add),
                         reads=[Sp.name, "base"], writes=[tt_.name])
                    rel = 4 * g - kb + 3
                    S.op("scalar", lambda e: e.activation(out=Pt[:], in_=tt_[:], func=AF.Exp, bias=cbt[:, h, rel:rel + 1], scale=1.0),
                         reads=[tt_.name, "cbt"], writes=[Pt.name])

                def emit_av(tl):
                    kb, m, Pt = tl["kb"], tl["m"], tl["P"]
                    r = kb - 4 * g
                    for qb in range(4):
                        if qb < r:
                            continue
                        p = qb // 2
                        O = OB[m][p]
                        st_ = (m, p) not in first_av
                        first_av[(m, p)] = True
                        c0 = (qb % 2) * 129
                        S.op("tensor", lambda e: e.matmul(O[:, c0:c0 + 129], lhsT=Pt[:, qb * 128:(qb + 1) * 128], rhs=Vaug[:, kb, h, :],
                                                          start=st_, stop=(kb == 4 * g + qb), skip_group_check=True),
                             reads=[Pt.name, "V_%d" % kb, "Vones"], writes=[O.name])

                flash_pipeline(S, tiles, emit_s, emit_mid, emit_av)
                for qb in range(4):
                    p = qb // 2
                    c0 = (qb % 2) * 129
                    O0, O1 = OB[0][p], OB[1][p]
                    S.op("vector", lambda e: e.reciprocal(out=sm[:, 4:5], in_=O0[:, c0 + 128:c0 + 129]), reads=[O0.name], writes=["sm4"])
                    S.op("vector", lambda e: e.reciprocal(out=sm[:, 5:6], in_=O1[:, c0 + 128:c0 + 129]), reads=[O1.name], writes=["sm5"])
                    S.op("vector", lambda e: e.tensor_tensor(out=sm[:, 5:6], in0=sm[:, 5:6], in1=lamneg[:], op=ALU.mult),
                         reads=["sm5", "lamneg"], writes=["sm5"])
                    S.op("vector", lambda e: e.tensor_scalar(out=o0s[:], in0=O0[:, c0:c0 + 128], scalar1=sm[:, 4:5], scalar2=None, op0=ALU.mult),
                         reads=[O0.name, "sm4"], writes=["o0s"])
                    S.op("vector", lambda e: e.scalar_tensor_tensor(out=od[:], in0=O1[:, c0:c0 + 128], scalar=sm[:, 5:6], in1=o0s[:],
                                                                    op0=ALU.mult, op1=ALU.add),
                         reads=[O1.name, "sm5", "o0s"], writes=["od"])
                    S.op("scalar", lambda e: e.activation(out=junk[:, 0:128], in_=od[:], func=AF.Square, accum_out=sm[:, 6:7]),
                         reads=["od"], writes=["junk", "sm6"])
                    S.op("scalar", lambda e: e.activation(out=sm[:, 6:7], in_=sm[:, 6:7], func=AF.Sqrt, bias=epsb[:], scale=1.0 / 128),
                         reads=["sm6", "epsb"], writes=["sm6"])
                    S.op("vector", lambda e: e.reciprocal(out=sm[:, 6:7], in_=sm[:, 6:7]), reads=["sm6"], writes=["sm6"])
                    S.op("vector", lambda e: e.scalar_tensor_tensor(out=ob[:, qb, h * 128:(h + 1) * 128], in0=od[:], scalar=sm[:, 6:7],
                                                                    in1=subg[:], op0=ALU.mult, op1=ALU.mult),
                         reads=["od", "sm6", "subg"], writes=["ob"])
            ot = oTs[ctr["o"] % 2]
            otk = "oTs%d" % (ctr["o"] % 2)
            ctr["o"] += 1
            for qb in range(4):
                for c in range(2):
                    S.op("tensor", lambda e: e.transpose(out=PT[:, c * 512 + qb * 128:c * 512 + (qb + 1) * 128],
                                                         in_=ob[:, qb, c * 128:(c + 1) * 128], identity=identb[:]),
                         reads=["ob", "identb"], writes=["PT"])
            S.op("vector", lambda e: e.tensor_copy(out=ot[:], in_=PT[:].rearrange("p (c t) -> p c t", c=2)), reads=["PT"], writes=[otk])
            S.dma("sync", odst(g), ot[:], reads=[otk], is_output=True)
        S.close()


def alibi_slopes_np(n):
    return np.exp2(-8.0 * np.arange(1, n + 1, dtype=np.float64) / n)


def diag_mask_tiles(strict):
    jj = np.arange(128)[:, None, None]
    r = np.arange(4)[None, :, None]
    q = np.arange(512)[None, None, :]
    d = q - jj - 128 * r
    ok = d >= (1 if strict else 0)
    return np.where(ok, 0.0, -BIGNEG).astype(np.float32)


def base_tile(slope):
    jj = np.arange(128)[:, None]
    q = np.arange(512)[None, :]
    return (-slope * (q - jj)).astype(np.float32)


def cb_table(slope):
    rel = np.arange(NREL) - 3
    return np.ascontiguousarray(np.broadcast_to((-slope * 128.0 * rel)[None, :], (128, NREL))).astype(np.float32)


def arrange_w(wcols):
    n = wcols.shape[1]
    return np.ascontiguousarray(wcols.reshape(8, 128, n).transpose(1, 0, 2))


def diff_inputs(pfx, w_in, lam, subln_g):
    slopes = alibi_slopes_np(8)
    dm = diag_mask_tiles(False).astype(ml_dtypes.bfloat16)
    lamb = np.ascontiguousarray(np.broadcast_to(lam.reshape(1, 256), (128, 256)))
    sgb = np.ascontiguousarray(np.broadcast_to(subln_g.reshape(1, 128), (128, 128)))
    out = []
    for hg in range(4):
        hs = [2 * hg, 2 * hg + 1]
        cols = np.concatenate([np.arange(h * 128, (h + 1) * 128) for h in hs])
        out.append({
            pfx + "wq": arrange_w(w_in[:, cols]),
            pfx + "wk": arrange_w(w_in[:, 1024 + cols]),
            pfx + "wv": arrange_w(w_in[:, 2048 + cols]),
            pfx + "base": np.ascontiguousarray(np.stack([base_tile(slopes[h]) for h in hs], axis=1)),
            pfx + "cb": np.ascontiguousarray(np.stack([cb_table(slopes[h]) for h in hs], axis=1)),
            pfx + "dmask": dm, pfx + "lam": lamb, pfx + "subg": sgb, pfx + "ident": _IDENT,
        })
    return out


def emit_sb(nc, pfx, binds, gathered):
    C = Ctx(nc, pfx, binds)
    nc, es = C.nc, C.es
    hTd = C.din("hT", [D, T], BF16)
    hsrc = hT_source(hTd, gathered)
    odst = None
    wqd = C.din("wq", [128, 8, 256])
    wkd = C.din("wk", [128, 8, 256])
    wvd = C.din("wv", [128, 8, 256])
    m01d = C.din("m01", [128, 4, 512], BF16)
    trid = C.din("tri", [128, 2, 128], BF16)
    identd = C.din("ident", [128, 128])
    oTd = C.dout("oT", [256, T], BF16)
    odst = o_dest(oTd, gathered)
    with es:
        S = Sched(nc, es, pfx)
        QT = [C.sb("QT%d" % h, [128, T], BF16) for h in range(2)]
        KT = [C.sb("KT%d" % h, [128, T], BF16) for h in range(2)]
        V = C.sb("V", [128, 64, 256], BF16)
        wq = C.sb("wq", [128, 8, 256], BF16)
        wk = C.sb("wk", [128, 8, 256], BF16)
        wv = C.sb("wv", [128, 8, 256], BF16)
        hTs = [C.sb("hTs%d" % i, [128, 8, 512], BF16) for i in range(2)]
        m01 = C.sb("m01", [128, 4, 512], BF16)
        tri = C.sb("tri", [128, 2, 128], BF16)
        eb = [C.sb("eb%d" % i, [128, 512], F32) for i in range(3)]
        spb = [C.sb("spb%d" % i, [128, 512], BF16) for i in range(3)]
        wb = [C.sb("wb%d" % i, [128, 512], F32) for i in range(2)]
        ab = [C.sb("ab%d" % i, [128, 512], BF16) for i in range(3)]
        identf = C.sb("identf", [128, 128], F32)
        identb = C.sb("identb", [128, 128], BF16)
        ob = C.sb("ob", [128, 4, 256], BF16)
        oTs = [C.sb("oTs%d" % i, [128, 2, 512], BF16) for i in range(2)]
        ZB = [C.ps("Zb%d" % i, [128, 512], F32) for i in range(3)]
        XB = [C.ps("Xb%d" % i, [128, 512], F32) for i in range(2)]
        OBk = [C.ps("Ob%d" % i, [128, 512], F32) for i in range(2)]
        PT = C.ps("PT", [128, 1024], BF16)
        S.psum_keys.update(C.psum_names)

        S.dma("sync", identf[:], identd[:, :], writes=["identf"])
        S.op("vector", lambda e: e.tensor_copy(out=identb[:], in_=identf[:]), reads=["identf"], writes=["identb"])
        S.dma("gpsimd", wq[:], wqd[:, :, :], writes=["wq"])
        S.dma("gpsimd", wk[:], wkd[:, :, :], writes=["wk"])
        S.dma("gpsimd", wv[:], wvd[:, :, :], writes=["wv"])
        S.dma("sync", m01[:], m01d[:, :, :], writes=["m01"])
        S.dma("sync", tri[:], trid[:, :, :], writes=["tri"])

        cp = [0]

        def evac(dst, src, rk, wk_):
            eng = "scalar" if cp[0] % 2 == 0 else "vector"
            cp[0] += 1
            if eng == "scalar":
                S.op("scalar", lambda e: e.activation(out=dst, in_=src, func=AF.Copy), reads=rk, writes=wk_)
            else:
                S.op("vector", lambda e: e.tensor_copy(out=dst, in_=src), reads=rk, writes=wk_)

        bank = [0]
        for tg in range(16):
            hs = hTs[tg % 2]
            hk = "hTs%d" % (tg % 2)
            S.dma("sync", hs[:], hsrc(tg), writes=[hk])
            for (w, wname, dstl, dname) in [(wq, "wq", QT, "QT"), (wk, "wk", KT, "KT")]:
                for h in range(2):
                    Pp = ZB[bank[0] % 3]
                    bank[0] += 1
                    for k in range(8):
                        S.op("tensor", lambda e: e.matmul(Pp[:], lhsT=w[:, k, h * 128:(h + 1) * 128], rhs=hs[:, k, :],
                                                          start=(k == 0), stop=(k == 7)),
                             reads=[wname, hk], writes=[Pp.name])
                    evac(dstl[h][:, tg * 512:(tg + 1) * 512], Pp[:], [Pp.name], ["%s%d_%d" % (dname, h, tg)])
            for tt in range(4):
                Pp = ZB[bank[0] % 3]
                bank[0] += 1
                for k in range(8):
                    S.op("tensor", lambda e: e.matmul(Pp[:, 0:256], lhsT=hs[:, k, tt * 128:(tt + 1) * 128], rhs=wv[:, k, :],
                                                      start=(k == 0), stop=(k == 7)),
                         reads=["wv", hk], writes=[Pp.name])
                blk = tg * 4 + tt
                evac(V[:, blk, :], Pp[:, 0:256], [Pp.name], ["V_%d" % blk])

        scale = 64 ** -0.5
        ctr = {"z": 0, "e": 0, "w": 0, "a": 0, "o": 0, "chain": 0}
        for g in range(16):
            chains = []
            for hh in range(4):
                ch = ctr["chain"]
                ctr["chain"] += 1
                kbs = list(range(4 * g + 3, -1, -1))
                av0 = [True]
                chains.append([dict(hh=hh, kb=kb, first=(i == 0), last=(i == len(kbs) - 1), X=XB[ch % 2], O=OBk[ch % 2], av0=av0)
                               for i, kb in enumerate(kbs)])
            tiles = []
            for pr in range(2):
                for ta, tb_ in zip(chains[2 * pr], chains[2 * pr + 1]):
                    tiles += [ta, tb_]

            def emit_Z(tl):
                hh, kb = tl["hh"], tl["kb"]
                p, half = hh // 2, hh % 2
                Zp = ZB[ctr["z"] % 3]
                ctr["z"] += 1
                tl["Z"] = Zp
                S.op("tensor", lambda e: e.matmul(Zp[:], lhsT=KT[p][half * 64:(half + 1) * 64, kb * 128:(kb + 1) * 128],
                                                  rhs=QT[p][half * 64:(half + 1) * 64, g * 512:(g + 1) * 512], start=True, stop=True),
                     reads=["KT%d_%d" % (p, kb // 4), "QT%d_%d" % (p, g)], writes=[Zp.name])

            def emit_esp(tl):
                kb, Zp = tl["kb"], tl["Z"]
                i = ctr["e"] % 3
                ctr["e"] += 1
                tl["e"], tl["sp"] = eb[i], spb[i]
                r = kb - 4 * g
                S.op("scalar", lambda e: e.activation(out=eb[i][:], in_=Zp[:], func=AF.Exp, scale=scale), reads=[Zp.name], writes=[eb[i].name])
                S.op("scalar", lambda e: e.activation(out=spb[i][:], in_=eb[i][:], func=AF.Ln, bias=1.0, scale=1.0),
                     reads=[eb[i].name], writes=[spb[i].name])
                if r >= 0:
                    S.op("gpsimd", lambda e: e.tensor_tensor(out=spb[i][:], in0=spb[i][:], in1=m01[:, r, :], op=ALU.mult),
                         reads=[spb[i].name, "m01"], writes=[spb[i].name])
                    S.op("gpsimd", lambda e: e.tensor_tensor(out=eb[i][:], in0=eb[i][:], in1=m01[:, r, :], op=ALU.mult),
                         reads=[eb[i].name, "m01"], writes=[eb[i].name])

            def emit_L(tl):
                X, sp = tl["X"], tl["sp"]
                S.op("tensor", lambda e: e.matmul(X[:], lhsT=tri[:, 0, :], rhs=sp[:], start=tl["first"], stop=False, skip_group_check=True),
                     reads=["tri", sp.name], writes=[X.name])

            def emit_w(tl):
                X = tl["X"]
                wi = wb[ctr["w"] % 2]
                ctr["w"] += 1
                tl["w"] = wi
                S.op("scalar", lambda e: e.activation(out=wi[:], in_=X[:], func=AF.Exp, scale=-1.0), reads=[X.name], writes=[wi.name])

            def emit_U(tl):
                X, sp, ee, wi = tl["X"], tl["sp"], tl["e"], tl["w"]
                ai = ab[ctr["a"] % 3]
                ctr["a"] += 1
                tl["a"] = ai
                S.op("tensor", lambda e: e.matmul(X[:], lhsT=tri[:, 1, :], rhs=sp[:], start=False, stop=tl["last"], skip_group_check=True),
                     reads=["tri", sp.name], writes=[X.name])
                S.op("vector", lambda e: e.tensor_tensor(out=ai[:], in0=ee[:], in1=wi[:], op=ALU.mult),
                     reads=[ee.name, wi.name], writes=[ai.name])

            def stage2(tl):
                hh, kb, O, ai = tl["hh"], tl["kb"], tl["O"], tl["a"]
                r = kb - 4 * g
                for qb in range(4):
                    if qb < r:
                        continue
                    st_ = tl["av0"][0]
                    tl["av0"][0] = False
                    S.op("tensor", lambda e: e.matmul(O[:, qb * 64:(qb + 1) * 64], lhsT=ai[:, qb * 128:(qb + 1) * 128],
                                                      rhs=V[:, kb, hh * 64:(hh + 1) * 64], start=st_, stop=(kb == 0),
                                                      skip_group_check=True),
                         reads=[ai.name, "V_%d" % kb], writes=[O.name])
                if tl["last"]:
                    S.op("vector", lambda e: e.tensor_copy(out=ob[:, :, hh * 64:(hh + 1) * 64],
                                                           in_=O[:, 0:256].rearrange("p (q d) -> p q d", q=4)),
                         reads=[O.name], writes=["ob"])

            n = len(tiles)
            for i in range(n + 2):
                if 1 <= i <= n:
                    emit_L(tiles[i - 1])
                if i < n:
                    emit_Z(tiles[i])
                if 1 <= i <= n:
                    emit_w(tiles[i - 1])
                if 2 <= i:
                    stage2(tiles[i - 2])
                if 1 <= i <= n:
                    emit_U(tiles[i - 1])
                if i < n:
                    emit_esp(tiles[i])
            ot = oTs[ctr["o"] % 2]
            otk = ot.name
            ctr["o"] += 1
            for qb in range(4):
                for c in range(2):
                    S.op("tensor", lambda e: e.transpose(out=PT[:, c * 512 + qb * 128:c * 512 + (qb + 1) * 128],
                                                         in_=ob[:, qb, c * 128:(c + 1) * 128], identity=identb[:]),
                         reads=["ob", "identb"], writes=["PT"])
            S.op("vector", lambda e: e.tensor_copy(out=ot[:], in_=PT[:].rearrange("p (c t) -> p c t", c=2)), reads=["PT"], writes=[otk])
            S.dma("sync", odst(g), ot[:], reads=[otk], is_output=True)
        S.close()


def sb_inputs(pfx, w_in):
    jj = np.arange(128)[:, None, None]
    r = np.arange(4)[None, :, None]
    q = np.arange(512)[None, None, :]
    m01 = ((q - jj - 128 * r) >= 1).astype(np.float32).astype(ml_dtypes.bfloat16)
    mm = np.arange(128)[:, None]
    j2 = np.arange(128)[None, :]
    tri = np.ascontiguousarray(np.stack([(mm >= j2), (mm < j2)], axis=1).astype(np.float32).astype(ml_dtypes.bfloat16))
    out = []
    for hg in range(4):
        cols = np.arange(hg * 256, (hg + 1) * 256)
        out.append({
            pfx + "wq": arrange_w(w_in[:, cols]),
            pfx + "wk": arrange_w(w_in[:, 1024 + cols]),
            pfx + "wv": arrange_w(w_in[:, 2048 + cols]),
            pfx + "m01": m01, pfx + "tri": tri, pfx + "ident": _IDENT,
        })
    return out


NSA_FORCE = 1e4
NSA_NEG = -1e30


class _Stop(Exception):
    pass


def emit_nsa(nc, pfx, binds, gathered, dbg=None):
    C = Ctx(nc, pfx, binds)
    nc, es = C.nc, C.es
    hTd = C.din("hT", [D, T], BF16)
    hsrc = hT_source(hTd, gathered)
    odst = None
    wfmd = C.din("wfm", [128, 8, 640])
    wtmd = C.din("wtm", [128, 8, 140])
    cw1d = C.din("cw1", [128, 32, 256])
    cped = C.din("cpe", [128, 32])
    cw2kd = C.din("cw2k", [128, 2, 128])
    cw2vd = C.din("cw2v", [128, 2, 64])
    ovld = C.din("ovl", [128, 4, 128], BF16)
    slpd = C.din("slp", [128, 4])
    cbd = C.din("cb", [128, 4, NREL])
    cbcd = C.din("cbc", [128, 4, 16])
    base0d = C.din("base0", [128, 512])
    basec0d = C.din("basec0", [128, 512])
    cmaskd = C.din("cmask", [128, 5, 512], BF16)
    dmaskd = C.din("dmask", [128, 4, 512], BF16)
    wmaskd = C.din("wmask", [128, 8, 512], BF16)
    indd = C.din("ind", [128, T], BF16)
    adjd = C.din("adj", [64, 128, 128])
    identd = C.din("ident", [128, 128])
    oTd = C.dout("oT", [256, T], BF16)
    odst = o_dest(oTd, gathered)
    with es:
        S = Sched(nc, es, pfx)
        try:
            QT = [C.sb("QT%d" % h, [128, T], BF16) for h in range(2)]
            ksT = C.sb("ksT", [128, T], BF16)
            kwT = C.sb("kwT", [128, T], BF16)
            kcvT = C.sb("kcvT", [128, T], BF16)
            vsA = C.sb("vsA", [128, 64, 65], BF16)
            vwA = C.sb("vwA", [128, 64, 65], BF16)
            gates = C.sb("gates", [128, 64, 12], F32)
            PBUF = C.sb("PBUF", [128, 14464], BF16)
            hTs = [PBUF[:, i * 4096:(i + 1) * 4096].rearrange("p (k t) -> p k t", k=8) for i in range(2)]
            wfm = PBUF[:, 8192:8192 + 5120].rearrange("p (k n) -> p k n", k=8)
            wtm = PBUF[:, 13312:13312 + 1120].rearrange("p (k n) -> p k n", k=8)
            cw1 = PBUF[:, 0:8192].rearrange("p (l f) -> p l f", l=32)
            ind = PBUF[:, 0:8192]
            cpe = C.sb("cpe", [128, 32], BF16)
            cw2k = C.sb("cw2k", [128, 2, 128], BF16)
            cw2v = C.sb("cw2v", [128, 2, 64], BF16)
            slp = C.sb("slp", [128, 4], F32)
            cbt = C.sb("cbt", [128, 4, NREL], F32)
            cbct = C.sb("cbct", [128, 4, 16], F32)
            base0 = C.sb("base0", [128, 512], F32)
            basec0 = C.sb("basec0", [128, 512], F32)
            cmask = C.sb("cmask", [128, 5, 512], BF16)
            dmask = C.sb("dmask", [128, 4, 512], BF16)
            wmask = C.sb("wmask", [128, 8, 512], BF16)
            tb = [C.sb("tb%d" % i, [128, 512], F32) for i in range(3)]
            Pb = [C.sb("Pb%d" % i, [128, 512], BF16) for i in range(4)]
            kcmpT = C.sb("kcmpT", [128, 512], BF16)
            vcA = C.sb("vcA", [128, 4, 193], BF16)
            glb = [C.sb("glb%d" % i, [128, 512], BF16) for i in range(4)]
            peb = C.sb("peb", [128, 4], F32)
            imp = C.sb("imp", [128, 4, 128], F32)
            adjt = [C.sb("adjt%d" % i, [128, 128], F32) for i in range(2)]
            impa = C.sb("impa", [128, 128], F32)
            impb = C.sb("impb", [128, 128], F32)
            m8 = C.sb("m8", [128, 16], F32)
            selb = C.sb("selb", [128, 128], BF16)
            MBT = [C.sb("MBT%d" % i, [128, 512], BF16) for i in range(2)]
            acco = C.sb("acco", [128, 4, 256], F32)
            ob = C.sb("ob", [128, 4, 256], BF16)
            oTs = [C.sb("oTs%d" % i, [128, 2, 512], BF16) for i in range(2)]
            sm = C.sb("sm", [128, 8], F32)
            identf = C.sb("identf", [128, 128], F32)
            identb = C.sb("identb", [128, 128], BF16)
            SB_ = [C.ps("Sb%d" % i, [128, 512], F32) for i in range(3)]
            AC = [C.ps("Ac%d" % i, [128, 512], F32) for i in range(4)]
            PT = C.ps("PT", [128, 1024], BF16)
            S.psum_keys.update(C.psum_names)

            S.dma("sync", identf[:], identd[:, :], writes=["identf"])
            S.op("vector", lambda e: e.tensor_copy(out=identb[:], in_=identf[:]), reads=["identf"], writes=["identb"])
            S.dma("gpsimd", wfm, wfmd[:, :, :], writes=["wfm"])
            S.dma("gpsimd", wtm, wtmd[:, :, :], writes=["wtm"])
            S.dma("gpsimd", cpe[:], cped[:, :], writes=["cpe"])
            S.dma("gpsimd", cw2k[:], cw2kd[:, :, :], writes=["cw2k"])
            S.dma("gpsimd", cw2v[:], cw2vd[:, :, :], writes=["cw2v"])
            for (dst, src, key) in [(slp, slpd, "slp"), (cbt, cbd, "cbt"), (cbct, cbcd, "cbct"), (base0, base0d, "base0"),
                                    (basec0, basec0d, "basec0"), (cmask, cmaskd, "cmask"), (dmask, dmaskd, "dmask"),
                                    (wmask, wmaskd, "wmask")]:
                S.dma("sync", dst[:], src, writes=[key])
            S.op("vector", lambda e: e.memset(vsA[:, :, 64:65], 1.0), writes=["vsones"])
            S.op("vector", lambda e: e.memset(vwA[:, :, 64:65], 1.0), writes=["vwones"])
            S.op("vector", lambda e: e.memset(vcA[:], 0.0), writes=["vcA"])
            S.op("vector", lambda e: e.memset(kcmpT[:], 0.0), writes=["kcmpT"])
            S.op("vector", lambda e: e.memset(vcA[:, :, 64:65], 1.0), reads=["vcA"], writes=["vcA"])
            S.dma("sync", vcA[:, :, 65:193], ovld[:, :, :], reads=["vcA"], writes=["vcA"])

            if dbg == 'const':
                raise _Stop
            cp = [0]

            def evac(dst, src, rk, wk_, scale=None):
                eng = "scalar" if cp[0] % 2 == 0 else "vector"
                cp[0] += 1
                if eng == "scalar" and scale is None:
                    S.op("scalar", lambda e: e.activation(out=dst, in_=src, func=AF.Copy), reads=rk, writes=wk_)
                else:
                    if scale is None:
                        S.op("vector", lambda e: e.tensor_copy(out=dst, in_=src), reads=rk, writes=wk_)
                    else:
                        S.op("vector", lambda e: e.tensor_scalar(out=dst, in0=src, scalar1=float(scale), scalar2=None, op0=ALU.mult),
                             reads=rk, writes=wk_)

            bank = [0]
            fm_dst = [(QT[0], "QT0", 0.125), (QT[1], "QT1", 0.125), (ksT, "ksT", None), (kwT, "kwT", None), (kcvT, "kcvT", None)]
            for tg in range(1 if dbg in ('proj1', 'proj1ns') else 16):
                hs = hTs[tg % 2]
                hk = "hTs%d" % (tg % 2)
                S.dma("sync", hs, hsrc(tg), writes=[hk])
                for fi, (dst, dname, sc) in enumerate(fm_dst):
                    Pp = SB_[bank[0] % 3]
                    bank[0] += 1
                    for k in range(8):
                        S.op("tensor", lambda e: e.matmul(Pp[:], lhsT=wfm[:, k, fi * 128:(fi + 1) * 128], rhs=hs[:, k, :],
                                                          start=(k == 0), stop=(k == 7)),
                             reads=["wfm", hk], writes=[Pp.name])
                    evac(dst[:, tg * 512:(tg + 1) * 512], Pp[:], [Pp.name], ["%s_%d" % (dname, tg)], scale=sc)
                for tt in range(4):
                    Pp = SB_[bank[0] % 3]
                    bank[0] += 1
                    for k in range(8):
                        S.op("tensor", lambda e: e.matmul(Pp[:, 0:140], lhsT=hs[:, k, tt * 128:(tt + 1) * 128], rhs=wtm[:, k, :],
                                                          start=(k == 0), stop=(k == 7)),
                             reads=["wtm", hk], writes=[Pp.name])
                    blk = tg * 4 + tt
                    S.op("vector", lambda e: e.tensor_copy(out=vsA[:, blk, 0:64], in_=Pp[:, 0:64]), reads=[Pp.name], writes=["vs_%d" % blk])
                    S.op("vector", lambda e: e.tensor_copy(out=vwA[:, blk, 0:64], in_=Pp[:, 64:128]), reads=[Pp.name], writes=["vw_%d" % blk])
                    S.op("scalar", lambda e: e.activation(out=gates[:, blk, :], in_=Pp[:, 128:140], func=AF.Exp, scale=-1.0),
                         reads=[Pp.name], writes=["gates_%d" % blk])
                    S.op("vector", lambda e: e.tensor_scalar(out=gates[:, blk, :], in0=gates[:, blk, :], scalar1=1.0, scalar2=None, op0=ALU.add),
                         reads=["gates_%d" % blk], writes=["gates_%d" % blk])
                    S.op("vector", lambda e: e.reciprocal(out=gates[:, blk, :], in_=gates[:, blk, :]),
                         reads=["gates_%d" % blk], writes=["gates_%d" % blk])
            if dbg in ('proj', 'proj1', 'proj1ns'):
                raise _Stop
            S.barrier()

            S.dma("gpsimd", cw1, cw1d[:, :, :], writes=["cw1"])
            kcv = kcvT[:, :].rearrange("p (n s) -> p n s", s=16)
            for j in range(2):
                lo, hi = j * 64, (j + 1) * 64
                for c in range(2):
                    Pp = SB_[bank[0] % 3]
                    bank[0] += 1
                    Pq = AC[0]
                    for l in range(32):
                        S.op("tensor", lambda e: e.matmul(Pq[:, 0:1], lhsT=cw1[lo:hi, l, c * 128:(c + 1) * 128], rhs=cpe[lo:hi, l:l + 1],
                                                          start=(l == 0), stop=(l == 31)),
                             reads=["cw1", "cpe"], writes=[Pq.name])
                    col = j * 2 + c
                    S.op("vector", lambda e: e.tensor_copy(out=peb[:, col:col + 1], in_=Pq[:, 0:1]), reads=[Pq.name], writes=["peb%d" % col])
                    for l in range(32):
                        S.op("tensor", lambda e: e.matmul(Pp[:, 0:511], lhsT=cw1[lo:hi, l, c * 128:(c + 1) * 128],
                                                          rhs=kcv[lo:hi, (l // 16):(l // 16) + 511, l % 16],
                                                          start=(l == 0), stop=(l == 31)),
                             reads=["cw1"] + ["kcvT_%d" % t_ for t_ in range(16)], writes=[Pp.name])
                    xg, x2, ug = tb[0], tb[1], tb[2]
                    S.op("scalar", lambda e: e.activation(out=xg[:, 0:511], in_=Pp[:, 0:511], func=AF.Identity, bias=peb[:, col:col + 1], scale=1.0),
                         reads=[Pp.name, "peb%d" % col], writes=[xg.name])
                    S.op("vector", lambda e: e.tensor_tensor(out=x2[:, 0:511], in0=xg[:, 0:511], in1=xg[:, 0:511], op=ALU.mult),
                         reads=[xg.name], writes=[x2.name])
                    S.op("vector", lambda e: e.tensor_scalar(out=x2[:, 0:511], in0=x2[:, 0:511], scalar1=0.044715, scalar2=1.0,
                                                             op0=ALU.mult, op1=ALU.add), reads=[x2.name], writes=[x2.name])
                    S.op("vector", lambda e: e.tensor_tensor(out=ug[:, 0:511], in0=x2[:, 0:511], in1=xg[:, 0:511], op=ALU.mult),
                         reads=[x2.name, xg.name], writes=[ug.name])
                    S.op("scalar", lambda e: e.activation(out=ug[:, 0:511], in_=ug[:, 0:511], func=AF.Exp, scale=-1.5957691216057308),
                         reads=[ug.name], writes=[ug.name])
                    S.op("vector", lambda e: e.tensor_scalar(out=ug[:, 0:511], in0=ug[:, 0:511], scalar1=1.0, scalar2=None, op0=ALU.add),
                         reads=[ug.name], writes=[ug.name])
                    S.op("vector", lambda e: e.reciprocal(out=ug[:, 0:511], in_=ug[:, 0:511]), reads=[ug.name], writes=[ug.name])
                    gl = glb[j * 2 + c]
                    S.op("vector", lambda e: e.memset(gl[:, 511:512], 0.0), writes=[gl.name])
                    S.op("vector", lambda e: e.tensor_tensor(out=gl[:, 0:511], in0=ug[:, 0:511], in1=xg[:, 0:511], op=ALU.mult),
                         reads=[ug.name, xg.name, gl.name], writes=[gl.name])
            if dbg == 'cmp1':
                raise _Stop
            Pp = SB_[bank[0] % 3]
            bank[0] += 1
            for c in range(2):
                S.op("tensor", lambda e: e.matmul(Pp[:, 0:511], lhsT=cw2k[:, c, :], rhs=glb[c][:, 0:511], start=(c == 0), stop=(c == 1)),
                     reads=["cw2k", glb[c].name], writes=[Pp.name])
            S.op("vector", lambda e: e.tensor_copy(out=kcmpT[:, 0:511], in_=Pp[:, 0:511]), reads=[Pp.name, "kcmpT"], writes=["kcmpT"])
            for nt in range(4):
                nn = 128 if nt < 3 else 127
                Pp = SB_[bank[0] % 3]
                bank[0] += 1
                for c in range(2):
                    S.op("tensor", lambda e: e.matmul(Pp[0:nn, 0:64], lhsT=glb[2 + c][:, nt * 128:nt * 128 + nn], rhs=cw2v[:, c, :],
                                                      start=(c == 0), stop=(c == 1)),
                         reads=["cw2v", glb[2 + c].name], writes=[Pp.name])
                S.op("vector", lambda e: e.tensor_copy(out=vcA[0:nn, nt, 0:64], in_=Pp[0:nn, 0:64]), reads=[Pp.name, "vcA"], writes=["vcA"])
            if dbg == 'cmp2':
                raise _Stop
            S.barrier()
            S.dma("sync", ind, indd[:, :], writes=["ind"])

            ctr = {"s": 0, "t": 0, "p": 0, "o": 0, "ac": 0, "adj": 0, "mbt": 0}

            def run_branch(g, hh, tiles, kT, vA, vkey, ncol, accs, acc_cols, basetile, bkey, cbtab, cbkey):
                p, half = hh // 2, hh % 2
                firsts = {}

                def emit_s(tl):
                    Sp = SB_[ctr["s"] % 3]
                    ctr["s"] += 1
                    tl["Sp"] = Sp
                    kb = tl["kb"]
                    mms = [(kT[half * 64:(half + 1) * 64, kb * 128:(kb + 1) * 128], QT[p][half * 64:(half + 1) * 64, g * 512:(g + 1) * 512],
                            tl["kkeys"] + ["QT%d_%d" % (p, g)])]
                    if tl.get("extra") is not None:
                        mms.append(tl["extra"])
                    if tl.get("mask") is not None:
                        mms.append((identb[:], tl["mask"], ["identb", "cmask", "dmask", "wmask"]))
                    for i, (l_, r_, keys) in enumerate(mms):
                        S.op("tensor", lambda e: e.matmul(Sp[:], lhsT=l_, rhs=r_, start=(i == 0), stop=(i == len(mms) - 1)),
                             reads=keys, writes=[Sp.name])

                def emit_mid(tl):
                    Sp = tl["Sp"]
                    tt_ = tb[ctr["t"] % 3]
                    ctr["t"] += 1
                    Pt = Pb[ctr["p"] % 4]
                    ctr["p"] += 1
                    tl["P"] = Pt
                    S.op("vector", lambda e: e.scalar_tensor_tensor(out=tt_[:], in0=basetile[:], scalar=slp[:, hh:hh + 1], in1=Sp[:],
                                                                    op0=ALU.mult, op1=ALU.add),
                         reads=[Sp.name, bkey, "slp"], writes=[tt_.name])
                    ci = tl["cbi"]
                    S.op("scalar", lambda e: e.activation(out=Pt[:], in_=tt_[:], func=AF.Exp, bias=cbtab[:, hh, ci:ci + 1], scale=1.0),
                         reads=[tt_.name, cbkey], writes=[Pt.name])

                def emit_av(tl):
                    Pt, kb = tl["P"], tl["kb"]
                    for qb in tl["qbs"]:
                        acc, c0 = accs[qb], acc_cols[qb]
                        st_ = acc.name not in firsts
                        firsts[acc.name] = True
                        S.op("tensor", lambda e: e.matmul(acc[:, c0:c0 + ncol], lhsT=Pt[:, qb * 128:(qb + 1) * 128], rhs=vA[:, kb, :],
                                                          start=st_, stop=False, skip_group_check=True),
                             reads=[Pt.name] + tl["vkeys"], writes=[acc.name])

                flash_pipeline(S, tiles, emit_s, emit_mid, emit_av)

            for g in range(16):
                ntmax = (512 * g + 480) // 2048
                for hh in range(4):
                    a0 = AC[(ctr["ac"] % 2) * 2]
                    a1 = AC[(ctr["ac"] % 2) * 2 + 1]
                    ctr["ac"] += 1
                    accs = [a0, a0, a1, a1]
                    cols = [0, 193, 0, 193]
                    tiles = []
                    for nt in range(ntmax + 1):
                        rel2 = g - 4 * nt
                        tiles.append(dict(kb=nt, kkeys=["kcmpT"], vkeys=["vcA"], mask=(cmask[:, rel2, :] if rel2 <= 4 else None),
                                          cbi=rel2, qbs=[0, 1, 2, 3]))
                    run_branch(g, hh, tiles, kcmpT, vcA, "vcA", 193, accs, cols, basec0, "basec0", cbct, "cbct")
                    for qb in range(4):
                        acc, c0 = accs[qb], cols[qb]
                        blk = g * 4 + qb
                        S.op("vector", lambda e: e.tensor_scalar(out=sm[:, 0:1], in0=acc[:, c0 + 64:c0 + 65], scalar1=1e-30, scalar2=None, op0=ALU.max),
                             reads=[acc.name], writes=["sm0"])
                        S.op("vector", lambda e: e.reciprocal(out=sm[:, 0:1], in_=sm[:, 0:1]), reads=["sm0"], writes=["sm0"])
                        S.op("vector", lambda e: e.tensor_tensor(out=sm[:, 1:2], in0=sm[:, 0:1], in1=gates[:, blk, hh * 3:hh * 3 + 1], op=ALU.mult),
                             reads=["sm0", "gates_%d" % blk], writes=["sm1"])
                        S.op("vector", lambda e: e.tensor_scalar(out=acco[:, qb, hh * 64:(hh + 1) * 64], in0=acc[:, c0:c0 + 64],
                                                                 scalar1=sm[:, 1:2], scalar2=None, op0=ALU.mult),
                             reads=[acc.name, "sm1"], writes=["acco%d" % qb])
                        if hh == 0:
                            S.op("vector", lambda e: e.tensor_scalar(out=imp[:, qb, :], in0=acc[:, c0 + 65:c0 + 193], scalar1=sm[:, 0:1],
                                                                     scalar2=None, op0=ALU.mult),
                                 reads=[acc.name, "sm0"], writes=["imp%d" % qb])
                        else:
                            S.op("vector", lambda e: e.scalar_tensor_tensor(out=imp[:, qb, :], in0=acc[:, c0 + 65:c0 + 193], scalar=sm[:, 0:1],
                                                                            in1=imp[:, qb, :], op0=ALU.mult, op1=ALU.add),
                                 reads=[acc.name, "sm0", "imp%d" % qb], writes=["imp%d" % qb])
                if dbg == 'g0c':
                    raise _Stop
                mbt = MBT[ctr["mbt"] % 2]
                ctr["mbt"] += 1
                for qb in range(4):
                    blk = g * 4 + qb
                    at = adjt[ctr["adj"] % 2]
                    ctr["adj"] += 1
                    S.dma("sync", at[:], adjd[blk], writes=[at.name])
                    S.op("vector", lambda e: e.tensor_tensor(out=impa[:], in0=imp[:, qb, :], in1=at[:], op=ALU.add),
                         reads=["imp%d" % qb, at.name], writes=["impa"])
                    S.op("vector", lambda e: e.max(out=m8[:, 0:8], in_=impa[:]), reads=["impa"], writes=["m8a"])
                    S.op("vector", lambda e: e.match_replace(out=impb[:], in_to_replace=m8[:, 0:8], in_values=impa[:], imm_value=-3.0e38),
                         reads=["impa", "m8a"], writes=["impb"])
                    S.op("vector", lambda e: e.max(out=m8[:, 8:16], in_=impb[:]), reads=["impb"], writes=["m8b"])
                    S.op("vector", lambda e: e.tensor_scalar(out=selb[:], in0=impa[:], scalar1=m8[:, 15:16], scalar2=1.0,
                                                             op0=ALU.is_ge, op1=ALU.subtract),
                         reads=["impa", "m8b"], writes=["selb"])
                    S.op("tensor", lambda e: e.transpose(out=PT[:, qb * 128:(qb + 1) * 128], in_=selb[:], identity=identb[:]),
                         reads=["selb", "identb"], writes=["PT"])
                S.op("vector", lambda e: e.tensor_copy(out=mbt[:], in_=PT[:, 0:512]), reads=["PT"], writes=[mbt.name])
                if dbg == 'g0k':
                    raise _Stop
                for hh in range(4):
                    a0 = AC[ctr["ac"] % 4]
                    ctr["ac"] += 1
                    accs = [a0] * 4
                    cols = [0, 65, 130, 195]
                    tiles = []
                    for kb in range(4 * g + 4):
                        r = kb - 4 * g
                        tiles.append(dict(kb=kb, kkeys=["ksT_%d" % (kb // 4)], vkeys=["vs_%d" % kb, "vsones"],
                                          extra=(ind[:, kb * 128:(kb + 1) * 128], mbt[:], ["ind", mbt.name]),
                                          mask=(dmask[:, r, :] if r >= 0 else None), cbi=4 * g - kb + 3,
                                          qbs=[qb for qb in range(4) if qb >= r]))
                    run_branch(g, hh, tiles, ksT, vsA, "vs", 65, accs, cols, base0, "base0", cbt, "cbt")
                    for qb in range(4):
                        c0 = cols[qb]
                        blk = g * 4 + qb
                        S.op("vector", lambda e: e.reciprocal(out=sm[:, 2:3], in_=a0[:, c0 + 64:c0 + 65]), reads=[a0.name], writes=["sm2"])
                        S.op("vector", lambda e: e.tensor_tensor(out=sm[:, 3:4], in0=sm[:, 2:3], in1=gates[:, blk, hh * 3 + 1:hh * 3 + 2], op=ALU.mult),
                             reads=["sm2", "gates_%d" % blk], writes=["sm3"])
                        S.op("vector", lambda e: e.scalar_tensor_tensor(out=acco[:, qb, hh * 64:(hh + 1) * 64], in0=a0[:, c0:c0 + 64],
                                                                        scalar=sm[:, 3:4], in1=acco[:, qb, hh * 64:(hh + 1) * 64],
                                                                        op0=ALU.mult, op1=ALU.add),
                             reads=[a0.name, "sm3", "acco%d" % qb], writes=["acco%d" % qb])
                if dbg == 'g0s':
                    raise _Stop
                for hh in range(4):
                    a0 = AC[ctr["ac"] % 4]
                    ctr["ac"] += 1
                    accs = [a0] * 4
                    cols = [0, 65, 130, 195]
                    tiles = []
                    for kb in range(max(0, 4 * g - 4), 4 * g + 4):
                        r = kb - 4 * g
                        tiles.append(dict(kb=kb, kkeys=["kwT_%d" % (kb // 4)], vkeys=["vw_%d" % kb, "vwones"],
                                          mask=wmask[:, r + 4, :], cbi=4 * g - kb + 3,
                                          qbs=[qb for qb in range(4) if qb >= r and qb - r < 5]))
                    run_branch(g, hh, tiles, kwT, vwA, "vw", 65, accs, cols, base0, "base0", cbt, "cbt")
                    for qb in range(4):
                        c0 = cols[qb]
                        blk = g * 4 + qb
                        S.op("vector", lambda e: e.reciprocal(out=sm[:, 4:5], in_=a0[:, c0 + 64:c0 + 65]), reads=[a0.name], writes=["sm4"])
                        S.op("vector", lambda e: e.tensor_tensor(out=sm[:, 5:6], in0=sm[:, 4:5], in1=gates[:, blk, hh * 3 + 2:hh * 3 + 3], op=ALU.mult),
                             reads=["sm4", "gates_%d" % blk], writes=["sm5"])
                        S.op("vector", lambda e: e.scalar_tensor_tensor(out=acco[:, qb, hh * 64:(hh + 1) * 64], in0=a0[:, c0:c0 + 64],
                                                                        scalar=sm[:, 5:6], in1=acco[:, qb, hh * 64:(hh + 1) * 64],
                                                                        op0=ALU.mult, op1=ALU.add),
                             reads=[a0.name, "sm5", "acco%d" % qb], writes=["acco%d" % qb])
                if dbg == 'g0w':
                    raise _Stop
                S.op("vector", lambda e: e.tensor_copy(out=ob[:], in_=acco[:]), reads=["acco%d" % q_ for q_ in range(4)], writes=["ob"])
                ot = oTs[ctr["o"] % 2]
                ctr["o"] += 1
                for qb in range(4):
                    for c in range(2):
                        S.op("tensor", lambda e: e.transpose(out=PT[:, c * 512 + qb * 128:c * 512 + (qb + 1) * 128],
                                                             in_=ob[:, qb, c * 128:(c + 1) * 128], identity=identb[:]),
                             reads=["ob", "identb"], writes=["PT"])
                S.op("vector", lambda e: e.tensor_copy(out=ot[:], in_=PT[:].rearrange("p (c t) -> p c t", c=2)), reads=["PT"], writes=[ot.name])
                S.dma("sync", odst(g), ot[:], reads=[ot.name], is_output=True)
                if dbg == 'g0':
                    raise _Stop
        except _Stop:
            pass
        S.close()


def nsa_consts():
    c = {}
    jj = np.arange(128)[:, None]
    q = np.arange(512)[None, :]
    c["base0"] = (q - jj).astype(np.float32) * -1.0
    c["basec0"] = -(q - 16 * jj - 31).astype(np.float32)
    rel2 = np.arange(5)[None, :, None]
    okc = (512 * rel2 + q[:, None, :].transpose(1, 0, 2) * 0 + q[None, :, :] * 1 - 16 * jj[:, :, None] - 31) >= 0
    c["cmask"] = np.where(okc, 0.0, -BIGNEG).astype(np.float32).astype(ml_dtypes.bfloat16)
    c["dmask"] = diag_mask_tiles(False).astype(ml_dtypes.bfloat16)
    r = (np.arange(8) - 4)[None, :, None]
    dd = q[None, :, :] - jj[:, :, None] - 128 * r
    c["wmask"] = np.where((dd >= 0) & (dd < 512), 0.0, -BIGNEG).astype(np.float32).astype(ml_dtypes.bfloat16)
    s_ = np.arange(128)[:, None]
    key = np.arange(T)[None, :]
    c["ind"] = np.where(key // 64 == s_, BIGNEG, 0.0).astype(np.float32).astype(ml_dtypes.bfloat16)
    n = np.arange(512)
    cs = n * 16
    ss = np.arange(128) * 64
    ov = ((cs[:, None] < ss[None, :] + 64) & (cs[:, None] + 32 > ss[None, :])).astype(np.float32)
    ov[511, :] = 0.0
    c["ovl"] = np.ascontiguousarray(ov.reshape(4, 128, 128).transpose(1, 0, 2)).astype(ml_dtypes.bfloat16)
    tt = np.arange(T)
    cur = tt // 64
    sid = np.arange(128)[None, :]
    forced = (sid == 0) | (sid == cur[:, None]) | (sid == cur[:, None] - 1)
    adj = np.where(forced, NSA_FORCE, 0.0)
    adj = np.where(sid <= cur[:, None], adj, NSA_NEG).astype(np.float32)
    c["adj"] = np.ascontiguousarray(adj.reshape(64, 128, 128))
    return c


_NSA_CONSTS = {}


def nsa_inputs(pfx, w_in, cmp_pe, cmp_w1, cmp_w2):
    if not _NSA_CONSTS:
        _NSA_CONSTS.update(nsa_consts())
    cst = _NSA_CONSTS
    slopes = alibi_slopes_np(16)
    cw1 = np.ascontiguousarray(np.concatenate([cmp_w1[j].reshape(32, 64, 256).transpose(1, 0, 2) for j in range(2)], axis=0))
    cpe = np.ascontiguousarray(np.concatenate([cmp_pe[j].T for j in range(2)], axis=0))
    cw2k = np.ascontiguousarray(np.concatenate([cmp_w2[0], cmp_w2[0]], axis=1).reshape(2, 128, 128).transpose(1, 0, 2))
    cw2v = np.ascontiguousarray(cmp_w2[1].reshape(2, 128, 64).transpose(1, 0, 2))
    rel = np.arange(NREL) - 3
    out = []
    for grp in range(4):
        hs = [4 * grp + r_ for r_ in range(4)]
        qc = np.arange(grp * 256, (grp + 1) * 256)
        kc = 1024 + grp * 64 + np.arange(64)
        vc, ks, vs, kw, vw = kc + 256, kc + 512, kc + 768, kc + 1024, kc + 1280
        gc = 2560 + grp * 12 + np.arange(12)
        fm_cols = np.concatenate([qc, ks, ks, kw, kw, kc, vc])
        tm_cols = np.concatenate([vs, vw, gc])
        sl = np.array([slopes[h] for h in hs])
        out.append({
            pfx + "wfm": arrange_w(w_in[:, fm_cols]),
            pfx + "wtm": arrange_w(w_in[:, tm_cols]),
            pfx + "cw1": cw1, pfx + "cpe": cpe, pfx + "cw2k": cw2k, pfx + "cw2v": cw2v,
            pfx + "ovl": cst["ovl"],
            pfx + "slp": np.ascontiguousarray(np.broadcast_to(sl[None, :], (128, 4))).astype(np.float32),
            pfx + "cb": np.ascontiguousarray(np.broadcast_to((-sl[:, None] * 128.0 * rel[None, :])[None], (128, 4, NREL))).astype(np.float32),
            pfx + "cbc": np.ascontiguousarray(np.broadcast_to((-sl[:, None] * 512.0 * np.arange(16)[None, :])[None], (128, 4, 16))).astype(np.float32),
            pfx + "base0": cst["base0"], pfx + "basec0": cst["basec0"], pfx + "cmask": cst["cmask"], pfx + "dmask": cst["dmask"],
            pfx + "wmask": cst["wmask"], pfx + "ind": cst["ind"], pfx + "adj": cst["adj"], pfx + "ident": _IDENT,
        })
    return out


DEPTH = 4
CC_GROUPS = [[0, 1, 2, 3], [4, 5, 6, 7]]


def build_fused(nphase=99):
    nc = bass.Bass("TRN2", target_bir_lowering=False)
    x_in = nc.dram_tensor("x", [TOK, D], F32, kind="ExternalInput").ap()
    out = nc.dram_tensor("out", [TOK, D], F32, kind="ExternalOutput").ap()
    x_scr = nc.dram_tensor("x_scr", [TOK, D], F32, kind="Internal").ap()
    hT_loc = [nc.dram_tensor("hT_loc%d" % i, [4, D, 512], BF16, kind="Internal").ap() for i in range(DEPTH)]
    hT_all = [nc.dram_tensor("hT_all%d" % i, [4, 4 * D, 512], BF16, kind="Internal").ap() for i in range(DEPTH)]
    o_loc = [nc.dram_tensor("o_loc%d" % i, [4, 256, 2048], BF16, kind="Internal").ap() for i in range(DEPTH)]
    o_all = [nc.dram_tensor("o_all%d" % i, [4, 4 * 256, 2048], BF16, kind="Internal").ap() for i in range(DEPTH)]
    rank = nc.sync.partition_id() % 4
    mod_loc = nc.dram_tensor("mod_loc", [9, D], F32, kind="Internal").ap()
    mod_all = nc.dram_tensor("mod_all", [36, D], F32, kind="Internal").ap()

    def allgather(name, src, dst, nch=4):
        cs = nc.alloc_semaphore(name=name)
        for ch in range(nch):
            nc.gpsimd.collective_compute("AllGather", ALU.bypass, replica_groups=CC_GROUPS,
                                         ins=[src[ch] if nch > 1 else src], outs=[dst[ch] if nch > 1 else dst]).then_inc(cs, 1)
        for eng in (nc.gpsimd, nc.sync, nc.tensor, nc.vector, nc.scalar):
            eng.wait_ge(cs, nch)
        free_sems(nc, [cs])

    ph = [0]

    def go():
        ph[0] += 1
        return ph[0] <= nphase

    emit_mods(nc, "pro_", {"rows_out": mod_loc})
    allgather("ccM", mod_loc, mod_all, nch=1)
    if go():
        emit_k1(nc, "k0_", False, 1, True, {"x": x_in, "xo": x_scr, "hTo": hT_loc[0], "hTo_chunked": True, "modrows": mod_all},
                k1_rows(None, [(0, 0)], 0))
    for i in range(DEPTH):
        if go():
            allgather("ccA%d" % i, hT_loc[i], hT_all[i])
        binds = {"hT": hT_all[i], "oT": o_loc[i]}
        kind = i % 3
        if go():
            if kind == 0:
                emit_nsa(nc, "m%d_" % i, binds, True)
            elif kind == 1:
                emit_sb(nc, "m%d_" % i, binds, True)
            else:
                emit_diff(nc, "m%d_" % i, binds, 0.8 - 0.6 * math.exp(-0.3 * i), True)
        if go():
            allgather("ccB%d" % i, o_loc[i], o_all[i])
        oT_ap = o_all[i][rank]
        if go():
            if i < DEPTH - 1:
                emit_k1(nc, "k%d_" % (i + 1), True, 2, True,
                        {"x": x_scr, "xo": x_scr, "hTo": hT_loc[i + 1], "hTo_chunked": True, "oT": oT_ap, "modrows": mod_all},
                        k1_rows(i, [(i, 1), (i + 1, 0)], i + 1))
            else:
                emit_k1(nc, "k%d_" % (i + 1), True, 1, False, {"x": x_scr, "xo": out, "oT": oT_ap, "modrows": mod_all},
                        k1_rows(i, [(i, 1)], None))
    return nc


_NPHASE = [99]


def kernel(x, c, ada_w, ada_b, norm_g, ffn_w1, ffn_w2, nsa_w_in, nsa_cmp_pe, nsa_cmp_w1, nsa_cmp_w2,
           nsa_w_out, sb_w_in, sb_w_out, diff_w_in, diff_lam, diff_subln_g, diff_w_out):
    f = lambda a: np.asarray(a, dtype=np.float32)
    x, c, ada_w, ada_b, norm_g, ffn_w1, ffn_w2 = map(f, (x, c, ada_w, ada_b, norm_g, ffn_w1, ffn_w2))
    nsa_w_in, nsa_cmp_pe, nsa_cmp_w1, nsa_cmp_w2, nsa_w_out = map(f, (nsa_w_in, nsa_cmp_pe, nsa_cmp_w1, nsa_cmp_w2, nsa_w_out))
    sb_w_in, sb_w_out, diff_w_in, diff_lam, diff_subln_g, diff_w_out = map(
        f, (sb_w_in, sb_w_out, diff_w_in, diff_lam, diff_subln_g, diff_w_out))
    nc = get_nc(("fused", _NPHASE[0]), lambda: build_fused(_NPHASE[0]))
    xt = x.reshape(B * T, D)
    common = {}
    per_b = [dict() for _ in range(B)]
    per_g = [dict() for _ in range(4)]

    def add_k1(pfx, mix, ffns, pre, wout=None):
        common.update(k1_inputs(pfx, ffn_w1, ffn_w2, mix, ffns, wout))

    pr, pb = mods_inputs("pro_", c, ada_w, ada_b, norm_g)
    for b in range(B):
        per_b[b].update(pb[b])
    for g in range(4):
        per_g[g].update(pr[g])

    add_k1("k0_", None, [(0, 0)], 0)
    for i in range(DEPTH):
        kind, j = i % 3, i // 3
        pfx = "m%d_" % i
        if kind == 0:
            pg = nsa_inputs(pfx, nsa_w_in[j], nsa_cmp_pe[j], nsa_cmp_w1[j], nsa_cmp_w2[j])
            wout = nsa_w_out[j]
        elif kind == 1:
            pg = sb_inputs(pfx, sb_w_in[j])
            wout = sb_w_out[j]
        else:
            pg = diff_inputs(pfx, diff_w_in[j], diff_lam[j], diff_subln_g[j])
            wout = diff_w_out[j]
        for g in range(4):
            per_g[g].update(pg[g])
        if i < DEPTH - 1:
            add_k1("k%d_" % (i + 1), i, [(i, 1), (i + 1, 0)], i + 1, wout)
        else:
            add_k1("k%d_" % (i + 1), i, [(i, 1)], None, wout)
    in_maps = []
    for core in range(NCORE):
        b, g = core // 4, core % 4
        m = dict(common)
        m.update(per_b[b])
        m.update(per_g[g])
        m["x"] = np.ascontiguousarray(xt[core * TOK:(core + 1) * TOK])
        in_maps.append(m)
    if _NPHASE[0] < 99:
        npfx = ["k0_"]
        for i in range(DEPTH):
            npfx += [None, "m%d_" % i, None, "k%d_" % (i + 1)]
        keep = set(p for p in npfx[:_NPHASE[0]] if p)
        in_maps = [{k: v for k, v in m.items() if k == "x" or k[:3] in keep or k.startswith("pro_")} for m in in_maps]
    res = run_bass_kernel_spmd(nc, in_maps, core_ids=list(range(NCORE)))
    xo = np.concatenate([res.results[i]["out"] for i in range(NCORE)], axis=0)
    return xo.reshape(B, T, D).astype(np.float32)
```

```python
import contextlib
import math
import numpy as np
import ml_dtypes
import concourse.bass as bass
import concourse.mybir as mybir
from concourse.bass_utils import run_bass_kernel_spmd

F32 = mybir.dt.float32
BF16 = mybir.dt.bfloat16
AF = mybir.ActivationFunctionType
ALU = mybir.AluOpType
AX = mybir.AxisListType

D = 1024
DFF = 2816
NFF = DFF // 128
B = 2
T = 8192
NCORE = 8
TOK = B * T // NCORE
NT = TOK // 128
EPS = 1e-6
NDMA = 24


def free_sems(nc, handles):
    nc.all_engine_barrier()
    nc.clear_and_free_semaphores(handles)
    nc.all_engine_barrier()


class Sched:
    def __init__(self, nc, es, pfx=""):
        self.nc = nc
        self.pfx = pfx
        self.engs = {}
        for name in ["tensor", "vector", "scalar", "gpsimd", "sync"]:
            sem = nc.alloc_semaphore(name=pfx + "sem_" + name)
            self.engs[name] = dict(obj=getattr(nc, name), sem=sem, cnt=0, waited={})
        self.dma_slots = [dict(sem=nc.alloc_semaphore(name=pfx + "dsem%d" % i), cnt=0) for i in range(NDMA)]
        self.dma_rr = 0
        self.last_write = {}
        self.reads = {}
        self.out_tokens = []
        self.psum_keys = set()

    def _wait(self, engname, tok):
        if tok is None:
            return
        semid, sem, val = tok
        if semid == engname and engname == "tensor":
            return
        e = self.engs[engname]
        if e["waited"].get(semid, 0) >= val:
            return
        e["obj"].wait_ge(sem, val)
        e["waited"][semid] = val

    def _norm(self, keys):
        p = self.pfx
        return [k[len(p):] if (p and k.startswith(p)) else k for k in keys]

    def _deps(self, engname, reads, writes):
        reads, writes = self._norm(reads), self._norm(writes)
        for k in reads:
            self._wait(engname, self.last_write.get(k))
            if k in self.psum_keys:
                for t in self.reads.get(k, []):
                    if t[0] != engname:
                        self._wait(engname, t)
        for k in writes:
            self._wait(engname, self.last_write.get(k))
            for t in self.reads.get(k, []):
                self._wait(engname, t)

    def _commit(self, tok, reads, writes):
        reads, writes = self._norm(reads), self._norm(writes)
        for k in writes:
            self.last_write[k] = tok
            self.reads[k] = []
        for k in reads:
            self.reads.setdefault(k, []).append(tok)

    def op(self, engname, fn, reads=(), writes=()):
        self._deps(engname, reads, writes)
        e = self.engs[engname]
        ins = fn(e["obj"])
        e["cnt"] += 1
        ins.then_inc(e["sem"], 1)
        tok = (engname, e["sem"], e["cnt"])
        self._commit(tok, reads, writes)
        return tok

    def dma(self, queue, out, in_, reads=(), writes=(), is_output=False):
        self._deps(queue, reads, writes)
        idx = self.dma_rr
        slot = self.dma_slots[idx]
        self.dma_rr = (self.dma_rr + 1) % NDMA
        if slot["cnt"] > 0:
            self._wait(queue, ("d%d" % idx, slot["sem"], slot["cnt"] * 16))
        ins = self.engs[queue]["obj"].dma_start(out=out, in_=in_)
        slot["cnt"] += 1
        ins.then_inc(slot["sem"], 16)
        tok = ("d%d" % idx, slot["sem"], slot["cnt"] * 16)
        self._commit(tok, reads, writes)
        if is_output:
            self.out_tokens.append(tok)
        return tok

    def barrier(self):
        toks = []
        for idx, slot in enumerate(self.dma_slots):
            if slot["cnt"] > 0:
                toks.append(("d%d" % idx, slot["sem"], slot["cnt"] * 16))
        for name, e in self.engs.items():
            if e["cnt"] > 0:
                toks.append((name, e["sem"], e["cnt"]))
        for name in self.engs:
            for tk in toks:
                if tk[0] != name:
                    self._wait(name, tk)

    def close(self):
        self.barrier()
        handles = [e["sem"] for e in self.engs.values()] + [sl["sem"] for sl in self.dma_slots]
        free_sems(self.nc, handles)

    def finish(self):
        for idx, slot in enumerate(self.dma_slots):
            if slot["cnt"] > 0:
                self._wait("sync", ("d%d" % idx, slot["sem"], slot["cnt"] * 16))
        for name, e in self.engs.items():
            if name != "sync" and e["cnt"] > 0:
                self._wait("sync", (name, e["sem"], e["cnt"]))


def emit_k1(nc, pfx, mix, n_ffn, pre, binds, rows):
    nv = (1 if mix else 0) + 3 * n_ffn + (2 if pre else 0)
    ng = (1 if mix else 0) + 2 * n_ffn + (1 if pre else 0)
    dr = {}

    def din(name, shape, dt=F32, kind="ExternalInput"):
        if name in binds:
            dr[name] = binds[name]
        else:
            dr[name] = nc.dram_tensor(pfx + name, list(shape), dt, kind=kind).ap()

    din("x", [TOK, D])
    din("ident", [128, 128])
    modrows = binds["modrows"]
    if mix:
        din("oT", [D, TOK], BF16)
        din("wout", [D, D])
    for f in range(n_ffn):
        din("w1_%d" % f, [NFF, 128, 8, 256])
        din("w2_%d" % f, [DFF, D])
    din("xo", [TOK, D], F32, "ExternalOutput")
    if pre:
        din("hTo", [D, TOK], BF16, "ExternalOutput")

    es = contextlib.ExitStack()
    with es:
        S = Sched(nc, es, pfx)

        def sb(name, shape, dt):
            return es.enter_context(nc.sbuf_tensor(pfx + name, shape, dt))

        def ps(name, shape, dt):
            return es.enter_context(nc.psum_tensor(pfx + name, shape, dt))

        xs = sb("xs", [128, NT, D], F32)
        hT = sb("hT", [128, 8, 1024], BF16)
        actT = sb("actT", [128, NFF, 1024], BF16)
        w1c = [sb("w1c%d" % i, [128, 8, 256], BF16) for i in range(3)]
        w2c = [sb("w2c%d" % i, [128, 1024], BF16) for i in range(3)]
        prmsets = [[sb("prm%d_%d" % (j, i), [128, D], F32) for i in range(3)] for j in range(2)]
        scr = [sb("scr%d" % i, [128, D], F32) for i in range(2)]
        hb = [sb("hb%d" % i, [128, D], BF16) for i in range(2)]
        junk = sb("junk", [128, D], BF16)
        sil = [sb("sil%d" % i, [128, 512], F32) for i in range(2)]
        identf = sb("identf", [128, 128], F32)
        identb = sb("identb", [128, 128], BF16)
        st = sb("st", [128, 8], F32)
        epsb = sb("epsb", [128, 1], F32)
        PA = ps("PA", [128, 1024], F32)
        PB = ps("PB", [128, 1024], F32)
        PC = ps("PC", [128, 1024], F32)
        PT = ps("PT", [128, 1024], F32)
        PTb = PT[:].bitcast(BF16)
        S.psum_keys.update(["PA0", "PA1", "PB0", "PB1", "PC0", "PC1", "PT0", "PT1"])

        xin = dr["x"].rearrange("(n p) d -> p n d", p=128)
        for q4 in range(4):
            S.dma("sync", xs[:, q4 * 4:(q4 + 1) * 4, :], xin[:, q4 * 4:(q4 + 1) * 4, :],
                  writes=["xs%d" % n for n in range(q4 * 4, q4 * 4 + 4)])
        S.dma("sync", identf[:], dr["ident"][:, :], writes=["identf"])
        S.op("vector", lambda e: e.tensor_copy(out=identb[:], in_=identf[:]), reads=["identf"], writes=["identb"])
        S.op("vector", lambda e: e.memset(epsb[:], EPS), writes=["epsb"])

        def load_row(row, dst):
            S.dma("sync", dst[:], modrows[row:row + 1, :].partition_broadcast(128), writes=[dst.name])

        pset = [0]

        def next_prm():
            pset[0] += 1
            return prmsets[pset[0] % 2]

        stc = [0]

        def rstd_of(src_ap, src_keys, col):
            S.op("scalar", lambda e: e.activation(out=junk[:], in_=src_ap, func=AF.Square, accum_out=st[:, col:col + 1]),
                 reads=src_keys, writes=["junk", "st%d" % col])
            S.op("scalar", lambda e: e.activation(out=st[:, col:col + 1], in_=st[:, col:col + 1], func=AF.Sqrt,
                                                  bias=epsb[:], scale=1.0 / D),
                 reads=["st%d" % col, "epsb"], writes=["st%d" % col])
            S.op("vector", lambda e: e.reciprocal(out=st[:, col:col + 1], in_=st[:, col:col + 1]),
                 reads=["st%d" % col], writes=["st%d" % col])

        def prenorm_tile(n, A, Bv, i):
            col = stc[0] % 4
            stc[0] += 1
            rstd_of(xs[:, n, :], ["xs%d" % n], col)
            S.op("vector", lambda e: e.scalar_tensor_tensor(out=scr[1][:], in0=xs[:, n, :], scalar=st[:, col:col + 1], in1=A[:],
                                                            op0=ALU.mult, op1=ALU.mult),
                 reads=["xs%d" % n, "st%d" % col, A.name], writes=["scr1"])
            S.op("gpsimd", lambda e: e.tensor_tensor(out=hb[i][:], in0=scr[1][:], in1=Bv[:], op=ALU.add),
                 reads=["scr1", Bv.name], writes=["hb%d" % i])

        def transpose_tile(i, half, dst_fn, dst_keys):
            pk = "PT%d" % half
            for k in range(8):
                S.op("tensor", lambda e, k=k: e.transpose(out=PTb[:, half * 1024 + k * 128: half * 1024 + (k + 1) * 128],
                                                          in_=hb[i][:, k * 128:(k + 1) * 128], identity=identb[:]),
                     reads=["hb%d" % i, "identb"], writes=[pk])
            S.op("scalar", lambda e: e.activation(out=dst_fn(), in_=PTb[:, half * 1024:(half + 1) * 1024].rearrange("p (k t) -> p k t", k=8),
                                                  func=AF.Copy),
                 reads=[pk], writes=dst_keys)

        def epilogue(Y, n, G):
            col = 4 + stc[0] % 4
            stc[0] += 1
            rstd_of(Y[:], [Y.name + "0", Y.name + "1"], col)
            S.op("vector", lambda e: e.scalar_tensor_tensor(out=scr[0][:], in0=Y[:], scalar=st[:, col:col + 1], in1=G[:],
                                                            op0=ALU.mult, op1=ALU.mult),
                 reads=[Y.name + "0", Y.name + "1", "st%d" % col, G.name], writes=["scr0"])
            S.op("gpsimd", lambda e: e.tensor_tensor(out=xs[:, n, :], in0=xs[:, n, :], in1=scr[0][:], op=ALU.add),
                 reads=["scr0", "xs%d" % n], writes=["xs%d" % n])

        vi = 0
        gi = 0
        Yb = [PA, PB]
        if mix:
            prm = next_prm()
            load_row(rows["mix"], prm[2])
            wo = actT[:, 0:8, :]
            S.dma("gpsimd", wo, dr["wout"].rearrange("(k p) n -> p k n", p=128), writes=["actT"])
            for half in range(2):
                oTs = actT[:, 8:16, :]
                S.dma("sync", oTs, dr["oT"][:, half * 1024:(half + 1) * 1024].rearrange("(k p) t -> p k t", p=128),
                      writes=["actT_o"])
                for tt in range(8):
                    n = half * 8 + tt
                    Y = Yb[n % 2]
                    for h2 in range(2):
                        for k in range(8):
                            S.op("tensor", lambda e, k=k, h2=h2, tt=tt, Y=Y: e.matmul(
                                Y[:, h2 * 512:(h2 + 1) * 512], lhsT=actT[:, 8 + k, tt * 128:(tt + 1) * 128],
                                rhs=actT[:, k, h2 * 512:(h2 + 1) * 512], start=(k == 0), stop=(k == 7)),
                                reads=["actT", "actT_o"], writes=[Y.name + str(h2)])
                    epilogue(Y, n, prm[2])

        wctr = [0, 0]
        for f in range(n_ffn):
            prm = next_prm()
            load_row(rows["ffn"][f][0], prm[0])
            load_row(rows["ffn"][f][1], prm[1])
            load_row(rows["ffn"][f][2], prm[2])
            w1d = dr["w1_%d" % f]
            w2d = dr["w2_%d" % f]
            for grp in range(2):
                for tt in range(8):
                    n = grp * 8 + tt
                    i = n % 2
                    prenorm_tile(n, prm[0], prm[1], i)
                    transpose_tile(i, n % 2, lambda tt=tt: hT[:, :, tt * 128:(tt + 1) * 128], ["hT"])
                for j in range(NFF):
                    wi = wctr[0] % 3
                    wctr[0] += 1
                    S.dma("gpsimd", w1c[wi][:], w1d[j], writes=["w1c%da" % wi, "w1c%db" % wi])
                    for th in range(2):
                        Pg = PA if th == 0 else PB
                        for part, (c0, key) in enumerate([(0, "a"), (128, "b")]):
                            for k in range(8):
                                S.op("tensor", lambda e, k=k, th=th, c0=c0, part=part, Pg=Pg, wi=wi: e.matmul(
                                    Pg[:, part * 512:(part + 1) * 512], lhsT=w1c[wi][:, k, c0:c0 + 128],
                                    rhs=hT[:, k, th * 512:(th + 1) * 512], start=(k == 0), stop=(k == 7)),
                                    reads=["hT", "w1c%d%s" % (wi, key)], writes=[Pg.name + str(part)])
                        si = th
                        S.op("scalar", lambda e, Pg=Pg, si=si: e.activation(out=sil[si][:], in_=Pg[:, 0:512], func=AF.Silu),
                             reads=[Pg.name + "0"], writes=["sil%d" % si])
                        S.op("vector", lambda e, Pg=Pg, si=si, j=j, th=th: e.tensor_tensor(
                            out=actT[:, j, th * 512:(th + 1) * 512], in0=Pg[:, 512:1024], in1=sil[si][:], op=ALU.mult),
                            reads=[Pg.name + "1", "sil%d" % si], writes=["actT%d" % j])
                Y4 = [PA, PB, PC, PT]
                for tp in range(2):
                    for j in range(NFF):
                        wi = wctr[1] % 3
                        wctr[1] += 1
                        S.dma("gpsimd", w2c[wi][:], w2d[j * 128:(j + 1) * 128, :], writes=["w2c%d" % wi])
                        for q in range(4):
                            tt = tp * 4 + q
                            Y = Y4[q]
                            for h2 in range(2):
                                S.op("tensor", lambda e, j=j, tt=tt, h2=h2, Y=Y, wi=wi: e.matmul(
                                    Y[:, h2 * 512:(h2 + 1) * 512], lhsT=actT[:, j, tt * 128:(tt + 1) * 128],
                                    rhs=w2c[wi][:, h2 * 512:(h2 + 1) * 512], start=(j == 0), stop=(j == NFF - 1)),
                                    reads=["actT%d" % j, "w2c%d" % wi], writes=[Y.name + str(h2)])
                    for q in range(4):
                        epilogue(Y4[q], grp * 8 + tp * 4 + q, prm[2])

        if pre:
            prm = next_prm()
            load_row(rows["pre"][0], prm[0])
            load_row(rows["pre"][1], prm[1])
            if binds.get("hTo_chunked"):
                def hTo_dst(n):
                    return dr["hTo"][n // 4, :, (n % 4) * 128:(n % 4 + 1) * 128].rearrange("(k p) t -> p k t", p=128)
            else:
                def hTo_dst(n):
                    return dr["hTo"].rearrange("(k p) t -> p k t", p=128)[:, :, n * 128:(n + 1) * 128]
            for n in range(NT):
                i = n % 2
                prenorm_tile(n, prm[0], prm[1], i)
                transpose_tile(i, n % 2, lambda n=n: hT[:, :, (n % 8) * 128:(n % 8 + 1) * 128], ["hTo%d" % (n % 8)])
                S.dma("sync", hTo_dst(n), hT[:, :, (n % 8) * 128:(n % 8 + 1) * 128],
                      reads=["hTo%d" % (n % 8)], is_output=True)
        xout = dr["xo"].rearrange("(n p) d -> p n d", p=128)
        for q4 in range(4):
            S.dma("sync", xout[:, q4 * 4:(q4 + 1) * 4, :], xs[:, q4 * 4:(q4 + 1) * 4, :],
                  reads=["xs%d" % n for n in range(q4 * 4, q4 * 4 + 4)], is_output=True)
        S.close()


def arrange_w1(w1):
    g = w1[:, :DFF].reshape(8, 128, NFF, 128)
    u = w1[:, DFF:].reshape(8, 128, NFF, 128)
    cat = np.concatenate([g, u], axis=3)
    return np.ascontiguousarray(cat.transpose(2, 1, 0, 3))


def arrange_adaw(cols):
    return np.ascontiguousarray(cols.reshape(8, 128, 8, 128).transpose(2, 1, 0, 3))


_IDENT = np.eye(128, dtype=np.float32)
_NC_CACHE = {}


def get_nc(key, builder):
    if key not in _NC_CACHE:
        _NC_CACHE[key] = builder()
    return _NC_CACHE[key]


def k1_inputs(pfx, ffn_w1, ffn_w2, mix, ffns, wout=None):
    common = {pfx + "ident": _IDENT}
    for f, (l, w) in enumerate(ffns):
        common[pfx + "w1_%d" % f] = arrange_w1(ffn_w1[l][w])
        common[pfx + "w2_%d" % f] = np.ascontiguousarray(ffn_w2[l][w])
    if mix is not None:
        common[pfx + "wout"] = np.ascontiguousarray(wout)
    return common


def k1_rows(mix, ffns, pre):
    rows = {"ffn": []}
    if mix is not None:
        rows["mix"] = mix * 9 + 3 + 2
    for (l, w) in ffns:
        base = l * 9 + (0 if w == 0 else 2) * 3
        rows["ffn"].append((base, base + 1, base + 2))
    if pre is not None:
        rows["pre"] = (pre * 9 + 3, pre * 9 + 4)
    return rows


def mods_inputs(pfx, c, ada_w, ada_b, norm_g):
    per_r = []
    for r in range(4):
        adaw = np.ascontiguousarray(ada_w[r].reshape(8, 128, 9, D).transpose(2, 1, 0, 3))
        per_r.append({pfx + "adaw": adaw, pfx + "adab": np.ascontiguousarray(ada_b[r].reshape(1, 9, D)),
                      pfx + "g": np.ascontiguousarray(norm_g[r].reshape(1, 6, D))})
    per_b = [{pfx + "c": np.ascontiguousarray(c[b].reshape(8, 128).T)} for b in range(B)]
    return per_r, per_b


BIGNEG = 240000.0


class Ctx:
    def __init__(self, nc, pfx, binds):
        self.nc = nc
        self.pfx = pfx
        self.binds = binds
        self.es = contextlib.ExitStack()
        self.dr = {}
        self.psum_names = []

    def din(self, name, shape, dt=F32):
        if name in self.binds:
            self.dr[name] = self.binds[name]
        else:
            self.dr[name] = self.nc.dram_tensor(self.pfx + name, list(shape), dt, kind="ExternalInput").ap()
        return self.dr[name]

    def dout(self, name, shape, dt=F32):
        if name in self.binds:
            self.dr[name] = self.binds[name]
        else:
            self.dr[name] = self.nc.dram_tensor(self.pfx + name, list(shape), dt, kind="ExternalOutput").ap()
        return self.dr[name]

    def sb(self, name, shape, dt):
        return self.es.enter_context(self.nc.sbuf_tensor(self.pfx + "s_" + name, list(shape), dt))

    def ps(self, name, shape, dt):
        self.psum_names.append(name)
        return self.es.enter_context(self.nc.psum_tensor(self.pfx + name, list(shape), dt))


def o_dest(oTd, gathered, p=128):
    if gathered:
        def f(g):
            return oTd[g // 4, :, (g % 4) * 512:(g % 4 + 1) * 512].rearrange("(c p) t -> p c t", p=p)
    else:
        def f(g):
            return oTd[:, g * 512:(g + 1) * 512].rearrange("(c p) t -> p c t", p=p)
    return f


def hT_source(hTd, gathered):
    if gathered:
        def f(tg):
            r, tc = tg // 4, tg % 4
            return hTd[tc, r * D:(r + 1) * D, :].rearrange("(k p) t -> p k t", p=128)
    else:
        def f(tg):
            return hTd[:, tg * 512:(tg + 1) * 512].rearrange("(k p) t -> p k t", p=128)
    return f


def flash_pipeline(S, tiles, emit_s, emit_mid, emit_av, lookahead=4):
    pend = []
    for tl in tiles:
        emit_s(tl)
        emit_mid(tl)
        pend.append(tl)
        if len(pend) > lookahead:
            emit_av(pend.pop(0))
    for tl in pend:
        emit_av(tl)


def emit_mods(nc, pfx, binds):
    C = Ctx(nc, pfx, binds)
    es = C.es
    adawd = C.din("adaw", [9, 128, 8, D])
    adabd = C.din("adab", [1, 9, D])
    gd = C.din("g", [1, 6, D])
    cd = C.din("c", [128, 8])
    outd = binds["rows_out"]
    with es:
        S = Sched(nc, es, pfx)
        adas = [C.sb("adas%d" % i, [128, 8, D], F32) for i in range(2)]
        adab = C.sb("adab", [1, 9, D], F32)
        gl = C.sb("gl", [1, 6, D], F32)
        cst = C.sb("cst", [128, 8], F32)
        cond = C.sb("cond", [128, 8], F32)
        mod = C.sb("mod", [1, 9, D], F32)
        R = C.sb("R", [1, 9, D], F32)
        tmp = C.sb("tmp", [1, D], F32)
        PP = [C.ps("PP%d" % i, [128, 512], F32) for i in range(4)]
        S.psum_keys.update(C.psum_names)
        S.dma("sync", cst[:], cd, writes=["cst"])
        S.dma("sync", adab[:], adabd, writes=["adab"])
        S.dma("sync", gl[:], gd, writes=["gl"])
        S.op("scalar", lambda e: e.activation(out=cond[:], in_=cst[:], func=AF.Silu), reads=["cst"], writes=["cond"])
        pc = [0]
        for v in range(9):
            ad = adas[v % 2]
            S.dma("sync" if v % 2 == 0 else "scalar", ad[:], adawd[v], writes=[ad.name])
            for h2 in range(2):
                Pp = PP[pc[0] % 4]
                pc[0] += 1
                for k in range(8):
                    S.op("tensor", lambda e: e.matmul(Pp[0:1, :], lhsT=cond[:, k:k + 1], rhs=ad[:, k, h2 * 512:(h2 + 1) * 512],
                                                      start=(k == 0), stop=(k == 7)),
                         reads=["cond", ad.name], writes=[Pp.name])
                S.op("vector", lambda e: e.tensor_tensor(out=mod[0:1, v, h2 * 512:(h2 + 1) * 512], in0=Pp[0:1, :],
                                                         in1=adab[0:1, v, h2 * 512:(h2 + 1) * 512], op=ALU.add),
                     reads=[Pp.name, "adab"], writes=["mod%d" % v])
        for s_ in range(3):
            res_w = 1.0 if s_ == 1 else 0.5
            sh, sc, gt = 3 * s_, 3 * s_ + 1, 3 * s_ + 2
            S.op("vector", lambda e: e.tensor_scalar(out=tmp[:], in0=mod[0:1, sc, :], scalar1=1.0, scalar2=None, op0=ALU.add),
                 reads=["mod%d" % sc], writes=["tmp"])
            S.op("vector", lambda e: e.tensor_tensor(out=R[0:1, 3 * s_ + 0, :], in0=tmp[:], in1=gl[0:1, 2 * s_, :], op=ALU.mult),
                 reads=["tmp", "gl"], writes=["R"])
            S.op("vector", lambda e: e.tensor_copy(out=R[0:1, 3 * s_ + 1, :], in_=mod[0:1, sh, :]), reads=["mod%d" % sh, "R"], writes=["R"])
            S.op("vector", lambda e: e.tensor_scalar(out=tmp[:], in0=mod[0:1, gt, :], scalar1=float(res_w), scalar2=None, op0=ALU.mult),
                 reads=["mod%d" % gt, "R"], writes=["tmp"])
            S.op("vector", lambda e: e.tensor_tensor(out=R[0:1, 3 * s_ + 2, :], in0=tmp[:], in1=gl[0:1, 2 * s_ + 1, :], op=ALU.mult),
                 reads=["tmp", "gl", "R"], writes=["R"])
        S.dma("sync", outd.rearrange("(o r) d -> o r d", o=1), R[:], reads=["R"], is_output=True)
        S.close()


NREL = 67


def emit_diff(nc, pfx, binds, lam_init, gathered):
    C = Ctx(nc, pfx, binds)
    nc, es = C.nc, C.es
    hTd = C.din("hT", [D, T], BF16)
    hsrc = hT_source(hTd, gathered)
    odst = None
    wqd = C.din("wq", [128, 8, 256])
    wkd = C.din("wk", [128, 8, 256])
    wvd = C.din("wv", [128, 8, 256])
    based = C.din("base", [128, 2, 512])
    cbd = C.din("cb", [128, 2, NREL])
    dmaskd = C.din("dmask", [128, 4, 512], BF16)
    lamd = C.din("lam", [128, 256])
    subgd = C.din("subg", [128, 128])
    identd = C.din("ident", [128, 128])
    oTd = C.dout("oT", [256, T], BF16)
    odst = o_dest(oTd, gathered)
    with es:
        S = Sched(nc, es, pfx)
        QT = [C.sb("QT%d" % h, [128, T], BF16) for h in range(2)]
        KT = [C.sb("KT%d" % h, [128, T], BF16) for h in range(2)]
        Vaug = C.sb("Vaug", [128, 64, 2, 129], BF16)
        wq = C.sb("wq", [128, 8, 256], BF16)
        wk = C.sb("wk", [128, 8, 256], BF16)
        wv = C.sb("wv", [128, 8, 256], BF16)
        hTs = [C.sb("hTs%d" % i, [128, 8, 512], BF16) for i in range(2)]
        base = C.sb("base", [128, 2, 512], F32)
        cbt = C.sb("cbt", [128, 2, NREL], F32)
        dmask = C.sb("dmask", [128, 4, 512], BF16)
        tb = [C.sb("tb%d" % i, [128, 512], F32) for i in range(2)]
        Pb = [C.sb("Pb%d" % i, [128, 512], BF16) for i in range(7)]
        lam = C.sb("lam", [128, 256], F32)
        subg = C.sb("subg", [128, 128], F32)
        identf = C.sb("identf", [128, 128], F32)
        identb = C.sb("identb", [128, 128], BF16)
        sm = C.sb("sm", [128, 16], F32)
        lamneg = C.sb("lamneg", [128, 1], F32)
        epsb = C.sb("epsb", [128, 1], F32)
        o0s = C.sb("o0s", [128, 128], F32)
        od = C.sb("od", [128, 128], F32)
        junk = C.sb("junk", [128, 256], F32)
        ob = C.sb("ob", [128, 4, 256], BF16)
        oTs = [C.sb("oTs%d" % i, [128, 2, 512], BF16) for i in range(2)]
        SB_ = [C.ps("Sb%d" % i, [128, 512], F32) for i in range(3)]
        OB = [[C.ps("O%d%d" % (m, p), [128, 512], F32) for p in range(2)] for m in range(2)]
        PT = C.ps("PT", [128, 1024], BF16)
        S.psum_keys.update(C.psum_names)

        S.dma("sync", identf[:], identd[:, :], writes=["identf"])
        S.op("vector", lambda e: e.tensor_copy(out=identb[:], in_=identf[:]), reads=["identf"], writes=["identb"])
        S.dma("gpsimd", wq[:], wqd[:, :, :], writes=["wq"])
        S.dma("gpsimd", wk[:], wkd[:, :, :], writes=["wk"])
        S.dma("gpsimd", wv[:], wvd[:, :, :], writes=["wv"])
        S.dma("sync", base[:], based[:, :, :], writes=["base"])
        S.dma("sync", cbt[:], cbd[:, :, :], writes=["cbt"])
        S.dma("sync", dmask[:], dmaskd[:, :, :], writes=["dmask"])
        S.dma("sync", lam[:], lamd[:, :], writes=["lam"])
        S.dma("sync", subg[:], subgd[:, :], writes=["subg"])
        S.op("vector", lambda e: e.memset(epsb[:], EPS), writes=["epsb"])
        S.op("vector", lambda e: e.memset(Vaug[:, :, :, 128:129], 1.0), writes=["Vones"])
        S.op("vector", lambda e: e.scalar_tensor_tensor(out=junk[:, 0:64], in0=lam[:, 0:64], scalar=1.0, in1=lam[:, 64:128],
                                                        op0=ALU.mult, op1=ALU.mult, accum_out=sm[:, 0:1]),
             reads=["lam"], writes=["junk", "sm0"])
        S.op("vector", lambda e: e.scalar_tensor_tensor(out=junk[:, 0:64], in0=lam[:, 128:192], scalar=1.0, in1=lam[:, 192:256],
                                                        op0=ALU.mult, op1=ALU.mult, accum_out=sm[:, 1:2]),
             reads=["lam", "junk"], writes=["junk", "sm1"])
        S.op("scalar", lambda e: e.activation(out=sm[:, 0:2], in_=sm[:, 0:2], func=AF.Exp), reads=["sm0", "sm1"],
             writes=["sm0", "sm1"])
        S.op("vector", lambda e: e.tensor_tensor(out=sm[:, 2:3], in0=sm[:, 1:2], in1=sm[:, 0:1], op=ALU.subtract),
             reads=["sm0", "sm1"], writes=["sm2"])
        S.op("vector", lambda e: e.tensor_scalar(out=lamneg[:], in0=sm[:, 2:3], scalar1=-float(lam_init), scalar2=None, op0=ALU.add),
             reads=["sm2"], writes=["lamneg"])
        S.op("vector", lambda e: e.tensor_scalar(out=subg[:], in0=subg[:], scalar1=float(1.0 - lam_init), scalar2=None, op0=ALU.mult),
             reads=["subg"], writes=["subg"])

        cp = [0]

        def evac(dst, src, rk, wk_):
            eng = "scalar" if cp[0] % 2 == 0 else "vector"
            cp[0] += 1
            if eng == "scalar":
                S.op("scalar", lambda e: e.activation(out=dst, in_=src, func=AF.Copy), reads=rk, writes=wk_)
            else:
                S.op("vector", lambda e: e.tensor_copy(out=dst, in_=src), reads=rk, writes=wk_)

        bank = [0]
        for tg in range(16):
            hs = hTs[tg % 2]
            hk = "hTs%d" % (tg % 2)
            S.dma("sync", hs[:], hsrc(tg), writes=[hk])
            for (w, wname, dstl, dname) in [(wq, "wq", QT, "QT"), (wk, "wk", KT, "KT")]:
                for h in range(2):
                    Pp = SB_[bank[0] % 3]
                    bank[0] += 1
                    for k in range(8):
                        S.op("tensor", lambda e: e.matmul(Pp[:], lhsT=w[:, k, h * 128:(h + 1) * 128], rhs=hs[:, k, :],
                                                          start=(k == 0), stop=(k == 7)),
                             reads=[wname, hk], writes=[Pp.name])
                    evac(dstl[h][:, tg * 512:(tg + 1) * 512], Pp[:], [Pp.name], ["%s%d_%d" % (dname, h, tg)])
            for tt in range(4):
                Pp = SB_[bank[0] % 3]
                bank[0] += 1
                for k in range(8):
                    S.op("tensor", lambda e: e.matmul(Pp[:, 0:256], lhsT=hs[:, k, tt * 128:(tt + 1) * 128], rhs=wv[:, k, :],
                                                      start=(k == 0), stop=(k == 7)),
                         reads=["wv", hk], writes=[Pp.name])
                blk = tg * 4 + tt
                evac(Vaug[:, blk, :, 0:128], Pp[:, 0:256].rearrange("p (h e) -> p h e", h=2), [Pp.name], ["V_%d" % blk])

        scale = 64 ** -0.5
        ctr = {"s": 0, "t": 0, "p": 0, "o": 0}
        for g in range(16):
            for h in range(2):
                tiles = [dict(kb=kb, m=m) for kb in range(4 * g + 4) for m in range(2)]
                first_av = {}

                def emit_s(tl):
                    kb, m = tl["kb"], tl["m"]
                    Sp = SB_[ctr["s"] % 3]
                    ctr["s"] += 1
                    tl["Sp"] = Sp
                    r = kb - 4 * g
                    S.op("tensor", lambda e: e.matmul(Sp[:], lhsT=KT[h][m * 64:(m + 1) * 64, kb * 128:(kb + 1) * 128],
                                                      rhs=QT[h][m * 64:(m + 1) * 64, g * 512:(g + 1) * 512], start=True, stop=(r < 0)),
                         reads=["KT%d_%d" % (h, kb // 4), "QT%d_%d" % (h, g)], writes=[Sp.name])
                    if r >= 0:
                        S.op("tensor", lambda e: e.matmul(Sp[:], lhsT=identb[:], rhs=dmask[:, r, :], start=False, stop=True),
                             reads=["identb", "dmask"], writes=[Sp.name])

                def emit_mid(tl):
                    kb, m, Sp = tl["kb"], tl["m"], tl["Sp"]
                    tt_ = tb[ctr["t"] % 2]
                    ctr["t"] += 1
                    Pt = Pb[ctr["p"] % 7]
                    ctr["p"] += 1
                    tl["P"] = Pt
                    S.op("vector", lambda e: e.scalar_tensor_tensor(out=tt_[:], in0=Sp[:], scalar=scale, in1=base[:, h, :],
                                                                    op0=ALU.mult, op1=ALU.add),
                         reads=[Sp.name, "base"], writes=[tt_.name])
                    rel = 4 * g - kb + 3
                    S.op("scalar", lambda e: e.activation(out=Pt[:], in_=tt_[:], func=AF.Exp, bias=cbt[:, h, rel:rel + 1], scale=1.0),
                         reads=[tt_.name, "cbt"], writes=[Pt.name])

                def emit_av(tl):
                    kb, m, Pt = tl["kb"], tl["m"], tl["P"]
                    r = kb - 4 * g
                    for qb in range(4):
                        if qb < r:
                            continue
                        p = qb // 2
                        O = OB[m][p]
                        st_ = (m, p) not in first_av
                        first_av[(m, p)] = True
                        c0 = (qb % 2) * 129
                        S.op("tensor", lambda e: e.matmul(O[:, c0:c0 + 129], lhsT=Pt[:, qb * 128:(qb + 1) * 128], rhs=Vaug[:, kb, h, :],
                                                          start=st_, stop=(kb == 4 * g + qb), skip_group_check=True),
                             reads=[Pt.name, "V_%d" % kb, "Vones"], writes=[O.name])

                flash_pipeline(S, tiles, emit_s, emit_mid, emit_av)
                for qb in range(4):
                    p = qb // 2
                    c0 = (qb % 2) * 129
                    O0, O1 = OB[0][p], OB[1][p]
                    S.op("vector", lambda e: e.reciprocal(out=sm[:, 4:5], in_=O0[:, c0 + 128:c0 + 129]), reads=[O0.name], writes=["sm4"])
                    S.op("vector", lambda e: e.reciprocal(out=sm[:, 5:6], in_=O1[:, c0 + 128:c0 + 129]), reads=[O1.name], writes=["sm5"])
                    S.op("vector", lambda e: e.tensor_tensor(out=sm[:, 5:6], in0=sm[:, 5:6], in1=lamneg[:], op=ALU.mult),
                         reads=["sm5", "lamneg"], writes=["sm5"])
                    S.op("vector", lambda e: e.tensor_scalar(out=o0s[:], in0=O0[:, c0:c0 + 128], scalar1=sm[:, 4:5], scalar2=None, op0=ALU.mult),
                         reads=[O0.name, "sm4"], writes=["o0s"])
                    S.op("vector", lambda e: e.scalar_tensor_tensor(out=od[:], in0=O1[:, c0:c0 + 128], scalar=sm[:, 5:6], in1=o0s[:],
                                                                    op0=ALU.mult, op1=ALU.add),
                         reads=[O1.name, "sm5", "o0s"], writes=["od"])
                    S.op("scalar", lambda e: e.activation(out=junk[:, 0:128], in_=od[:], func=AF.Square, accum_out=sm[:, 6:7]),
                         reads=["od"], writes=["junk", "sm6"])
                    S.op("scalar", lambda e: e.activation(out=sm[:, 6:7], in_=sm[:, 6:7], func=AF.Sqrt, bias=epsb[:], scale=1.0 / 128),
                         reads=["sm6", "epsb"], writes=["sm6"])
                    S.op("vector", lambda e: e.reciprocal(out=sm[:, 6:7], in_=sm[:, 6:7]), reads=["sm6"], writes=["sm6"])
                    S.op("vector", lambda e: e.scalar_tensor_tensor(out=ob[:, qb, h * 128:(h + 1) * 128], in0=od[:], scalar=sm[:, 6:7],
                                                                    in1=subg[:], op0=ALU.mult, op1=ALU.mult),
                         reads=["od", "sm6", "subg"], writes=["ob"])
            ot = oTs[ctr["o"] % 2]
            otk = "oTs%d" % (ctr["o"] % 2)
            ctr["o"] += 1
            for qb in range(4):
                for c in range(2):
                    S.op("tensor", lambda e: e.transpose(out=PT[:, c * 512 + qb * 128:c * 512 + (qb + 1) * 128],
                                                         in_=ob[:, qb, c * 128:(c + 1) * 128], identity=identb[:]),
                         reads=["ob", "identb"], writes=["PT"])
            S.op("vector", lambda e: e.tensor_copy(out=ot[:], in_=PT[:].rearrange("p (c t) -> p c t", c=2)), reads=["PT"], writes=[otk])
            S.dma("sync", odst(g), ot[:], reads=[otk], is_output=True)
        S.close()


def alibi_slopes_np(n):
    return np.exp2(-8.0 * np.arange(1, n + 1, dtype=np.float64) / n)


def diag_mask_tiles(strict):
    jj = np.arange(128)[:, None, None]
    r = np.arange(4)[None, :, None]
    q = np.arange(512)[None, None, :]
    d = q - jj - 128 * r
    ok = d >= (1 if strict else 0)
    return np.where(ok, 0.0, -BIGNEG).astype(np.float32)


def base_tile(slope):
    jj = np.arange(128)[:, None]
    q = np.arange(512)[None, :]
    return (-slope * (q - jj)).astype(np.float32)


def cb_table(slope):
    rel = np.arange(NREL) - 3
    return np.ascontiguousarray(np.broadcast_to((-slope * 128.0 * rel)[None, :], (128, NREL))).astype(np.float32)


def arrange_w(wcols):
    n = wcols.shape[1]
    return np.ascontiguousarray(wcols.reshape(8, 128, n).transpose(1, 0, 2))


def diff_inputs(pfx, w_in, lam, subln_g):
    slopes = alibi_slopes_np(8)
    dm = diag_mask_tiles(False).astype(ml_dtypes.bfloat16)
    lamb = np.ascontiguousarray(np.broadcast_to(lam.reshape(1, 256), (128, 256)))
    sgb = np.ascontiguousarray(np.broadcast_to(subln_g.reshape(1, 128), (128, 128)))
    out = []
    for hg in range(4):
        hs = [2 * hg, 2 * hg + 1]
        cols = np.concatenate([np.arange(h * 128, (h + 1) * 128) for h in hs])
        out.append({
            pfx + "wq": arrange_w(w_in[:, cols]),
            pfx + "wk": arrange_w(w_in[:, 1024 + cols]),
            pfx + "wv": arrange_w(w_in[:, 2048 + cols]),
            pfx + "base": np.ascontiguousarray(np.stack([base_tile(slopes[h]) for h in hs], axis=1)),
            pfx + "cb": np.ascontiguousarray(np.stack([cb_table(slopes[h]) for h in hs], axis=1)),
            pfx + "dmask": dm, pfx + "lam": lamb, pfx + "subg": sgb, pfx + "ident": _IDENT,
        })
    return out


def emit_sb(nc, pfx, binds, gathered):
    C = Ctx(nc, pfx, binds)
    nc, es = C.nc, C.es
    hTd = C.din("hT", [D, T], BF16)
    hsrc = hT_source(hTd, gathered)
    odst = None
    wqd = C.din("wq", [128, 8, 256])
    wkd = C.din("wk", [128, 8, 256])
    wvd = C.din("wv", [128, 8, 256])
    m01d = C.din("m01", [128, 4, 512], BF16)
    trid = C.din("tri", [128, 2, 128], BF16)
    identd = C.din("ident", [128, 128])
    oTd = C.dout("oT", [256, T], BF16)
    odst64 = o_dest(oTd, gathered, 64)
    with es:
        S = Sched(nc, es, pfx)
        QT = [C.sb("QT%d" % h, [128, T], BF16) for h in range(2)]
        KT = [C.sb("KT%d" % h, [128, T], BF16) for h in range(2)]
        V = C.sb("V", [128, 64, 256], BF16)
        wq = C.sb("wq", [128, 8, 256], BF16)
        wk = C.sb("wk", [128, 8, 256], BF16)
        wv = C.sb("wv", [128, 8, 256], BF16)
        hTs = [C.sb("hTs%d" % i, [128, 8, 512], BF16) for i in range(2)]
        m01 = C.sb("m01", [128, 4, 512], BF16)
        tri = C.sb("tri", [128, 2, 128], BF16)
        eb = [C.sb("eb%d" % i, [128, 512], F32) for i in range(4)]
        spb = [C.sb("spb%d" % i, [128, 512], BF16) for i in range(4)]
        wb = [C.sb("wb%d" % i, [128, 512], F32) for i in range(2)]
        ab = [C.sb("ab%d" % i, [128, 512], BF16) for i in range(3)]
        identf = C.sb("identf", [128, 128], F32)
        identb = C.sb("identb", [128, 128], BF16)
        obT = [C.sb("obT%d" % i, [64, 4, 512], BF16) for i in range(2)]
        ZB = [C.ps("Zb%d" % i, [128, 512], F32) for i in range(3)]
        XB = [C.ps("Xb%d" % i, [128, 512], F32) for i in range(2)]
        OBk = [C.ps("Ob%d" % i, [128, 512], F32) for i in range(2)]
        PT = C.ps("PT", [128, 1024], BF16)
        S.psum_keys.update(C.psum_names)

        S.dma("sync", identf[:], identd[:, :], writes=["identf"])
        S.op("vector", lambda e: e.tensor_copy(out=identb[:], in_=identf[:]), reads=["identf"], writes=["identb"])
        S.dma("gpsimd", wq[:], wqd[:, :, :], writes=["wq"])
        S.dma("gpsimd", wk[:], wkd[:, :, :], writes=["wk"])
        S.dma("gpsimd", wv[:], wvd[:, :, :], writes=["wv"])
        S.dma("sync", m01[:], m01d[:, :, :], writes=["m01"])
        S.dma("sync", tri[:], trid[:, :, :], writes=["tri"])

        cp = [0]

        def evac(dst, src, rk, wk_):
            eng = "scalar" if cp[0] % 2 == 0 else "vector"
            cp[0] += 1
            if eng == "scalar":
                S.op("scalar", lambda e: e.activation(out=dst, in_=src, func=AF.Copy), reads=rk, writes=wk_)
            else:
                S.op("vector", lambda e: e.tensor_copy(out=dst, in_=src), reads=rk, writes=wk_)

        bank = [0]
        for tg in range(16):
            hs = hTs[tg % 2]
            hk = "hTs%d" % (tg % 2)
            S.dma("sync", hs[:], hsrc(tg), writes=[hk])
            for (w, wname, dstl, dname) in [(wq, "wq", QT, "QT"), (wk, "wk", KT, "KT")]:
                for h in range(2):
                    Pp = ZB[bank[0] % 3]
                    bank[0] += 1
                    for k in range(8):
                        S.op("tensor", lambda e: e.matmul(Pp[:], lhsT=w[:, k, h * 128:(h + 1) * 128], rhs=hs[:, k, :],
                                                          start=(k == 0), stop=(k == 7)),
                             reads=[wname, hk], writes=[Pp.name])
                    evac(dstl[h][:, tg * 512:(tg + 1) * 512], Pp[:], [Pp.name], ["%s%d_%d" % (dname, h, tg)])
            for tt in range(4):
                Pp = ZB[bank[0] % 3]
                bank[0] += 1
                for k in range(8):
                    S.op("tensor", lambda e: e.matmul(Pp[:, 0:256], lhsT=hs[:, k, tt * 128:(tt + 1) * 128], rhs=wv[:, k, :],
                                                      start=(k == 0), stop=(k == 7)),
                         reads=["wv", hk], writes=[Pp.name])
                blk = tg * 4 + tt
                evac(V[:, blk, :], Pp[:, 0:256], [Pp.name], ["V_%d" % blk])

        scale = 64 ** -0.5
        ctr = {"z": 0, "e": 0, "w": 0, "a": 0, "o": 0, "chain": 0}
        for g in range(16):
            chains = []
            for hh in range(4):
                ch = ctr["chain"]
                ctr["chain"] += 1
                kbs = list(range(4 * g + 3, -1, -1))
                av0 = [True]
                chains.append([dict(hh=hh, kb=kb, first=(i == 0), last=(i == len(kbs) - 1), X=XB[ch % 2], O=OBk[ch % 2], av0=av0)
                               for i, kb in enumerate(kbs)])
            tiles = []
            for pr in range(2):
                for ta, tb_ in zip(chains[2 * pr], chains[2 * pr + 1]):
                    tiles += [ta, tb_]

            def emit_Z(tl):
                hh, kb = tl["hh"], tl["kb"]
                p, half = hh // 2, hh % 2
                Zp = ZB[ctr["z"] % 3]
                ctr["z"] += 1
                tl["Z"] = Zp
                S.op("tensor", lambda e: e.matmul(Zp[:], lhsT=KT[p][half * 64:(half + 1) * 64, kb * 128:(kb + 1) * 128],
                                                  rhs=QT[p][half * 64:(half + 1) * 64, g * 512:(g + 1) * 512], start=True, stop=True),
                     reads=["KT%d_%d" % (p, kb // 4), "QT%d_%d" % (p, g)], writes=[Zp.name])

            def emit_esp(tl):
                kb, Zp = tl["kb"], tl["Z"]
                i = ctr["e"] % 4
                ctr["e"] += 1
                tl["e"], tl["sp"] = eb[i], spb[i]
                r = kb - 4 * g
                S.op("scalar", lambda e: e.activation(out=eb[i][:], in_=Zp[:], func=AF.Exp, scale=scale), reads=[Zp.name], writes=[eb[i].name])
                S.op("scalar", lambda e: e.activation(out=spb[i][:], in_=eb[i][:], func=AF.Ln, bias=1.0, scale=1.0),
                     reads=[eb[i].name], writes=[spb[i].name])
                if r >= 0:
                    S.op("gpsimd", lambda e: e.tensor_tensor(out=spb[i][:], in0=spb[i][:], in1=m01[:, r, :], op=ALU.mult),
                         reads=[spb[i].name, "m01"], writes=[spb[i].name])
                    S.op("gpsimd", lambda e: e.tensor_tensor(out=eb[i][:], in0=eb[i][:], in1=m01[:, r, :], op=ALU.mult),
                         reads=[eb[i].name, "m01"], writes=[eb[i].name])

            def emit_L(tl):
                X, sp = tl["X"], tl["sp"]
                S.op("tensor", lambda e: e.matmul(X[:], lhsT=tri[:, 0, :], rhs=sp[:], start=tl["first"], stop=False, skip_group_check=True),
                     reads=["tri", sp.name], writes=[X.name])

            def emit_w(tl):
                X = tl["X"]
                wi = wb[ctr["w"] % 2]
                ctr["w"] += 1
                tl["w"] = wi
                S.op("scalar", lambda e: e.activation(out=wi[:], in_=X[:], func=AF.Exp, scale=-1.0), reads=[X.name], writes=[wi.name])

            def emit_U(tl):
                X, sp = tl["X"], tl["sp"]
                S.op("tensor", lambda e: e.matmul(X[:], lhsT=tri[:, 1, :], rhs=sp[:], start=False, stop=tl["last"], skip_group_check=True),
                     reads=["tri", sp.name], writes=[X.name])

            def emit_a(tl):
                ee, wi = tl["e"], tl["w"]
                ai = ab[ctr["a"] % 3]
                ctr["a"] += 1
                tl["a"] = ai
                S.op("vector", lambda e: e.tensor_tensor(out=ai[:], in0=ee[:], in1=wi[:], op=ALU.mult),
                     reads=[ee.name, wi.name], writes=[ai.name])

            obt = obT[g % 2]

            def stage2(tl):
                hh, kb, O, ai = tl["hh"], tl["kb"], tl["O"], tl["a"]
                st_ = tl["av0"][0]
                tl["av0"][0] = False
                S.op("tensor", lambda e: e.matmul(O[0:64, :], lhsT=V[:, kb, hh * 64:(hh + 1) * 64], rhs=ai[:],
                                                  start=st_, stop=(kb == 0), skip_group_check=True),
                     reads=[ai.name, "V_%d" % kb], writes=[O.name])
                if tl["last"]:
                    S.op("vector", lambda e: e.tensor_copy(out=obt[:, hh, :], in_=O[0:64, :]), reads=[O.name], writes=[obt.name])

            n = len(tiles)
            emit_Z(tiles[0])
            for i in range(n + 2):
                if 1 <= i <= n:
                    emit_L(tiles[i - 1])
                if 2 <= i:
                    emit_U(tiles[i - 2])
                if i + 1 < n:
                    emit_Z(tiles[i + 1])
                if 2 <= i:
                    stage2(tiles[i - 2])
                if i < n:
                    emit_esp(tiles[i])
                if 1 <= i <= n:
                    emit_w(tiles[i - 1])
                    emit_a(tiles[i - 1])
            S.dma("sync", odst64(g), obt[:], reads=[obt.name], is_output=True)
        S.close()


def sb_inputs(pfx, w_in):
    jj = np.arange(128)[:, None, None]
    r = np.arange(4)[None, :, None]
    q = np.arange(512)[None, None, :]
    m01 = ((q - jj - 128 * r) >= 1).astype(np.float32).astype(ml_dtypes.bfloat16)
    mm = np.arange(128)[:, None]
    j2 = np.arange(128)[None, :]
    tri = np.ascontiguousarray(np.stack([(mm >= j2), (mm < j2)], axis=1).astype(np.float32).astype(ml_dtypes.bfloat16))
    out = []
    for hg in range(4):
        cols = np.arange(hg * 256, (hg + 1) * 256)
        out.append({
            pfx + "wq": arrange_w(w_in[:, cols]),
            pfx + "wk": arrange_w(w_in[:, 1024 + cols]),
            pfx + "wv": arrange_w(w_in[:, 2048 + cols]),
            pfx + "m01": m01, pfx + "tri": tri, pfx + "ident": _IDENT,
        })
    return out


NSA_FORCE = 1e4
NSA_NEG = -1e30


class _Stop(Exception):
    pass


def emit_nsa(nc, pfx, binds, gathered, dbg=None):
    C = Ctx(nc, pfx, binds)
    nc, es = C.nc, C.es
    hTd = C.din("hT", [D, T], BF16)
    hsrc = hT_source(hTd, gathered)
    odst = None
    wfmd = C.din("wfm", [128, 8, 640])
    wtmd = C.din("wtm", [128, 8, 140])
    cw1d = C.din("cw1", [128, 32, 256])
    cped = C.din("cpe", [128, 32])
    cw2kd = C.din("cw2k", [128, 2, 128])
    cw2vd = C.din("cw2v", [128, 2, 64])
    ovld = C.din("ovl", [128, 4, 128], BF16)
    slpd = C.din("slp", [128, 4])
    cbd = C.din("cb", [128, 4, NREL])
    cbcd = C.din("cbc", [128, 4, 16])
    base0d = C.din("base0", [128, 512])
    basec0d = C.din("basec0", [128, 512])
    cmaskd = C.din("cmask", [128, 5, 512], BF16)
    dmaskd = C.din("dmask", [128, 4, 512], BF16)
    wmaskd = C.din("wmask", [128, 8, 512], BF16)
    indd = C.din("ind", [128, T], BF16)
    adjd = C.din("adj", [64, 128, 128])
    identd = C.din("ident", [128, 128])
    oTd = C.dout("oT", [256, T], BF16)
    odst = o_dest(oTd, gathered)
    with es:
        S = Sched(nc, es, pfx)
        try:
            QT = [C.sb("QT%d" % h, [128, T], BF16) for h in range(2)]
            ksT = C.sb("ksT", [128, T], BF16)
            kwT = C.sb("kwT", [128, T], BF16)
            kcvT = C.sb("kcvT", [128, T], BF16)
            vsA = C.sb("vsA", [128, 64, 65], BF16)
            vwA = C.sb("vwA", [128, 64, 65], BF16)
            gates = C.sb("gates", [128, 64, 12], F32)
            PBUF = C.sb("PBUF", [128, 14464], BF16)
            hTs = [PBUF[:, i * 4096:(i + 1) * 4096].rearrange("p (k t) -> p k t", k=8) for i in range(2)]
            wfm = PBUF[:, 8192:8192 + 5120].rearrange("p (k n) -> p k n", k=8)
            wtm = PBUF[:, 13312:13312 + 1120].rearrange("p (k n) -> p k n", k=8)
            cw1 = PBUF[:, 0:8192].rearrange("p (l f) -> p l f", l=32)
            ind = PBUF[:, 0:8192]
            cpe = C.sb("cpe", [128, 32], BF16)
            cw2k = C.sb("cw2k", [128, 2, 128], BF16)
            cw2v = C.sb("cw2v", [128, 2, 64], BF16)
            slp = C.sb("slp", [128, 4], F32)
            cbt = C.sb("cbt", [128, 4, NREL], F32)
            cbct = C.sb("cbct", [128, 4, 16], F32)
            base0 = C.sb("base0", [128, 512], F32)
            basec0 = C.sb("basec0", [128, 512], F32)
            cmask = C.sb("cmask", [128, 5, 512], BF16)
            dmask = C.sb("dmask", [128, 4, 512], BF16)
            wmask = C.sb("wmask", [128, 8, 512], BF16)
            tb = [C.sb("tb%d" % i, [128, 512], F32) for i in range(3)]
            Pb = [C.sb("Pb%d" % i, [128, 512], BF16) for i in range(7)]
            kcmpT = C.sb("kcmpT", [128, 512], BF16)
            vcA = C.sb("vcA", [128, 4, 193], BF16)
            glb = [C.sb("glb%d" % i, [128, 512], BF16) for i in range(4)]
            peb = C.sb("peb", [128, 4], F32)
            imp = C.sb("imp", [128, 4, 128], F32)
            adjt = [C.sb("adjt%d" % i, [128, 128], F32) for i in range(2)]
            impa = C.sb("impa", [128, 128], F32)
            impb = C.sb("impb", [128, 128], F32)
            m8 = C.sb("m8", [128, 16], F32)
            selb = C.sb("selb", [128, 128], BF16)
            MBT = [C.sb("MBT%d" % i, [128, 512], BF16) for i in range(2)]
            acco = C.sb("acco", [128, 4, 256], F32)
            ob = C.sb("ob", [128, 4, 256], BF16)
            oTs = [C.sb("oTs%d" % i, [128, 2, 512], BF16) for i in range(2)]
            sm = C.sb("sm", [128, 8], F32)
            otf = C.sb("otf", [65, 512], F32)
            identf = C.sb("identf", [128, 128], F32)
            identb = C.sb("identb", [128, 128], BF16)
            SB_ = [C.ps("Sb%d" % i, [128, 512], F32) for i in range(3)]
            AC = [C.ps("Ac%d" % i, [128, 512], F32) for i in range(4)]
            PT = C.ps("PT", [128, 1024], BF16)
            S.psum_keys.update(C.psum_names)

            S.dma("sync", identf[:], identd[:, :], writes=["identf"])
            S.op("vector", lambda e: e.tensor_copy(out=identb[:], in_=identf[:]), reads=["identf"], writes=["identb"])
            S.dma("gpsimd", wfm, wfmd[:, :, :], writes=["wfm"])
            S.dma("gpsimd", wtm, wtmd[:, :, :], writes=["wtm"])
            S.dma("gpsimd", cpe[:], cped[:, :], writes=["cpe"])
            S.dma("gpsimd", cw2k[:], cw2kd[:, :, :], writes=["cw2k"])
            S.dma("gpsimd", cw2v[:], cw2vd[:, :, :], writes=["cw2v"])
            for (dst, src, key) in [(slp, slpd, "slp"), (cbt, cbd, "cbt"), (cbct, cbcd, "cbct"), (base0, base0d, "base0"),
                                    (basec0, basec0d, "basec0"), (cmask, cmaskd, "cmask"), (dmask, dmaskd, "dmask"),
                                    (wmask, wmaskd, "wmask")]:
                S.dma("sync", dst[:], src, writes=[key])
            S.op("vector", lambda e: e.memset(vsA[:, :, 64:65], 1.0), writes=["vsones"])
            S.op("vector", lambda e: e.memset(vwA[:, :, 64:65], 1.0), writes=["vwones"])
            S.op("vector", lambda e: e.memset(vcA[:], 0.0), writes=["vcA"])
            S.op("vector", lambda e: e.memset(kcmpT[:], 0.0), writes=["kcmpT"])
            S.op("vector", lambda e: e.memset(vcA[:, :, 64:65], 1.0), reads=["vcA"], writes=["vcA"])
            S.dma("sync", vcA[:, :, 65:193], ovld[:, :, :], reads=["vcA"], writes=["vcA"])

            if dbg == 'const':
                raise _Stop
            cp = [0]

            def evac(dst, src, rk, wk_, scale=None):
                eng = "scalar" if cp[0] % 2 == 0 else "vector"
                cp[0] += 1
                if eng == "scalar" and scale is None:
                    S.op("scalar", lambda e: e.activation(out=dst, in_=src, func=AF.Copy), reads=rk, writes=wk_)
                else:
                    if scale is None:
                        S.op("vector", lambda e: e.tensor_copy(out=dst, in_=src), reads=rk, writes=wk_)
                    else:
                        S.op("vector", lambda e: e.tensor_scalar(out=dst, in0=src, scalar1=float(scale), scalar2=None, op0=ALU.mult),
                             reads=rk, writes=wk_)

            bank = [0]
            fm_dst = [(QT[0], "QT0", 0.125), (QT[1], "QT1", 0.125), (ksT, "ksT", None), (kwT, "kwT", None), (kcvT, "kcvT", None)]
            for tg in range(1 if dbg in ('proj1', 'proj1ns') else 16):
                hs = hTs[tg % 2]
                hk = "hTs%d" % (tg % 2)
                S.dma("sync", hs, hsrc(tg), writes=[hk])
                for fi, (dst, dname, sc) in enumerate(fm_dst):
                    Pp = SB_[bank[0] % 3]
                    bank[0] += 1
                    for k in range(8):
                        S.op("tensor", lambda e: e.matmul(Pp[:], lhsT=wfm[:, k, fi * 128:(fi + 1) * 128], rhs=hs[:, k, :],
                                                          start=(k == 0), stop=(k == 7)),
                             reads=["wfm", hk], writes=[Pp.name])
                    evac(dst[:, tg * 512:(tg + 1) * 512], Pp[:], [Pp.name], ["%s_%d" % (dname, tg)], scale=sc)
                for tt in range(4):
                    Pp = SB_[bank[0] % 3]
                    bank[0] += 1
                    for k in range(8):
                        S.op("tensor", lambda e: e.matmul(Pp[:, 0:140], lhsT=hs[:, k, tt * 128:(tt + 1) * 128], rhs=wtm[:, k, :],
                                                          start=(k == 0), stop=(k == 7)),
                             reads=["wtm", hk], writes=[Pp.name])
                    blk = tg * 4 + tt
                    S.op("vector", lambda e: e.tensor_copy(out=vsA[:, blk, 0:64], in_=Pp[:, 0:64]), reads=[Pp.name], writes=["vs_%d" % blk])
                    S.op("vector", lambda e: e.tensor_copy(out=vwA[:, blk, 0:64], in_=Pp[:, 64:128]), reads=[Pp.name], writes=["vw_%d" % blk])
                    S.op("scalar", lambda e: e.activation(out=gates[:, blk, :], in_=Pp[:, 128:140], func=AF.Exp, scale=-1.0),
                         reads=[Pp.name], writes=["gates_%d" % blk])
                    S.op("vector", lambda e: e.tensor_scalar(out=gates[:, blk, :], in0=gates[:, blk, :], scalar1=1.0, scalar2=None, op0=ALU.add),
                         reads=["gates_%d" % blk], writes=["gates_%d" % blk])
                    S.op("vector", lambda e: e.reciprocal(out=gates[:, blk, :], in_=gates[:, blk, :]),
                         reads=["gates_%d" % blk], writes=["gates_%d" % blk])
            if dbg in ('proj', 'proj1', 'proj1ns'):
                raise _Stop
            S.barrier()

            S.dma("gpsimd", cw1, cw1d[:, :, :], writes=["cw1"])
            kcv = kcvT[:, :].rearrange("p (n s) -> p n s", s=16)
            for j in range(2):
                lo, hi = j * 64, (j + 1) * 64
                for c in range(2):
                    Pp = SB_[bank[0] % 3]
                    bank[0] += 1
                    Pq = AC[0]
                    for l in range(32):
                        S.op("tensor", lambda e: e.matmul(Pq[:, 0:1], lhsT=cw1[lo:hi, l, c * 128:(c + 1) * 128], rhs=cpe[lo:hi, l:l + 1],
                                                          start=(l == 0), stop=(l == 31)),
                             reads=["cw1", "cpe"], writes=[Pq.name])
                    col = j * 2 + c
                    S.op("vector", lambda e: e.tensor_copy(out=peb[:, col:col + 1], in_=Pq[:, 0:1]), reads=[Pq.name], writes=["peb%d" % col])
                    for l in range(32):
                        S.op("tensor", lambda e: e.matmul(Pp[:, 0:511], lhsT=cw1[lo:hi, l, c * 128:(c + 1) * 128],
                                                          rhs=kcv[lo:hi, (l // 16):(l // 16) + 511, l % 16],
                                                          start=(l == 0), stop=(l == 31)),
                             reads=["cw1"] + ["kcvT_%d" % t_ for t_ in range(16)], writes=[Pp.name])
                    xg, x2, ug = tb[0], tb[1], tb[2]
                    S.op("scalar", lambda e: e.activation(out=xg[:, 0:511], in_=Pp[:, 0:511], func=AF.Identity, bias=peb[:, col:col + 1], scale=1.0),
                         reads=[Pp.name, "peb%d" % col], writes=[xg.name])
                    S.op("vector", lambda e: e.tensor_tensor(out=x2[:, 0:511], in0=xg[:, 0:511], in1=xg[:, 0:511], op=ALU.mult),
                         reads=[xg.name], writes=[x2.name])
                    S.op("vector", lambda e: e.tensor_scalar(out=x2[:, 0:511], in0=x2[:, 0:511], scalar1=0.044715, scalar2=1.0,
                                                             op0=ALU.mult, op1=ALU.add), reads=[x2.name], writes=[x2.name])
                    S.op("vector", lambda e: e.tensor_tensor(out=ug[:, 0:511], in0=x2[:, 0:511], in1=xg[:, 0:511], op=ALU.mult),
                         reads=[x2.name, xg.name], writes=[ug.name])
                    S.op("scalar", lambda e: e.activation(out=ug[:, 0:511], in_=ug[:, 0:511], func=AF.Exp, scale=-1.5957691216057308),
                         reads=[ug.name], writes=[ug.name])
                    S.op("vector", lambda e: e.tensor_scalar(out=ug[:, 0:511], in0=ug[:, 0:511], scalar1=1.0, scalar2=None, op0=ALU.add),
                         reads=[ug.name], writes=[ug.name])
                    S.op("vector", lambda e: e.reciprocal(out=ug[:, 0:511], in_=ug[:, 0:511]), reads=[ug.name], writes=[ug.name])
                    gl = glb[j * 2 + c]
                    S.op("vector", lambda e: e.memset(gl[:, 511:512], 0.0), writes=[gl.name])
                    S.op("vector", lambda e: e.tensor_tensor(out=gl[:, 0:511], in0=ug[:, 0:511], in1=xg[:, 0:511], op=ALU.mult),
                         reads=[ug.name, xg.name, gl.name], writes=[gl.name])
            if dbg == 'cmp1':
                raise _Stop
            Pp = SB_[bank[0] % 3]
            bank[0] += 1
            for c in range(2):
                S.op("tensor", lambda e: e.matmul(Pp[:, 0:511], lhsT=cw2k[:, c, :], rhs=glb[c][:, 0:511], start=(c == 0), stop=(c == 1)),
                     reads=["cw2k", glb[c].name], writes=[Pp.name])
            S.op("vector", lambda e: e.tensor_copy(out=kcmpT[:, 0:511], in_=Pp[:, 0:511]), reads=[Pp.name, "kcmpT"], writes=["kcmpT"])
            for nt in range(4):
                nn = 128 if nt < 3 else 127
                Pp = SB_[bank[0] % 3]
                bank[0] += 1
                for c in range(2):
                    S.op("tensor", lambda e: e.matmul(Pp[0:nn, 0:64], lhsT=glb[2 + c][:, nt * 128:nt * 128 + nn], rhs=cw2v[:, c, :],
                                                      start=(c == 0), stop=(c == 1)),
                         reads=["cw2v", glb[2 + c].name], writes=[Pp.name])
                S.op("vector", lambda e: e.tensor_copy(out=vcA[0:nn, nt, 0:64], in_=Pp[0:nn, 0:64]), reads=[Pp.name, "vcA"], writes=["vcA"])
            if dbg == 'cmp2':
                raise _Stop
            S.barrier()
            S.dma("sync", ind, indd[:, :], writes=["ind"])

            ctr = {"s": 0, "t": 0, "p": 0, "o": 0, "ac": 0, "adj": 0, "mbt": 0}

            def run_branch(g, hh, tiles, kT, vA, vkey, ncol, accs, acc_cols, basetile, bkey, cbtab, cbkey, accT=None):
                p, half = hh // 2, hh % 2
                firsts = {}

                def emit_s(tl):
                    Sp = SB_[ctr["s"] % 3]
                    ctr["s"] += 1
                    tl["Sp"] = Sp
                    kb = tl["kb"]
                    mms = [(kT[half * 64:(half + 1) * 64, kb * 128:(kb + 1) * 128], QT[p][half * 64:(half + 1) * 64, g * 512:(g + 1) * 512],
                            tl["kkeys"] + ["QT%d_%d" % (p, g)])]
                    if tl.get("extra") is not None:
                        mms.append(tl["extra"])
                    if tl.get("mask") is not None:
                        mms.append((identb[:], tl["mask"], ["identb", "cmask", "dmask", "wmask"]))
                    for i, (l_, r_, keys) in enumerate(mms):
                        S.op("tensor", lambda e: e.matmul(Sp[:], lhsT=l_, rhs=r_, start=(i == 0), stop=(i == len(mms) - 1)),
                             reads=keys, writes=[Sp.name])

                def emit_mid(tl):
                    Sp = tl["Sp"]
                    tt_ = tb[ctr["t"] % 3]
                    ctr["t"] += 1
                    Pt = Pb[ctr["p"] % 7]
                    ctr["p"] += 1
                    tl["P"] = Pt
                    S.op("vector", lambda e: e.scalar_tensor_tensor(out=tt_[:], in0=basetile[:], scalar=slp[:, hh:hh + 1], in1=Sp[:],
                                                                    op0=ALU.mult, op1=ALU.add),
                         reads=[Sp.name, bkey, "slp"], writes=[tt_.name])
                    ci = tl["cbi"]
                    S.op("scalar", lambda e: e.activation(out=Pt[:], in_=tt_[:], func=AF.Exp, bias=cbtab[:, hh, ci:ci + 1], scale=1.0),
                         reads=[tt_.name, cbkey], writes=[Pt.name])

                def emit_av(tl):
                    Pt, kb = tl["P"], tl["kb"]
                    if accT is not None:
                        st_ = accT.name not in firsts
                        firsts[accT.name] = True
                        S.op("tensor", lambda e: e.matmul(accT[0:ncol, :], lhsT=vA[:, kb, :], rhs=Pt[:], start=st_, stop=False,
                                                          skip_group_check=True),
                             reads=[Pt.name] + tl["vkeys"], writes=[accT.name])
                        return
                    for qb in tl["qbs"]:
                        acc, c0 = accs[qb], acc_cols[qb]
                        st_ = acc.name not in firsts
                        firsts[acc.name] = True
                        S.op("tensor", lambda e: e.matmul(acc[:, c0:c0 + ncol], lhsT=Pt[:, qb * 128:(qb + 1) * 128], rhs=vA[:, kb, :],
                                                          start=st_, stop=False, skip_group_check=True),
                             reads=[Pt.name] + tl["vkeys"], writes=[acc.name])

                flash_pipeline(S, tiles, emit_s, emit_mid, emit_av)
                if accT is not None:
                    TP = accs[0]
                    S.op("vector", lambda e: e.tensor_copy(out=otf[0:ncol, :], in_=accT[0:ncol, :]), reads=[accT.name], writes=["otf"])
                    for qb in range(4):
                        S.op("tensor", lambda e: e.transpose(out=TP[:, acc_cols[qb]:acc_cols[qb] + ncol], in_=otf[0:ncol, qb * 128:(qb + 1) * 128],
                                                             identity=identf[0:ncol, 0:ncol]),
                             reads=["otf", "identf"], writes=[TP.name])

            for g in range(16):
                ntmax = (512 * g + 480) // 2048
                for hh in range(4):
                    a0 = AC[(ctr["ac"] % 2) * 2]
                    a1 = AC[(ctr["ac"] % 2) * 2 + 1]
                    ctr["ac"] += 1
                    accs = [a0, a0, a1, a1]
                    cols = [0, 193, 0, 193]
                    tiles = []
                    for nt in range(ntmax + 1):
                        rel2 = g - 4 * nt
                        tiles.append(dict(kb=nt, kkeys=["kcmpT"], vkeys=["vcA"], mask=(cmask[:, rel2, :] if rel2 <= 4 else None),
                                          cbi=rel2, qbs=[0, 1, 2, 3]))
                    run_branch(g, hh, tiles, kcmpT, vcA, "vcA", 193, accs, cols, basec0, "basec0", cbct, "cbct")
                    for qb in range(4):
                        acc, c0 = accs[qb], cols[qb]
                        blk = g * 4 + qb
                        S.op("vector", lambda e: e.tensor_scalar(out=sm[:, 0:1], in0=acc[:, c0 + 64:c0 + 65], scalar1=1e-30, scalar2=None, op0=ALU.max),
                             reads=[acc.name], writes=["sm0"])
                        S.op("vector", lambda e: e.reciprocal(out=sm[:, 0:1], in_=sm[:, 0:1]), reads=["sm0"], writes=["sm0"])
                        S.op("vector", lambda e: e.tensor_tensor(out=sm[:, 1:2], in0=sm[:, 0:1], in1=gates[:, blk, hh * 3:hh * 3 + 1], op=ALU.mult),
                             reads=["sm0", "gates_%d" % blk], writes=["sm1"])
                        S.op("vector", lambda e: e.tensor_scalar(out=acco[:, qb, hh * 64:(hh + 1) * 64], in0=acc[:, c0:c0 + 64],
                                                                 scalar1=sm[:, 1:2], scalar2=None, op0=ALU.mult),
                             reads=[acc.name, "sm1"], writes=["acco%d" % qb])
                        if hh == 0:
                            S.op("vector", lambda e: e.tensor_scalar(out=imp[:, qb, :], in0=acc[:, c0 + 65:c0 + 193], scalar1=sm[:, 0:1],
                                                                     scalar2=None, op0=ALU.mult),
                                 reads=[acc.name, "sm0"], writes=["imp%d" % qb])
                        else:
                            S.op("vector", lambda e: e.scalar_tensor_tensor(out=imp[:, qb, :], in0=acc[:, c0 + 65:c0 + 193], scalar=sm[:, 0:1],
                                                                            in1=imp[:, qb, :], op0=ALU.mult, op1=ALU.add),
                                 reads=[acc.name, "sm0", "imp%d" % qb], writes=["imp%d" % qb])
                if dbg == 'g0c':
                    raise _Stop
                mbt = MBT[ctr["mbt"] % 2]
                ctr["mbt"] += 1
                for qb in range(4):
                    blk = g * 4 + qb
                    at = adjt[ctr["adj"] % 2]
                    ctr["adj"] += 1
                    S.dma("sync", at[:], adjd[blk], writes=[at.name])
                    S.op("vector", lambda e: e.tensor_tensor(out=impa[:], in0=imp[:, qb, :], in1=at[:], op=ALU.add),
                         reads=["imp%d" % qb, at.name], writes=["impa"])
                    S.op("vector", lambda e: e.max(out=m8[:, 0:8], in_=impa[:]), reads=["impa"], writes=["m8a"])
                    S.op("vector", lambda e: e.match_replace(out=impb[:], in_to_replace=m8[:, 0:8], in_values=impa[:], imm_value=-3.0e38),
                         reads=["impa", "m8a"], writes=["impb"])
                    S.op("vector", lambda e: e.max(out=m8[:, 8:16], in_=impb[:]), reads=["impb"], writes=["m8b"])
                    S.op("vector", lambda e: e.tensor_scalar(out=selb[:], in0=impa[:], scalar1=m8[:, 15:16], scalar2=1.0,
                                                             op0=ALU.is_ge, op1=ALU.subtract),
                         reads=["impa", "m8b"], writes=["selb"])
                    S.op("tensor", lambda e: e.transpose(out=PT[:, qb * 128:(qb + 1) * 128], in_=selb[:], identity=identb[:]),
                         reads=["selb", "identb"], writes=["PT"])
                S.op("vector", lambda e: e.tensor_copy(out=mbt[:], in_=PT[:, 0:512]), reads=["PT"], writes=[mbt.name])
                if dbg == 'g0k':
                    raise _Stop
                for hh in range(4):
                    a0 = AC[ctr["ac"] % 4]
                    aT = None
                    ctr["ac"] += 1
                    accs = [a0] * 4
                    cols = [0, 65, 130, 195]
                    tiles = []
                    for kb in range(4 * g + 4):
                        r = kb - 4 * g
                        tiles.append(dict(kb=kb, kkeys=["ksT_%d" % (kb // 4)], vkeys=["vs_%d" % kb, "vsones"],
                                          extra=(ind[:, kb * 128:(kb + 1) * 128], mbt[:], ["ind", mbt.name]),
                                          mask=(dmask[:, r, :] if r >= 0 else None), cbi=4 * g - kb + 3,
                                          qbs=[qb for qb in range(4) if qb >= r]))
                    run_branch(g, hh, tiles, ksT, vsA, "vs", 65, accs, cols, base0, "base0", cbt, "cbt", accT=aT)
                    for qb in range(4):
                        c0 = cols[qb]
                        blk = g * 4 + qb
                        S.op("vector", lambda e: e.reciprocal(out=sm[:, 2:3], in_=a0[:, c0 + 64:c0 + 65]), reads=[a0.name], writes=["sm2"])
                        S.op("vector", lambda e: e.tensor_tensor(out=sm[:, 3:4], in0=sm[:, 2:3], in1=gates[:, blk, hh * 3 + 1:hh * 3 + 2], op=ALU.mult),
                             reads=["sm2", "gates_%d" % blk], writes=["sm3"])
                        S.op("vector", lambda e: e.scalar_tensor_tensor(out=acco[:, qb, hh * 64:(hh + 1) * 64], in0=a0[:, c0:c0 + 64],
                                                                        scalar=sm[:, 3:4], in1=acco[:, qb, hh * 64:(hh + 1) * 64],
                                                                        op0=ALU.mult, op1=ALU.add),
                             reads=[a0.name, "sm3", "acco%d" % qb], writes=["acco%d" % qb])
                if dbg == 'g0s':
                    raise _Stop
                for hh in range(4):
                    a0 = AC[ctr["ac"] % 4]
                    aT = None
                    ctr["ac"] += 1
                    accs = [a0] * 4
                    cols = [0, 65, 130, 195]
                    tiles = []
                    for kb in range(max(0, 4 * g - 4), 4 * g + 4):
                        r = kb - 4 * g
                        tiles.append(dict(kb=kb, kkeys=["kwT_%d" % (kb // 4)], vkeys=["vw_%d" % kb, "vwones"],
                                          mask=wmask[:, r + 4, :], cbi=4 * g - kb + 3,
                                          qbs=[qb for qb in range(4) if qb >= r and qb - r < 5]))
                    run_branch(g, hh, tiles, kwT, vwA, "vw", 65, accs, cols, base0, "base0", cbt, "cbt", accT=aT)
                    for qb in range(4):
                        c0 = cols[qb]
                        blk = g * 4 + qb
                        S.op("vector", lambda e: e.reciprocal(out=sm[:, 4:5], in_=a0[:, c0 + 64:c0 + 65]), reads=[a0.name], writes=["sm4"])
                        S.op("vector", lambda e: e.tensor_tensor(out=sm[:, 5:6], in0=sm[:, 4:5], in1=gates[:, blk, hh * 3 + 2:hh * 3 + 3], op=ALU.mult),
                             reads=["sm4", "gates_%d" % blk], writes=["sm5"])
                        S.op("vector", lambda e: e.scalar_tensor_tensor(out=acco[:, qb, hh * 64:(hh + 1) * 64], in0=a0[:, c0:c0 + 64],
                                                                        scalar=sm[:, 5:6], in1=acco[:, qb, hh * 64:(hh + 1) * 64],
                                                                        op0=ALU.mult, op1=ALU.add),
                             reads=[a0.name, "sm5", "acco%d" % qb], writes=["acco%d" % qb])
                if dbg == 'g0w':
                    raise _Stop
                S.op("vector", lambda e: e.tensor_copy(out=ob[:], in_=acco[:]), reads=["acco%d" % q_ for q_ in range(4)], writes=["ob"])
                ot = oTs[ctr["o"] % 2]
                ctr["o"] += 1
                for qb in range(4):
                    for c in range(2):
                        S.op("tensor", lambda e: e.transpose(out=PT[:, c * 512 + qb * 128:c * 512 + (qb + 1) * 128],
                                                             in_=ob[:, qb, c * 128:(c + 1) * 128], identity=identb[:]),
                             reads=["ob", "identb"], writes=["PT"])
                S.op("vector", lambda e: e.tensor_copy(out=ot[:], in_=PT[:].rearrange("p (c t) -> p c t", c=2)), reads=["PT"], writes=[ot.name])
                S.dma("sync", odst(g), ot[:], reads=[ot.name], is_output=True)
                if dbg == 'g0':
                    raise _Stop
        except _Stop:
            pass
        S.close()


def nsa_consts():
    c = {}
    jj = np.arange(128)[:, None]
    q = np.arange(512)[None, :]
    c["base0"] = (q - jj).astype(np.float32) * -1.0
    c["basec0"] = -(q - 16 * jj - 31).astype(np.float32)
    rel2 = np.arange(5)[None, :, None]
    okc = (512 * rel2 + q[:, None, :].transpose(1, 0, 2) * 0 + q[None, :, :] * 1 - 16 * jj[:, :, None] - 31) >= 0
    c["cmask"] = np.where(okc, 0.0, -BIGNEG).astype(np.float32).astype(ml_dtypes.bfloat16)
    c["dmask"] = diag_mask_tiles(False).astype(ml_dtypes.bfloat16)
    r = (np.arange(8) - 4)[None, :, None]
    dd = q[None, :, :] - jj[:, :, None] - 128 * r
    c["wmask"] = np.where((dd >= 0) & (dd < 512), 0.0, -BIGNEG).astype(np.float32).astype(ml_dtypes.bfloat16)
    s_ = np.arange(128)[:, None]
    key = np.arange(T)[None, :]
    c["ind"] = np.where(key // 64 == s_, BIGNEG, 0.0).astype(np.float32).astype(ml_dtypes.bfloat16)
    n = np.arange(512)
    cs = n * 16
    ss = np.arange(128) * 64
    ov = ((cs[:, None] < ss[None, :] + 64) & (cs[:, None] + 32 > ss[None, :])).astype(np.float32)
    ov[511, :] = 0.0
    c["ovl"] = np.ascontiguousarray(ov.reshape(4, 128, 128).transpose(1, 0, 2)).astype(ml_dtypes.bfloat16)
    tt = np.arange(T)
    cur = tt // 64
    sid = np.arange(128)[None, :]
    forced = (sid == 0) | (sid == cur[:, None]) | (sid == cur[:, None] - 1)
    adj = np.where(forced, NSA_FORCE, 0.0)
    adj = np.where(sid <= cur[:, None], adj, NSA_NEG).astype(np.float32)
    c["adj"] = np.ascontiguousarray(adj.reshape(64, 128, 128))
    return c


_NSA_CONSTS = {}


def nsa_inputs(pfx, w_in, cmp_pe, cmp_w1, cmp_w2):
    if not _NSA_CONSTS:
        _NSA_CONSTS.update(nsa_consts())
    cst = _NSA_CONSTS
    slopes = alibi_slopes_np(16)
    cw1 = np.ascontiguousarray(np.concatenate([cmp_w1[j].reshape(32, 64, 256).transpose(1, 0, 2) for j in range(2)], axis=0))
    cpe = np.ascontiguousarray(np.concatenate([cmp_pe[j].T for j in range(2)], axis=0))
    cw2k = np.ascontiguousarray(np.concatenate([cmp_w2[0], cmp_w2[0]], axis=1).reshape(2, 128, 128).transpose(1, 0, 2))
    cw2v = np.ascontiguousarray(cmp_w2[1].reshape(2, 128, 64).transpose(1, 0, 2))
    rel = np.arange(NREL) - 3
    out = []
    for grp in range(4):
        hs = [4 * grp + r_ for r_ in range(4)]
        qc = np.arange(grp * 256, (grp + 1) * 256)
        kc = 1024 + grp * 64 + np.arange(64)
        vc, ks, vs, kw, vw = kc + 256, kc + 512, kc + 768, kc + 1024, kc + 1280
        gc = 2560 + grp * 12 + np.arange(12)
        fm_cols = np.concatenate([qc, ks, ks, kw, kw, kc, vc])
        tm_cols = np.concatenate([vs, vw, gc])
        sl = np.array([slopes[h] for h in hs])
        out.append({
            pfx + "wfm": arrange_w(w_in[:, fm_cols]),
            pfx + "wtm": arrange_w(w_in[:, tm_cols]),
            pfx + "cw1": cw1, pfx + "cpe": cpe, pfx + "cw2k": cw2k, pfx + "cw2v": cw2v,
            pfx + "ovl": cst["ovl"],
            pfx + "slp": np.ascontiguousarray(np.broadcast_to(sl[None, :], (128, 4))).astype(np.float32),
            pfx + "cb": np.ascontiguousarray(np.broadcast_to((-sl[:, None] * 128.0 * rel[None, :])[None], (128, 4, NREL))).astype(np.float32),
            pfx + "cbc": np.ascontiguousarray(np.broadcast_to((-sl[:, None] * 512.0 * np.arange(16)[None, :])[None], (128, 4, 16))).astype(np.float32),
            pfx + "base0": cst["base0"], pfx + "basec0": cst["basec0"], pfx + "cmask": cst["cmask"], pfx + "dmask": cst["dmask"],
            pfx + "wmask": cst["wmask"], pfx + "ind": cst["ind"], pfx + "adj": cst["adj"], pfx + "ident": _IDENT,
        })
    return out


DEPTH = 4
CC_GROUPS = [[0, 1, 2, 3], [4, 5, 6, 7]]


def build_fused(nphase=99):
    nc = bass.Bass("TRN2", target_bir_lowering=False)
    x_in = nc.dram_tensor("x", [TOK, D], F32, kind="ExternalInput").ap()
    out = nc.dram_tensor("out", [TOK, D], F32, kind="ExternalOutput").ap()
    x_scr = nc.dram_tensor("x_scr", [TOK, D], F32, kind="Internal").ap()
    hT_loc = [nc.dram_tensor("hT_loc%d" % i, [4, D, 512], BF16, kind="Internal").ap() for i in range(DEPTH)]
    hT_all = [nc.dram_tensor("hT_all%d" % i, [4, 4 * D, 512], BF16, kind="Internal").ap() for i in range(DEPTH)]
    o_loc = [nc.dram_tensor("o_loc%d" % i, [4, 256, 2048], BF16, kind="Internal").ap() for i in range(DEPTH)]
    o_all = [nc.dram_tensor("o_all%d" % i, [4, 4 * 256, 2048], BF16, kind="Internal").ap() for i in range(DEPTH)]
    rank = nc.sync.partition_id() % 4
    mod_loc = nc.dram_tensor("mod_loc", [9, D], F32, kind="Internal").ap()
    mod_all = nc.dram_tensor("mod_all", [36, D], F32, kind="Internal").ap()

    def allgather(name, src, dst, nch=4):
        cs = nc.alloc_semaphore(name=name)
        for ch in range(nch):
            nc.gpsimd.collective_compute("AllGather", ALU.bypass, replica_groups=CC_GROUPS,
                                         ins=[src[ch] if nch > 1 else src], outs=[dst[ch] if nch > 1 else dst]).then_inc(cs, 1)
        for eng in (nc.gpsimd, nc.sync, nc.tensor, nc.vector, nc.scalar):
            eng.wait_ge(cs, nch)
        free_sems(nc, [cs])

    ph = [0]

    def go():
        ph[0] += 1
        return ph[0] <= nphase

    emit_mods(nc, "pro_", {"rows_out": mod_loc})
    allgather("ccM", mod_loc, mod_all, nch=1)
    if go():
        emit_k1(nc, "k0_", False, 1, True, {"x": x_in, "xo": x_scr, "hTo": hT_loc[0], "hTo_chunked": True, "modrows": mod_all},
                k1_rows(None, [(0, 0)], 0))
    for i in range(DEPTH):
        if go():
            allgather("ccA%d" % i, hT_loc[i], hT_all[i])
        binds = {"hT": hT_all[i], "oT": o_loc[i]}
        kind = i % 3
        if go():
            if kind == 0:
                emit_nsa(nc, "m%d_" % i, binds, True)
            elif kind == 1:
                emit_sb(nc, "m%d_" % i, binds, True)
            else:
                emit_diff(nc, "m%d_" % i, binds, 0.8 - 0.6 * math.exp(-0.3 * i), True)
        if go():
            allgather("ccB%d" % i, o_loc[i], o_all[i])
        oT_ap = o_all[i][rank]
        if go():
            if i < DEPTH - 1:
                emit_k1(nc, "k%d_" % (i + 1), True, 2, True,
                        {"x": x_scr, "xo": x_scr, "hTo": hT_loc[i + 1], "hTo_chunked": True, "oT": oT_ap, "modrows": mod_all},
                        k1_rows(i, [(i, 1), (i + 1, 0)], i + 1))
            else:
                emit_k1(nc, "k%d_" % (i + 1), True, 1, False, {"x": x_scr, "xo": out, "oT": oT_ap, "modrows": mod_all},
                        k1_rows(i, [(i, 1)], None))
    return nc


_NPHASE = [99]


def kernel(x, c, ada_w, ada_b, norm_g, ffn_w1, ffn_w2, nsa_w_in, nsa_cmp_pe, nsa_cmp_w1, nsa_cmp_w2,
           nsa_w_out, sb_w_in, sb_w_out, diff_w_in, diff_lam, diff_subln_g, diff_w_out):
    f = lambda a: np.asarray(a, dtype=np.float32)
    x, c, ada_w, ada_b, norm_g, ffn_w1, ffn_w2 = map(f, (x, c, ada_w, ada_b, norm_g, ffn_w1, ffn_w2))
    nsa_w_in, nsa_cmp_pe, nsa_cmp_w1, nsa_cmp_w2, nsa_w_out = map(f, (nsa_w_in, nsa_cmp_pe, nsa_cmp_w1, nsa_cmp_w2, nsa_w_out))
    sb_w_in, sb_w_out, diff_w_in, diff_lam, diff_subln_g, diff_w_out = map(
        f, (sb_w_in, sb_w_out, diff_w_in, diff_lam, diff_subln_g, diff_w_out))
    nc = get_nc(("fused", _NPHASE[0]), lambda: build_fused(_NPHASE[0]))
    xt = x.reshape(B * T, D)
    common = {}
    per_b = [dict() for _ in range(B)]
    per_g = [dict() for _ in range(4)]

    def add_k1(pfx, mix, ffns, pre, wout=None):
        common.update(k1_inputs(pfx, ffn_w1, ffn_w2, mix, ffns, wout))

    pr, pb = mods_inputs("pro_", c, ada_w, ada_b, norm_g)
    for b in range(B):
        per_b[b].update(pb[b])
    for g in range(4):
        per_g[g].update(pr[g])

    add_k1("k0_", None, [(0, 0)], 0)
    for i in range(DEPTH):
        kind, j = i % 3, i // 3
        pfx = "m%d_" % i
        if kind == 0:
            pg = nsa_inputs(pfx, nsa_w_in[j], nsa_cmp_pe[j], nsa_cmp_w1[j], nsa_cmp_w2[j])
            wout = nsa_w_out[j]
        elif kind == 1:
            pg = sb_inputs(pfx, sb_w_in[j])
            wout = sb_w_out[j]
        else:
            pg = diff_inputs(pfx, diff_w_in[j], diff_lam[j], diff_subln_g[j])
            wout = diff_w_out[j]
        for g in range(4):
            per_g[g].update(pg[g])
        if i < DEPTH - 1:
            add_k1("k%d_" % (i + 1), i, [(i, 1), (i + 1, 0)], i + 1, wout)
        else:
            add_k1("k%d_" % (i + 1), i, [(i, 1)], None, wout)
    in_maps = []
    for core in range(NCORE):
        b, g = core // 4, core % 4
        m = dict(common)
        m.update(per_b[b])
        m.update(per_g[g])
        m["x"] = np.ascontiguousarray(xt[core * TOK:(core + 1) * TOK])
        in_maps.append(m)
    if _NPHASE[0] < 99:
        npfx = ["k0_"]
        for i in range(DEPTH):
            npfx += [None, "m%d_" % i, None, "k%d_" % (i + 1)]
        keep = set(p for p in npfx[:_NPHASE[0]] if p)
        in_maps = [{k: v for k, v in m.items() if k == "x" or k[:3] in keep or k.startswith("pro_")} for m in in_maps]
    res = run_bass_kernel_spmd(nc, in_maps, core_ids=list(range(NCORE)))
    xo = np.concatenate([res.results[i]["out"] for i in range(NCORE)], axis=0)
    return xo.reshape(B, T, D).astype(np.float32)
```

```python
import contextlib
import math
import numpy as np
import ml_dtypes
import concourse.bass as bass
import concourse.mybir as mybir
from concourse.bass_utils import run_bass_kernel_spmd

F32 = mybir.dt.float32
BF16 = mybir.dt.bfloat16
AF = mybir.ActivationFunctionType
ALU = mybir.AluOpType
AX = mybir.AxisListType

D = 1024
DFF = 2816
NFF = DFF // 128
B = 2
T = 8192
NCORE = 8
TOK = B * T // NCORE
NT = TOK // 128
EPS = 1e-6
NDMA = 24


def free_sems(nc, handles):
    nc.all_engine_barrier()
    nc.clear_and_free_semaphores(handles)
    nc.all_engine_barrier()


class Sched:
    def __init__(self, nc, es, pfx=""):
        self.nc = nc
        self.pfx = pfx
        self.engs = {}
        for name in ["tensor", "vector", "scalar", "gpsimd", "sync"]:
            sem = nc.alloc_semaphore(name=pfx + "sem_" + name)
            self.engs[name] = dict(obj=getattr(nc, name), sem=sem, cnt=0, waited={})
        self.dma_slots = [dict(sem=nc.alloc_semaphore(name=pfx + "dsem%d" % i), cnt=0) for i in range(NDMA)]
        self.dma_rr = 0
        self.last_write = {}
        self.reads = {}
        self.out_tokens = []
        self.psum_keys = set()

    def _wait(self, engname, tok):
        if tok is None:
            return
        semid, sem, val = tok
        if semid == engname and engname == "tensor":
            return
        e = self.engs[engname]
        if e["waited"].get(semid, 0) >= val:
            return
        e["obj"].wait_ge(sem, val)
        e["waited"][semid] = val

    def _norm(self, keys):
        p = self.pfx
        return [k[len(p):] if (p and k.startswith(p)) else k for k in keys]

    def _deps(self, engname, reads, writes):
        reads, writes = self._norm(reads), self._norm(writes)
        for k in reads:
            self._wait(engname, self.last_write.get(k))
            if k in self.psum_keys:
                for t in self.reads.get(k, []):
                    if t[0] != engname:
                        self._wait(engname, t)
        for k in writes:
            self._wait(engname, self.last_write.get(k))
            for t in self.reads.get(k, []):
                self._wait(engname, t)

    def _commit(self, tok, reads, writes):
        reads, writes = self._norm(reads), self._norm(writes)
        for k in writes:
            self.last_write[k] = tok
            self.reads[k] = []
        for k in reads:
            self.reads.setdefault(k, []).append(tok)

    def op(self, engname, fn, reads=(), writes=()):
        self._deps(engname, reads, writes)
        e = self.engs[engname]
        ins = fn(e["obj"])
        e["cnt"] += 1
        ins.then_inc(e["sem"], 1)
        tok = (engname, e["sem"], e["cnt"])
        self._commit(tok, reads, writes)
        return tok

    def dma(self, queue, out, in_, reads=(), writes=(), is_output=False):
        self._deps(queue, reads, writes)
        idx = self.dma_rr
        slot = self.dma_slots[idx]
        self.dma_rr = (self.dma_rr + 1) % NDMA
        if slot["cnt"] > 0:
            self._wait(queue, ("d%d" % idx, slot["sem"], slot["cnt"] * 16))
        ins = self.engs[queue]["obj"].dma_start(out=out, in_=in_)
        slot["cnt"] += 1
        ins.then_inc(slot["sem"], 16)
        tok = ("d%d" % idx, slot["sem"], slot["cnt"] * 16)
        self._commit(tok, reads, writes)
        if is_output:
            self.out_tokens.append(tok)
        return tok

    def barrier(self):
        toks = []
        for idx, slot in enumerate(self.dma_slots):
            if slot["cnt"] > 0:
                toks.append(("d%d" % idx, slot["sem"], slot["cnt"] * 16))
        for name, e in self.engs.items():
            if e["cnt"] > 0:
                toks.append((name, e["sem"], e["cnt"]))
        for name in self.engs:
            for tk in toks:
                if tk[0] != name:
                    self._wait(name, tk)

    def close(self):
        self.barrier()
        handles = [e["sem"] for e in self.engs.values()] + [sl["sem"] for sl in self.dma_slots]
        free_sems(self.nc, handles)

    def finish(self):
        for idx, slot in enumerate(self.dma_slots):
            if slot["cnt"] > 0:
                self._wait("sync", ("d%d" % idx, slot["sem"], slot["cnt"] * 16))
        for name, e in self.engs.items():
            if name != "sync" and e["cnt"] > 0:
                self._wait("sync", (name, e["sem"], e["cnt"]))


def emit_k1(nc, pfx, mix, n_ffn, pre, binds, rows):
    nv = (1 if mix else 0) + 3 * n_ffn + (2 if pre else 0)
    ng = (1 if mix else 0) + 2 * n_ffn + (1 if pre else 0)
    dr = {}

    def din(name, shape, dt=F32, kind="ExternalInput"):
        if name in binds:
            dr[name] = binds[name]
        else:
            dr[name] = nc.dram_tensor(pfx + name, list(shape), dt, kind=kind).ap()

    din("x", [TOK, D])
    din("ident", [128, 128])
    modrows = binds["modrows"]
    if mix:
        din("oT", [D, TOK], BF16)
        din("wout", [D, D])
    for f in range(n_ffn):
        din("w1_%d" % f, [NFF, 128, 8, 256])
        din("w2_%d" % f, [DFF, D])
    din("xo", [TOK, D], F32, "ExternalOutput")
    if pre:
        din("hTo", [D, TOK], BF16, "ExternalOutput")

    es = contextlib.ExitStack()
    with es:
        S = Sched(nc, es, pfx)

        def sb(name, shape, dt):
            return es.enter_context(nc.sbuf_tensor(pfx + name, shape, dt))

        def ps(name, shape, dt):
            return es.enter_context(nc.psum_tensor(pfx + name, shape, dt))

        xs = sb("xs", [128, NT, D], F32)
        hT = sb("hT", [128, 8, 1024], BF16)
        actT = sb("actT", [128, NFF, 1024], BF16)
        w1c = [sb("w1c%d" % i, [128, 8, 256], BF16) for i in range(2)]
        w2c = [sb("w2c%d" % i, [128, 512], BF16) for i in range(4)]
        ysave = sb("ysave", [128, 8, 512], F32)
        ssq = sb("ssq", [128, 16], F32)
        prmsets = [[sb("prm%d_%d" % (j, i), [128, D], F32) for i in range(3)] for j in range(1)]
        scr = [sb("scr%d" % i, [128, D], F32) for i in range(2)]
        hb = [sb("hb%d" % i, [128, D], BF16) for i in range(2)]
        junk = sb("junk", [128, D], BF16)
        sil = [sb("sil%d" % i, [128, 512], F32) for i in range(2)]
        identf = sb("identf", [128, 128], F32)
        identb = sb("identb", [128, 128], BF16)
        st = sb("st", [128, 8], F32)
        epsb = sb("epsb", [128, 1], F32)
        PA = ps("PA", [128, 1024], F32)
        PB = ps("PB", [128, 1024], F32)
        PC = ps("PC", [128, 1024], F32)
        PT = ps("PT", [128, 1024], F32)
        PTb = PT[:].bitcast(BF16)
        S.psum_keys.update(["PA0", "PA1", "PB0", "PB1", "PC0", "PC1", "PT0", "PT1"])

        xin = dr["x"].rearrange("(n p) d -> p n d", p=128)
        for q4 in range(4):
            S.dma("sync", xs[:, q4 * 4:(q4 + 1) * 4, :], xin[:, q4 * 4:(q4 + 1) * 4, :],
                  writes=["xs%d" % n for n in range(q4 * 4, q4 * 4 + 4)])
        S.dma("sync", identf[:], dr["ident"][:, :], writes=["identf"])
        S.op("vector", lambda e: e.tensor_copy(out=identb[:], in_=identf[:]), reads=["identf"], writes=["identb"])
        S.op("vector", lambda e: e.memset(epsb[:], EPS), writes=["epsb"])

        def load_row(row, dst):
            S.dma("sync", dst[:], modrows[row:row + 1, :].partition_broadcast(128), writes=[dst.name])

        pset = [0]

        def next_prm():
            pset[0] += 1
            return prmsets[0]

        stc = [0]

        def rstd_of(src_ap, src_keys, col):
            S.op("scalar", lambda e: e.activation(out=junk[:], in_=src_ap, func=AF.Square, accum_out=st[:, col:col + 1]),
                 reads=src_keys, writes=["junk", "st%d" % col])
            S.op("scalar", lambda e: e.activation(out=st[:, col:col + 1], in_=st[:, col:col + 1], func=AF.Sqrt,
                                                  bias=epsb[:], scale=1.0 / D),
                 reads=["st%d" % col, "epsb"], writes=["st%d" % col])
            S.op("vector", lambda e: e.reciprocal(out=st[:, col:col + 1], in_=st[:, col:col + 1]),
                 reads=["st%d" % col], writes=["st%d" % col])

        def prenorm_tile(n, A, Bv, i):
            col = stc[0] % 4
            stc[0] += 1
            rstd_of(xs[:, n, :], ["xs%d" % n], col)
            S.op("vector", lambda e: e.scalar_tensor_tensor(out=scr[1][:], in0=xs[:, n, :], scalar=st[:, col:col + 1], in1=A[:],
                                                            op0=ALU.mult, op1=ALU.mult),
                 reads=["xs%d" % n, "st%d" % col, A.name], writes=["scr1"])
            S.op("gpsimd", lambda e: e.tensor_tensor(out=hb[i][:], in0=scr[1][:], in1=Bv[:], op=ALU.add),
                 reads=["scr1", Bv.name], writes=["hb%d" % i])

        def transpose_tile(i, half, dst_fn, dst_keys):
            pk = "PT%d" % half
            for k in range(8):
                S.op("tensor", lambda e, k=k: e.transpose(out=PTb[:, half * 1024 + k * 128: half * 1024 + (k + 1) * 128],
                                                          in_=hb[i][:, k * 128:(k + 1) * 128], identity=identb[:]),
                     reads=["hb%d" % i, "identb"], writes=[pk])
            S.op("scalar", lambda e: e.activation(out=dst_fn(), in_=PTb[:, half * 1024:(half + 1) * 1024].rearrange("p (k t) -> p k t", k=8),
                                                  func=AF.Copy),
                 reads=[pk], writes=dst_keys)

        def epilogue(Y, n, G):
            col = 4 + stc[0] % 4
            stc[0] += 1
            rstd_of(Y[:], [Y.name + "0", Y.name + "1"], col)
            S.op("vector", lambda e: e.scalar_tensor_tensor(out=scr[0][:], in0=Y[:], scalar=st[:, col:col + 1], in1=G[:],
                                                            op0=ALU.mult, op1=ALU.mult),
                 reads=[Y.name + "0", Y.name + "1", "st%d" % col, G.name], writes=["scr0"])
            S.op("gpsimd", lambda e: e.tensor_tensor(out=xs[:, n, :], in0=xs[:, n, :], in1=scr[0][:], op=ALU.add),
                 reads=["scr0", "xs%d" % n], writes=["xs%d" % n])

        vi = 0
        gi = 0
        Yb = [PA, PB]
        if mix:
            prm = next_prm()
            load_row(rows["mix"], prm[2])
            wo = actT[:, 0:8, :]
            S.dma("gpsimd", wo, dr["wout"].rearrange("(k p) n -> p k n", p=128), writes=["actT"])
            for half in range(2):
                oTs = actT[:, 8:16, :]
                S.dma("sync", oTs, dr["oT"][:, half * 1024:(half + 1) * 1024].rearrange("(k p) t -> p k t", p=128),
                      writes=["actT_o"])
                for tt in range(8):
                    n = half * 8 + tt
                    Y = Yb[n % 2]
                    for h2 in range(2):
                        for k in range(8):
                            S.op("tensor", lambda e, k=k, h2=h2, tt=tt, Y=Y: e.matmul(
                                Y[:, h2 * 512:(h2 + 1) * 512], lhsT=actT[:, 8 + k, tt * 128:(tt + 1) * 128],
                                rhs=actT[:, k, h2 * 512:(h2 + 1) * 512], start=(k == 0), stop=(k == 7)),
                                reads=["actT", "actT_o"], writes=[Y.name + str(h2)])
                    epilogue(Y, n, prm[2])

        wctr = [0, 0]
        for f in range(n_ffn):
            prm = next_prm()
            load_row(rows["ffn"][f][0], prm[0])
            load_row(rows["ffn"][f][1], prm[1])
            load_row(rows["ffn"][f][2], prm[2])
            w1d = dr["w1_%d" % f]
            w2d = dr["w2_%d" % f]
            for grp in range(2):
                for tt in range(8):
                    n = grp * 8 + tt
                    i = n % 2
                    prenorm_tile(n, prm[0], prm[1], i)
                    transpose_tile(i, n % 2, lambda tt=tt: hT[:, :, tt * 128:(tt + 1) * 128], ["hT"])
                for j in range(NFF):
                    wi = wctr[0] % 2
                    wctr[0] += 1
                    S.dma("gpsimd", w1c[wi][:], w1d[j], writes=["w1c%da" % wi, "w1c%db" % wi])
                    for th in range(2):
                        Pg = PA if th == 0 else PB
                        for part, (c0, key) in enumerate([(0, "a"), (128, "b")]):
                            for k in range(8):
                                S.op("tensor", lambda e, k=k, th=th, c0=c0, part=part, Pg=Pg, wi=wi: e.matmul(
                                    Pg[:, part * 512:(part + 1) * 512], lhsT=w1c[wi][:, k, c0:c0 + 128],
                                    rhs=hT[:, k, th * 512:(th + 1) * 512], start=(k == 0), stop=(k == 7)),
                                    reads=["hT", "w1c%d%s" % (wi, key)], writes=[Pg.name + str(part)])
                        si = th
                        S.op("scalar", lambda e, Pg=Pg, si=si: e.activation(out=sil[si][:], in_=Pg[:, 0:512], func=AF.Silu),
                             reads=[Pg.name + "0"], writes=["sil%d" % si])
                        S.op("vector", lambda e, Pg=Pg, si=si, j=j, th=th: e.tensor_tensor(
                            out=actT[:, j, th * 512:(th + 1) * 512], in0=Pg[:, 512:1024], in1=sil[si][:], op=ALU.mult),
                            reads=[Pg.name + "1", "sil%d" % si], writes=["actT%d" % j])
                banks = [(PA, 0), (PA, 1), (PB, 0), (PB, 1), (PC, 0), (PC, 1), (PT, 0), (PT, 1)]
                Gt = prm[2]
                for h2 in range(2):
                    for j in range(NFF):
                        wi = wctr[1] % 4
                        wctr[1] += 1
                        S.dma("gpsimd", w2c[wi][:], w2d[j * 128:(j + 1) * 128, h2 * 512:(h2 + 1) * 512], writes=["w2c%d" % wi])
                        for tt in range(8):
                            Yt, hb_ = banks[tt]
                            S.op("tensor", lambda e: e.matmul(Yt[:, hb_ * 512:(hb_ + 1) * 512], lhsT=actT[:, j, tt * 128:(tt + 1) * 128],
                                                              rhs=w2c[wi][:], start=(j == 0), stop=(j == NFF - 1)),
                                 reads=["actT%d" % j, "w2c%d" % wi], writes=[Yt.name + str(hb_)])
                    for tt in range(8):
                        Yt, hb_ = banks[tt]
                        bk = Yt.name + str(hb_)
                        yv = Yt[:, hb_ * 512:(hb_ + 1) * 512]
                        n = grp * 8 + tt
                        col = h2 * 8 + tt
                        S.op("scalar", lambda e: e.activation(out=junk[:, 0:512], in_=yv, func=AF.Square, accum_out=ssq[:, col:col + 1]),
                             reads=[bk], writes=["junk", "ssq%d" % col])
                        if h2 == 0:
                            S.op("vector", lambda e: e.tensor_copy(out=ysave[:, tt, :], in_=yv), reads=[bk], writes=["ysave%d" % tt])
                        else:
                            S.op("vector", lambda e: e.tensor_tensor(out=ssq[:, col:col + 1], in0=ssq[:, col:col + 1], in1=ssq[:, tt:tt + 1], op=ALU.add),
                                 reads=["ssq%d" % col, "ssq%d" % tt], writes=["ssq%d" % col])
                            S.op("scalar", lambda e: e.activation(out=ssq[:, col:col + 1], in_=ssq[:, col:col + 1], func=AF.Sqrt,
                                                                  bias=epsb[:], scale=1.0 / D),
                                 reads=["ssq%d" % col, "epsb"], writes=["ssq%d" % col])
                            S.op("vector", lambda e: e.reciprocal(out=ssq[:, col:col + 1], in_=ssq[:, col:col + 1]),
                                 reads=["ssq%d" % col], writes=["ssq%d" % col])
                            S.op("vector", lambda e: e.scalar_tensor_tensor(out=scr[0][:, 0:512], in0=ysave[:, tt, :], scalar=ssq[:, col:col + 1],
                                                                            in1=Gt[:, 0:512], op0=ALU.mult, op1=ALU.mult),
                                 reads=["ysave%d" % tt, "ssq%d" % col, Gt.name], writes=["scr0"])
                            S.op("vector", lambda e: e.scalar_tensor_tensor(out=scr[0][:, 512:1024], in0=yv, scalar=ssq[:, col:col + 1],
                                                                            in1=Gt[:, 512:1024], op0=ALU.mult, op1=ALU.mult),
                                 reads=[bk, "ssq%d" % col, Gt.name, "scr0"], writes=["scr0"])
                            S.op("gpsimd", lambda e: e.tensor_tensor(out=xs[:, n, :], in0=xs[:, n, :], in1=scr[0][:], op=ALU.add),
                                 reads=["scr0", "xs%d" % n], writes=["xs%d" % n])

        if pre:
            prm = next_prm()
            load_row(rows["pre"][0], prm[0])
            load_row(rows["pre"][1], prm[1])
            if binds.get("hTo_chunked"):
                def hTo_dst(n):
                    return dr["hTo"][n // 4, :, (n % 4) * 128:(n % 4 + 1) * 128].rearrange("(k p) t -> p k t", p=128)
            else:
                def hTo_dst(n):
                    return dr["hTo"].rearrange("(k p) t -> p k t", p=128)[:, :, n * 128:(n + 1) * 128]
            for n in range(NT):
                i = n % 2
                prenorm_tile(n, prm[0], prm[1], i)
                transpose_tile(i, n % 2, lambda n=n: hT[:, :, (n % 8) * 128:(n % 8 + 1) * 128], ["hTo%d" % (n % 8)])
                S.dma("sync", hTo_dst(n), hT[:, :, (n % 8) * 128:(n % 8 + 1) * 128],
                      reads=["hTo%d" % (n % 8)], writes=["hTc%d_%d" % (n // 4, n % 4)], is_output=True)
                if binds.get("cc") is not None and n % 4 == 3:
                    binds["cc"].ready(S, n // 4, ["hTc%d_%d" % (n // 4, q_) for q_ in range(4)])
        xout = dr["xo"].rearrange("(n p) d -> p n d", p=128)
        for q4 in range(4):
            S.dma("sync", xout[:, q4 * 4:(q4 + 1) * 4, :], xs[:, q4 * 4:(q4 + 1) * 4, :],
                  reads=["xs%d" % n for n in range(q4 * 4, q4 * 4 + 4)], is_output=True)
        S.close()


def arrange_w1(w1):
    g = w1[:, :DFF].reshape(8, 128, NFF, 128)
    u = w1[:, DFF:].reshape(8, 128, NFF, 128)
    cat = np.concatenate([g, u], axis=3)
    return np.ascontiguousarray(cat.transpose(2, 1, 0, 3))


def arrange_adaw(cols):
    return np.ascontiguousarray(cols.reshape(8, 128, 8, 128).transpose(2, 1, 0, 3))


_IDENT = np.eye(128, dtype=np.float32)
_NC_CACHE = {}


def get_nc(key, builder):
    if key not in _NC_CACHE:
        _NC_CACHE[key] = builder()
    return _NC_CACHE[key]


def k1_inputs(pfx, ffn_w1, ffn_w2, mix, ffns, wout=None):
    common = {pfx + "ident": _IDENT}
    for f, (l, w) in enumerate(ffns):
        common[pfx + "w1_%d" % f] = arrange_w1(ffn_w1[l][w])
        common[pfx + "w2_%d" % f] = np.ascontiguousarray(ffn_w2[l][w])
    if mix is not None:
        common[pfx + "wout"] = np.ascontiguousarray(wout)
    return common


def k1_rows(mix, ffns, pre):
    rows = {"ffn": []}
    if mix is not None:
        rows["mix"] = mix * 9 + 3 + 2
    for (l, w) in ffns:
        base = l * 9 + (0 if w == 0 else 2) * 3
        rows["ffn"].append((base, base + 1, base + 2))
    if pre is not None:
        rows["pre"] = (pre * 9 + 3, pre * 9 + 4)
    return rows


def mods_inputs(pfx, c, ada_w, ada_b, norm_g):
    per_r = []
    for r in range(4):
        adaw = np.ascontiguousarray(ada_w[r].reshape(8, 128, 9, D).transpose(2, 1, 0, 3))
        per_r.append({pfx + "adaw": adaw, pfx + "adab": np.ascontiguousarray(ada_b[r].reshape(1, 9, D)),
                      pfx + "g": np.ascontiguousarray(norm_g[r].reshape(1, 6, D))})
    per_b = [{pfx + "c": np.ascontiguousarray(c[b].reshape(8, 128).T)} for b in range(B)]
    return per_r, per_b


BIGNEG = 240000.0


class Ctx:
    def __init__(self, nc, pfx, binds):
        self.nc = nc
        self.pfx = pfx
        self.binds = binds
        self.es = contextlib.ExitStack()
        self.dr = {}
        self.psum_names = []

    def din(self, name, shape, dt=F32):
        if name in self.binds:
            self.dr[name] = self.binds[name]
        else:
            self.dr[name] = self.nc.dram_tensor(self.pfx + name, list(shape), dt, kind="ExternalInput").ap()
        return self.dr[name]

    def dout(self, name, shape, dt=F32):
        if name in self.binds:
            self.dr[name] = self.binds[name]
        else:
            self.dr[name] = self.nc.dram_tensor(self.pfx + name, list(shape), dt, kind="ExternalOutput").ap()
        return self.dr[name]

    def sb(self, name, shape, dt):
        return self.es.enter_context(self.nc.sbuf_tensor(self.pfx + "s_" + name, list(shape), dt))

    def ps(self, name, shape, dt):
        self.psum_names.append(name)
        return self.es.enter_context(self.nc.psum_tensor(self.pfx + name, list(shape), dt))


def o_dest(oTd, gathered, p=128):
    if gathered:
        def f(g):
            return oTd[g // 4, :, (g % 4) * 512:(g % 4 + 1) * 512].rearrange("(c p) t -> p c t", p=p)
    else:
        def f(g):
            return oTd[:, g * 512:(g + 1) * 512].rearrange("(c p) t -> p c t", p=p)
    return f


def hT_source(hTd, gathered):
    if gathered:
        def f(tg):
            r, tc = tg // 4, tg % 4
            return hTd[tc, r * D:(r + 1) * D, :].rearrange("(k p) t -> p k t", p=128)
    else:
        def f(tg):
            return hTd[:, tg * 512:(tg + 1) * 512].rearrange("(k p) t -> p k t", p=128)
    return f


def flash_pipeline(S, tiles, emit_s, emit_mid, emit_av, lookahead=4):
    pend = []
    for tl in tiles:
        emit_s(tl)
        emit_mid(tl)
        pend.append(tl)
        if len(pend) > lookahead:
            emit_av(pend.pop(0))
    for tl in pend:
        emit_av(tl)


def emit_mods(nc, pfx, binds):
    C = Ctx(nc, pfx, binds)
    es = C.es
    adawd = C.din("adaw", [9, 128, 8, D])
    adabd = C.din("adab", [1, 9, D])
    gd = C.din("g", [1, 6, D])
    cd = C.din("c", [128, 8])
    outd = binds["rows_out"]
    with es:
        S = Sched(nc, es, pfx)
        adas = [C.sb("adas%d" % i, [128, 8, D], F32) for i in range(2)]
        adab = C.sb("adab", [1, 9, D], F32)
        gl = C.sb("gl", [1, 6, D], F32)
        cst = C.sb("cst", [128, 8], F32)
        cond = C.sb("cond", [128, 8], F32)
        mod = C.sb("mod", [1, 9, D], F32)
        R = C.sb("R", [1, 9, D], F32)
        tmp = C.sb("tmp", [1, D], F32)
        PP = [C.ps("PP%d" % i, [128, 512], F32) for i in range(4)]
        S.psum_keys.update(C.psum_names)
        S.dma("sync", cst[:], cd, writes=["cst"])
        S.dma("sync", adab[:], adabd, writes=["adab"])
        S.dma("sync", gl[:], gd, writes=["gl"])
        S.op("scalar", lambda e: e.activation(out=cond[:], in_=cst[:], func=AF.Silu), reads=["cst"], writes=["cond"])
        pc = [0]
        for v in range(9):
            ad = adas[v % 2]
            S.dma("sync" if v % 2 == 0 else "scalar", ad[:], adawd[v], writes=[ad.name])
            for h2 in range(2):
                Pp = PP[pc[0] % 4]
                pc[0] += 1
                for k in range(8):
                    S.op("tensor", lambda e: e.matmul(Pp[0:1, :], lhsT=cond[:, k:k + 1], rhs=ad[:, k, h2 * 512:(h2 + 1) * 512],
                                                      start=(k == 0), stop=(k == 7)),
                         reads=["cond", ad.name], writes=[Pp.name])
                S.op("vector", lambda e: e.tensor_tensor(out=mod[0:1, v, h2 * 512:(h2 + 1) * 512], in0=Pp[0:1, :],
                                                         in1=adab[0:1, v, h2 * 512:(h2 + 1) * 512], op=ALU.add),
                     reads=[Pp.name, "adab"], writes=["mod%d" % v])
        for s_ in range(3):
            res_w = 1.0 if s_ == 1 else 0.5
            sh, sc, gt = 3 * s_, 3 * s_ + 1, 3 * s_ + 2
            S.op("vector", lambda e: e.tensor_scalar(out=tmp[:], in0=mod[0:1, sc, :], scalar1=1.0, scalar2=None, op0=ALU.add),
                 reads=["mod%d" % sc], writes=["tmp"])
            S.op("vector", lambda e: e.tensor_tensor(out=R[0:1, 3 * s_ + 0, :], in0=tmp[:], in1=gl[0:1, 2 * s_, :], op=ALU.mult),
                 reads=["tmp", "gl"], writes=["R"])
            S.op("vector", lambda e: e.tensor_copy(out=R[0:1, 3 * s_ + 1, :], in_=mod[0:1, sh, :]), reads=["mod%d" % sh, "R"], writes=["R"])
            S.op("vector", lambda e: e.tensor_scalar(out=tmp[:], in0=mod[0:1, gt, :], scalar1=float(res_w), scalar2=None, op0=ALU.mult),
                 reads=["mod%d" % gt, "R"], writes=["tmp"])
            S.op("vector", lambda e: e.tensor_tensor(out=R[0:1, 3 * s_ + 2, :], in0=tmp[:], in1=gl[0:1, 2 * s_ + 1, :], op=ALU.mult),
                 reads=["tmp", "gl", "R"], writes=["R"])
        S.dma("sync", outd.rearrange("(o r) d -> o r d", o=1), R[:], reads=["R"], is_output=True)
        S.close()


NREL = 67


def emit_diff(nc, pfx, binds, lam_init, gathered):
    C = Ctx(nc, pfx, binds)
    nc, es = C.nc, C.es
    hTd = C.din("hT", [D, T], BF16)
    hsrc = hT_source(hTd, gathered)
    odst = None
    wqd = C.din("wq", [128, 8, 256])
    wkd = C.din("wk", [128, 8, 256])
    wvd = C.din("wv", [128, 8, 256])
    based = C.din("base", [128, 2, 512])
    cbd = C.din("cb", [128, 2, NREL])
    dmaskd = C.din("dmask", [128, 4, 512], BF16)
    lamd = C.din("lam", [128, 256])
    subgd = C.din("subg", [128, 128])
    identd = C.din("ident", [128, 128])
    oTd = C.dout("oT", [256, T], BF16)
    odst = o_dest(oTd, gathered)
    with es:
        S = Sched(nc, es, pfx)
        QT = [C.sb("QT%d" % h, [128, T], BF16) for h in range(2)]
        KT = [C.sb("KT%d" % h, [128, T], BF16) for h in range(2)]
        Vaug = C.sb("Vaug", [128, 64, 2, 129], BF16)
        wq = C.sb("wq", [128, 8, 256], BF16)
        wk = C.sb("wk", [128, 8, 256], BF16)
        wv = C.sb("wv", [128, 8, 256], BF16)
        hTs = [C.sb("hTs%d" % i, [128, 8, 512], BF16) for i in range(2)]
        base = C.sb("base", [128, 2, 512], F32)
        cbt = C.sb("cbt", [128, 2, NREL], F32)
        dmask = C.sb("dmask", [128, 4, 512], BF16)
        tb = [C.sb("tb%d" % i, [128, 512], F32) for i in range(2)]
        Pb = [C.sb("Pb%d" % i, [128, 512], BF16) for i in range(7)]
        lam = C.sb("lam", [128, 256], F32)
        subg = C.sb("subg", [128, 128], F32)
        identf = C.sb("identf", [128, 128], F32)
        identb = C.sb("identb", [128, 128], BF16)
        sm = C.sb("sm", [128, 16], F32)
        lamneg = C.sb("lamneg", [128, 1], F32)
        epsb = C.sb("epsb", [128, 1], F32)
        o0s = C.sb("o0s", [128, 128], F32)
        od = C.sb("od", [128, 128], F32)
        junk = C.sb("junk", [128, 256], F32)
        ob = C.sb("ob", [128, 4, 256], BF16)
        oTs = [C.sb("oTs%d" % i, [128, 2, 512], BF16) for i in range(2)]
        SB_ = [C.ps("Sb%d" % i, [128, 512], F32) for i in range(3)]
        OB = [[C.ps("O%d%d" % (m, p), [128, 512], F32) for p in range(2)] for m in range(2)]
        PT = C.ps("PT", [128, 1024], BF16)
        S.psum_keys.update(C.psum_names)

        S.dma("sync", identf[:], identd[:, :], writes=["identf"])
        S.op("vector", lambda e: e.tensor_copy(out=identb[:], in_=identf[:]), reads=["identf"], writes=["identb"])
        S.dma("gpsimd", wq[:], wqd[:, :, :], writes=["wq"])
        S.dma("gpsimd", wk[:], wkd[:, :, :], writes=["wk"])
        S.dma("gpsimd", wv[:], wvd[:, :, :], writes=["wv"])
        S.dma("sync", base[:], based[:, :, :], writes=["base"])
        S.dma("sync", cbt[:], cbd[:, :, :], writes=["cbt"])
        S.dma("sync", dmask[:], dmaskd[:, :, :], writes=["dmask"])
        S.dma("sync", lam[:], lamd[:, :], writes=["lam"])
        S.dma("sync", subg[:], subgd[:, :], writes=["subg"])
        S.op("vector", lambda e: e.memset(epsb[:], EPS), writes=["epsb"])
        S.op("vector", lambda e: e.memset(Vaug[:, :, :, 128:129], 1.0), writes=["Vones"])
        S.op("vector", lambda e: e.scalar_tensor_tensor(out=junk[:, 0:64], in0=lam[:, 0:64], scalar=1.0, in1=lam[:, 64:128],
                                                        op0=ALU.mult, op1=ALU.mult, accum_out=sm[:, 0:1]),
             reads=["lam"], writes=["junk", "sm0"])
        S.op("vector", lambda e: e.scalar_tensor_tensor(out=junk[:, 0:64], in0=lam[:, 128:192], scalar=1.0, in1=lam[:, 192:256],
                                                        op0=ALU.mult, op1=ALU.mult, accum_out=sm[:, 1:2]),
             reads=["lam", "junk"], writes=["junk", "sm1"])
        S.op("scalar", lambda e: e.activation(out=sm[:, 0:2], in_=sm[:, 0:2], func=AF.Exp), reads=["sm0", "sm1"],
             writes=["sm0", "sm1"])
        S.op("vector", lambda e: e.tensor_tensor(out=sm[:, 2:3], in0=sm[:, 1:2], in1=sm[:, 0:1], op=ALU.subtract),
             reads=["sm0", "sm1"], writes=["sm2"])
        S.op("vector", lambda e: e.tensor_scalar(out=lamneg[:], in0=sm[:, 2:3], scalar1=-float(lam_init), scalar2=None, op0=ALU.add),
             reads=["sm2"], writes=["lamneg"])
        S.op("vector", lambda e: e.tensor_scalar(out=subg[:], in0=subg[:], scalar1=float(1.0 - lam_init), scalar2=None, op0=ALU.mult),
             reads=["subg"], writes=["subg"])

        cp = [0]

        def evac(dst, src, rk, wk_):
            eng = "scalar" if cp[0] % 2 == 0 else "vector"
            cp[0] += 1
            if eng == "scalar":
                S.op("scalar", lambda e: e.activation(out=dst, in_=src, func=AF.Copy), reads=rk, writes=wk_)
            else:
                S.op("vector", lambda e: e.tensor_copy(out=dst, in_=src), reads=rk, writes=wk_)

        bank = [0]
        for tg in range(16):
            hs = hTs[tg % 2]
            hk = "hTs%d" % (tg % 2)
            S.dma("sync", hs[:], hsrc(tg), writes=[hk])
            for (w, wname, dstl, dname) in [(wq, "wq", QT, "QT"), (wk, "wk", KT, "KT")]:
                for h in range(2):
                    Pp = SB_[bank[0] % 3]
                    bank[0] += 1
                    for k in range(8):
                        S.op("tensor", lambda e: e.matmul(Pp[:], lhsT=w[:, k, h * 128:(h + 1) * 128], rhs=hs[:, k, :],
                                                          start=(k == 0), stop=(k == 7)),
                             reads=[wname, hk], writes=[Pp.name])
                    evac(dstl[h][:, tg * 512:(tg + 1) * 512], Pp[:], [Pp.name], ["%s%d_%d" % (dname, h, tg)])
            for tt in range(4):
                Pp = SB_[bank[0] % 3]
                bank[0] += 1
                for k in range(8):
                    S.op("tensor", lambda e: e.matmul(Pp[:, 0:256], lhsT=hs[:, k, tt * 128:(tt + 1) * 128], rhs=wv[:, k, :],
                                                      start=(k == 0), stop=(k == 7)),
                         reads=["wv", hk], writes=[Pp.name])
                blk = tg * 4 + tt
                evac(Vaug[:, blk, :, 0:128], Pp[:, 0:256].rearrange("p (h e) -> p h e", h=2), [Pp.name], ["V_%d" % blk])

        scale = 64 ** -0.5
        ctr = {"s": 0, "t": 0, "p": 0, "o": 0}
        for g in range(16):
            for h in range(2):
                tiles = [dict(kb=kb, m=m) for kb in range(4 * g + 4) for m in range(2)]
                first_av = {}

                def emit_s(tl):
                    kb, m = tl["kb"], tl["m"]
                    Sp = SB_[ctr["s"] % 3]
                    ctr["s"] += 1
                    tl["Sp"] = Sp
                    r = kb - 4 * g
                    S.op("tensor", lambda e: e.matmul(Sp[:], lhsT=KT[h][m * 64:(m + 1) * 64, kb * 128:(kb + 1) * 128],
                                                      rhs=QT[h][m * 64:(m + 1) * 64, g * 512:(g + 1) * 512], start=True, stop=(r < 0)),
                         reads=["KT%d_%d" % (h, kb // 4), "QT%d_%d" % (h, g)], writes=[Sp.name])
                    if r >= 0:
                        S.op("tensor", lambda e: e.matmul(Sp[:], lhsT=identb[:], rhs=dmask[:, r, :], start=False, stop=True),
                             reads=["identb", "dmask"], writes=[Sp.name])

                def emit_mid(tl):
                    kb, m, Sp = tl["kb"], tl["m"], tl["Sp"]
                    tt_ = tb[ctr["t"] % 2]
                    ctr["t"] += 1
                    Pt = Pb[ctr["p"] % 7]
                    ctr["p"] += 1
                    tl["P"] = Pt
                    S.op("vector", lambda e: e.scalar_tensor_tensor(out=tt_[:], in0=Sp[:], scalar=scale, in1=base[:, h, :],
                                                                    op0=ALU.mult, op1=ALU.add),
                         reads=[Sp.name, "base"], writes=[tt_.name])
                    rel = 4 * g - kb + 3
                    S.op("scalar", lambda e: e.activation(out=Pt[:], in_=tt_[:], func=AF.Exp, bias=cbt[:, h, rel:rel + 1], scale=1.0),
                         reads=[tt_.name, "cbt"], writes=[Pt.name])

                def emit_av(tl):
                    kb, m, Pt = tl["kb"], tl["m"], tl["P"]
                    r = kb - 4 * g
                    for qb in range(4):
                        if qb < r:
                            continue
                        p = qb // 2
                        O = OB[m][p]
                        st_ = (m, p) not in first_av
                        first_av[(m, p)] = True
                        c0 = (qb % 2) * 129
                        S.op("tensor", lambda e: e.matmul(O[:, c0:c0 + 129], lhsT=Pt[:, qb * 128:(qb + 1) * 128], rhs=Vaug[:, kb, h, :],
                                                          start=st_, stop=(kb == 4 * g + qb), skip_group_check=True),
                             reads=[Pt.name, "V_%d" % kb, "Vones"], writes=[O.name])

                flash_pipeline(S, tiles, emit_s, emit_mid, emit_av)
                for qb in range(4):
                    p = qb // 2
                    c0 = (qb % 2) * 129
                    O0, O1 = OB[0][p], OB[1][p]
                    S.op("vector", lambda e: e.reciprocal(out=sm[:, 4:5], in_=O0[:, c0 + 128:c0 + 129]), reads=[O0.name], writes=["sm4"])
                    S.op("vector", lambda e: e.reciprocal(out=sm[:, 5:6], in_=O1[:, c0 + 128:c0 + 129]), reads=[O1.name], writes=["sm5"])
                    S.op("vector", lambda e: e.tensor_tensor(out=sm[:, 5:6], in0=sm[:, 5:6], in1=lamneg[:], op=ALU.mult),
                         reads=["sm5", "lamneg"], writes=["sm5"])
                    S.op("vector", lambda e: e.tensor_scalar(out=o0s[:], in0=O0[:, c0:c0 + 128], scalar1=sm[:, 4:5], scalar2=None, op0=ALU.mult),
                         reads=[O0.name, "sm4"], writes=["o0s"])
                    S.op("vector", lambda e: e.scalar_tensor_tensor(out=od[:], in0=O1[:, c0:c0 + 128], scalar=sm[:, 5:6], in1=o0s[:],
                                                                    op0=ALU.mult, op1=ALU.add),
                         reads=[O1.name, "sm5", "o0s"], writes=["od"])
                    S.op("scalar", lambda e: e.activation(out=junk[:, 0:128], in_=od[:], func=AF.Square, accum_out=sm[:, 6:7]),
                         reads=["od"], writes=["junk", "sm6"])
                    S.op("scalar", lambda e: e.activation(out=sm[:, 6:7], in_=sm[:, 6:7], func=AF.Sqrt, bias=epsb[:], scale=1.0 / 128),
                         reads=["sm6", "epsb"], writes=["sm6"])
                    S.op("vector", lambda e: e.reciprocal(out=sm[:, 6:7], in_=sm[:, 6:7]), reads=["sm6"], writes=["sm6"])
                    S.op("vector", lambda e: e.scalar_tensor_tensor(out=ob[:, qb, h * 128:(h + 1) * 128], in0=od[:], scalar=sm[:, 6:7],
                                                                    in1=subg[:], op0=ALU.mult, op1=ALU.mult),
                         reads=["od", "sm6", "subg"], writes=["ob"])
            ot = oTs[ctr["o"] % 2]
            otk = "oTs%d" % (ctr["o"] % 2)
            ctr["o"] += 1
            for qb in range(4):
                for c in range(2):
                    S.op("tensor", lambda e: e.transpose(out=PT[:, c * 512 + qb * 128:c * 512 + (qb + 1) * 128],
                                                         in_=ob[:, qb, c * 128:(c + 1) * 128], identity=identb[:]),
                         reads=["ob", "identb"], writes=["PT"])
            S.op("vector", lambda e: e.tensor_copy(out=ot[:], in_=PT[:].rearrange("p (c t) -> p c t", c=2)), reads=["PT"], writes=[otk])
            S.dma("sync", odst(g), ot[:], reads=[otk], writes=["oc%d_%d" % (g // 4, g % 4)], is_output=True)
            if binds.get("cc") is not None and g % 4 == 3:
                binds["cc"].ready(S, g // 4, ["oc%d_%d" % (g // 4, q_) for q_ in range(4)])
        S.close()


def alibi_slopes_np(n):
    return np.exp2(-8.0 * np.arange(1, n + 1, dtype=np.float64) / n)


def diag_mask_tiles(strict):
    jj = np.arange(128)[:, None, None]
    r = np.arange(4)[None, :, None]
    q = np.arange(512)[None, None, :]
    d = q - jj - 128 * r
    ok = d >= (1 if strict else 0)
    return np.where(ok, 0.0, -BIGNEG).astype(np.float32)


def base_tile(slope):
    jj = np.arange(128)[:, None]
    q = np.arange(512)[None, :]
    return (-slope * (q - jj)).astype(np.float32)


def cb_table(slope):
    rel = np.arange(NREL) - 3
    return np.ascontiguousarray(np.broadcast_to((-slope * 128.0 * rel)[None, :], (128, NREL))).astype(np.float32)


def arrange_w(wcols):
    n = wcols.shape[1]
    return np.ascontiguousarray(wcols.reshape(8, 128, n).transpose(1, 0, 2))


def diff_inputs(pfx, w_in, lam, subln_g):
    slopes = alibi_slopes_np(8)
    dm = diag_mask_tiles(False).astype(ml_dtypes.bfloat16)
    lamb = np.ascontiguousarray(np.broadcast_to(lam.reshape(1, 256), (128, 256)))
    sgb = np.ascontiguousarray(np.broadcast_to(subln_g.reshape(1, 128), (128, 128)))
    out = []
    for hg in range(4):
        hs = [2 * hg, 2 * hg + 1]
        cols = np.concatenate([np.arange(h * 128, (h + 1) * 128) for h in hs])
        out.append({
            pfx + "wq": arrange_w(w_in[:, cols]),
            pfx + "wk": arrange_w(w_in[:, 1024 + cols]),
            pfx + "wv": arrange_w(w_in[:, 2048 + cols]),
            pfx + "base": np.ascontiguousarray(np.stack([base_tile(slopes[h]) for h in hs], axis=1)),
            pfx + "cb": np.ascontiguousarray(np.stack([cb_table(slopes[h]) for h in hs], axis=1)),
            pfx + "dmask": dm, pfx + "lam": lamb, pfx + "subg": sgb, pfx + "ident": _IDENT,
        })
    return out


def emit_sb(nc, pfx, binds, gathered):
    C = Ctx(nc, pfx, binds)
    nc, es = C.nc, C.es
    hTd = C.din("hT", [D, T], BF16)
    hsrc = hT_source(hTd, gathered)
    odst = None
    wqd = C.din("wq", [128, 8, 256])
    wkd = C.din("wk", [128, 8, 256])
    wvd = C.din("wv", [128, 8, 256])
    m01d = C.din("m01", [128, 4, 512], BF16)
    trid = C.din("tri", [128, 2, 128], BF16)
    identd = C.din("ident", [128, 128])
    oTd = C.dout("oT", [256, T], BF16)
    odst64 = o_dest(oTd, gathered, 64)
    with es:
        S = Sched(nc, es, pfx)
        QT = [C.sb("QT%d" % h, [128, T], BF16) for h in range(2)]
        KT = [C.sb("KT%d" % h, [128, T], BF16) for h in range(2)]
        V = C.sb("V", [128, 64, 256], BF16)
        wq = C.sb("wq", [128, 8, 256], BF16)
        wk = C.sb("wk", [128, 8, 256], BF16)
        wv = C.sb("wv", [128, 8, 256], BF16)
        hTs = [C.sb("hTs%d" % i, [128, 8, 512], BF16) for i in range(2)]
        m01 = C.sb("m01", [128, 4, 512], BF16)
        tri = C.sb("tri", [128, 2, 128], BF16)
        eb = [C.sb("eb%d" % i, [128, 512], F32) for i in range(4)]
        spb = [C.sb("spb%d" % i, [128, 512], BF16) for i in range(4)]
        wb = [C.sb("wb%d" % i, [128, 512], F32) for i in range(2)]
        ab = [C.sb("ab%d" % i, [128, 512], BF16) for i in range(3)]
        identf = C.sb("identf", [128, 128], F32)
        identb = C.sb("identb", [128, 128], BF16)
        obT = [C.sb("obT%d" % i, [64, 4, 512], BF16) for i in range(2)]
        ZB = [C.ps("Zb%d" % i, [128, 512], F32) for i in range(3)]
        XB = [C.ps("Xb%d" % i, [128, 512], F32) for i in range(2)]
        OBk = [C.ps("Ob%d" % i, [128, 512], F32) for i in range(2)]
        PT = C.ps("PT", [128, 1024], BF16)
        S.psum_keys.update(C.psum_names)

        S.dma("sync", identf[:], identd[:, :], writes=["identf"])
        S.op("vector", lambda e: e.tensor_copy(out=identb[:], in_=identf[:]), reads=["identf"], writes=["identb"])
        S.dma("gpsimd", wq[:], wqd[:, :, :], writes=["wq"])
        S.dma("gpsimd", wk[:], wkd[:, :, :], writes=["wk"])
        S.dma("gpsimd", wv[:], wvd[:, :, :], writes=["wv"])
        S.dma("sync", m01[:], m01d[:, :, :], writes=["m01"])
        S.dma("sync", tri[:], trid[:, :, :], writes=["tri"])

        cp = [0]

        def evac(dst, src, rk, wk_):
            eng = "scalar" if cp[0] % 2 == 0 else "vector"
            cp[0] += 1
            if eng == "scalar":
                S.op("scalar", lambda e: e.activation(out=dst, in_=src, func=AF.Copy), reads=rk, writes=wk_)
            else:
                S.op("vector", lambda e: e.tensor_copy(out=dst, in_=src), reads=rk, writes=wk_)

        bank = [0]
        for tg in range(16):
            hs = hTs[tg % 2]
            hk = "hTs%d" % (tg % 2)
            S.dma("sync", hs[:], hsrc(tg), writes=[hk])
            for (w, wname, dstl, dname) in [(wq, "wq", QT, "QT"), (wk, "wk", KT, "KT")]:
                for h in range(2):
                    Pp = ZB[bank[0] % 3]
                    bank[0] += 1
                    for k in range(8):
                        S.op("tensor", lambda e: e.matmul(Pp[:], lhsT=w[:, k, h * 128:(h + 1) * 128], rhs=hs[:, k, :],
                                                          start=(k == 0), stop=(k == 7)),
                             reads=[wname, hk], writes=[Pp.name])
                    evac(dstl[h][:, tg * 512:(tg + 1) * 512], Pp[:], [Pp.name], ["%s%d_%d" % (dname, h, tg)])
            for tt in range(4):
                Pp = ZB[bank[0] % 3]
                bank[0] += 1
                for k in range(8):
                    S.op("tensor", lambda e: e.matmul(Pp[:, 0:256], lhsT=hs[:, k, tt * 128:(tt + 1) * 128], rhs=wv[:, k, :],
                                                      start=(k == 0), stop=(k == 7)),
                         reads=["wv", hk], writes=[Pp.name])
                blk = tg * 4 + tt
                evac(V[:, blk, :], Pp[:, 0:256], [Pp.name], ["V_%d" % blk])

        scale = 64 ** -0.5
        ctr = {"z": 0, "e": 0, "w": 0, "a": 0, "o": 0, "chain": 0}
        for g in range(16):
            chains = []
            for hh in range(4):
                ch = ctr["chain"]
                ctr["chain"] += 1
                kbs = list(range(4 * g + 3, -1, -1))
                av0 = [True]
                chains.append([dict(hh=hh, kb=kb, first=(i == 0), last=(i == len(kbs) - 1), X=XB[ch % 2], O=OBk[ch % 2], av0=av0)
                               for i, kb in enumerate(kbs)])
            tiles = []
            for pr in range(2):
                for ta, tb_ in zip(chains[2 * pr], chains[2 * pr + 1]):
                    tiles += [ta, tb_]

            def emit_Z(tl):
                hh, kb = tl["hh"], tl["kb"]
                p, half = hh // 2, hh % 2
                Zp = ZB[ctr["z"] % 3]
                ctr["z"] += 1
                tl["Z"] = Zp
                S.op("tensor", lambda e: e.matmul(Zp[:], lhsT=KT[p][half * 64:(half + 1) * 64, kb * 128:(kb + 1) * 128],
                                                  rhs=QT[p][half * 64:(half + 1) * 64, g * 512:(g + 1) * 512], start=True, stop=True),
                     reads=["KT%d_%d" % (p, kb // 4), "QT%d_%d" % (p, g)], writes=[Zp.name])

            def emit_esp(tl):
                kb, Zp = tl["kb"], tl["Z"]
                i = ctr["e"] % 4
                ctr["e"] += 1
                tl["e"], tl["sp"] = eb[i], spb[i]
                r = kb - 4 * g
                S.op("scalar", lambda e: e.activation(out=eb[i][:], in_=Zp[:], func=AF.Exp, scale=scale), reads=[Zp.name], writes=[eb[i].name])
                S.op("scalar", lambda e: e.activation(out=spb[i][:], in_=eb[i][:], func=AF.Ln, bias=1.0, scale=1.0),
                     reads=[eb[i].name], writes=[spb[i].name])
                if r >= 0:
                    S.op("gpsimd", lambda e: e.tensor_tensor(out=spb[i][:], in0=spb[i][:], in1=m01[:, r, :], op=ALU.mult),
                         reads=[spb[i].name, "m01"], writes=[spb[i].name])
                    S.op("gpsimd", lambda e: e.tensor_tensor(out=eb[i][:], in0=eb[i][:], in1=m01[:, r, :], op=ALU.mult),
                         reads=[eb[i].name, "m01"], writes=[eb[i].name])

            def emit_L(tl):
                X, sp = tl["X"], tl["sp"]
                S.op("tensor", lambda e: e.matmul(X[:], lhsT=tri[:, 0, :], rhs=sp[:], start=tl["first"], stop=False, skip_group_check=True),
                     reads=["tri", sp.name], writes=[X.name])

            def emit_w(tl):
                X = tl["X"]
                wi = wb[ctr["w"] % 2]
                ctr["w"] += 1
                tl["w"] = wi
                S.op("scalar", lambda e: e.activation(out=wi[:], in_=X[:], func=AF.Exp, scale=-1.0), reads=[X.name], writes=[wi.name])

            def emit_U(tl):
                X, sp = tl["X"], tl["sp"]
                S.op("tensor", lambda e: e.matmul(X[:], lhsT=tri[:, 1, :], rhs=sp[:], start=False, stop=tl["last"], skip_group_check=True),
                     reads=["tri", sp.name], writes=[X.name])

            def emit_a(tl):
                ee, wi = tl["e"], tl["w"]
                ai = ab[ctr["a"] % 3]
                ctr["a"] += 1
                tl["a"] = ai
                S.op("vector", lambda e: e.tensor_tensor(out=ai[:], in0=ee[:], in1=wi[:], op=ALU.mult),
                     reads=[ee.name, wi.name], writes=[ai.name])

            obt = obT[g % 2]

            def stage2(tl):
                hh, kb, O, ai = tl["hh"], tl["kb"], tl["O"], tl["a"]
                st_ = tl["av0"][0]
                tl["av0"][0] = False
                S.op("tensor", lambda e: e.matmul(O[0:64, :], lhsT=V[:, kb, hh * 64:(hh + 1) * 64], rhs=ai[:],
                                                  start=st_, stop=(kb == 0), skip_group_check=True),
                     reads=[ai.name, "V_%d" % kb], writes=[O.name])
                if tl["last"]:
                    S.op("vector", lambda e: e.tensor_copy(out=obt[:, hh, :], in_=O[0:64, :]), reads=[O.name], writes=[obt.name])

            n = len(tiles)
            emit_Z(tiles[0])
            for i in range(n + 2):
                if 1 <= i <= n:
                    emit_L(tiles[i - 1])
                if 2 <= i:
                    emit_U(tiles[i - 2])
                if i + 1 < n:
                    emit_Z(tiles[i + 1])
                if 2 <= i:
                    stage2(tiles[i - 2])
                if i < n:
                    emit_esp(tiles[i])
                if 1 <= i <= n:
                    emit_w(tiles[i - 1])
                    emit_a(tiles[i - 1])
            S.dma("sync", odst64(g), obt[:], reads=[obt.name], writes=["oc%d_%d" % (g // 4, g % 4)], is_output=True)
            if binds.get("cc") is not None and g % 4 == 3:
                binds["cc"].ready(S, g // 4, ["oc%d_%d" % (g // 4, q_) for q_ in range(4)])
        S.close()


def sb_inputs(pfx, w_in):
    jj = np.arange(128)[:, None, None]
    r = np.arange(4)[None, :, None]
    q = np.arange(512)[None, None, :]
    m01 = ((q - jj - 128 * r) >= 1).astype(np.float32).astype(ml_dtypes.bfloat16)
    mm = np.arange(128)[:, None]
    j2 = np.arange(128)[None, :]
    tri = np.ascontiguousarray(np.stack([(mm >= j2), (mm < j2)], axis=1).astype(np.float32).astype(ml_dtypes.bfloat16))
    out = []
    for hg in range(4):
        cols = np.arange(hg * 256, (hg + 1) * 256)
        out.append({
            pfx + "wq": arrange_w(w_in[:, cols]),
            pfx + "wk": arrange_w(w_in[:, 1024 + cols]),
            pfx + "wv": arrange_w(w_in[:, 2048 + cols]),
            pfx + "m01": m01, pfx + "tri": tri, pfx + "ident": _IDENT,
        })
    return out


NSA_FORCE = 1e4
NSA_NEG = -1e30


class _Stop(Exception):
    pass


def emit_nsa(nc, pfx, binds, gathered, dbg=None):
    C = Ctx(nc, pfx, binds)
    nc, es = C.nc, C.es
    hTd = C.din("hT", [D, T], BF16)
    hsrc = hT_source(hTd, gathered)
    odst = None
    wfmd = C.din("wfm", [128, 8, 640])
    wtmd = C.din("wtm", [128, 8, 140])
    cw1d = C.din("cw1", [128, 32, 256])
    cped = C.din("cpe", [128, 32])
    cw2kd = C.din("cw2k", [128, 2, 128])
    cw2vd = C.din("cw2v", [128, 2, 64])
    ovld = C.din("ovl", [128, 4, 128], BF16)
    slpd = C.din("slp", [128, 4])
    cbd = C.din("cb", [128, 4, NREL])
    cbcd = C.din("cbc", [128, 4, 16])
    base0d = C.din("base0", [128, 512])
    basec0d = C.din("basec0", [128, 512])
    cmaskd = C.din("cmask", [128, 5, 512], BF16)
    dmaskd = C.din("dmask", [128, 4, 512], BF16)
    wmaskd = C.din("wmask", [128, 8, 512], BF16)
    indd = C.din("ind", [128, T], BF16)
    adjd = C.din("adj", [64, 128, 128])
    identd = C.din("ident", [128, 128])
    oTd = C.dout("oT", [256, T], BF16)
    odst = o_dest(oTd, gathered)
    with es:
        S = Sched(nc, es, pfx)
        try:
            QT = [C.sb("QT%d" % h, [128, T], BF16) for h in range(2)]
            ksT = C.sb("ksT", [128, T], BF16)
            kwT = C.sb("kwT", [128, T], BF16)
            kcvT = C.sb("kcvT", [128, T], BF16)
            vsA = C.sb("vsA", [128, 64, 65], BF16)
            vwA = C.sb("vwA", [128, 64, 65], BF16)
            gates = C.sb("gates", [128, 64, 12], F32)
            PBUF = C.sb("PBUF", [128, 14464], BF16)
            hTs = [PBUF[:, i * 4096:(i + 1) * 4096].rearrange("p (k t) -> p k t", k=8) for i in range(2)]
            wfm = PBUF[:, 8192:8192 + 5120].rearrange("p (k n) -> p k n", k=8)
            wtm = PBUF[:, 13312:13312 + 1120].rearrange("p (k n) -> p k n", k=8)
            cw1 = PBUF[:, 0:8192].rearrange("p (l f) -> p l f", l=32)
            ind = PBUF[:, 0:8192]
            cpe = C.sb("cpe", [128, 32], BF16)
            cw2k = C.sb("cw2k", [128, 2, 128], BF16)
            cw2v = C.sb("cw2v", [128, 2, 64], BF16)
            slp = C.sb("slp", [128, 4], F32)
            cbt = C.sb("cbt", [128, 4, NREL], F32)
            cbct = C.sb("cbct", [128, 4, 16], F32)
            base0 = C.sb("base0", [128, 512], F32)
            basec0 = C.sb("basec0", [128, 512], F32)
            cmask = C.sb("cmask", [128, 5, 512], BF16)
            dmask = C.sb("dmask", [128, 4, 512], BF16)
            wmask = C.sb("wmask", [128, 8, 512], BF16)
            tb = [C.sb("tb%d" % i, [128, 512], F32) for i in range(3)]
            Pb = [C.sb("Pb%d" % i, [128, 512], BF16) for i in range(7)]
            kcmpT = C.sb("kcmpT", [128, 512], BF16)
            vcA = C.sb("vcA", [128, 4, 193], BF16)
            glb = [C.sb("glb%d" % i, [128, 512], BF16) for i in range(4)]
            peb = C.sb("peb", [128, 4], F32)
            imp = C.sb("imp", [128, 4, 128], F32)
            adjt = [C.sb("adjt%d" % i, [128, 128], F32) for i in range(2)]
            impa = C.sb("impa", [128, 128], F32)
            impb = C.sb("impb", [128, 128], F32)
            m8 = C.sb("m8", [128, 16], F32)
            selb = C.sb("selb", [128, 128], BF16)
            MBT = [C.sb("MBT%d" % i, [128, 512], BF16) for i in range(2)]
            acco = C.sb("acco", [128, 4, 256], F32)
            ob = C.sb("ob", [128, 4, 256], BF16)
            oTs = [C.sb("oTs%d" % i, [128, 2, 512], BF16) for i in range(2)]
            sm = C.sb("sm", [128, 8], F32)
            otf = C.sb("otf", [65, 512], F32)
            identf = C.sb("identf", [128, 128], F32)
            identb = C.sb("identb", [128, 128], BF16)
            SB_ = [C.ps("Sb%d" % i, [128, 512], F32) for i in range(3)]
            AC = [C.ps("Ac%d" % i, [128, 512], F32) for i in range(4)]
            PT = C.ps("PT", [128, 1024], BF16)
            S.psum_keys.update(C.psum_names)

            S.dma("sync", identf[:], identd[:, :], writes=["identf"])
            S.op("vector", lambda e: e.tensor_copy(out=identb[:], in_=identf[:]), reads=["identf"], writes=["identb"])
            S.dma("gpsimd", wfm, wfmd[:, :, :], writes=["wfm"])
            S.dma("gpsimd", wtm, wtmd[:, :, :], writes=["wtm"])
            S.dma("gpsimd", cpe[:], cped[:, :], writes=["cpe"])
            S.dma("gpsimd", cw2k[:], cw2kd[:, :, :], writes=["cw2k"])
            S.dma("gpsimd", cw2v[:], cw2vd[:, :, :], writes=["cw2v"])
            for (dst, src, key) in [(slp, slpd, "slp"), (cbt, cbd, "cbt"), (cbct, cbcd, "cbct"), (base0, base0d, "base0"),
                                    (basec0, basec0d, "basec0"), (cmask, cmaskd, "cmask"), (dmask, dmaskd, "dmask"),
                                    (wmask, wmaskd, "wmask")]:
                S.dma("sync", dst[:], src, writes=[key])
            S.op("vector", lambda e: e.memset(vsA[:, :, 64:65], 1.0), writes=["vsones"])
            S.op("vector", lambda e: e.memset(vwA[:, :, 64:65], 1.0), writes=["vwones"])
            S.op("vector", lambda e: e.memset(vcA[:], 0.0), writes=["vcA"])
            S.op("vector", lambda e: e.memset(kcmpT[:], 0.0), writes=["kcmpT"])
            S.op("vector", lambda e: e.memset(vcA[:, :, 64:65], 1.0), reads=["vcA"], writes=["vcA"])
            S.dma("sync", vcA[:, :, 65:193], ovld[:, :, :], reads=["vcA"], writes=["vcA"])

            if dbg == 'const':
                raise _Stop
            cp = [0]

            def evac(dst, src, rk, wk_, scale=None):
                eng = "scalar" if cp[0] % 2 == 0 else "vector"
                cp[0] += 1
                if eng == "scalar" and scale is None:
                    S.op("scalar", lambda e: e.activation(out=dst, in_=src, func=AF.Copy), reads=rk, writes=wk_)
                else:
                    if scale is None:
                        S.op("vector", lambda e: e.tensor_copy(out=dst, in_=src), reads=rk, writes=wk_)
                    else:
                        S.op("vector", lambda e: e.tensor_scalar(out=dst, in0=src, scalar1=float(scale), scalar2=None, op0=ALU.mult),
                             reads=rk, writes=wk_)

            bank = [0]
            fm_dst = [(QT[0], "QT0", 0.125), (QT[1], "QT1", 0.125), (ksT, "ksT", None), (kwT, "kwT", None), (kcvT, "kcvT", None)]
            for tg in range(1 if dbg in ('proj1', 'proj1ns') else 16):
                hs = hTs[tg % 2]
                hk = "hTs%d" % (tg % 2)
                S.dma("sync", hs, hsrc(tg), writes=[hk])
                for fi, (dst, dname, sc) in enumerate(fm_dst):
                    Pp = SB_[bank[0] % 3]
                    bank[0] += 1
                    for k in range(8):
                        S.op("tensor", lambda e: e.matmul(Pp[:], lhsT=wfm[:, k, fi * 128:(fi + 1) * 128], rhs=hs[:, k, :],
                                                          start=(k == 0), stop=(k == 7)),
                             reads=["wfm", hk], writes=[Pp.name])
                    evac(dst[:, tg * 512:(tg + 1) * 512], Pp[:], [Pp.name], ["%s_%d" % (dname, tg)], scale=sc)
                for tt in range(4):
                    Pp = SB_[bank[0] % 3]
                    bank[0] += 1
                    for k in range(8):
                        S.op("tensor", lambda e: e.matmul(Pp[:, 0:140], lhsT=hs[:, k, tt * 128:(tt + 1) * 128], rhs=wtm[:, k, :],
                                                          start=(k == 0), stop=(k == 7)),
                             reads=["wtm", hk], writes=[Pp.name])
                    blk = tg * 4 + tt
                    S.op("vector", lambda e: e.tensor_copy(out=vsA[:, blk, 0:64], in_=Pp[:, 0:64]), reads=[Pp.name], writes=["vs_%d" % blk])
                    S.op("vector", lambda e: e.tensor_copy(out=vwA[:, blk, 0:64], in_=Pp[:, 64:128]), reads=[Pp.name], writes=["vw_%d" % blk])
                    S.op("scalar", lambda e: e.activation(out=gates[:, blk, :], in_=Pp[:, 128:140], func=AF.Exp, scale=-1.0),
                         reads=[Pp.name], writes=["gates_%d" % blk])
                    S.op("vector", lambda e: e.tensor_scalar(out=gates[:, blk, :], in0=gates[:, blk, :], scalar1=1.0, scalar2=None, op0=ALU.add),
                         reads=["gates_%d" % blk], writes=["gates_%d" % blk])
                    S.op("vector", lambda e: e.reciprocal(out=gates[:, blk, :], in_=gates[:, blk, :]),
                         reads=["gates_%d" % blk], writes=["gates_%d" % blk])
            if dbg in ('proj', 'proj1', 'proj1ns'):
                raise _Stop
            S.barrier()

            S.dma("gpsimd", cw1, cw1d[:, :, :], writes=["cw1"])
            kcv = kcvT[:, :].rearrange("p (n s) -> p n s", s=16)
            for j in range(2):
                lo, hi = j * 64, (j + 1) * 64
                for c in range(2):
                    Pp = SB_[bank[0] % 3]
                    bank[0] += 1
                    Pq = AC[0]
                    for l in range(32):
                        S.op("tensor", lambda e: e.matmul(Pq[:, 0:1], lhsT=cw1[lo:hi, l, c * 128:(c + 1) * 128], rhs=cpe[lo:hi, l:l + 1],
                                                          start=(l == 0), stop=(l == 31)),
                             reads=["cw1", "cpe"], writes=[Pq.name])
                    col = j * 2 + c
                    S.op("vector", lambda e: e.tensor_copy(out=peb[:, col:col + 1], in_=Pq[:, 0:1]), reads=[Pq.name], writes=["peb%d" % col])
                    for l in range(32):
                        S.op("tensor", lambda e: e.matmul(Pp[:, 0:511], lhsT=cw1[lo:hi, l, c * 128:(c + 1) * 128],
                                                          rhs=kcv[lo:hi, (l // 16):(l // 16) + 511, l % 16],
                                                          start=(l == 0), stop=(l == 31)),
                             reads=["cw1"] + ["kcvT_%d" % t_ for t_ in range(16)], writes=[Pp.name])
                    xg, x2, ug = tb[0], tb[1], tb[2]
                    S.op("scalar", lambda e: e.activation(out=xg[:, 0:511], in_=Pp[:, 0:511], func=AF.Identity, bias=peb[:, col:col + 1], scale=1.0),
                         reads=[Pp.name, "peb%d" % col], writes=[xg.name])
                    S.op("vector", lambda e: e.tensor_tensor(out=x2[:, 0:511], in0=xg[:, 0:511], in1=xg[:, 0:511], op=ALU.mult),
                         reads=[xg.name], writes=[x2.name])
                    S.op("vector", lambda e: e.tensor_scalar(out=x2[:, 0:511], in0=x2[:, 0:511], scalar1=0.044715, scalar2=1.0,
                                                             op0=ALU.mult, op1=ALU.add), reads=[x2.name], writes=[x2.name])
                    S.op("vector", lambda e: e.tensor_tensor(out=ug[:, 0:511], in0=x2[:, 0:511], in1=xg[:, 0:511], op=ALU.mult),
                         reads=[x2.name, xg.name], writes=[ug.name])
                    S.op("scalar", lambda e: e.activation(out=ug[:, 0:511], in_=ug[:, 0:511], func=AF.Exp, scale=-1.5957691216057308),
                         reads=[ug.name], writes=[ug.name])
                    S.op("vector", lambda e: e.tensor_scalar(out=ug[:, 0:511], in0=ug[:, 0:511], scalar1=1.0, scalar2=None, op0=ALU.add),
                         reads=[ug.name], writes=[ug.name])
                    S.op("vector", lambda e: e.reciprocal(out=ug[:, 0:511], in_=ug[:, 0:511]), reads=[ug.name], writes=[ug.name])
                    gl = glb[j * 2 + c]
                    S.op("vector", lambda e: e.memset(gl[:, 511:512], 0.0), writes=[gl.name])
                    S.op("vector", lambda e: e.tensor_tensor(out=gl[:, 0:511], in0=ug[:, 0:511], in1=xg[:, 0:511], op=ALU.mult),
                         reads=[ug.name, xg.name, gl.name], writes=[gl.name])
            if dbg == 'cmp1':
                raise _Stop
            Pp = SB_[bank[0] % 3]
            bank[0] += 1
            for c in range(2):
                S.op("tensor", lambda e: e.matmul(Pp[:, 0:511], lhsT=cw2k[:, c, :], rhs=glb[c][:, 0:511], start=(c == 0), stop=(c == 1)),
                     reads=["cw2k", glb[c].name], writes=[Pp.name])
            S.op("vector", lambda e: e.tensor_copy(out=kcmpT[:, 0:511], in_=Pp[:, 0:511]), reads=[Pp.name, "kcmpT"], writes=["kcmpT"])
            for nt in range(4):
                nn = 128 if nt < 3 else 127
                Pp = SB_[bank[0] % 3]
                bank[0] += 1
                for c in range(2):
                    S.op("tensor", lambda e: e.matmul(Pp[0:nn, 0:64], lhsT=glb[2 + c][:, nt * 128:nt * 128 + nn], rhs=cw2v[:, c, :],
                                                      start=(c == 0), stop=(c == 1)),
                         reads=["cw2v", glb[2 + c].name], writes=[Pp.name])
                S.op("vector", lambda e: e.tensor_copy(out=vcA[0:nn, nt, 0:64], in_=Pp[0:nn, 0:64]), reads=[Pp.name, "vcA"], writes=["vcA"])
            if dbg == 'cmp2':
                raise _Stop
            S.barrier()
            S.dma("sync", ind, indd[:, :], writes=["ind"])

            ctr = {"s": 0, "t": 0, "p": 0, "o": 0, "ac": 0, "adj": 0, "mbt": 0}

            def run_branch(g, hh, tiles, kT, vA, vkey, ncol, accs, acc_cols, basetile, bkey, cbtab, cbkey, accT=None):
                p, half = hh // 2, hh % 2
                firsts = {}

                def emit_s(tl):
                    Sp = SB_[ctr["s"] % 3]
                    ctr["s"] += 1
                    tl["Sp"] = Sp
                    kb = tl["kb"]
                    mms = [(kT[half * 64:(half + 1) * 64, kb * 128:(kb + 1) * 128], QT[p][half * 64:(half + 1) * 64, g * 512:(g + 1) * 512],
                            tl["kkeys"] + ["QT%d_%d" % (p, g)])]
                    if tl.get("extra") is not None:
                        mms.append(tl["extra"])
                    if tl.get("mask") is not None:
                        mms.append((identb[:], tl["mask"], ["identb", "cmask", "dmask", "wmask"]))
                    for i, (l_, r_, keys) in enumerate(mms):
                        S.op("tensor", lambda e: e.matmul(Sp[:], lhsT=l_, rhs=r_, start=(i == 0), stop=(i == len(mms) - 1)),
                             reads=keys, writes=[Sp.name])

                def emit_mid(tl):
                    Sp = tl["Sp"]
                    tt_ = tb[ctr["t"] % 3]
                    ctr["t"] += 1
                    Pt = Pb[ctr["p"] % 7]
                    ctr["p"] += 1
                    tl["P"] = Pt
                    S.op("vector", lambda e: e.scalar_tensor_tensor(out=tt_[:], in0=basetile[:], scalar=slp[:, hh:hh + 1], in1=Sp[:],
                                                                    op0=ALU.mult, op1=ALU.add),
                         reads=[Sp.name, bkey, "slp"], writes=[tt_.name])
                    ci = tl["cbi"]
                    S.op("scalar", lambda e: e.activation(out=Pt[:], in_=tt_[:], func=AF.Exp, bias=cbtab[:, hh, ci:ci + 1], scale=1.0),
                         reads=[tt_.name, cbkey], writes=[Pt.name])

                def emit_av(tl):
                    Pt, kb = tl["P"], tl["kb"]
                    if accT is not None:
                        st_ = accT.name not in firsts
                        firsts[accT.name] = True
                        S.op("tensor", lambda e: e.matmul(accT[0:ncol, :], lhsT=vA[:, kb, :], rhs=Pt[:], start=st_, stop=False,
                                                          skip_group_check=True),
                             reads=[Pt.name] + tl["vkeys"], writes=[accT.name])
                        return
                    for qb in tl["qbs"]:
                        acc, c0 = accs[qb], acc_cols[qb]
                        st_ = acc.name not in firsts
                        firsts[acc.name] = True
                        S.op("tensor", lambda e: e.matmul(acc[:, c0:c0 + ncol], lhsT=Pt[:, qb * 128:(qb + 1) * 128], rhs=vA[:, kb, :],
                                                          start=st_, stop=False, skip_group_check=True),
                             reads=[Pt.name] + tl["vkeys"], writes=[acc.name])

                flash_pipeline(S, tiles, emit_s, emit_mid, emit_av)
                if accT is not None:
                    TP = accs[0]
                    S.op("vector", lambda e: e.tensor_copy(out=otf[0:ncol, :], in_=accT[0:ncol, :]), reads=[accT.name], writes=["otf"])
                    for qb in range(4):
                        S.op("tensor", lambda e: e.transpose(out=TP[:, acc_cols[qb]:acc_cols[qb] + ncol], in_=otf[0:ncol, qb * 128:(qb + 1) * 128],
                                                             identity=identf[0:ncol, 0:ncol]),
                             reads=["otf", "identf"], writes=[TP.name])

            for g in range(16):
                ntmax = (512 * g + 480) // 2048
                for hh in range(4):
                    a0 = AC[(ctr["ac"] % 2) * 2]
                    a1 = AC[(ctr["ac"] % 2) * 2 + 1]
                    ctr["ac"] += 1
                    accs = [a0, a0, a1, a1]
                    cols = [0, 193, 0, 193]
                    tiles = []
                    for nt in range(ntmax + 1):
                        rel2 = g - 4 * nt
                        tiles.append(dict(kb=nt, kkeys=["kcmpT"], vkeys=["vcA"], mask=(cmask[:, rel2, :] if rel2 <= 4 else None),
                                          cbi=rel2, qbs=[0, 1, 2, 3]))
                    run_branch(g, hh, tiles, kcmpT, vcA, "vcA", 193, accs, cols, basec0, "basec0", cbct, "cbct")
                    for qb in range(4):
                        acc, c0 = accs[qb], cols[qb]
                        blk = g * 4 + qb
                        S.op("vector", lambda e: e.tensor_scalar(out=sm[:, 0:1], in0=acc[:, c0 + 64:c0 + 65], scalar1=1e-30, scalar2=None, op0=ALU.max),
                             reads=[acc.name], writes=["sm0"])
                        S.op("vector", lambda e: e.reciprocal(out=sm[:, 0:1], in_=sm[:, 0:1]), reads=["sm0"], writes=["sm0"])
                        S.op("vector", lambda e: e.tensor_tensor(out=sm[:, 1:2], in0=sm[:, 0:1], in1=gates[:, blk, hh * 3:hh * 3 + 1], op=ALU.mult),
                             reads=["sm0", "gates_%d" % blk], writes=["sm1"])
                        S.op("vector", lambda e: e.tensor_scalar(out=acco[:, qb, hh * 64:(hh + 1) * 64], in0=acc[:, c0:c0 + 64],
                                                                 scalar1=sm[:, 1:2], scalar2=None, op0=ALU.mult),
                             reads=[acc.name, "sm1"], writes=["acco%d" % qb])
                        if hh == 0:
                            S.op("vector", lambda e: e.tensor_scalar(out=imp[:, qb, :], in0=acc[:, c0 + 65:c0 + 193], scalar1=sm[:, 0:1],
                                                                     scalar2=None, op0=ALU.mult),
                                 reads=[acc.name, "sm0"], writes=["imp%d" % qb])
                        else:
                            S.op("vector", lambda e: e.scalar_tensor_tensor(out=imp[:, qb, :], in0=acc[:, c0 + 65:c0 + 193], scalar=sm[:, 0:1],
                                                                            in1=imp[:, qb, :], op0=ALU.mult, op1=ALU.add),
                                 reads=[acc.name, "sm0", "imp%d" % qb], writes=["imp%d" % qb])
                if dbg == 'g0c':
                    raise _Stop
                mbt = MBT[ctr["mbt"] % 2]
                ctr["mbt"] += 1
                for qb in range(4):
                    blk = g * 4 + qb
                    at = adjt[ctr["adj"] % 2]
                    ctr["adj"] += 1
                    S.dma("sync", at[:], adjd[blk], writes=[at.name])
                    S.op("vector", lambda e: e.tensor_tensor(out=impa[:], in0=imp[:, qb, :], in1=at[:], op=ALU.add),
                         reads=["imp%d" % qb, at.name], writes=["impa"])
                    S.op("vector", lambda e: e.max(out=m8[:, 0:8], in_=impa[:]), reads=["impa"], writes=["m8a"])
                    S.op("vector", lambda e: e.match_replace(out=impb[:], in_to_replace=m8[:, 0:8], in_values=impa[:], imm_value=-3.0e38),
                         reads=["impa", "m8a"], writes=["impb"])
                    S.op("vector", lambda e: e.max(out=m8[:, 8:16], in_=impb[:]), reads=["impb"], writes=["m8b"])
                    S.op("vector", lambda e: e.tensor_scalar(out=selb[:], in0=impa[:], scalar1=m8[:, 15:16], scalar2=1.0,
                                                             op0=ALU.is_ge, op1=ALU.subtract),
                         reads=["impa", "m8b"], writes=["selb"])
                    S.op("tensor", lambda e: e.transpose(out=PT[:, qb * 128:(qb + 1) * 128], in_=selb[:], identity=identb[:]),
                         reads=["selb", "identb"], writes=["PT"])
                S.op("vector", lambda e: e.tensor_copy(out=mbt[:], in_=PT[:, 0:512]), reads=["PT"], writes=[mbt.name])
                if dbg == 'g0k':
                    raise _Stop
                for hh in range(4):
                    a0 = AC[ctr["ac"] % 4]
                    aT = None
                    ctr["ac"] += 1
                    accs = [a0] * 4
                    cols = [0, 65, 130, 195]
                    tiles = []
                    for kb in range(4 * g + 4):
                        r = kb - 4 * g
                        tiles.append(dict(kb=kb, kkeys=["ksT_%d" % (kb // 4)], vkeys=["vs_%d" % kb, "vsones"],
                                          extra=(ind[:, kb * 128:(kb + 1) * 128], mbt[:], ["ind", mbt.name]),
                                          mask=(dmask[:, r, :] if r >= 0 else None), cbi=4 * g - kb + 3,
                                          qbs=[qb for qb in range(4) if qb >= r]))
                    run_branch(g, hh, tiles, ksT, vsA, "vs", 65, accs, cols, base0, "base0", cbt, "cbt", accT=aT)
                    for qb in range(4):
                        c0 = cols[qb]
                        blk = g * 4 + qb
                        S.op("vector", lambda e: e.reciprocal(out=sm[:, 2:3], in_=a0[:, c0 + 64:c0 + 65]), reads=[a0.name], writes=["sm2"])
                        S.op("vector", lambda e: e.tensor_tensor(out=sm[:, 3:4], in0=sm[:, 2:3], in1=gates[:, blk, hh * 3 + 1:hh * 3 + 2], op=ALU.mult),
                             reads=["sm2", "gates_%d" % blk], writes=["sm3"])
                        S.op("vector", lambda e: e.scalar_tensor_tensor(out=acco[:, qb, hh * 64:(hh + 1) * 64], in0=a0[:, c0:c0 + 64],
                                                                        scalar=sm[:, 3:4], in1=acco[:, qb, hh * 64:(hh + 1) * 64],
                                                                        op0=ALU.mult, op1=ALU.add),
                             reads=[a0.name, "sm3", "acco%d" % qb], writes=["acco%d" % qb])
                if dbg == 'g0s':
                    raise _Stop
                for hh in range(4):
                    a0 = AC[ctr["ac"] % 4]
                    aT = None
                    ctr["ac"] += 1
                    accs = [a0] * 4
                    cols = [0, 65, 130, 195]
                    tiles = []
                    for kb in range(max(0, 4 * g - 4), 4 * g + 4):
                        r = kb - 4 * g
                        tiles.append(dict(kb=kb, kkeys=["kwT_%d" % (kb // 4)], vkeys=["vw_%d" % kb, "vwones"],
                                          mask=wmask[:, r + 4, :], cbi=4 * g - kb + 3,
                                          qbs=[qb for qb in range(4) if qb >= r and qb - r < 5]))
                    run_branch(g, hh, tiles, kwT, vwA, "vw", 65, accs, cols, base0, "base0", cbt, "cbt", accT=aT)
                    for qb in range(4):
                        c0 = cols[qb]
                        blk = g * 4 + qb
                        S.op("vector", lambda e: e.reciprocal(out=sm[:, 4:5], in_=a0[:, c0 + 64:c0 + 65]), reads=[a0.name], writes=["sm4"])
                        S.op("vector", lambda e: e.tensor_tensor(out=sm[:, 5:6], in0=sm[:, 4:5], in1=gates[:, blk, hh * 3 + 2:hh * 3 + 3], op=ALU.mult),
                             reads=["sm4", "gates_%d" % blk], writes=["sm5"])
                        S.op("vector", lambda e: e.scalar_tensor_tensor(out=acco[:, qb, hh * 64:(hh + 1) * 64], in0=a0[:, c0:c0 + 64],
                                                                        scalar=sm[:, 5:6], in1=acco[:, qb, hh * 64:(hh + 1) * 64],
                                                                        op0=ALU.mult, op1=ALU.add),
                             reads=[a0.name, "sm5", "acco%d" % qb], writes=["acco%d" % qb])
                if dbg == 'g0w':
                    raise _Stop
                S.op("vector", lambda e: e.tensor_copy(out=ob[:], in_=acco[:]), reads=["acco%d" % q_ for q_ in range(4)], writes=["ob"])
                ot = oTs[ctr["o"] % 2]
                ctr["o"] += 1
                for qb in range(4):
                    for c in range(2):
                        S.op("tensor", lambda e: e.transpose(out=PT[:, c * 512 + qb * 128:c * 512 + (qb + 1) * 128],
                                                             in_=ob[:, qb, c * 128:(c + 1) * 128], identity=identb[:]),
                             reads=["ob", "identb"], writes=["PT"])
                S.op("vector", lambda e: e.tensor_copy(out=ot[:], in_=PT[:].rearrange("p (c t) -> p c t", c=2)), reads=["PT"], writes=[ot.name])
                S.dma("sync", odst(g), ot[:], reads=[ot.name], writes=["oc%d_%d" % (g // 4, g % 4)], is_output=True)
                if binds.get("cc") is not None and g % 4 == 3:
                    binds["cc"].ready(S, g // 4, ["oc%d_%d" % (g // 4, q_) for q_ in range(4)])
                if dbg == 'g0':
                    raise _Stop
        except _Stop:
            pass
        S.close()


def nsa_consts():
    c = {}
    jj = np.arange(128)[:, None]
    q = np.arange(512)[None, :]
    c["base0"] = (q - jj).astype(np.float32) * -1.0
    c["basec0"] = -(q - 16 * jj - 31).astype(np.float32)
    rel2 = np.arange(5)[None, :, None]
    okc = (512 * rel2 + q[:, None, :].transpose(1, 0, 2) * 0 + q[None, :, :] * 1 - 16 * jj[:, :, None] - 31) >= 0
    c["cmask"] = np.where(okc, 0.0, -BIGNEG).astype(np.float32).astype(ml_dtypes.bfloat16)
    c["dmask"] = diag_mask_tiles(False).astype(ml_dtypes.bfloat16)
    r = (np.arange(8) - 4)[None, :, None]
    dd = q[None, :, :] - jj[:, :, None] - 128 * r
    c["wmask"] = np.where((dd >= 0) & (dd < 512), 0.0, -BIGNEG).astype(np.float32).astype(ml_dtypes.bfloat16)
    s_ = np.arange(128)[:, None]
    key = np.arange(T)[None, :]
    c["ind"] = np.where(key // 64 == s_, BIGNEG, 0.0).astype(np.float32).astype(ml_dtypes.bfloat16)
    n = np.arange(512)
    cs = n * 16
    ss = np.arange(128) * 64
    ov = ((cs[:, None] < ss[None, :] + 64) & (cs[:, None] + 32 > ss[None, :])).astype(np.float32)
    ov[511, :] = 0.0
    c["ovl"] = np.ascontiguousarray(ov.reshape(4, 128, 128).transpose(1, 0, 2)).astype(ml_dtypes.bfloat16)
    tt = np.arange(T)
    cur = tt // 64
    sid = np.arange(128)[None, :]
    forced = (sid == 0) | (sid == cur[:, None]) | (sid == cur[:, None] - 1)
    adj = np.where(forced, NSA_FORCE, 0.0)
    adj = np.where(sid <= cur[:, None], adj, NSA_NEG).astype(np.float32)
    c["adj"] = np.ascontiguousarray(adj.reshape(64, 128, 128))
    return c


_NSA_CONSTS = {}


def nsa_inputs(pfx, w_in, cmp_pe, cmp_w1, cmp_w2):
    if not _NSA_CONSTS:
        _NSA_CONSTS.update(nsa_consts())
    cst = _NSA_CONSTS
    slopes = alibi_slopes_np(16)
    cw1 = np.ascontiguousarray(np.concatenate([cmp_w1[j].reshape(32, 64, 256).transpose(1, 0, 2) for j in range(2)], axis=0))
    cpe = np.ascontiguousarray(np.concatenate([cmp_pe[j].T for j in range(2)], axis=0))
    cw2k = np.ascontiguousarray(np.concatenate([cmp_w2[0], cmp_w2[0]], axis=1).reshape(2, 128, 128).transpose(1, 0, 2))
    cw2v = np.ascontiguousarray(cmp_w2[1].reshape(2, 128, 64).transpose(1, 0, 2))
    rel = np.arange(NREL) - 3
    out = []
    for grp in range(4):
        hs = [4 * grp + r_ for r_ in range(4)]
        qc = np.arange(grp * 256, (grp + 1) * 256)
        kc = 1024 + grp * 64 + np.arange(64)
        vc, ks, vs, kw, vw = kc + 256, kc + 512, kc + 768, kc + 1024, kc + 1280
        gc = 2560 + grp * 12 + np.arange(12)
        fm_cols = np.concatenate([qc, ks, ks, kw, kw, kc, vc])
        tm_cols = np.concatenate([vs, vw, gc])
        sl = np.array([slopes[h] for h in hs])
        out.append({
            pfx + "wfm": arrange_w(w_in[:, fm_cols]),
            pfx + "wtm": arrange_w(w_in[:, tm_cols]),
            pfx + "cw1": cw1, pfx + "cpe": cpe, pfx + "cw2k": cw2k, pfx + "cw2v": cw2v,
            pfx + "ovl": cst["ovl"],
            pfx + "slp": np.ascontiguousarray(np.broadcast_to(sl[None, :], (128, 4))).astype(np.float32),
            pfx + "cb": np.ascontiguousarray(np.broadcast_to((-sl[:, None] * 128.0 * rel[None, :])[None], (128, 4, NREL))).astype(np.float32),
            pfx + "cbc": np.ascontiguousarray(np.broadcast_to((-sl[:, None] * 512.0 * np.arange(16)[None, :])[None], (128, 4, 16))).astype(np.float32),
            pfx + "base0": cst["base0"], pfx + "basec0": cst["basec0"], pfx + "cmask": cst["cmask"], pfx + "dmask": cst["dmask"],
            pfx + "wmask": cst["wmask"], pfx + "ind": cst["ind"], pfx + "adj": cst["adj"], pfx + "ident": _IDENT,
        })
    return out


DEPTH = 4
CC_GROUPS = [[0, 1, 2, 3], [4, 5, 6, 7]]


class ChunkAG:
    def __init__(self, nc, name, src, dst):
        self.nc, self.src, self.dst = nc, src, dst
        self.cs = nc.alloc_semaphore(name=name)
        self.n = 0

    def ready(self, S, ch, keys):
        S._deps("gpsimd", keys, [])
        self.nc.gpsimd.collective_compute("AllGather", ALU.bypass, replica_groups=CC_GROUPS,
                                          ins=[self.src[ch]], outs=[self.dst[ch]]).then_inc(self.cs, 1)
        self.n += 1

    def finish(self):
        nc = self.nc
        for eng in (nc.gpsimd, nc.sync, nc.tensor, nc.vector, nc.scalar):
            eng.wait_ge(self.cs, self.n)
        free_sems(nc, [self.cs])


def build_fused(nphase=99):
    nc = bass.Bass("TRN2", target_bir_lowering=False)
    x_in = nc.dram_tensor("x", [TOK, D], F32, kind="ExternalInput").ap()
    out = nc.dram_tensor("out", [TOK, D], F32, kind="ExternalOutput").ap()
    x_scr = nc.dram_tensor("x_scr", [TOK, D], F32, kind="Internal").ap()
    hT_loc = [nc.dram_tensor("hT_loc%d" % i, [4, D, 512], BF16, kind="Internal").ap() for i in range(DEPTH)]
    hT_all = [nc.dram_tensor("hT_all%d" % i, [4, 4 * D, 512], BF16, kind="Internal").ap() for i in range(DEPTH)]
    o_loc = [nc.dram_tensor("o_loc%d" % i, [4, 256, 2048], BF16, kind="Internal").ap() for i in range(DEPTH)]
    o_all = [nc.dram_tensor("o_all%d" % i, [4, 4 * 256, 2048], BF16, kind="Internal").ap() for i in range(DEPTH)]
    rank = nc.sync.partition_id() % 4
    mod_loc = nc.dram_tensor("mod_loc", [9, D], F32, kind="Internal").ap()
    mod_all = nc.dram_tensor("mod_all", [36, D], F32, kind="Internal").ap()

    def allgather(name, src, dst, nch=4):
        cs = nc.alloc_semaphore(name=name)
        for ch in range(nch):
            nc.gpsimd.collective_compute("AllGather", ALU.bypass, replica_groups=CC_GROUPS,
                                         ins=[src[ch] if nch > 1 else src], outs=[dst[ch] if nch > 1 else dst]).then_inc(cs, 1)
        for eng in (nc.gpsimd, nc.sync, nc.tensor, nc.vector, nc.scalar):
            eng.wait_ge(cs, nch)
        free_sems(nc, [cs])

    emit_mods(nc, "pro_", {"rows_out": mod_loc})
    allgather("ccM", mod_loc, mod_all, nch=1)
    emit_k1(nc, "k0_", False, 1, True, {"x": x_in, "xo": x_scr, "hTo": hT_loc[0], "hTo_chunked": True, "modrows": mod_all},
            k1_rows(None, [(0, 0)], 0))
    allgather("ccA0", hT_loc[0], hT_all[0])
    for i in range(DEPTH):
        binds = {"hT": hT_all[i], "oT": o_loc[i]}
        kind = i % 3
        if kind == 0:
            emit_nsa(nc, "m%d_" % i, binds, True)
        elif kind == 1:
            emit_sb(nc, "m%d_" % i, binds, True)
        else:
            emit_diff(nc, "m%d_" % i, binds, 0.8 - 0.6 * math.exp(-0.3 * i), True)
        allgather("ccB%d" % i, o_loc[i], o_all[i])
        oT_ap = o_all[i][rank]
        if i < DEPTH - 1:
            emit_k1(nc, "k%d_" % (i + 1), True, 2, True,
                    {"x": x_scr, "xo": x_scr, "hTo": hT_loc[i + 1], "hTo_chunked": True, "oT": oT_ap, "modrows": mod_all},
                    k1_rows(i, [(i, 1), (i + 1, 0)], i + 1))
            allgather("ccA%d" % (i + 1), hT_loc[i + 1], hT_all[i + 1])
        else:
            emit_k1(nc, "k%d_" % (i + 1), True, 1, False, {"x": x_scr, "xo": out, "oT": oT_ap, "modrows": mod_all},
                    k1_rows(i, [(i, 1)], None))
    return nc


_NPHASE = [99]


def kernel(x, c, ada_w, ada_b, norm_g, ffn_w1, ffn_w2, nsa_w_in, nsa_cmp_pe, nsa_cmp_w1, nsa_cmp_w2,
           nsa_w_out, sb_w_in, sb_w_out, diff_w_in, diff_lam, diff_subln_g, diff_w_out):
    f = lambda a: np.asarray(a, dtype=np.float32)
    x, c, ada_w, ada_b, norm_g, ffn_w1, ffn_w2 = map(f, (x, c, ada_w, ada_b, norm_g, ffn_w1, ffn_w2))
    nsa_w_in, nsa_cmp_pe, nsa_cmp_w1, nsa_cmp_w2, nsa_w_out = map(f, (nsa_w_in, nsa_cmp_pe, nsa_cmp_w1, nsa_cmp_w2, nsa_w_out))
    sb_w_in, sb_w_out, diff_w_in, diff_lam, diff_subln_g, diff_w_out = map(
        f, (sb_w_in, sb_w_out, diff_w_in, diff_lam, diff_subln_g, diff_w_out))
    nc = get_nc(("fused", _NPHASE[0]), lambda: build_fused(_NPHASE[0]))
    xt = x.reshape(B * T, D)
    common = {}
    per_b = [dict() for _ in range(B)]
    per_g = [dict() for _ in range(4)]

    def add_k1(pfx, mix, ffns, pre, wout=None):
        common.update(k1_inputs(pfx, ffn_w1, ffn_w2, mix, ffns, wout))

    pr, pb = mods_inputs("pro_", c, ada_w, ada_b, norm_g)
    for b in range(B):
        per_b[b].update(pb[b])
    for g in range(4):
        per_g[g].update(pr[g])

    add_k1("k0_", None, [(0, 0)], 0)
    for i in range(DEPTH):
        kind, j = i % 3, i // 3
        pfx = "m%d_" % i
        if kind == 0:
            pg = nsa_inputs(pfx, nsa_w_in[j], nsa_cmp_pe[j], nsa_cmp_w1[j], nsa_cmp_w2[j])
            wout = nsa_w_out[j]
        elif kind == 1:
            pg = sb_inputs(pfx, sb_w_in[j])
            wout = sb_w_out[j]
        else:
            pg = diff_inputs(pfx, diff_w_in[j], diff_lam[j], diff_subln_g[j])
            wout = diff_w_out[j]
        for g in range(4):
            per_g[g].update(pg[g])
        if i < DEPTH - 1:
            add_k1("k%d_" % (i + 1), i, [(i, 1), (i + 1, 0)], i + 1, wout)
        else:
            add_k1("k%d_" % (i + 1), i, [(i, 1)], None, wout)
    in_maps = []
    for core in range(NCORE):
        b, g = core // 4, core % 4
        m = dict(common)
        m.update(per_b[b])
        m.update(per_g[g])
        m["x"] = np.ascontiguousarray(xt[core * TOK:(core + 1) * TOK])
        in_maps.append(m)
    if _NPHASE[0] < 99:
        npfx = ["k0_"]
        for i in range(DEPTH):
            npfx += [None, "m%d_" % i, None, "k%d_" % (i + 1)]
        keep = set(p for p in npfx[:_NPHASE[0]] if p)
        in_maps = [{k: v for k, v in m.items() if k == "x" or k[:3] in keep or k.startswith("pro_")} for m in in_maps]
    res = run_bass_kernel_spmd(nc, in_maps, core_ids=list(range(NCORE)))
    xo = np.concatenate([res.results[i]["out"] for i in range(NCORE)], axis=0)
    return xo.reshape(B, T, D).astype(np.float32)
```

```python
import contextlib
import math
import numpy as np
import ml_dtypes
import concourse.bass as bass
import concourse.mybir as mybir
from concourse.bass_utils import run_bass_kernel_spmd

F32 = mybir.dt.float32
BF16 = mybir.dt.bfloat16
AF = mybir.ActivationFunctionType
ALU = mybir.AluOpType
AX = mybir.AxisListType

D = 1024
DFF = 2816
NFF = DFF // 128
B = 2
T = 8192
NCORE = 8
TOK = B * T // NCORE
NT = TOK // 128
EPS = 1e-6
NDMA = 24


def free_sems(nc, handles):
    nc.all_engine_barrier()
    nc.clear_and_free_semaphores(handles)
    nc.all_engine_barrier()


class Sched:
    def __init__(self, nc, es, pfx=""):
        self.nc = nc
        self.pfx = pfx
        self.engs = {}
        for name in ["tensor", "vector", "scalar", "gpsimd", "sync"]:
            sem = nc.alloc_semaphore(name=pfx + "sem_" + name)
            self.engs[name] = dict(obj=getattr(nc, name), sem=sem, cnt=0, waited={})
        self.dma_slots = [dict(sem=nc.alloc_semaphore(name=pfx + "dsem%d" % i), cnt=0) for i in range(NDMA)]
        self.dma_rr = 0
        self.last_write = {}
        self.reads = {}
        self.out_tokens = []
        self.psum_keys = set()

    def _wait(self, engname, tok):
        if tok is None:
            return
        semid, sem, val = tok
        if semid == engname and engname == "tensor":
            return
        e = self.engs[engname]
        if e["waited"].get(semid, 0) >= val:
            return
        e["obj"].wait_ge(sem, val)
        e["waited"][semid] = val

    def _norm(self, keys):
        p = self.pfx
        return [k[len(p):] if (p and k.startswith(p)) else k for k in keys]

    def _deps(self, engname, reads, writes):
        reads, writes = self._norm(reads), self._norm(writes)
        for k in reads:
            self._wait(engname, self.last_write.get(k))
            if k in self.psum_keys:
                for t in self.reads.get(k, []):
                    if t[0] != engname:
                        self._wait(engname, t)
        for k in writes:
            self._wait(engname, self.last_write.get(k))
            for t in self.reads.get(k, []):
                self._wait(engname, t)

    def _commit(self, tok, reads, writes):
        reads, writes = self._norm(reads), self._norm(writes)
        for k in writes:
            self.last_write[k] = tok
            self.reads[k] = []
        for k in reads:
            self.reads.setdefault(k, []).append(tok)

    def op(self, engname, fn, reads=(), writes=()):
        self._deps(engname, reads, writes)
        e = self.engs[engname]
        ins = fn(e["obj"])
        e["cnt"] += 1
        ins.then_inc(e["sem"], 1)
        tok = (engname, e["sem"], e["cnt"])
        self._commit(tok, reads, writes)
        return tok

    def dma(self, queue, out, in_, reads=(), writes=(), is_output=False):
        self._deps(queue, reads, writes)
        idx = self.dma_rr
        slot = self.dma_slots[idx]
        self.dma_rr = (self.dma_rr + 1) % NDMA
        if slot["cnt"] > 0:
            self._wait(queue, ("d%d" % idx, slot["sem"], slot["cnt"] * 16))
        ins = self.engs[queue]["obj"].dma_start(out=out, in_=in_)
        slot["cnt"] += 1
        ins.then_inc(slot["sem"], 16)
        tok = ("d%d" % idx, slot["sem"], slot["cnt"] * 16)
        self._commit(tok, reads, writes)
        if is_output:
            self.out_tokens.append(tok)
        return tok

    def barrier(self):
        toks = []
        for idx, slot in enumerate(self.dma_slots):
            if slot["cnt"] > 0:
                toks.append(("d%d" % idx, slot["sem"], slot["cnt"] * 16))
        for name, e in self.engs.items():
            if e["cnt"] > 0:
                toks.append((name, e["sem"], e["cnt"]))
        for name in self.engs:
            for tk in toks:
                if tk[0] != name:
                    self._wait(name, tk)

    def close(self):
        self.barrier()
        handles = [e["sem"] for e in self.engs.values()] + [sl["sem"] for sl in self.dma_slots]
        free_sems(self.nc, handles)

    def finish(self):
        for idx, slot in enumerate(self.dma_slots):
            if slot["cnt"] > 0:
                self._wait("sync", ("d%d" % idx, slot["sem"], slot["cnt"] * 16))
        for name, e in self.engs.items():
            if name != "sync" and e["cnt"] > 0:
                self._wait("sync", (name, e["sem"], e["cnt"]))


def emit_k1(nc, pfx, mix, n_ffn, pre, binds, rows):
    nv = (1 if mix else 0) + 3 * n_ffn + (2 if pre else 0)
    ng = (1 if mix else 0) + 2 * n_ffn + (1 if pre else 0)
    dr = {}

    def din(name, shape, dt=F32, kind="ExternalInput"):
        if name in binds:
            dr[name] = binds[name]
        else:
            dr[name] = nc.dram_tensor(pfx + name, list(shape), dt, kind=kind).ap()

    din("x", [TOK, D])
    din("ident", [128, 128])
    modrows = binds["modrows"]
    if mix:
        din("oT", [D, TOK], BF16)
        din("wout", [D, D])
    for f in range(n_ffn):
        din("w1_%d" % f, [NFF, 128, 8, 256])
        din("w2_%d" % f, [DFF, D])
    din("xo", [TOK, D], F32, "ExternalOutput")
    if pre:
        din("hTo", [D, TOK], BF16, "ExternalOutput")

    es = contextlib.ExitStack()
    with es:
        S = Sched(nc, es, pfx)

        def sb(name, shape, dt):
            return es.enter_context(nc.sbuf_tensor(pfx + name, shape, dt))

        def ps(name, shape, dt):
            return es.enter_context(nc.psum_tensor(pfx + name, shape, dt))

        xs = sb("xs", [128, NT, D], F32)
        hT = sb("hT", [128, 8, 1024], BF16)
        actT = sb("actT", [128, NFF, 1024], BF16)
        w1c = [sb("w1c%d" % i, [128, 8, 256], BF16) for i in range(2)]
        w2c = [sb("w2c%d" % i, [128, 512], BF16) for i in range(4)]
        ysave = sb("ysave", [128, 8, 512], F32)
        ssq = sb("ssq", [128, 16], F32)
        prmsets = [[sb("prm%d_%d" % (j, i), [128, D], F32) for i in range(3)] for j in range(1)]
        scr = [sb("scr%d" % i, [128, D], F32) for i in range(2)]
        hb = [sb("hb%d" % i, [128, D], BF16) for i in range(2)]
        junk = sb("junk", [128, D], BF16)
        sil = [sb("sil%d" % i, [128, 512], F32) for i in range(2)]
        identf = sb("identf", [128, 128], F32)
        identb = sb("identb", [128, 128], BF16)
        st = sb("st", [128, 8], F32)
        epsb = sb("epsb", [128, 1], F32)
        PA = ps("PA", [128, 1024], F32)
        PB = ps("PB", [128, 1024], F32)
        PC = ps("PC", [128, 1024], F32)
        PT = ps("PT", [128, 1024], F32)
        PTb = PT[:].bitcast(BF16)
        S.psum_keys.update(["PA0", "PA1", "PB0", "PB1", "PC0", "PC1", "PT0", "PT1"])

        xin = dr["x"].rearrange("(n p) d -> p n d", p=128)
        for q4 in range(4):
            S.dma("sync", xs[:, q4 * 4:(q4 + 1) * 4, :], xin[:, q4 * 4:(q4 + 1) * 4, :],
                  writes=["xs%d" % n for n in range(q4 * 4, q4 * 4 + 4)])
        S.dma("sync", identf[:], dr["ident"][:, :], writes=["identf"])
        S.op("vector", lambda e: e.tensor_copy(out=identb[:], in_=identf[:]), reads=["identf"], writes=["identb"])
        S.op("vector", lambda e: e.memset(epsb[:], EPS), writes=["epsb"])

        def load_row(row, dst):
            S.dma("sync", dst[:], modrows[row:row + 1, :].partition_broadcast(128), writes=[dst.name])

        pset = [0]

        def next_prm():
            pset[0] += 1
            return prmsets[0]

        stc = [0]

        def rstd_of(src_ap, src_keys, col):
            S.op("scalar", lambda e: e.activation(out=junk[:], in_=src_ap, func=AF.Square, accum_out=st[:, col:col + 1]),
                 reads=src_keys, writes=["junk", "st%d" % col])
            S.op("scalar", lambda e: e.activation(out=st[:, col:col + 1], in_=st[:, col:col + 1], func=AF.Sqrt,
                                                  bias=epsb[:], scale=1.0 / D),
                 reads=["st%d" % col, "epsb"], writes=["st%d" % col])
            S.op("vector", lambda e: e.reciprocal(out=st[:, col:col + 1], in_=st[:, col:col + 1]),
                 reads=["st%d" % col], writes=["st%d" % col])

        def prenorm_tile(n, A, Bv, i):
            col = stc[0] % 4
            stc[0] += 1
            rstd_of(xs[:, n, :], ["xs%d" % n], col)
            S.op("vector", lambda e: e.scalar_tensor_tensor(out=scr[1][:], in0=xs[:, n, :], scalar=st[:, col:col + 1], in1=A[:],
                                                            op0=ALU.mult, op1=ALU.mult),
                 reads=["xs%d" % n, "st%d" % col, A.name], writes=["scr1"])
            S.op("gpsimd", lambda e: e.tensor_tensor(out=hb[i][:], in0=scr[1][:], in1=Bv[:], op=ALU.add),
                 reads=["scr1", Bv.name], writes=["hb%d" % i])

        def transpose_tile(i, half, dst_fn, dst_keys):
            pk = "PT%d" % half
            for k in range(8):
                S.op("tensor", lambda e, k=k: e.transpose(out=PTb[:, half * 1024 + k * 128: half * 1024 + (k + 1) * 128],
                                                          in_=hb[i][:, k * 128:(k + 1) * 128], identity=identb[:]),
                     reads=["hb%d" % i, "identb"], writes=[pk])
            S.op("scalar", lambda e: e.activation(out=dst_fn(), in_=PTb[:, half * 1024:(half + 1) * 1024].rearrange("p (k t) -> p k t", k=8),
                                                  func=AF.Copy),
                 reads=[pk], writes=dst_keys)

        def epilogue(Y, n, G):
            col = 4 + stc[0] % 4
            stc[0] += 1
            rstd_of(Y[:], [Y.name + "0", Y.name + "1"], col)
            S.op("vector", lambda e: e.scalar_tensor_tensor(out=scr[0][:], in0=Y[:], scalar=st[:, col:col + 1], in1=G[:],
                                                            op0=ALU.mult, op1=ALU.mult),
                 reads=[Y.name + "0", Y.name + "1", "st%d" % col, G.name], writes=["scr0"])
            S.op("gpsimd", lambda e: e.tensor_tensor(out=xs[:, n, :], in0=xs[:, n, :], in1=scr[0][:], op=ALU.add),
                 reads=["scr0", "xs%d" % n], writes=["xs%d" % n])

        vi = 0
        gi = 0
        Yb = [PA, PB]
        if mix:
            prm = next_prm()
            load_row(rows["mix"], prm[2])
            wo = actT[:, 0:8, :]
            S.dma("gpsimd", wo, dr["wout"].rearrange("(k p) n -> p k n", p=128), writes=["actT"])
            for half in range(2):
                oTs = actT[:, 8:16, :]
                S.dma("sync", oTs, dr["oT"][:, half * 1024:(half + 1) * 1024].rearrange("(k p) t -> p k t", p=128),
                      writes=["actT_o"])
                for tt in range(8):
                    n = half * 8 + tt
                    Y = Yb[n % 2]
                    for h2 in range(2):
                        for k in range(8):
                            S.op("tensor", lambda e, k=k, h2=h2, tt=tt, Y=Y: e.matmul(
                                Y[:, h2 * 512:(h2 + 1) * 512], lhsT=actT[:, 8 + k, tt * 128:(tt + 1) * 128],
                                rhs=actT[:, k, h2 * 512:(h2 + 1) * 512], start=(k == 0), stop=(k == 7)),
                                reads=["actT", "actT_o"], writes=[Y.name + str(h2)])
                    epilogue(Y, n, prm[2])

        wctr = [0, 0]
        for f in range(n_ffn):
            prm = next_prm()
            load_row(rows["ffn"][f][0], prm[0])
            load_row(rows["ffn"][f][1], prm[1])
            load_row(rows["ffn"][f][2], prm[2])
            w1d = dr["w1_%d" % f]
            w2d = dr["w2_%d" % f]
            for grp in range(2):
                for tt in range(8):
                    n = grp * 8 + tt
                    i = n % 2
                    prenorm_tile(n, prm[0], prm[1], i)
                    transpose_tile(i, n % 2, lambda tt=tt: hT[:, :, tt * 128:(tt + 1) * 128], ["hT"])
                for j in range(NFF):
                    wi = wctr[0] % 2
                    wctr[0] += 1
                    S.dma("gpsimd", w1c[wi][:], w1d[j], writes=["w1c%da" % wi, "w1c%db" % wi])
                    for th in range(2):
                        Pg = PA if th == 0 else PB
                        for part, (c0, key) in enumerate([(0, "a"), (128, "b")]):
                            for k in range(8):
                                S.op("tensor", lambda e, k=k, th=th, c0=c0, part=part, Pg=Pg, wi=wi: e.matmul(
                                    Pg[:, part * 512:(part + 1) * 512], lhsT=w1c[wi][:, k, c0:c0 + 128],
                                    rhs=hT[:, k, th * 512:(th + 1) * 512], start=(k == 0), stop=(k == 7)),
                                    reads=["hT", "w1c%d%s" % (wi, key)], writes=[Pg.name + str(part)])
                        si = th
                        S.op("scalar", lambda e, Pg=Pg, si=si: e.activation(out=sil[si][:], in_=Pg[:, 0:512], func=AF.Silu),
                             reads=[Pg.name + "0"], writes=["sil%d" % si])
                        S.op("vector", lambda e, Pg=Pg, si=si, j=j, th=th: e.tensor_tensor(
                            out=actT[:, j, th * 512:(th + 1) * 512], in0=Pg[:, 512:1024], in1=sil[si][:], op=ALU.mult),
                            reads=[Pg.name + "1", "sil%d" % si], writes=["actT%d" % j])
                banks = [(PA, 0), (PA, 1), (PB, 0), (PB, 1), (PC, 0), (PC, 1), (PT, 0), (PT, 1)]
                Gt = prm[2]
                for h2 in range(2):
                    for j in range(NFF):
                        wi = wctr[1] % 4
                        wctr[1] += 1
                        S.dma("gpsimd", w2c[wi][:], w2d[j * 128:(j + 1) * 128, h2 * 512:(h2 + 1) * 512], writes=["w2c%d" % wi])
                        for tt in range(8):
                            Yt, hb_ = banks[tt]
                            S.op("tensor", lambda e: e.matmul(Yt[:, hb_ * 512:(hb_ + 1) * 512], lhsT=actT[:, j, tt * 128:(tt + 1) * 128],
                                                              rhs=w2c[wi][:], start=(j == 0), stop=(j == NFF - 1)),
                                 reads=["actT%d" % j, "w2c%d" % wi], writes=[Yt.name + str(hb_)])
                    for tt in range(8):
                        Yt, hb_ = banks[tt]
                        bk = Yt.name + str(hb_)
                        yv = Yt[:, hb_ * 512:(hb_ + 1) * 512]
                        n = grp * 8 + tt
                        col = h2 * 8 + tt
                        S.op("scalar", lambda e: e.activation(out=junk[:, 0:512], in_=yv, func=AF.Square, accum_out=ssq[:, col:col + 1]),
                             reads=[bk], writes=["junk", "ssq%d" % col])
                        if h2 == 0:
                            S.op("vector", lambda e: e.tensor_copy(out=ysave[:, tt, :], in_=yv), reads=[bk], writes=["ysave%d" % tt])
                        else:
                            S.op("vector", lambda e: e.tensor_tensor(out=ssq[:, col:col + 1], in0=ssq[:, col:col + 1], in1=ssq[:, tt:tt + 1], op=ALU.add),
                                 reads=["ssq%d" % col, "ssq%d" % tt], writes=["ssq%d" % col])
                            S.op("scalar", lambda e: e.activation(out=ssq[:, col:col + 1], in_=ssq[:, col:col + 1], func=AF.Sqrt,
                                                                  bias=epsb[:], scale=1.0 / D),
                                 reads=["ssq%d" % col, "epsb"], writes=["ssq%d" % col])
                            S.op("vector", lambda e: e.reciprocal(out=ssq[:, col:col + 1], in_=ssq[:, col:col + 1]),
                                 reads=["ssq%d" % col], writes=["ssq%d" % col])
                            S.op("vector", lambda e: e.scalar_tensor_tensor(out=scr[0][:, 0:512], in0=ysave[:, tt, :], scalar=ssq[:, col:col + 1],
                                                                            in1=Gt[:, 0:512], op0=ALU.mult, op1=ALU.mult),
                                 reads=["ysave%d" % tt, "ssq%d" % col, Gt.name], writes=["scr0"])
                            S.op("vector", lambda e: e.scalar_tensor_tensor(out=scr[0][:, 512:1024], in0=yv, scalar=ssq[:, col:col + 1],
                                                                            in1=Gt[:, 512:1024], op0=ALU.mult, op1=ALU.mult),
                                 reads=[bk, "ssq%d" % col, Gt.name, "scr0"], writes=["scr0"])
                            S.op("gpsimd", lambda e: e.tensor_tensor(out=xs[:, n, :], in0=xs[:, n, :], in1=scr[0][:], op=ALU.add),
                                 reads=["scr0", "xs%d" % n], writes=["xs%d" % n])

        if pre:
            prm = next_prm()
            load_row(rows["pre"][0], prm[0])
            load_row(rows["pre"][1], prm[1])
            if binds.get("hTo_chunked"):
                def hTo_dst(n):
                    return dr["hTo"][n // 4, :, (n % 4) * 128:(n % 4 + 1) * 128].rearrange("(k p) t -> p k t", p=128)
            else:
                def hTo_dst(n):
                    return dr["hTo"].rearrange("(k p) t -> p k t", p=128)[:, :, n * 128:(n + 1) * 128]
            for n in range(NT):
                i = n % 2
                prenorm_tile(n, prm[0], prm[1], i)
                transpose_tile(i, n % 2, lambda n=n: hT[:, :, (n % 8) * 128:(n % 8 + 1) * 128], ["hTo%d" % (n % 8)])
                S.dma("sync", hTo_dst(n), hT[:, :, (n % 8) * 128:(n % 8 + 1) * 128],
                      reads=["hTo%d" % (n % 8)], writes=["hTc%d_%d" % (n // 4, n % 4)], is_output=True)
                if binds.get("cc") is not None and n % 4 == 3:
                    binds["cc"].ready(S, n // 4, ["hTc%d_%d" % (n // 4, q_) for q_ in range(4)])
        xout = dr["xo"].rearrange("(n p) d -> p n d", p=128)
        for q4 in range(4):
            S.dma("sync", xout[:, q4 * 4:(q4 + 1) * 4, :], xs[:, q4 * 4:(q4 + 1) * 4, :],
                  reads=["xs%d" % n for n in range(q4 * 4, q4 * 4 + 4)], is_output=True)
        S.close()


def arrange_w1(w1):
    g = w1[:, :DFF].reshape(8, 128, NFF, 128)
    u = w1[:, DFF:].reshape(8, 128, NFF, 128)
    cat = np.concatenate([g, u], axis=3)
    return np.ascontiguousarray(cat.transpose(2, 1, 0, 3))


def arrange_adaw(cols):
    return np.ascontiguousarray(cols.reshape(8, 128, 8, 128).transpose(2, 1, 0, 3))


_IDENT = np.eye(128, dtype=np.float32)
_NC_CACHE = {}


def get_nc(key, builder):
    if key not in _NC_CACHE:
        _NC_CACHE[key] = builder()
    return _NC_CACHE[key]


def k1_inputs(pfx, ffn_w1, ffn_w2, mix, ffns, wout=None):
    common = {pfx + "ident": _IDENT}
    for f, (l, w) in enumerate(ffns):
        common[pfx + "w1_%d" % f] = arrange_w1(ffn_w1[l][w])
        common[pfx + "w2_%d" % f] = np.ascontiguousarray(ffn_w2[l][w])
    if mix is not None:
        common[pfx + "wout"] = np.ascontiguousarray(wout)
    return common


def k1_rows(mix, ffns, pre):
    rows = {"ffn": []}
    if mix is not None:
        rows["mix"] = mix * 9 + 3 + 2
    for (l, w) in ffns:
        base = l * 9 + (0 if w == 0 else 2) * 3
        rows["ffn"].append((base, base + 1, base + 2))
    if pre is not None:
        rows["pre"] = (pre * 9 + 3, pre * 9 + 4)
    return rows


def mods_inputs(pfx, c, ada_w, ada_b, norm_g):
    per_r = []
    for r in range(4):
        adaw = np.ascontiguousarray(ada_w[r].reshape(8, 128, 9, D).transpose(2, 1, 0, 3))
        per_r.append({pfx + "adaw": adaw, pfx + "adab": np.ascontiguousarray(ada_b[r].reshape(1, 9, D)),
                      pfx + "g": np.ascontiguousarray(norm_g[r].reshape(1, 6, D))})
    per_b = [{pfx + "c": np.ascontiguousarray(c[b].reshape(8, 128).T)} for b in range(B)]
    return per_r, per_b


BIGNEG = 240000.0


class Ctx:
    def __init__(self, nc, pfx, binds):
        self.nc = nc
        self.pfx = pfx
        self.binds = binds
        self.es = contextlib.ExitStack()
        self.dr = {}
        self.psum_names = []

    def din(self, name, shape, dt=F32):
        if name in self.binds:
            self.dr[name] = self.binds[name]
        else:
            self.dr[name] = self.nc.dram_tensor(self.pfx + name, list(shape), dt, kind="ExternalInput").ap()
        return self.dr[name]

    def dout(self, name, shape, dt=F32):
        if name in self.binds:
            self.dr[name] = self.binds[name]
        else:
            self.dr[name] = self.nc.dram_tensor(self.pfx + name, list(shape), dt, kind="ExternalOutput").ap()
        return self.dr[name]

    def sb(self, name, shape, dt):
        return self.es.enter_context(self.nc.sbuf_tensor(self.pfx + "s_" + name, list(shape), dt))

    def ps(self, name, shape, dt):
        self.psum_names.append(name)
        return self.es.enter_context(self.nc.psum_tensor(self.pfx + name, list(shape), dt))


def o_dest(oTd, gathered, p=128):
    if gathered:
        def f(g):
            return oTd[g // 4, :, (g % 4) * 512:(g % 4 + 1) * 512].rearrange("(c p) t -> p c t", p=p)
    else:
        def f(g):
            return oTd[:, g * 512:(g + 1) * 512].rearrange("(c p) t -> p c t", p=p)
    return f


def hT_source(hTd, gathered):
    if gathered:
        def f(tg):
            r, tc = tg // 4, tg % 4
            return hTd[tc, r * D:(r + 1) * D, :].rearrange("(k p) t -> p k t", p=128)
    else:
        def f(tg):
            return hTd[:, tg * 512:(tg + 1) * 512].rearrange("(k p) t -> p k t", p=128)
    return f


def flash_pipeline(S, tiles, emit_s, emit_mid, emit_av, lookahead=4):
    pend = []
    for tl in tiles:
        emit_s(tl)
        emit_mid(tl)
        pend.append(tl)
        if len(pend) > lookahead:
            emit_av(pend.pop(0))
    for tl in pend:
        emit_av(tl)


def emit_mods(nc, pfx, binds):
    C = Ctx(nc, pfx, binds)
    es = C.es
    adawd = C.din("adaw", [9, 128, 8, D])
    adabd = C.din("adab", [1, 9, D])
    gd = C.din("g", [1, 6, D])
    cd = C.din("c", [128, 8])
    outd = binds["rows_out"]
    with es:
        S = Sched(nc, es, pfx)
        adas = [C.sb("adas%d" % i, [128, 8, D], F32) for i in range(2)]
        adab = C.sb("adab", [1, 9, D], F32)
        gl = C.sb("gl", [1, 6, D], F32)
        cst = C.sb("cst", [128, 8], F32)
        cond = C.sb("cond", [128, 8], F32)
        mod = C.sb("mod", [1, 9, D], F32)
        R = C.sb("R", [1, 9, D], F32)
        tmp = C.sb("tmp", [1, D], F32)
        PP = [C.ps("PP%d" % i, [128, 512], F32) for i in range(4)]
        S.psum_keys.update(C.psum_names)
        S.dma("sync", cst[:], cd, writes=["cst"])
        S.dma("sync", adab[:], adabd, writes=["adab"])
        S.dma("sync", gl[:], gd, writes=["gl"])
        S.op("scalar", lambda e: e.activation(out=cond[:], in_=cst[:], func=AF.Silu), reads=["cst"], writes=["cond"])
        pc = [0]
        for v in range(9):
            ad = adas[v % 2]
            S.dma("sync" if v % 2 == 0 else "scalar", ad[:], adawd[v], writes=[ad.name])
            for h2 in range(2):
                Pp = PP[pc[0] % 4]
                pc[0] += 1
                for k in range(8):
                    S.op("tensor", lambda e: e.matmul(Pp[0:1, :], lhsT=cond[:, k:k + 1], rhs=ad[:, k, h2 * 512:(h2 + 1) * 512],
                                                      start=(k == 0), stop=(k == 7)),
                         reads=["cond", ad.name], writes=[Pp.name])
                S.op("vector", lambda e: e.tensor_tensor(out=mod[0:1, v, h2 * 512:(h2 + 1) * 512], in0=Pp[0:1, :],
                                                         in1=adab[0:1, v, h2 * 512:(h2 + 1) * 512], op=ALU.add),
                     reads=[Pp.name, "adab"], writes=["mod%d" % v])
        for s_ in range(3):
            res_w = 1.0 if s_ == 1 else 0.5
            sh, sc, gt = 3 * s_, 3 * s_ + 1, 3 * s_ + 2
            S.op("vector", lambda e: e.tensor_scalar(out=tmp[:], in0=mod[0:1, sc, :], scalar1=1.0, scalar2=None, op0=ALU.add),
                 reads=["mod%d" % sc], writes=["tmp"])
            S.op("vector", lambda e: e.tensor_tensor(out=R[0:1, 3 * s_ + 0, :], in0=tmp[:], in1=gl[0:1, 2 * s_, :], op=ALU.mult),
                 reads=["tmp", "gl"], writes=["R"])
            S.op("vector", lambda e: e.tensor_copy(out=R[0:1, 3 * s_ + 1, :], in_=mod[0:1, sh, :]), reads=["mod%d" % sh, "R"], writes=["R"])
            S.op("vector", lambda e: e.tensor_scalar(out=tmp[:], in0=mod[0:1, gt, :], scalar1=float(res_w), scalar2=None, op0=ALU.mult),
                 reads=["mod%d" % gt, "R"], writes=["tmp"])
            S.op("vector", lambda e: e.tensor_tensor(out=R[0:1, 3 * s_ + 2, :], in0=tmp[:], in1=gl[0:1, 2 * s_ + 1, :], op=ALU.mult),
                 reads=["tmp", "gl", "R"], writes=["R"])
        S.dma("sync", outd.rearrange("(o r) d -> o r d", o=1), R[:], reads=["R"], is_output=True)
        S.close()


NREL = 67


def emit_diff(nc, pfx, binds, lam_init, gathered):
    C = Ctx(nc, pfx, binds)
    nc, es = C.nc, C.es
    hTd = C.din("hT", [D, T], BF16)
    hsrc = hT_source(hTd, gathered)
    odst = None
    wqd = C.din("wq", [128, 8, 256])
    wkd = C.din("wk", [128, 8, 256])
    wvd = C.din("wv", [128, 8, 256])
    based = C.din("base", [128, 2, 512])
    cbd = C.din("cb", [128, 2, NREL])
    dmaskd = C.din("dmask", [128, 4, 512], BF16)
    lamd = C.din("lam", [128, 256])
    subgd = C.din("subg", [128, 128])
    identd = C.din("ident", [128, 128])
    oTd = C.dout("oT", [256, T], BF16)
    odst = o_dest(oTd, gathered)
    with es:
        S = Sched(nc, es, pfx)
        QT = [C.sb("QT%d" % h, [128, T], BF16) for h in range(2)]
        KT = [C.sb("KT%d" % h, [128, T], BF16) for h in range(2)]
        Vaug = C.sb("Vaug", [128, 64, 2, 129], BF16)
        wq = C.sb("wq", [128, 8, 256], BF16)
        wk = C.sb("wk", [128, 8, 256], BF16)
        wv = C.sb("wv", [128, 8, 256], BF16)
        hTs = [C.sb("hTs%d" % i, [128, 8, 512], BF16) for i in range(2)]
        base = C.sb("base", [128, 2, 512], F32)
        cbt = C.sb("cbt", [128, 2, NREL], F32)
        dmask = C.sb("dmask", [128, 4, 512], BF16)
        tb = [C.sb("tb%d" % i, [128, 512], F32) for i in range(2)]
        Pb = [C.sb("Pb%d" % i, [128, 512], BF16) for i in range(7)]
        lam = C.sb("lam", [128, 256], F32)
        subg = C.sb("subg", [128, 128], F32)
        identf = C.sb("identf", [128, 128], F32)
        identb = C.sb("identb", [128, 128], BF16)
        sm = C.sb("sm", [128, 16], F32)
        lamneg = C.sb("lamneg", [128, 1], F32)
        epsb = C.sb("epsb", [128, 1], F32)
        o0s = C.sb("o0s", [128, 128], F32)
        od = C.sb("od", [128, 128], F32)
        junk = C.sb("junk", [128, 256], F32)
        ob = C.sb("ob", [128, 4, 256], BF16)
        oTs = [C.sb("oTs%d" % i, [128, 2, 512], BF16) for i in range(2)]
        SB_ = [C.ps("Sb%d" % i, [128, 512], F32) for i in range(3)]
        OB = [[C.ps("O%d%d" % (m, p), [128, 512], F32) for p in range(2)] for m in range(2)]
        PT = C.ps("PT", [128, 1024], BF16)
        S.psum_keys.update(C.psum_names)

        S.dma("sync", identf[:], identd[:, :], writes=["identf"])
        S.op("vector", lambda e: e.tensor_copy(out=identb[:], in_=identf[:]), reads=["identf"], writes=["identb"])
        S.dma("gpsimd", wq[:], wqd[:, :, :], writes=["wq"])
        S.dma("gpsimd", wk[:], wkd[:, :, :], writes=["wk"])
        S.dma("gpsimd", wv[:], wvd[:, :, :], writes=["wv"])
        S.dma("sync", base[:], based[:, :, :], writes=["base"])
        S.dma("sync", cbt[:], cbd[:, :, :], writes=["cbt"])
        S.dma("sync", dmask[:], dmaskd[:, :, :], writes=["dmask"])
        S.dma("sync", lam[:], lamd[:, :], writes=["lam"])
        S.dma("sync", subg[:], subgd[:, :], writes=["subg"])
        S.op("vector", lambda e: e.memset(epsb[:], EPS), writes=["epsb"])
        S.op("vector", lambda e: e.memset(Vaug[:, :, :, 128:129], 1.0), writes=["Vones"])
        S.op("vector", lambda e: e.scalar_tensor_tensor(out=junk[:, 0:64], in0=lam[:, 0:64], scalar=1.0, in1=lam[:, 64:128],
                                                        op0=ALU.mult, op1=ALU.mult, accum_out=sm[:, 0:1]),
             reads=["lam"], writes=["junk", "sm0"])
        S.op("vector", lambda e: e.scalar_tensor_tensor(out=junk[:, 0:64], in0=lam[:, 128:192], scalar=1.0, in1=lam[:, 192:256],
                                                        op0=ALU.mult, op1=ALU.mult, accum_out=sm[:, 1:2]),
             reads=["lam", "junk"], writes=["junk", "sm1"])
        S.op("scalar", lambda e: e.activation(out=sm[:, 0:2], in_=sm[:, 0:2], func=AF.Exp), reads=["sm0", "sm1"],
             writes=["sm0", "sm1"])
        S.op("vector", lambda e: e.tensor_tensor(out=sm[:, 2:3], in0=sm[:, 1:2], in1=sm[:, 0:1], op=ALU.subtract),
             reads=["sm0", "sm1"], writes=["sm2"])
        S.op("vector", lambda e: e.tensor_scalar(out=lamneg[:], in0=sm[:, 2:3], scalar1=-float(lam_init), scalar2=None, op0=ALU.add),
             reads=["sm2"], writes=["lamneg"])
        S.op("vector", lambda e: e.tensor_scalar(out=subg[:], in0=subg[:], scalar1=float(1.0 - lam_init), scalar2=None, op0=ALU.mult),
             reads=["subg"], writes=["subg"])

        cp = [0]

        def evac(dst, src, rk, wk_):
            eng = "scalar" if cp[0] % 2 == 0 else "vector"
            cp[0] += 1
            if eng == "scalar":
                S.op("scalar", lambda e: e.activation(out=dst, in_=src, func=AF.Copy), reads=rk, writes=wk_)
            else:
                S.op("vector", lambda e: e.tensor_copy(out=dst, in_=src), reads=rk, writes=wk_)

        bank = [0]
        for tg in range(16):
            hs = hTs[tg % 2]
            hk = "hTs%d" % (tg % 2)
            S.dma("sync", hs[:], hsrc(tg), writes=[hk])
            for (w, wname, dstl, dname) in [(wq, "wq", QT, "QT"), (wk, "wk", KT, "KT")]:
                for h in range(2):
                    Pp = SB_[bank[0] % 3]
                    bank[0] += 1
                    for k in range(8):
                        S.op("tensor", lambda e: e.matmul(Pp[:], lhsT=w[:, k, h * 128:(h + 1) * 128], rhs=hs[:, k, :],
                                                          start=(k == 0), stop=(k == 7)),
                             reads=[wname, hk], writes=[Pp.name])
                    evac(dstl[h][:, tg * 512:(tg + 1) * 512], Pp[:], [Pp.name], ["%s%d_%d" % (dname, h, tg)])
            for tt in range(4):
                Pp = SB_[bank[0] % 3]
                bank[0] += 1
                for k in range(8):
                    S.op("tensor", lambda e: e.matmul(Pp[:, 0:256], lhsT=hs[:, k, tt * 128:(tt + 1) * 128], rhs=wv[:, k, :],
                                                      start=(k == 0), stop=(k == 7)),
                         reads=["wv", hk], writes=[Pp.name])
                blk = tg * 4 + tt
                evac(Vaug[:, blk, :, 0:128], Pp[:, 0:256].rearrange("p (h e) -> p h e", h=2), [Pp.name], ["V_%d" % blk])

        scale = 64 ** -0.5
        ctr = {"s": 0, "t": 0, "p": 0, "o": 0}
        for g in range(16):
            for h in range(2):
                tiles = [dict(kb=kb, m=m) for kb in range(4 * g + 4) for m in range(2)]
                first_av = {}

                def emit_s(tl):
                    kb, m = tl["kb"], tl["m"]
                    Sp = SB_[ctr["s"] % 3]
                    ctr["s"] += 1
                    tl["Sp"] = Sp
                    r = kb - 4 * g
                    S.op("tensor", lambda e: e.matmul(Sp[:], lhsT=KT[h][m * 64:(m + 1) * 64, kb * 128:(kb + 1) * 128],
                                                      rhs=QT[h][m * 64:(m + 1) * 64, g * 512:(g + 1) * 512], start=True, stop=(r < 0)),
                         reads=["KT%d_%d" % (h, kb // 4), "QT%d_%d" % (h, g)], writes=[Sp.name])
                    if r >= 0:
                        S.op("tensor", lambda e: e.matmul(Sp[:], lhsT=identb[:], rhs=dmask[:, r, :], start=False, stop=True),
                             reads=["identb", "dmask"], writes=[Sp.name])

                def emit_mid(tl):
                    kb, m, Sp = tl["kb"], tl["m"], tl["Sp"]
                    tt_ = tb[ctr["t"] % 2]
                    ctr["t"] += 1
                    Pt = Pb[ctr["p"] % 7]
                    ctr["p"] += 1
                    tl["P"] = Pt
                    S.op("vector", lambda e: e.scalar_tensor_tensor(out=tt_[:], in0=Sp[:], scalar=scale, in1=base[:, h, :],
                                                                    op0=ALU.mult, op1=ALU.add),
                         reads=[Sp.name, "base"], writes=[tt_.name])
                    rel = 4 * g - kb + 3
                    S.op("scalar", lambda e: e.activation(out=Pt[:], in_=tt_[:], func=AF.Exp, bias=cbt[:, h, rel:rel + 1], scale=1.0),
                         reads=[tt_.name, "cbt"], writes=[Pt.name])

                def emit_av(tl):
                    kb, m, Pt = tl["kb"], tl["m"], tl["P"]
                    r = kb - 4 * g
                    for qb in range(4):
                        if qb < r:
                            continue
                        p = qb // 2
                        O = OB[m][p]
                        st_ = (m, p) not in first_av
                        first_av[(m, p)] = True
                        c0 = (qb % 2) * 129
                        S.op("tensor", lambda e: e.matmul(O[:, c0:c0 + 129], lhsT=Pt[:, qb * 128:(qb + 1) * 128], rhs=Vaug[:, kb, h, :],
                                                          start=st_, stop=(kb == 4 * g + qb), skip_group_check=True),
                             reads=[Pt.name, "V_%d" % kb, "Vones"], writes=[O.name])

                flash_pipeline(S, tiles, emit_s, emit_mid, emit_av)
                for qb in range(4):
                    p = qb // 2
                    c0 = (qb % 2) * 129
                    O0, O1 = OB[0][p], OB[1][p]
                    S.op("vector", lambda e: e.reciprocal(out=sm[:, 4:5], in_=O0[:, c0 + 128:c0 + 129]), reads=[O0.name], writes=["sm4"])
                    S.op("vector", lambda e: e.reciprocal(out=sm[:, 5:6], in_=O1[:, c0 + 128:c0 + 129]), reads=[O1.name], writes=["sm5"])
                    S.op("vector", lambda e: e.tensor_tensor(out=sm[:, 5:6], in0=sm[:, 5:6], in1=lamneg[:], op=ALU.mult),
                         reads=["sm5", "lamneg"], writes=["sm5"])
                    S.op("vector", lambda e: e.tensor_scalar(out=o0s[:], in0=O0[:, c0:c0 + 128], scalar1=sm[:, 4:5], scalar2=None, op0=ALU.mult),
                         reads=[O0.name, "sm4"], writes=["o0s"])
                    S.op("vector", lambda e: e.scalar_tensor_tensor(out=od[:], in0=O1[:, c0:c0 + 128], scalar=sm[:, 5:6], in1=o0s[:],
                                                                    op0=ALU.mult, op1=ALU.add),
                         reads=[O1.name, "sm5", "o0s"], writes=["od"])
                    S.op("scalar", lambda e: e.activation(out=junk[:, 0:128], in_=od[:], func=AF.Square, accum_out=sm[:, 6:7]),
                         reads=["od"], writes=["junk", "sm6"])
                    S.op("scalar", lambda e: e.activation(out=sm[:, 6:7], in_=sm[:, 6:7], func=AF.Sqrt, bias=epsb[:], scale=1.0 / 128),
                         reads=["sm6", "epsb"], writes=["sm6"])
                    S.op("vector", lambda e: e.reciprocal(out=sm[:, 6:7], in_=sm[:, 6:7]), reads=["sm6"], writes=["sm6"])
                    S.op("vector", lambda e: e.scalar_tensor_tensor(out=ob[:, qb, h * 128:(h + 1) * 128], in0=od[:], scalar=sm[:, 6:7],
                                                                    in1=subg[:], op0=ALU.mult, op1=ALU.mult),
                         reads=["od", "sm6", "subg"], writes=["ob"])
            ot = oTs[ctr["o"] % 2]
            otk = "oTs%d" % (ctr["o"] % 2)
            ctr["o"] += 1
            for qb in range(4):
                for c in range(2):
                    S.op("tensor", lambda e: e.transpose(out=PT[:, c * 512 + qb * 128:c * 512 + (qb + 1) * 128],
                                                         in_=ob[:, qb, c * 128:(c + 1) * 128], identity=identb[:]),
                         reads=["ob", "identb"], writes=["PT"])
            S.op("vector", lambda e: e.tensor_copy(out=ot[:], in_=PT[:].rearrange("p (c t) -> p c t", c=2)), reads=["PT"], writes=[otk])
            S.dma("sync", odst(g), ot[:], reads=[otk], writes=["oc%d_%d" % (g // 4, g % 4)], is_output=True)
            if binds.get("cc") is not None and g % 4 == 3:
                binds["cc"].ready(S, g // 4, ["oc%d_%d" % (g // 4, q_) for q_ in range(4)])
        S.close()


def alibi_slopes_np(n):
    return np.exp2(-8.0 * np.arange(1, n + 1, dtype=np.float64) / n)


def diag_mask_tiles(strict):
    jj = np.arange(128)[:, None, None]
    r = np.arange(4)[None, :, None]
    q = np.arange(512)[None, None, :]
    d = q - jj - 128 * r
    ok = d >= (1 if strict else 0)
    return np.where(ok, 0.0, -BIGNEG).astype(np.float32)


def base_tile(slope):
    jj = np.arange(128)[:, None]
    q = np.arange(512)[None, :]
    return (-slope * (q - jj)).astype(np.float32)


def cb_table(slope):
    rel = np.arange(NREL) - 3
    return np.ascontiguousarray(np.broadcast_to((-slope * 128.0 * rel)[None, :], (128, NREL))).astype(np.float32)


def arrange_w(wcols):
    n = wcols.shape[1]
    return np.ascontiguousarray(wcols.reshape(8, 128, n).transpose(1, 0, 2))


def diff_inputs(pfx, w_in, lam, subln_g):
    slopes = alibi_slopes_np(8)
    dm = diag_mask_tiles(False).astype(ml_dtypes.bfloat16)
    lamb = np.ascontiguousarray(np.broadcast_to(lam.reshape(1, 256), (128, 256)))
    sgb = np.ascontiguousarray(np.broadcast_to(subln_g.reshape(1, 128), (128, 128)))
    out = []
    for hg in range(4):
        hs = [2 * hg, 2 * hg + 1]
        cols = np.concatenate([np.arange(h * 128, (h + 1) * 128) for h in hs])
        out.append({
            pfx + "wq": arrange_w(w_in[:, cols]),
            pfx + "wk": arrange_w(w_in[:, 1024 + cols]),
            pfx + "wv": arrange_w(w_in[:, 2048 + cols]),
            pfx + "base": np.ascontiguousarray(np.stack([base_tile(slopes[h]) for h in hs], axis=1)),
            pfx + "cb": np.ascontiguousarray(np.stack([cb_table(slopes[h]) for h in hs], axis=1)),
            pfx + "dmask": dm, pfx + "lam": lamb, pfx + "subg": sgb, pfx + "ident": _IDENT,
        })
    return out


def emit_sb(nc, pfx, binds, gathered):
    C = Ctx(nc, pfx, binds)
    nc, es = C.nc, C.es
    hTd = C.din("hT", [D, T], BF16)
    hsrc = hT_source(hTd, gathered)
    odst = None
    wqd = C.din("wq", [128, 8, 256])
    wkd = C.din("wk", [128, 8, 256])
    wvd = C.din("wv", [128, 8, 256])
    m01d = C.din("m01", [128, 4, 512], BF16)
    trid = C.din("tri", [128, 2, 128], BF16)
    identd = C.din("ident", [128, 128])
    oTd = C.dout("oT", [256, T], BF16)
    odst64 = o_dest(oTd, gathered, 64)
    with es:
        S = Sched(nc, es, pfx)
        QT = [C.sb("QT%d" % h, [128, T], BF16) for h in range(2)]
        KT = [C.sb("KT%d" % h, [128, T], BF16) for h in range(2)]
        V = C.sb("V", [128, 64, 256], BF16)
        wq = C.sb("wq", [128, 8, 256], BF16)
        wk = C.sb("wk", [128, 8, 256], BF16)
        wv = C.sb("wv", [128, 8, 256], BF16)
        hTs = [C.sb("hTs%d" % i, [128, 8, 512], BF16) for i in range(2)]
        m01 = C.sb("m01", [128, 4, 512], BF16)
        tri = C.sb("tri", [128, 2, 128], BF16)
        eb = [C.sb("eb%d" % i, [128, 512], F32) for i in range(4)]
        spb = [C.sb("spb%d" % i, [128, 512], BF16) for i in range(4)]
        wb = [C.sb("wb%d" % i, [128, 512], F32) for i in range(2)]
        ab = [C.sb("ab%d" % i, [128, 512], BF16) for i in range(3)]
        identf = C.sb("identf", [128, 128], F32)
        identb = C.sb("identb", [128, 128], BF16)
        obT = [C.sb("obT%d" % i, [64, 4, 512], BF16) for i in range(2)]
        ZB = [C.ps("Zb%d" % i, [128, 512], F32) for i in range(3)]
        XB = [C.ps("Xb%d" % i, [128, 512], F32) for i in range(2)]
        OBk = [C.ps("Ob%d" % i, [128, 512], F32) for i in range(2)]
        PT = C.ps("PT", [128, 1024], BF16)
        S.psum_keys.update(C.psum_names)

        S.dma("sync", identf[:], identd[:, :], writes=["identf"])
        S.op("vector", lambda e: e.tensor_copy(out=identb[:], in_=identf[:]), reads=["identf"], writes=["identb"])
        S.dma("gpsimd", wq[:], wqd[:, :, :], writes=["wq"])
        S.dma("gpsimd", wk[:], wkd[:, :, :], writes=["wk"])
        S.dma("gpsimd", wv[:], wvd[:, :, :], writes=["wv"])
        S.dma("sync", m01[:], m01d[:, :, :], writes=["m01"])
        S.dma("sync", tri[:], trid[:, :, :], writes=["tri"])

        cp = [0]

        def evac(dst, src, rk, wk_):
            eng = "scalar" if cp[0] % 2 == 0 else "vector"
            cp[0] += 1
            if eng == "scalar":
                S.op("scalar", lambda e: e.activation(out=dst, in_=src, func=AF.Copy), reads=rk, writes=wk_)
            else:
                S.op("vector", lambda e: e.tensor_copy(out=dst, in_=src), reads=rk, writes=wk_)

        bank = [0]
        for tg in range(16):
            hs = hTs[tg % 2]
            hk = "hTs%d" % (tg % 2)
            S.dma("sync", hs[:], hsrc(tg), writes=[hk])
            for (w, wname, dstl, dname) in [(wq, "wq", QT, "QT"), (wk, "wk", KT, "KT")]:
                for h in range(2):
                    Pp = ZB[bank[0] % 3]
                    bank[0] += 1
                    for k in range(8):
                        S.op("tensor", lambda e: e.matmul(Pp[:], lhsT=w[:, k, h * 128:(h + 1) * 128], rhs=hs[:, k, :],
                                                          start=(k == 0), stop=(k == 7)),
                             reads=[wname, hk], writes=[Pp.name])
                    evac(dstl[h][:, tg * 512:(tg + 1) * 512], Pp[:], [Pp.name], ["%s%d_%d" % (dname, h, tg)])
            for tt in range(4):
                Pp = ZB[bank[0] % 3]
                bank[0] += 1
                for k in range(8):
                    S.op("tensor", lambda e: e.matmul(Pp[:, 0:256], lhsT=hs[:, k, tt * 128:(tt + 1) * 128], rhs=wv[:, k, :],
                                                      start=(k == 0), stop=(k == 7)),
                         reads=["wv", hk], writes=[Pp.name])
                blk = tg * 4 + tt
                evac(V[:, blk, :], Pp[:, 0:256], [Pp.name], ["V_%d" % blk])

        scale = 64 ** -0.5
        ctr = {"z": 0, "e": 0, "w": 0, "a": 0, "o": 0, "chain": 0}
        for g in range(16):
            chains = []
            for hh in range(4):
                ch = ctr["chain"]
                ctr["chain"] += 1
                kbs = list(range(4 * g + 3, -1, -1))
                av0 = [True]
                chains.append([dict(hh=hh, kb=kb, first=(i == 0), last=(i == len(kbs) - 1), X=XB[ch % 2], O=OBk[ch % 2], av0=av0)
                               for i, kb in enumerate(kbs)])
            tiles = []
            for pr in range(2):
                for ta, tb_ in zip(chains[2 * pr], chains[2 * pr + 1]):
                    tiles += [ta, tb_]

            def emit_Z(tl):
                hh, kb = tl["hh"], tl["kb"]
                p, half = hh // 2, hh % 2
                Zp = ZB[ctr["z"] % 3]
                ctr["z"] += 1
                tl["Z"] = Zp
                S.op("tensor", lambda e: e.matmul(Zp[:], lhsT=KT[p][half * 64:(half + 1) * 64, kb * 128:(kb + 1) * 128],
                                                  rhs=QT[p][half * 64:(half + 1) * 64, g * 512:(g + 1) * 512], start=True, stop=True),
                     reads=["KT%d_%d" % (p, kb // 4), "QT%d_%d" % (p, g)], writes=[Zp.name])

            def emit_esp(tl):
                kb, Zp = tl["kb"], tl["Z"]
                i = ctr["e"] % 4
                ctr["e"] += 1
                tl["e"], tl["sp"] = eb[i], spb[i]
                r = kb - 4 * g
                S.op("scalar", lambda e: e.activation(out=eb[i][:], in_=Zp[:], func=AF.Exp, scale=scale), reads=[Zp.name], writes=[eb[i].name])
                S.op("scalar", lambda e: e.activation(out=spb[i][:], in_=eb[i][:], func=AF.Ln, bias=1.0, scale=1.0),
                     reads=[eb[i].name], writes=[spb[i].name])
                if r >= 0:
                    S.op("gpsimd", lambda e: e.tensor_tensor(out=spb[i][:], in0=spb[i][:], in1=m01[:, r, :], op=ALU.mult),
                         reads=[spb[i].name, "m01"], writes=[spb[i].name])
                    S.op("gpsimd", lambda e: e.tensor_tensor(out=eb[i][:], in0=eb[i][:], in1=m01[:, r, :], op=ALU.mult),
                         reads=[eb[i].name, "m01"], writes=[eb[i].name])

            def emit_L(tl):
                X, sp = tl["X"], tl["sp"]
                S.op("tensor", lambda e: e.matmul(X[:], lhsT=tri[:, 0, :], rhs=sp[:], start=tl["first"], stop=False, skip_group_check=True),
                     reads=["tri", sp.name], writes=[X.name])

            def emit_w(tl):
                X = tl["X"]
                wi = wb[ctr["w"] % 2]
                ctr["w"] += 1
                tl["w"] = wi
                S.op("scalar", lambda e: e.activation(out=wi[:], in_=X[:], func=AF.Exp, scale=-1.0), reads=[X.name], writes=[wi.name])

            def emit_U(tl):
                X, sp = tl["X"], tl["sp"]
                S.op("tensor", lambda e: e.matmul(X[:], lhsT=tri[:, 1, :], rhs=sp[:], start=False, stop=tl["last"], skip_group_check=True),
                     reads=["tri", sp.name], writes=[X.name])

            def emit_a(tl):
                ee, wi = tl["e"], tl["w"]
                ai = ab[ctr["a"] % 3]
                ctr["a"] += 1
                tl["a"] = ai
                S.op("vector", lambda e: e.tensor_tensor(out=ai[:], in0=ee[:], in1=wi[:], op=ALU.mult),
                     reads=[ee.name, wi.name], writes=[ai.name])

            obt = obT[g % 2]

            def stage2(tl):
                hh, kb, O, ai = tl["hh"], tl["kb"], tl["O"], tl["a"]
                st_ = tl["av0"][0]
                tl["av0"][0] = False
                S.op("tensor", lambda e: e.matmul(O[0:64, :], lhsT=V[:, kb, hh * 64:(hh + 1) * 64], rhs=ai[:],
                                                  start=st_, stop=(kb == 0), skip_group_check=True),
                     reads=[ai.name, "V_%d" % kb], writes=[O.name])
                if tl["last"]:
                    S.op("vector", lambda e: e.tensor_copy(out=obt[:, hh, :], in_=O[0:64, :]), reads=[O.name], writes=[obt.name])

            n = len(tiles)
            emit_Z(tiles[0])
            for i in range(n + 2):
                if 1 <= i <= n:
                    emit_L(tiles[i - 1])
                if 2 <= i:
                    emit_U(tiles[i - 2])
                if i + 1 < n:
                    emit_Z(tiles[i + 1])
                if 2 <= i:
                    stage2(tiles[i - 2])
                if i < n:
                    emit_esp(tiles[i])
                if 1 <= i <= n:
                    emit_w(tiles[i - 1])
                    emit_a(tiles[i - 1])
            S.dma("sync", odst64(g), obt[:], reads=[obt.name], writes=["oc%d_%d" % (g // 4, g % 4)], is_output=True)
            if binds.get("cc") is not None and g % 4 == 3:
                binds["cc"].ready(S, g // 4, ["oc%d_%d" % (g // 4, q_) for q_ in range(4)])
        S.close()


def sb_inputs(pfx, w_in):
    jj = np.arange(128)[:, None, None]
    r = np.arange(4)[None, :, None]
    q = np.arange(512)[None, None, :]
    m01 = ((q - jj - 128 * r) >= 1).astype(np.float32).astype(ml_dtypes.bfloat16)
    mm = np.arange(128)[:, None]
    j2 = np.arange(128)[None, :]
    tri = np.ascontiguousarray(np.stack([(mm >= j2), (mm < j2)], axis=1).astype(np.float32).astype(ml_dtypes.bfloat16))
    out = []
    for hg in range(4):
        cols = np.arange(hg * 256, (hg + 1) * 256)
        out.append({
            pfx + "wq": arrange_w(w_in[:, cols]),
            pfx + "wk": arrange_w(w_in[:, 1024 + cols]),
            pfx + "wv": arrange_w(w_in[:, 2048 + cols]),
            pfx + "m01": m01, pfx + "tri": tri, pfx + "ident": _IDENT,
        })
    return out


NSA_FORCE = 1e4
NSA_NEG = -1e30


class _Stop(Exception):
    pass


def emit_nsa(nc, pfx, binds, gathered, dbg=None):
    C = Ctx(nc, pfx, binds)
    nc, es = C.nc, C.es
    hTd = C.din("hT", [D, T], BF16)
    hsrc = hT_source(hTd, gathered)
    odst = None
    wfmd = C.din("wfm", [128, 8, 640])
    wtmd = C.din("wtm", [128, 8, 140])
    cw1d = C.din("cw1", [128, 32, 256])
    cped = C.din("cpe", [128, 32])
    cw2kd = C.din("cw2k", [128, 2, 128])
    cw2vd = C.din("cw2v", [128, 2, 64])
    ovld = C.din("ovl", [128, 4, 128], BF16)
    slpd = C.din("slp", [128, 4])
    cbd = C.din("cb", [128, 4, NREL])
    cbcd = C.din("cbc", [128, 4, 16])
    base0d = C.din("base0", [128, 512])
    basec0d = C.din("basec0", [128, 512])
    cmaskd = C.din("cmask", [128, 5, 512], BF16)
    dmaskd = C.din("dmask", [128, 4, 512], BF16)
    wmaskd = C.din("wmask", [128, 8, 512], BF16)
    indd = C.din("ind", [128, T], BF16)
    adjd = C.din("adj", [64, 128, 128])
    identd = C.din("ident", [128, 128])
    oTd = C.dout("oT", [256, T], BF16)
    odst = o_dest(oTd, gathered)
    with es:
        S = Sched(nc, es, pfx)
        try:
            QT = [C.sb("QT%d" % h, [128, T], BF16) for h in range(2)]
            ksT = C.sb("ksT", [128, T], BF16)
            kwT = C.sb("kwT", [128, T], BF16)
            kcvT = C.sb("kcvT", [128, T], BF16)
            vsA = C.sb("vsA", [128, 64, 65], BF16)
            vwA = C.sb("vwA", [128, 64, 65], BF16)
            gates = C.sb("gates", [128, 64, 12], F32)
            PBUF = C.sb("PBUF", [128, 14464], BF16)
            hTs = [PBUF[:, i * 4096:(i + 1) * 4096].rearrange("p (k t) -> p k t", k=8) for i in range(2)]
            wfm = PBUF[:, 8192:8192 + 5120].rearrange("p (k n) -> p k n", k=8)
            wtm = PBUF[:, 13312:13312 + 1120].rearrange("p (k n) -> p k n", k=8)
            cw1 = PBUF[:, 0:8192].rearrange("p (l f) -> p l f", l=32)
            ind = PBUF[:, 0:8192]
            cpe = C.sb("cpe", [128, 32], BF16)
            cw2k = C.sb("cw2k", [128, 2, 128], BF16)
            cw2v = C.sb("cw2v", [128, 2, 64], BF16)
            slp = C.sb("slp", [128, 4], F32)
            cbt = C.sb("cbt", [128, 4, NREL], F32)
            cbct = C.sb("cbct", [128, 4, 16], F32)
            base0 = C.sb("base0", [128, 512], F32)
            basec0 = C.sb("basec0", [128, 512], F32)
            cmask = C.sb("cmask", [128, 5, 512], BF16)
            dmask = C.sb("dmask", [128, 4, 512], BF16)
            wmask = C.sb("wmask", [128, 8, 512], BF16)
            tb = [C.sb("tb%d" % i, [128, 512], F32) for i in range(3)]
            Pb = [C.sb("Pb%d" % i, [128, 512], BF16) for i in range(7)]
            kcmpT = C.sb("kcmpT", [128, 512], BF16)
            vcA = C.sb("vcA", [128, 4, 193], BF16)
            glb = [C.sb("glb%d" % i, [128, 512], BF16) for i in range(4)]
            peb = C.sb("peb", [128, 4], F32)
            imp = C.sb("imp", [128, 4, 128], F32)
            adjt = [C.sb("adjt%d" % i, [128, 128], F32) for i in range(2)]
            impa = C.sb("impa", [128, 128], F32)
            impb = C.sb("impb", [128, 128], F32)
            m8 = C.sb("m8", [128, 16], F32)
            selb = C.sb("selb", [128, 128], BF16)
            MBT = [C.sb("MBT%d" % i, [128, 512], BF16) for i in range(2)]
            acco = C.sb("acco", [128, 4, 256], F32)
            ob = C.sb("ob", [128, 4, 256], BF16)
            oTs = [C.sb("oTs%d" % i, [128, 2, 512], BF16) for i in range(2)]
            sm = C.sb("sm", [128, 8], F32)
            otf = C.sb("otf", [65, 512], F32)
            identf = C.sb("identf", [128, 128], F32)
            identb = C.sb("identb", [128, 128], BF16)
            SB_ = [C.ps("Sb%d" % i, [128, 512], F32) for i in range(3)]
            AC = [C.ps("Ac%d" % i, [128, 512], F32) for i in range(4)]
            PT = C.ps("PT", [128, 1024], BF16)
            S.psum_keys.update(C.psum_names)

            S.dma("sync", identf[:], identd[:, :], writes=["identf"])
            S.op("vector", lambda e: e.tensor_copy(out=identb[:], in_=identf[:]), reads=["identf"], writes=["identb"])
            S.dma("gpsimd", wfm, wfmd[:, :, :], writes=["wfm"])
            S.dma("gpsimd", wtm, wtmd[:, :, :], writes=["wtm"])
            S.dma("gpsimd", cpe[:], cped[:, :], writes=["cpe"])
            S.dma("gpsimd", cw2k[:], cw2kd[:, :, :], writes=["cw2k"])
            S.dma("gpsimd", cw2v[:], cw2vd[:, :, :], writes=["cw2v"])
            for (dst, src, key) in [(slp, slpd, "slp"), (cbt, cbd, "cbt"), (cbct, cbcd, "cbct"), (base0, base0d, "base0"),
                                    (basec0, basec0d, "basec0"), (cmask, cmaskd, "cmask"), (dmask, dmaskd, "dmask"),
                                    (wmask, wmaskd, "wmask")]:
                S.dma("sync", dst[:], src, writes=[key])
            S.op("vector", lambda e: e.memset(vsA[:, :, 64:65], 1.0), writes=["vsones"])
            S.op("vector", lambda e: e.memset(vwA[:, :, 64:65], 1.0), writes=["vwones"])
            S.op("vector", lambda e: e.memset(vcA[:], 0.0), writes=["vcA"])
            S.op("vector", lambda e: e.memset(kcmpT[:], 0.0), writes=["kcmpT"])
            S.op("vector", lambda e: e.memset(vcA[:, :, 64:65], 1.0), reads=["vcA"], writes=["vcA"])
            S.dma("sync", vcA[:, :, 65:193], ovld[:, :, :], reads=["vcA"], writes=["vcA"])

            if dbg == 'const':
                raise _Stop
            cp = [0]

            def evac(dst, src, rk, wk_, scale=None):
                eng = "scalar" if cp[0] % 2 == 0 else "vector"
                cp[0] += 1
                if eng == "scalar" and scale is None:
                    S.op("scalar", lambda e: e.activation(out=dst, in_=src, func=AF.Copy), reads=rk, writes=wk_)
                else:
                    if scale is None:
                        S.op("vector", lambda e: e.tensor_copy(out=dst, in_=src), reads=rk, writes=wk_)
                    else:
                        S.op("vector", lambda e: e.tensor_scalar(out=dst, in0=src, scalar1=float(scale), scalar2=None, op0=ALU.mult),
                             reads=rk, writes=wk_)

            bank = [0]
            fm_dst = [(QT[0], "QT0", 0.125), (QT[1], "QT1", 0.125), (ksT, "ksT", None), (kwT, "kwT", None), (kcvT, "kcvT", None)]
            for tg in range(1 if dbg in ('proj1', 'proj1ns') else 16):
                hs = hTs[tg % 2]
                hk = "hTs%d" % (tg % 2)
                S.dma("sync", hs, hsrc(tg), writes=[hk])
                for fi, (dst, dname, sc) in enumerate(fm_dst):
                    Pp = SB_[bank[0] % 3]
                    bank[0] += 1
                    for k in range(8):
                        S.op("tensor", lambda e: e.matmul(Pp[:], lhsT=wfm[:, k, fi * 128:(fi + 1) * 128], rhs=hs[:, k, :],
                                                          start=(k == 0), stop=(k == 7)),
                             reads=["wfm", hk], writes=[Pp.name])
                    evac(dst[:, tg * 512:(tg + 1) * 512], Pp[:], [Pp.name], ["%s_%d" % (dname, tg)], scale=sc)
                for tt in range(4):
                    Pp = SB_[bank[0] % 3]
                    bank[0] += 1
                    for k in range(8):
                        S.op("tensor", lambda e: e.matmul(Pp[:, 0:140], lhsT=hs[:, k, tt * 128:(tt + 1) * 128], rhs=wtm[:, k, :],
                                                          start=(k == 0), stop=(k == 7)),
                             reads=["wtm", hk], writes=[Pp.name])
                    blk = tg * 4 + tt
                    S.op("vector", lambda e: e.tensor_copy(out=vsA[:, blk, 0:64], in_=Pp[:, 0:64]), reads=[Pp.name], writes=["vs_%d" % blk])
                    S.op("vector", lambda e: e.tensor_copy(out=vwA[:, blk, 0:64], in_=Pp[:, 64:128]), reads=[Pp.name], writes=["vw_%d" % blk])
                    S.op("scalar", lambda e: e.activation(out=gates[:, blk, :], in_=Pp[:, 128:140], func=AF.Exp, scale=-1.0),
                         reads=[Pp.name], writes=["gates_%d" % blk])
                    S.op("vector", lambda e: e.tensor_scalar(out=gates[:, blk, :], in0=gates[:, blk, :], scalar1=1.0, scalar2=None, op0=ALU.add),
                         reads=["gates_%d" % blk], writes=["gates_%d" % blk])
                    S.op("vector", lambda e: e.reciprocal(out=gates[:, blk, :], in_=gates[:, blk, :]),
                         reads=["gates_%d" % blk], writes=["gates_%d" % blk])
            if dbg in ('proj', 'proj1', 'proj1ns'):
                raise _Stop
            S.barrier()

            S.dma("gpsimd", cw1, cw1d[:, :, :], writes=["cw1"])
            kcv = kcvT[:, :].rearrange("p (n s) -> p n s", s=16)
            for j in range(2):
                lo, hi = j * 64, (j + 1) * 64
                for c in range(2):
                    Pp = SB_[bank[0] % 3]
                    bank[0] += 1
                    Pq = AC[0]
                    for l in range(32):
                        S.op("tensor", lambda e: e.matmul(Pq[:, 0:1], lhsT=cw1[lo:hi, l, c * 128:(c + 1) * 128], rhs=cpe[lo:hi, l:l + 1],
                                                          start=(l == 0), stop=(l == 31)),
                             reads=["cw1", "cpe"], writes=[Pq.name])
                    col = j * 2 + c
                    S.op("vector", lambda e: e.tensor_copy(out=peb[:, col:col + 1], in_=Pq[:, 0:1]), reads=[Pq.name], writes=["peb%d" % col])
                    for l in range(32):
                        S.op("tensor", lambda e: e.matmul(Pp[:, 0:511], lhsT=cw1[lo:hi, l, c * 128:(c + 1) * 128],
                                                          rhs=kcv[lo:hi, (l // 16):(l // 16) + 511, l % 16],
                                                          start=(l == 0), stop=(l == 31)),
                             reads=["cw1"] + ["kcvT_%d" % t_ for t_ in range(16)], writes=[Pp.name])
                    xg, x2, ug = tb[0], tb[1], tb[2]
                    S.op("scalar", lambda e: e.activation(out=xg[:, 0:511], in_=Pp[:, 0:511], func=AF.Identity, bias=peb[:, col:col + 1], scale=1.0),
                         reads=[Pp.name, "peb%d" % col], writes=[xg.name])
                    S.op("vector", lambda e: e.tensor_tensor(out=x2[:, 0:511], in0=xg[:, 0:511], in1=xg[:, 0:511], op=ALU.mult),
                         reads=[xg.name], writes=[x2.name])
                    S.op("vector", lambda e: e.tensor_scalar(out=x2[:, 0:511], in0=x2[:, 0:511], scalar1=0.044715, scalar2=1.0,
                                                             op0=ALU.mult, op1=ALU.add), reads=[x2.name], writes=[x2.name])
                    S.op("vector", lambda e: e.tensor_tensor(out=ug[:, 0:511], in0=x2[:, 0:511], in1=xg[:, 0:511], op=ALU.mult),
                         reads=[x2.name, xg.name], writes=[ug.name])
                    S.op("scalar", lambda e: e.activation(out=ug[:, 0:511], in_=ug[:, 0:511], func=AF.Exp, scale=-1.5957691216057308),
                         reads=[ug.name], writes=[ug.name])
                    S.op("vector", lambda e: e.tensor_scalar(out=ug[:, 0:511], in0=ug[:, 0:511], scalar1=1.0, scalar2=None, op0=ALU.add),
                         reads=[ug.name], writes=[ug.name])
                    S.op("vector", lambda e: e.reciprocal(out=ug[:, 0:511], in_=ug[:, 0:511]), reads=[ug.name], writes=[ug.name])
                    gl = glb[j * 2 + c]
                    S.op("vector", lambda e: e.memset(gl[:, 511:512], 0.0), writes=[gl.name])
                    S.op("vector", lambda e: e.tensor_tensor(out=gl[:, 0:511], in0=ug[:, 0:511], in1=xg[:, 0:511], op=ALU.mult),
                         reads=[ug.name, xg.name, gl.name], writes=[gl.name])
            if dbg == 'cmp1':
                raise _Stop
            Pp = SB_[bank[0] % 3]
            bank[0] += 1
            for c in range(2):
                S.op("tensor", lambda e: e.matmul(Pp[:, 0:511], lhsT=cw2k[:, c, :], rhs=glb[c][:, 0:511], start=(c == 0), stop=(c == 1)),
                     reads=["cw2k", glb[c].name], writes=[Pp.name])
            S.op("vector", lambda e: e.tensor_copy(out=kcmpT[:, 0:511], in_=Pp[:, 0:511]), reads=[Pp.name, "kcmpT"], writes=["kcmpT"])
            for nt in range(4):
                nn = 128 if nt < 3 else 127
                Pp = SB_[bank[0] % 3]
                bank[0] += 1
                for c in range(2):
                    S.op("tensor", lambda e: e.matmul(Pp[0:nn, 0:64], lhsT=glb[2 + c][:, nt * 128:nt * 128 + nn], rhs=cw2v[:, c, :],
                                                      start=(c == 0), stop=(c == 1)),
                         reads=["cw2v", glb[2 + c].name], writes=[Pp.name])
                S.op("vector", lambda e: e.tensor_copy(out=vcA[0:nn, nt, 0:64], in_=Pp[0:nn, 0:64]), reads=[Pp.name, "vcA"], writes=["vcA"])
            if dbg == 'cmp2':
                raise _Stop
            S.barrier()
            S.dma("sync", ind, indd[:, :], writes=["ind"])

            ctr = {"s": 0, "t": 0, "p": 0, "o": 0, "ac": 0, "adj": 0, "mbt": 0}

            def run_branch(g, hh, tiles, kT, vA, vkey, ncol, accs, acc_cols, basetile, bkey, cbtab, cbkey, accT=None):
                p, half = hh // 2, hh % 2
                firsts = {}

                def emit_s(tl):
                    Sp = SB_[ctr["s"] % 3]
                    ctr["s"] += 1
                    tl["Sp"] = Sp
                    kb = tl["kb"]
                    mms = [(kT[half * 64:(half + 1) * 64, kb * 128:(kb + 1) * 128], QT[p][half * 64:(half + 1) * 64, g * 512:(g + 1) * 512],
                            tl["kkeys"] + ["QT%d_%d" % (p, g)])]
                    if tl.get("extra") is not None:
                        mms.append(tl["extra"])
                    if tl.get("mask") is not None:
                        mms.append((identb[:], tl["mask"], ["identb", "cmask", "dmask", "wmask"]))
                    for i, (l_, r_, keys) in enumerate(mms):
                        S.op("tensor", lambda e: e.matmul(Sp[:], lhsT=l_, rhs=r_, start=(i == 0), stop=(i == len(mms) - 1)),
                             reads=keys, writes=[Sp.name])

                def emit_mid(tl):
                    Sp = tl["Sp"]
                    tt_ = tb[ctr["t"] % 3]
                    ctr["t"] += 1
                    Pt = Pb[ctr["p"] % 7]
                    ctr["p"] += 1
                    tl["P"] = Pt
                    S.op("vector", lambda e: e.scalar_tensor_tensor(out=tt_[:], in0=basetile[:], scalar=slp[:, hh:hh + 1], in1=Sp[:],
                                                                    op0=ALU.mult, op1=ALU.add),
                         reads=[Sp.name, bkey, "slp"], writes=[tt_.name])
                    ci = tl["cbi"]
                    S.op("scalar", lambda e: e.activation(out=Pt[:], in_=tt_[:], func=AF.Exp, bias=cbtab[:, hh, ci:ci + 1], scale=1.0),
                         reads=[tt_.name, cbkey], writes=[Pt.name])

                def emit_av(tl):
                    Pt, kb = tl["P"], tl["kb"]
                    if accT is not None:
                        st_ = accT.name not in firsts
                        firsts[accT.name] = True
                        S.op("tensor", lambda e: e.matmul(accT[0:ncol, :], lhsT=vA[:, kb, :], rhs=Pt[:], start=st_, stop=False,
                                                          skip_group_check=True),
                             reads=[Pt.name] + tl["vkeys"], writes=[accT.name])
                        return
                    for qb in tl["qbs"]:
                        acc, c0 = accs[qb], acc_cols[qb]
                        st_ = acc.name not in firsts
                        firsts[acc.name] = True
                        S.op("tensor", lambda e: e.matmul(acc[:, c0:c0 + ncol], lhsT=Pt[:, qb * 128:(qb + 1) * 128], rhs=vA[:, kb, :],
                                                          start=st_, stop=False, skip_group_check=True),
                             reads=[Pt.name] + tl["vkeys"], writes=[acc.name])

                flash_pipeline(S, tiles, emit_s, emit_mid, emit_av)
                if accT is not None:
                    TP = accs[0]
                    S.op("vector", lambda e: e.tensor_copy(out=otf[0:ncol, :], in_=accT[0:ncol, :]), reads=[accT.name], writes=["otf"])
                    for qb in range(4):
                        S.op("tensor", lambda e: e.transpose(out=TP[:, acc_cols[qb]:acc_cols[qb] + ncol], in_=otf[0:ncol, qb * 128:(qb + 1) * 128],
                                                             identity=identf[0:ncol, 0:ncol]),
                             reads=["otf", "identf"], writes=[TP.name])

            for g in range(16):
                ntmax = (512 * g + 480) // 2048
                for hh in range(4):
                    a0 = AC[(ctr["ac"] % 2) * 2]
                    a1 = AC[(ctr["ac"] % 2) * 2 + 1]
                    ctr["ac"] += 1
                    accs = [a0, a0, a1, a1]
                    cols = [0, 193, 0, 193]
                    tiles = []
                    for nt in range(ntmax + 1):
                        rel2 = g - 4 * nt
                        tiles.append(dict(kb=nt, kkeys=["kcmpT"], vkeys=["vcA"], mask=(cmask[:, rel2, :] if rel2 <= 4 else None),
                                          cbi=rel2, qbs=[0, 1, 2, 3]))
                    run_branch(g, hh, tiles, kcmpT, vcA, "vcA", 193, accs, cols, basec0, "basec0", cbct, "cbct")
                    for qb in range(4):
                        acc, c0 = accs[qb], cols[qb]
                        blk = g * 4 + qb
                        S.op("vector", lambda e: e.tensor_scalar(out=sm[:, 0:1], in0=acc[:, c0 + 64:c0 + 65], scalar1=1e-30, scalar2=None, op0=ALU.max),
                             reads=[acc.name], writes=["sm0"])
                        S.op("vector", lambda e: e.reciprocal(out=sm[:, 0:1], in_=sm[:, 0:1]), reads=["sm0"], writes=["sm0"])
                        S.op("vector", lambda e: e.tensor_tensor(out=sm[:, 1:2], in0=sm[:, 0:1], in1=gates[:, blk, hh * 3:hh * 3 + 1], op=ALU.mult),
                             reads=["sm0", "gates_%d" % blk], writes=["sm1"])
                        S.op("vector", lambda e: e.tensor_scalar(out=acco[:, qb, hh * 64:(hh + 1) * 64], in0=acc[:, c0:c0 + 64],
                                                                 scalar1=sm[:, 1:2], scalar2=None, op0=ALU.mult),
                             reads=[acc.name, "sm1"], writes=["acco%d" % qb])
                        if hh == 0:
                            S.op("vector", lambda e: e.tensor_scalar(out=imp[:, qb, :], in0=acc[:, c0 + 65:c0 + 193], scalar1=sm[:, 0:1],
                                                                     scalar2=None, op0=ALU.mult),
                                 reads=[acc.name, "sm0"], writes=["imp%d" % qb])
                        else:
                            S.op("vector", lambda e: e.scalar_tensor_tensor(out=imp[:, qb, :], in0=acc[:, c0 + 65:c0 + 193], scalar=sm[:, 0:1],
                                                                            in1=imp[:, qb, :], op0=ALU.mult, op1=ALU.add),
                                 reads=[acc.name, "sm0", "imp%d" % qb], writes=["imp%d" % qb])
                if dbg == 'g0c':
                    raise _Stop
                mbt = MBT[ctr["mbt"] % 2]
                ctr["mbt"] += 1
                for qb in range(4):
                    blk = g * 4 + qb
                    at = adjt[ctr["adj"] % 2]
                    ctr["adj"] += 1
                    S.dma("sync", at[:], adjd[blk], writes=[at.name])
                    S.op("vector", lambda e: e.tensor_tensor(out=impa[:], in0=imp[:, qb, :], in1=at[:], op=ALU.add),
                         reads=["imp%d" % qb, at.name], writes=["impa"])
                    S.op("vector", lambda e: e.max(out=m8[:, 0:8], in_=impa[:]), reads=["impa"], writes=["m8a"])
                    S.op("vector", lambda e: e.match_replace(out=impb[:], in_to_replace=m8[:, 0:8], in_values=impa[:], imm_value=-3.0e38),
                         reads=["impa", "m8a"], writes=["impb"])
                    S.op("vector", lambda e: e.max(out=m8[:, 8:16], in_=impb[:]), reads=["impb"], writes=["m8b"])
                    S.op("vector", lambda e: e.tensor_scalar(out=selb[:], in0=impa[:], scalar1=m8[:, 15:16], scalar2=1.0,
                                                             op0=ALU.is_ge, op1=ALU.subtract),
                         reads=["impa", "m8b"], writes=["selb"])
                    S.op("tensor", lambda e: e.transpose(out=PT[:, qb * 128:(qb + 1) * 128], in_=selb[:], identity=identb[:]),
                         reads=["selb", "identb"], writes=["PT"])
                S.op("vector", lambda e: e.tensor_copy(out=mbt[:], in_=PT[:, 0:512]), reads=["PT"], writes=[mbt.name])
                if dbg == 'g0k':
                    raise _Stop
                for hh in range(4):
                    a0 = AC[ctr["ac"] % 4]
                    aT = None
                    ctr["ac"] += 1
                    accs = [a0] * 4
                    cols = [0, 65, 130, 195]
                    tiles = []
                    for kb in range(4 * g + 4):
                        r = kb - 4 * g
                        tiles.append(dict(kb=kb, kkeys=["ksT_%d" % (kb // 4)], vkeys=["vs_%d" % kb, "vsones"],
                                          extra=(ind[:, kb * 128:(kb + 1) * 128], mbt[:], ["ind", mbt.name]),
                                          mask=(dmask[:, r, :] if r >= 0 else None), cbi=4 * g - kb + 3,
                                          qbs=[qb for qb in range(4) if qb >= r]))
                    run_branch(g, hh, tiles, ksT, vsA, "vs", 65, accs, cols, base0, "base0", cbt, "cbt", accT=aT)
                    for qb in range(4):
                        c0 = cols[qb]
                        blk = g * 4 + qb
                        S.op("vector", lambda e: e.reciprocal(out=sm[:, 2:3], in_=a0[:, c0 + 64:c0 + 65]), reads=[a0.name], writes=["sm2"])
                        S.op("vector", lambda e: e.tensor_tensor(out=sm[:, 3:4], in0=sm[:, 2:3], in1=gates[:, blk, hh * 3 + 1:hh * 3 + 2], op=ALU.mult),
                             reads=["sm2", "gates_%d" % blk], writes=["sm3"])
                        S.op("vector", lambda e: e.scalar_tensor_tensor(out=acco[:, qb, hh * 64:(hh + 1) * 64], in0=a0[:, c0:c0 + 64],
                                                                        scalar=sm[:, 3:4], in1=acco[:, qb, hh * 64:(hh + 1) * 64],
                                                                        op0=ALU.mult, op1=ALU.add),
                             reads=[a0.name, "sm3", "acco%d" % qb], writes=["acco%d" % qb])
                if dbg == 'g0s':
                    raise _Stop
                for hh in range(4):
                    a0 = AC[ctr["ac"] % 4]
                    aT = None
                    ctr["ac"] += 1
                    accs = [a0] * 4
                    cols = [0, 65, 130, 195]
                    tiles = []
                    for kb in range(max(0, 4 * g - 4), 4 * g + 4):
                        r = kb - 4 * g
                        tiles.append(dict(kb=kb, kkeys=["kwT_%d" % (kb // 4)], vkeys=["vw_%d" % kb, "vwones"],
                                          mask=wmask[:, r + 4, :], cbi=4 * g - kb + 3,
                                          qbs=[qb for qb in range(4) if qb >= r and qb - r < 5]))
                    run_branch(g, hh, tiles, kwT, vwA, "vw", 65, accs, cols, base0, "base0", cbt, "cbt", accT=aT)
                    for qb in range(4):
                        c0 = cols[qb]
                        blk = g * 4 + qb
                        S.op("vector", lambda e: e.reciprocal(out=sm[:, 4:5], in_=a0[:, c0 + 64:c0 + 65]), reads=[a0.name], writes=["sm4"])
                        S.op("vector", lambda e: e.tensor_tensor(out=sm[:, 5:6], in0=sm[:, 4:5], in1=gates[:, blk, hh * 3 + 2:hh * 3 + 3], op=ALU.mult),
                             reads=["sm4", "gates_%d" % blk], writes=["sm5"])
                        S.op("vector", lambda e: e.scalar_tensor_tensor(out=acco[:, qb, hh * 64:(hh + 1) * 64], in0=a0[:, c0:c0 + 64],
                                                                        scalar=sm[:, 5:6], in1=acco[:, qb, hh * 64:(hh + 1) * 64],
                                                                        op0=ALU.mult, op1=ALU.add),
                             reads=[a0.name, "sm5", "acco%d" % qb], writes=["acco%d" % qb])
                if dbg == 'g0w':
                    raise _Stop
                S.op("vector", lambda e: e.tensor_copy(out=ob[:], in_=acco[:]), reads=["acco%d" % q_ for q_ in range(4)], writes=["ob"])
                ot = oTs[ctr["o"] % 2]
                ctr["o"] += 1
                for qb in range(4):
                    for c in range(2):
                        S.op("tensor", lambda e: e.transpose(out=PT[:, c * 512 + qb * 128:c * 512 + (qb + 1) * 128],
                                                             in_=ob[:, qb, c * 128:(c + 1) * 128], identity=identb[:]),
                             reads=["ob", "identb"], writes=["PT"])
                S.op("vector", lambda e: e.tensor_copy(out=ot[:], in_=PT[:].rearrange("p (c t) -> p c t", c=2)), reads=["PT"], writes=[ot.name])
                S.dma("sync", odst(g), ot[:], reads=[ot.name], writes=["oc%d_%d" % (g // 4, g % 4)], is_output=True)
                if binds.get("cc") is not None and g % 4 == 3:
                    binds["cc"].ready(S, g // 4, ["oc%d_%d" % (g // 4, q_) for q_ in range(4)])
                if dbg == 'g0':
                    raise _Stop
        except _Stop:
            pass
        S.close()


def nsa_consts():
    c = {}
    jj = np.arange(128)[:, None]
    q = np.arange(512)[None, :]
    c["base0"] = (q - jj).astype(np.float32) * -1.0
    c["basec0"] = -(q - 16 * jj - 31).astype(np.float32)
    rel2 = np.arange(5)[None, :, None]
    okc = (512 * rel2 + q[:, None, :].transpose(1, 0, 2) * 0 + q[None, :, :] * 1 - 16 * jj[:, :, None] - 31) >= 0
    c["cmask"] = np.where(okc, 0.0, -BIGNEG).astype(np.float32).astype(ml_dtypes.bfloat16)
    c["dmask"] = diag_mask_tiles(False).astype(ml_dtypes.bfloat16)
    r = (np.arange(8) - 4)[None, :, None]
    dd = q[None, :, :] - jj[:, :, None] - 128 * r
    c["wmask"] = np.where((dd >= 0) & (dd < 512), 0.0, -BIGNEG).astype(np.float32).astype(ml_dtypes.bfloat16)
    s_ = np.arange(128)[:, None]
    key = np.arange(T)[None, :]
    c["ind"] = np.where(key // 64 == s_, BIGNEG, 0.0).astype(np.float32).astype(ml_dtypes.bfloat16)
    n = np.arange(512)
    cs = n * 16
    ss = np.arange(128) * 64
    ov = ((cs[:, None] < ss[None, :] + 64) & (cs[:, None] + 32 > ss[None, :])).astype(np.float32)
    ov[511, :] = 0.0
    c["ovl"] = np.ascontiguousarray(ov.reshape(4, 128, 128).transpose(1, 0, 2)).astype(ml_dtypes.bfloat16)
    tt = np.arange(T)
    cur = tt // 64
    sid = np.arange(128)[None, :]
    forced = (sid == 0) | (sid == cur[:, None]) | (sid == cur[:, None] - 1)
    adj = np.where(forced, NSA_FORCE, 0.0)
    adj = np.where(sid <= cur[:, None], adj, NSA_NEG).astype(np.float32)
    c["adj"] = np.ascontiguousarray(adj.reshape(64, 128, 128))
    return c


_NSA_CONSTS = {}


def nsa_inputs(pfx, w_in, cmp_pe, cmp_w1, cmp_w2):
    if not _NSA_CONSTS:
        _NSA_CONSTS.update(nsa_consts())
    cst = _NSA_CONSTS
    slopes = alibi_slopes_np(16)
    cw1 = np.ascontiguousarray(np.concatenate([cmp_w1[j].reshape(32, 64, 256).transpose(1, 0, 2) for j in range(2)], axis=0))
    cpe = np.ascontiguousarray(np.concatenate([cmp_pe[j].T for j in range(2)], axis=0))
    cw2k = np.ascontiguousarray(np.concatenate([cmp_w2[0], cmp_w2[0]], axis=1).reshape(2, 128, 128).transpose(1, 0, 2))
    cw2v = np.ascontiguousarray(cmp_w2[1].reshape(2, 128, 64).transpose(1, 0, 2))
    rel = np.arange(NREL) - 3
    out = []
    for grp in range(4):
        hs = [4 * grp + r_ for r_ in range(4)]
        qc = np.arange(grp * 256, (grp + 1) * 256)
        kc = 1024 + grp * 64 + np.arange(64)
        vc, ks, vs, kw, vw = kc + 256, kc + 512, kc + 768, kc + 1024, kc + 1280
        gc = 2560 + grp * 12 + np.arange(12)
        fm_cols = np.concatenate([qc, ks, ks, kw, kw, kc, vc])
        tm_cols = np.concatenate([vs, vw, gc])
        sl = np.array([slopes[h] for h in hs])
        out.append({
            pfx + "wfm": arrange_w(w_in[:, fm_cols]),
            pfx + "wtm": arrange_w(w_in[:, tm_cols]),
            pfx + "cw1": cw1, pfx + "cpe": cpe, pfx + "cw2k": cw2k, pfx + "cw2v": cw2v,
            pfx + "ovl": cst["ovl"],
            pfx + "slp": np.ascontiguousarray(np.broadcast_to(sl[None, :], (128, 4))).astype(np.float32),
            pfx + "cb": np.ascontiguousarray(np.broadcast_to((-sl[:, None] * 128.0 * rel[None, :])[None], (128, 4, NREL))).astype(np.float32),
            pfx + "cbc": np.ascontiguousarray(np.broadcast_to((-sl[:, None] * 512.0 * np.arange(16)[None, :])[None], (128, 4, 16))).astype(np.float32),
            pfx + "base0": cst["base0"], pfx + "basec0": cst["basec0"], pfx + "cmask": cst["cmask"], pfx + "dmask": cst["dmask"],
            pfx + "wmask": cst["wmask"], pfx + "ind": cst["ind"], pfx + "adj": cst["adj"], pfx + "ident": _IDENT,
        })
    return out


DEPTH = 4
CC_GROUPS = [[0, 1, 2, 3], [4, 5, 6, 7]]


class ChunkAG:
    def __init__(self, nc, name, src, dst):
        self.nc, self.src, self.dst = nc, src, dst
        self.cs = nc.alloc_semaphore(name=name)
        self.n = 0

    def ready(self, S, ch, keys):
        S._deps("gpsimd", keys, [])
        self.nc.gpsimd.collective_compute("AllGather", ALU.bypass, replica_groups=CC_GROUPS,
                                          ins=[self.src[ch]], outs=[self.dst[ch]]).then_inc(self.cs, 1)
        self.n += 1

    def finish(self):
        nc = self.nc
        for eng in (nc.gpsimd, nc.sync, nc.tensor, nc.vector, nc.scalar):
            eng.wait_ge(self.cs, self.n)
        free_sems(nc, [self.cs])


def build_fused(nphase=99):
    nc = bass.Bass("TRN2", target_bir_lowering=False)
    x_in = nc.dram_tensor("x", [TOK, D], F32, kind="ExternalInput").ap()
    out = nc.dram_tensor("out", [TOK, D], F32, kind="ExternalOutput").ap()
    x_scr = nc.dram_tensor("x_scr", [TOK, D], F32, kind="Internal").ap()
    hT_loc = [nc.dram_tensor("hT_loc%d" % i, [4, D, 512], BF16, kind="Internal").ap() for i in range(DEPTH)]
    hT_all = [nc.dram_tensor("hT_all%d" % i, [4, 4 * D, 512], BF16, kind="Internal").ap() for i in range(DEPTH)]
    o_loc = [nc.dram_tensor("o_loc%d" % i, [4, 256, 2048], BF16, kind="Internal").ap() for i in range(DEPTH)]
    o_all = [nc.dram_tensor("o_all%d" % i, [4, 4 * 256, 2048], BF16, kind="Internal").ap() for i in range(DEPTH)]
    rank = nc.sync.partition_id() % 4
    mod_loc = nc.dram_tensor("mod_loc", [9, D], F32, kind="Internal").ap()
    mod_all = nc.dram_tensor("mod_all", [36, D], F32, kind="Internal").ap()

    def allgather(name, src, dst, nch=4):
        cs = nc.alloc_semaphore(name=name)
        for ch in range(nch):
            nc.gpsimd.collective_compute("AllGather", ALU.bypass, replica_groups=CC_GROUPS,
                                         ins=[src[ch] if nch > 1 else src], outs=[dst[ch] if nch > 1 else dst]).then_inc(cs, 1)
        for eng in (nc.gpsimd, nc.sync, nc.tensor, nc.vector, nc.scalar):
            eng.wait_ge(cs, nch)
        free_sems(nc, [cs])

    emit_mods(nc, "pro_", {"rows_out": mod_loc})
    allgather("ccM", mod_loc, mod_all, nch=1)
    ccA = ChunkAG(nc, "ccA0", hT_loc[0], hT_all[0])
    emit_k1(nc, "k0_", False, 1, True, {"x": x_in, "xo": x_scr, "hTo": hT_loc[0], "hTo_chunked": True, "modrows": mod_all, "cc": ccA},
            k1_rows(None, [(0, 0)], 0))
    ccA.finish()
    for i in range(DEPTH):
        ccB = ChunkAG(nc, "ccB%d" % i, o_loc[i], o_all[i])
        binds = {"hT": hT_all[i], "oT": o_loc[i], "cc": ccB}
        kind = i % 3
        if kind == 0:
            emit_nsa(nc, "m%d_" % i, binds, True)
        elif kind == 1:
            emit_sb(nc, "m%d_" % i, binds, True)
        else:
            emit_diff(nc, "m%d_" % i, binds, 0.8 - 0.6 * math.exp(-0.3 * i), True)
        ccB.finish()
        oT_ap = o_all[i][rank]
        if i < DEPTH - 1:
            ccA = ChunkAG(nc, "ccA%d" % (i + 1), hT_loc[i + 1], hT_all[i + 1])
            emit_k1(nc, "k%d_" % (i + 1), True, 2, True,
                    {"x": x_scr, "xo": x_scr, "hTo": hT_loc[i + 1], "hTo_chunked": True, "oT": oT_ap, "modrows": mod_all, "cc": ccA},
                    k1_rows(i, [(i, 1), (i + 1, 0)], i + 1))
            ccA.finish()
        else:
            emit_k1(nc, "k%d_" % (i + 1), True, 1, False, {"x": x_scr, "xo": out, "oT": oT_ap, "modrows": mod_all},
                    k1_rows(i, [(i, 1)], None))
    return nc


_NPHASE = [99]


def kernel(x, c, ada_w, ada_b, norm_g, ffn_w1, ffn_w2, nsa_w_in, nsa_cmp_pe, nsa_cmp_w1, nsa_cmp_w2,
           nsa_w_out, sb_w_in, sb_w_out, diff_w_in, diff_lam, diff_subln_g, diff_w_out):
    f = lambda a: np.asarray(a, dtype=np.float32)
    x, c, ada_w, ada_b, norm_g, ffn_w1, ffn_w2 = map(f, (x, c, ada_w, ada_b, norm_g, ffn_w1, ffn_w2))
    nsa_w_in, nsa_cmp_pe, nsa_cmp_w1, nsa_cmp_w2, nsa_w_out = map(f, (nsa_w_in, nsa_cmp_pe, nsa_cmp_w1, nsa_cmp_w2, nsa_w_out))
    sb_w_in, sb_w_out, diff_w_in, diff_lam, diff_subln_g, diff_w_out = map(
        f, (sb_w_in, sb_w_out, diff_w_in, diff_lam, diff_subln_g, diff_w_out))
    nc = get_nc(("fused", _NPHASE[0]), lambda: build_fused(_NPHASE[0]))
    xt = x.reshape(B * T, D)
    common = {}
    per_b = [dict() for _ in range(B)]
    per_g = [dict() for _ in range(4)]

    def add_k1(pfx, mix, ffns, pre, wout=None):
        common.update(k1_inputs(pfx, ffn_w1, ffn_w2, mix, ffns, wout))

    pr, pb = mods_inputs("pro_", c, ada_w, ada_b, norm_g)
    for b in range(B):
        per_b[b].update(pb[b])
    for g in range(4):
        per_g[g].update(pr[g])

    add_k1("k0_", None, [(0, 0)], 0)
    for i in range(DEPTH):
        kind, j = i % 3, i // 3
        pfx = "m%d_" % i
        if kind == 0:
            pg = nsa_inputs(pfx, nsa_w_in[j], nsa_cmp_pe[j], nsa_cmp_w1[j], nsa_cmp_w2[j])
            wout = nsa_w_out[j]
        elif kind == 1:
            pg = sb_inputs(pfx, sb_w_in[j])
            wout = sb_w_out[j]
        else:
            pg = diff_inputs(pfx, diff_w_in[j], diff_lam[j], diff_subln_g[j])
            wout = diff_w_out[j]
        for g in range(4):
            per_g[g].update(pg[g])
        if i < DEPTH - 1:
            add_k1("k%d_" % (i + 1), i, [(i, 1), (i + 1, 0)], i + 1, wout)
        else:
            add_k1("k%d_" % (i + 1), i, [(i, 1)], None, wout)
    in_maps = []
    for core in range(NCORE):
        b, g = core // 4, core % 4
        m = dict(common)
        m.update(per_b[b])
        m.update(per_g[g])
        m["x"] = np.ascontiguousarray(xt[core * TOK:(core + 1) * TOK])
        in_maps.append(m)
    if _NPHASE[0] < 99:
        npfx = ["k0_"]
        for i in range(DEPTH):
            npfx += [None, "m%d_" % i, None, "k%d_" % (i + 1)]
        keep = set(p for p in npfx[:_NPHASE[0]] if p)
        in_maps = [{k: v for k, v in m.items() if k == "x" or k[:3] in keep or k.startswith("pro_")} for m in in_maps]
    res = run_bass_kernel_spmd(nc, in_maps, core_ids=list(range(NCORE)))
    xo = np.concatenate([res.results[i]["out"] for i in range(NCORE)], axis=0)
    return xo.reshape(B, T, D).astype(np.float32)
```

```python
import contextlib
import math
import numpy as np
import ml_dtypes
import concourse.bass as bass
import concourse.mybir as mybir
from concourse.bass_utils import run_bass_kernel_spmd

F32 = mybir.dt.float32
BF16 = mybir.dt.bfloat16
AF = mybir.ActivationFunctionType
ALU = mybir.AluOpType
AX = mybir.AxisListType

D = 1024
DFF = 2816
NFF = DFF // 128
B = 2
T = 8192
NCORE = 8
TOK = B * T // NCORE
NT = TOK // 128
EPS = 1e-6
NDMA = 24


def free_sems(nc, handles):
    nc.all_engine_barrier()
    nc.clear_and_free_semaphores(handles)
    nc.all_engine_barrier()


class Sched:
    def __init__(self, nc, es, pfx=""):
        self.nc = nc
        self.pfx = pfx
        self.engs = {}
        for name in ["tensor", "vector", "scalar", "gpsimd", "sync"]:
            sem = nc.alloc_semaphore(name=pfx + "sem_" + name)
            self.engs[name] = dict(obj=getattr(nc, name), sem=sem, cnt=0, waited={})
        self.dma_slots = [dict(sem=nc.alloc_semaphore(name=pfx + "dsem%d" % i), cnt=0) for i in range(NDMA)]
        self.dma_rr = 0
        self.last_write = {}
        self.reads = {}
        self.out_tokens = []
        self.psum_keys = set()

    def _wait(self, engname, tok):
        if tok is None:
            return
        semid, sem, val = tok
        if semid == engname and engname == "tensor":
            return
        e = self.engs[engname]
        if e["waited"].get(semid, 0) >= val:
            return
        e["obj"].wait_ge(sem, val)
        e["waited"][semid] = val

    def _norm(self, keys):
        p = self.pfx
        return [k[len(p):] if (p and k.startswith(p)) else k for k in keys]

    def _deps(self, engname, reads, writes):
        reads, writes = self._norm(reads), self._norm(writes)
        for k in reads:
            self._wait(engname, self.last_write.get(k))
            if k in self.psum_keys:
                for t in self.reads.get(k, []):
                    if t[0] != engname:
                        self._wait(engname, t)
        for k in writes:
            self._wait(engname, self.last_write.get(k))
            for t in self.reads.get(k, []):
                self._wait(engname, t)

    def _commit(self, tok, reads, writes):
        reads, writes = self._norm(reads), self._norm(writes)
        for k in writes:
            self.last_write[k] = tok
            self.reads[k] = []
        for k in reads:
            self.reads.setdefault(k, []).append(tok)

    def op(self, engname, fn, reads=(), writes=()):
        self._deps(engname, reads, writes)
        e = self.engs[engname]
        ins = fn(e["obj"])
        e["cnt"] += 1
        ins.then_inc(e["sem"], 1)
        tok = (engname, e["sem"], e["cnt"])
        self._commit(tok, reads, writes)
        return tok

    def dma(self, queue, out, in_, reads=(), writes=(), is_output=False):
        self._deps(queue, reads, writes)
        idx = self.dma_rr
        slot = self.dma_slots[idx]
        self.dma_rr = (self.dma_rr + 1) % NDMA
        if slot["cnt"] > 0:
            self._wait(queue, ("d%d" % idx, slot["sem"], slot["cnt"] * 16))
        ins = self.engs[queue]["obj"].dma_start(out=out, in_=in_)
        slot["cnt"] += 1
        ins.then_inc(slot["sem"], 16)
        tok = ("d%d" % idx, slot["sem"], slot["cnt"] * 16)
        self._commit(tok, reads, writes)
        if is_output:
            self.out_tokens.append(tok)
        return tok

    def barrier(self):
        toks = []
        for idx, slot in enumerate(self.dma_slots):
            if slot["cnt"] > 0:
                toks.append(("d%d" % idx, slot["sem"], slot["cnt"] * 16))
        for name, e in self.engs.items():
            if e["cnt"] > 0:
                toks.append((name, e["sem"], e["cnt"]))
        for name in self.engs:
            for tk in toks:
                if tk[0] != name:
                    self._wait(name, tk)

    def close(self):
        self.barrier()
        handles = [e["sem"] for e in self.engs.values()] + [sl["sem"] for sl in self.dma_slots]
        free_sems(self.nc, handles)

    def finish(self):
        for idx, slot in enumerate(self.dma_slots):
            if slot["cnt"] > 0:
                self._wait("sync", ("d%d" % idx, slot["sem"], slot["cnt"] * 16))
        for name, e in self.engs.items():
            if name != "sync" and e["cnt"] > 0:
                self._wait("sync", (name, e["sem"], e["cnt"]))


def emit_k1(nc, pfx, mix, n_ffn, pre, binds, rows):
    nv = (1 if mix else 0) + 3 * n_ffn + (2 if pre else 0)
    ng = (1 if mix else 0) + 2 * n_ffn + (1 if pre else 0)
    dr = {}

    def din(name, shape, dt=F32, kind="ExternalInput"):
        if name in binds:
            dr[name] = binds[name]
        else:
            dr[name] = nc.dram_tensor(pfx + name, list(shape), dt, kind=kind).ap()

    din("x", [TOK, D])
    din("ident", [128, 128])
    modrows = binds["modrows"]
    if mix:
        din("oT", [D, TOK], BF16)
        din("wout", [D, D])
    for f in range(n_ffn):
        din("w1_%d" % f, [NFF, 128, 8, 256])
        din("w2_%d" % f, [DFF, D])
    din("xo", [TOK, D], F32, "ExternalOutput")
    if pre:
        din("hTo", [D, TOK], BF16, "ExternalOutput")

    es = contextlib.ExitStack()
    with es:
        S = Sched(nc, es, pfx)

        def sb(name, shape, dt):
            return es.enter_context(nc.sbuf_tensor(pfx + name, shape, dt))

        def ps(name, shape, dt):
            return es.enter_context(nc.psum_tensor(pfx + name, shape, dt))

        xs = sb("xs", [128, NT, D], F32)
        hT = sb("hT", [128, 8, 1024], BF16)
        actT = sb("actT", [128, NFF, 1024], BF16)
        w1c = [sb("w1c%d" % i, [128, 8, 256], BF16) for i in range(2)]
        w2c = [sb("w2c%d" % i, [128, 512], BF16) for i in range(4)]
        ysave = sb("ysave", [128, 8, 512], F32)
        ssq = sb("ssq", [128, 16], F32)
        prmsets = [[sb("prm%d_%d" % (j, i), [128, D], F32) for i in range(3)] for j in range(1)]
        scr = [sb("scr%d" % i, [128, D], F32) for i in range(2)]
        hb = [sb("hb%d" % i, [128, D], BF16) for i in range(2)]
        junk = sb("junk", [128, D], BF16)
        sil = [sb("sil%d" % i, [128, 512], F32) for i in range(2)]
        identf = sb("identf", [128, 128], F32)
        identb = sb("identb", [128, 128], BF16)
        st = sb("st", [128, 8], F32)
        epsb = sb("epsb", [128, 1], F32)
        PA = ps("PA", [128, 1024], F32)
        PB = ps("PB", [128, 1024], F32)
        PC = ps("PC", [128, 1024], F32)
        PT = ps("PT", [128, 1024], F32)
        PTb = PT[:].bitcast(BF16)
        S.psum_keys.update(["PA0", "PA1", "PB0", "PB1", "PC0", "PC1", "PT0", "PT1"])

        xin = dr["x"].rearrange("(n p) d -> p n d", p=128)
        for q4 in range(4):
            S.dma("sync", xs[:, q4 * 4:(q4 + 1) * 4, :], xin[:, q4 * 4:(q4 + 1) * 4, :],
                  writes=["xs%d" % n for n in range(q4 * 4, q4 * 4 + 4)])
        S.dma("sync", identf[:], dr["ident"][:, :], writes=["identf"])
        S.op("vector", lambda e: e.tensor_copy(out=identb[:], in_=identf[:]), reads=["identf"], writes=["identb"])
        S.op("vector", lambda e: e.memset(epsb[:], EPS), writes=["epsb"])

        def load_row(row, dst):
            S.dma("sync", dst[:], modrows[row:row + 1, :].partition_broadcast(128), writes=[dst.name])

        pset = [0]

        def next_prm():
            pset[0] += 1
            return prmsets[0]

        stc = [0]

        def rstd_of(src_ap, src_keys, col):
            S.op("scalar", lambda e: e.activation(out=junk[:], in_=src_ap, func=AF.Square, accum_out=st[:, col:col + 1]),
                 reads=src_keys, writes=["junk", "st%d" % col])
            S.op("scalar", lambda e: e.activation(out=st[:, col:col + 1], in_=st[:, col:col + 1], func=AF.Sqrt,
                                                  bias=epsb[:], scale=1.0 / D),
                 reads=["st%d" % col, "epsb"], writes=["st%d" % col])
            S.op("vector", lambda e: e.reciprocal(out=st[:, col:col + 1], in_=st[:, col:col + 1]),
                 reads=["st%d" % col], writes=["st%d" % col])

        def prenorm_tile(n, A, Bv, i):
            col = stc[0] % 4
            stc[0] += 1
            rstd_of(xs[:, n, :], ["xs%d" % n], col)
            S.op("vector", lambda e: e.scalar_tensor_tensor(out=scr[1][:], in0=xs[:, n, :], scalar=st[:, col:col + 1], in1=A[:],
                                                            op0=ALU.mult, op1=ALU.mult),
                 reads=["xs%d" % n, "st%d" % col, A.name], writes=["scr1"])
            S.op("gpsimd", lambda e: e.tensor_tensor(out=hb[i][:], in0=scr[1][:], in1=Bv[:], op=ALU.add),
                 reads=["scr1", Bv.name], writes=["hb%d" % i])

        def transpose_tile(i, half, dst_fn, dst_keys):
            pk = "PT%d" % half
            for k in range(8):
                S.op("tensor", lambda e, k=k: e.transpose(out=PTb[:, half * 1024 + k * 128: half * 1024 + (k + 1) * 128],
                                                          in_=hb[i][:, k * 128:(k + 1) * 128], identity=identb[:]),
                     reads=["hb%d" % i, "identb"], writes=[pk])
            S.op("scalar", lambda e: e.activation(out=dst_fn(), in_=PTb[:, half * 1024:(half + 1) * 1024].rearrange("p (k t) -> p k t", k=8),
                                                  func=AF.Copy),
                 reads=[pk], writes=dst_keys)

        def epilogue(Y, n, G):
            col = 4 + stc[0] % 4
            stc[0] += 1
            rstd_of(Y[:], [Y.name + "0", Y.name + "1"], col)
            S.op("vector", lambda e: e.scalar_tensor_tensor(out=scr[0][:], in0=Y[:], scalar=st[:, col:col + 1], in1=G[:],
                                                            op0=ALU.mult, op1=ALU.mult),
                 reads=[Y.name + "0", Y.name + "1", "st%d" % col, G.name], writes=["scr0"])
            S.op("gpsimd", lambda e: e.tensor_tensor(out=xs[:, n, :], in0=xs[:, n, :], in1=scr[0][:], op=ALU.add),
                 reads=["scr0", "xs%d" % n], writes=["xs%d" % n])

        vi = 0
        gi = 0
        Yb = [PA, PB]
        if mix:
            prm = next_prm()
            load_row(rows["mix"], prm[2])
            wo = actT[:, 0:8, :]
            S.dma("gpsimd", wo, dr["wout"].rearrange("(k p) n -> p k n", p=128), writes=["actT"])
            for half in range(2):
                oTs = actT[:, 8:16, :]
                S.dma("sync", oTs, dr["oT"][:, half * 1024:(half + 1) * 1024].rearrange("(k p) t -> p k t", p=128),
                      writes=["actT_o"])
                for tt in range(8):
                    n = half * 8 + tt
                    Y = Yb[n % 2]
                    for h2 in range(2):
                        for k in range(8):
                            S.op("tensor", lambda e, k=k, h2=h2, tt=tt, Y=Y: e.matmul(
                                Y[:, h2 * 512:(h2 + 1) * 512], lhsT=actT[:, 8 + k, tt * 128:(tt + 1) * 128],
                                rhs=actT[:, k, h2 * 512:(h2 + 1) * 512], start=(k == 0), stop=(k == 7)),
                                reads=["actT", "actT_o"], writes=[Y.name + str(h2)])
                    epilogue(Y, n, prm[2])

        wctr = [0, 0]
        for f in range(n_ffn):
            prm = next_prm()
            load_row(rows["ffn"][f][0], prm[0])
            load_row(rows["ffn"][f][1], prm[1])
            load_row(rows["ffn"][f][2], prm[2])
            w1d = dr["w1_%d" % f]
            w2d = dr["w2_%d" % f]
            for grp in range(2):
                for tt in range(8):
                    n = grp * 8 + tt
                    i = n % 2
                    prenorm_tile(n, prm[0], prm[1], i)
                    transpose_tile(i, n % 2, lambda tt=tt: hT[:, :, tt * 128:(tt + 1) * 128], ["hT"])
                for j in range(NFF):
                    wi = wctr[0] % 2
                    wctr[0] += 1
                    S.dma("gpsimd", w1c[wi][:], w1d[j], writes=["w1c%da" % wi, "w1c%db" % wi])
                    for th in range(2):
                        Pg = PA if th == 0 else PB
                        for part, (c0, key) in enumerate([(0, "a"), (128, "b")]):
                            for k in range(8):
                                S.op("tensor", lambda e, k=k, th=th, c0=c0, part=part, Pg=Pg, wi=wi: e.matmul(
                                    Pg[:, part * 512:(part + 1) * 512], lhsT=w1c[wi][:, k, c0:c0 + 128],
                                    rhs=hT[:, k, th * 512:(th + 1) * 512], start=(k == 0), stop=(k == 7)),
                                    reads=["hT", "w1c%d%s" % (wi, key)], writes=[Pg.name + str(part)])
                        si = th
                        S.op("scalar", lambda e, Pg=Pg, si=si: e.activation(out=sil[si][:], in_=Pg[:, 0:512], func=AF.Silu),
                             reads=[Pg.name + "0"], writes=["sil%d" % si])
                        S.op("vector", lambda e, Pg=Pg, si=si, j=j, th=th: e.tensor_tensor(
                            out=actT[:, j, th * 512:(th + 1) * 512], in0=Pg[:, 512:1024], in1=sil[si][:], op=ALU.mult),
                            reads=[Pg.name + "1", "sil%d" % si], writes=["actT%d" % j])
                banks = [(PA, 0), (PA, 1), (PB, 0), (PB, 1), (PC, 0), (PC, 1), (PT, 0), (PT, 1)]
                Gt = prm[2]
                for h2 in range(2):
                    for j in range(NFF):
                        wi = wctr[1] % 4
                        wctr[1] += 1
                        S.dma("gpsimd", w2c[wi][:], w2d[j * 128:(j + 1) * 128, h2 * 512:(h2 + 1) * 512], writes=["w2c%d" % wi])
                        for tt in range(8):
                            Yt, hb_ = banks[tt]
                            S.op("tensor", lambda e: e.matmul(Yt[:, hb_ * 512:(hb_ + 1) * 512], lhsT=actT[:, j, tt * 128:(tt + 1) * 128],
                                                              rhs=w2c[wi][:], start=(j == 0), stop=(j == NFF - 1)),
                                 reads=["actT%d" % j, "w2c%d" % wi], writes=[Yt.name + str(hb_)])
                    for tt in range(8):
                        Yt, hb_ = banks[tt]
                        bk = Yt.name + str(hb_)
                        yv = Yt[:, hb_ * 512:(hb_ + 1) * 512]
                        n = grp * 8 + tt
                        col = h2 * 8 + tt
                        S.op("scalar", lambda e: e.activation(out=junk[:, 0:512], in_=yv, func=AF.Square, accum_out=ssq[:, col:col + 1]),
                             reads=[bk], writes=["junk", "ssq%d" % col])
                        if h2 == 0:
                            S.op("vector", lambda e: e.tensor_copy(out=ysave[:, tt, :], in_=yv), reads=[bk], writes=["ysave%d" % tt])
                        else:
                            S.op("vector", lambda e: e.tensor_tensor(out=ssq[:, col:col + 1], in0=ssq[:, col:col + 1], in1=ssq[:, tt:tt + 1], op=ALU.add),
                                 reads=["ssq%d" % col, "ssq%d" % tt], writes=["ssq%d" % col])
                            S.op("scalar", lambda e: e.activation(out=ssq[:, col:col + 1], in_=ssq[:, col:col + 1], func=AF.Sqrt,
                                                                  bias=epsb[:], scale=1.0 / D),
                                 reads=["ssq%d" % col, "epsb"], writes=["ssq%d" % col])
                            S.op("vector", lambda e: e.reciprocal(out=ssq[:, col:col + 1], in_=ssq[:, col:col + 1]),
                                 reads=["ssq%d" % col], writes=["ssq%d" % col])
                            S.op("vector", lambda e: e.scalar_tensor_tensor(out=scr[0][:, 0:512], in0=ysave[:, tt, :], scalar=ssq[:, col:col + 1],
                                                                            in1=Gt[:, 0:512], op0=ALU.mult, op1=ALU.mult),
                                 reads=["ysave%d" % tt, "ssq%d" % col, Gt.name], writes=["scr0"])
                            S.op("vector", lambda e: e.scalar_tensor_tensor(out=scr[0][:, 512:1024], in0=yv, scalar=ssq[:, col:col + 1],
                                                                            in1=Gt[:, 512:1024], op0=ALU.mult, op1=ALU.mult),
                                 reads=[bk, "ssq%d" % col, Gt.name, "scr0"], writes=["scr0"])
                            S.op("gpsimd", lambda e: e.tensor_tensor(out=xs[:, n, :], in0=xs[:, n, :], in1=scr[0][:], op=ALU.add),
                                 reads=["scr0", "xs%d" % n], writes=["xs%d" % n])

        if pre:
            prm = next_prm()
            load_row(rows["pre"][0], prm[0])
            load_row(rows["pre"][1], prm[1])
            if binds.get("hTo_chunked"):
                def hTo_dst(n):
                    return dr["hTo"][n // 4, :, (n % 4) * 128:(n % 4 + 1) * 128].rearrange("(k p) t -> p k t", p=128)
            else:
                def hTo_dst(n):
                    return dr["hTo"].rearrange("(k p) t -> p k t", p=128)[:, :, n * 128:(n + 1) * 128]
            for n in range(NT):
                i = n % 2
                prenorm_tile(n, prm[0], prm[1], i)
                transpose_tile(i, n % 2, lambda n=n: hT[:, :, (n % 8) * 128:(n % 8 + 1) * 128], ["hTo%d" % (n % 8)])
                S.dma("sync", hTo_dst(n), hT[:, :, (n % 8) * 128:(n % 8 + 1) * 128],
                      reads=["hTo%d" % (n % 8)], writes=["hTc%d_%d" % (n // 4, n % 4)], is_output=True)
                if binds.get("cc") is not None and n % 4 == 3:
                    binds["cc"].ready(S, n // 4, ["hTc%d_%d" % (n // 4, q_) for q_ in range(4)])
        xout = dr["xo"].rearrange("(n p) d -> p n d", p=128)
        for q4 in range(4):
            S.dma("sync", xout[:, q4 * 4:(q4 + 1) * 4, :], xs[:, q4 * 4:(q4 + 1) * 4, :],
                  reads=["xs%d" % n for n in range(q4 * 4, q4 * 4 + 4)], is_output=True)
        S.close()


def arrange_w1(w1):
    g = w1[:, :DFF].reshape(8, 128, NFF, 128)
    u = w1[:, DFF:].reshape(8, 128, NFF, 128)
    cat = np.concatenate([g, u], axis=3)
    return np.ascontiguousarray(cat.transpose(2, 1, 0, 3))


def arrange_adaw(cols):
    return np.ascontiguousarray(cols.reshape(8, 128, 8, 128).transpose(2, 1, 0, 3))


_IDENT = np.eye(128, dtype=np.float32)
_NC_CACHE = {}


def get_nc(key, builder):
    if key not in _NC_CACHE:
        _NC_CACHE[key] = builder()
    return _NC_CACHE[key]


def k1_inputs(pfx, ffn_w1, ffn_w2, mix, ffns, wout=None):
    common = {pfx + "ident": _IDENT}
    for f, (l, w) in enumerate(ffns):
        common[pfx + "w1_%d" % f] = arrange_w1(ffn_w1[l][w])
        common[pfx + "w2_%d" % f] = np.ascontiguousarray(ffn_w2[l][w])
    if mix is not None:
        common[pfx + "wout"] = np.ascontiguousarray(wout)
    return common


def k1_rows(mix, ffns, pre):
    rows = {"ffn": []}
    if mix is not None:
        rows["mix"] = mix * 9 + 3 + 2
    for (l, w) in ffns:
        base = l * 9 + (0 if w == 0 else 2) * 3
        rows["ffn"].append((base, base + 1, base + 2))
    if pre is not None:
        rows["pre"] = (pre * 9 + 3, pre * 9 + 4)
    return rows


def mods_inputs(pfx, c, ada_w, ada_b, norm_g):
    per_r = []
    for r in range(4):
        adaw = np.ascontiguousarray(ada_w[r].reshape(8, 128, 9, D).transpose(2, 1, 0, 3))
        per_r.append({pfx + "adaw": adaw, pfx + "adab": np.ascontiguousarray(ada_b[r].reshape(1, 9, D)),
                      pfx + "g": np.ascontiguousarray(norm_g[r].reshape(1, 6, D))})
    per_b = [{pfx + "c": np.ascontiguousarray(c[b].reshape(8, 128).T)} for b in range(B)]
    return per_r, per_b


BIGNEG = 240000.0


class Ctx:
    def __init__(self, nc, pfx, binds):
        self.nc = nc
        self.pfx = pfx
        self.binds = binds
        self.es = contextlib.ExitStack()
        self.dr = {}
        self.psum_names = []

    def din(self, name, shape, dt=F32):
        if name in self.binds:
            self.dr[name] = self.binds[name]
        else:
            self.dr[name] = self.nc.dram_tensor(self.pfx + name, list(shape), dt, kind="ExternalInput").ap()
        return self.dr[name]

    def dout(self, name, shape, dt=F32):
        if name in self.binds:
            self.dr[name] = self.binds[name]
        else:
            self.dr[name] = self.nc.dram_tensor(self.pfx + name, list(shape), dt, kind="ExternalOutput").ap()
        return self.dr[name]

    def sb(self, name, shape, dt):
        return self.es.enter_context(self.nc.sbuf_tensor(self.pfx + "s_" + name, list(shape), dt))

    def ps(self, name, shape, dt):
        self.psum_names.append(name)
        return self.es.enter_context(self.nc.psum_tensor(self.pfx + name, list(shape), dt))


def o_dest(oTd, gathered, p=128):
    if gathered:
        def f(g):
            return oTd[g // 4, :, (g % 4) * 512:(g % 4 + 1) * 512].rearrange("(c p) t -> p c t", p=p)
    else:
        def f(g):
            return oTd[:, g * 512:(g + 1) * 512].rearrange("(c p) t -> p c t", p=p)
    return f


def hT_source(hTd, gathered):
    if gathered:
        def f(tg):
            r, tc = tg // 4, tg % 4
            return hTd[tc, r * D:(r + 1) * D, :].rearrange("(k p) t -> p k t", p=128)
    else:
        def f(tg):
            return hTd[:, tg * 512:(tg + 1) * 512].rearrange("(k p) t -> p k t", p=128)
    return f


def flash_pipeline(S, tiles, emit_s, emit_mid, emit_av, lookahead=4):
    pend = []
    for tl in tiles:
        emit_s(tl)
        emit_mid(tl)
        pend.append(tl)
        if len(pend) > lookahead:
            emit_av(pend.pop(0))
    for tl in pend:
        emit_av(tl)


def emit_mods(nc, pfx, binds):
    C = Ctx(nc, pfx, binds)
    es = C.es
    adawd = C.din("adaw", [9, 128, 8, D])
    adabd = C.din("adab", [1, 9, D])
    gd = C.din("g", [1, 6, D])
    cd = C.din("c", [128, 8])
    outd = binds["rows_out"]
    with es:
        S = Sched(nc, es, pfx)
        adas = [C.sb("adas%d" % i, [128, 8, D], F32) for i in range(2)]
        adab = C.sb("adab", [1, 9, D], F32)
        gl = C.sb("gl", [1, 6, D], F32)
        cst = C.sb("cst", [128, 8], F32)
        cond = C.sb("cond", [128, 8], F32)
        mod = C.sb("mod", [1, 9, D], F32)
        R = C.sb("R", [1, 9, D], F32)
        tmp = C.sb("tmp", [1, D], F32)
        PP = [C.ps("PP%d" % i, [128, 512], F32) for i in range(4)]
        S.psum_keys.update(C.psum_names)
        S.dma("sync", cst[:], cd, writes=["cst"])
        S.dma("sync", adab[:], adabd, writes=["adab"])
        S.dma("sync", gl[:], gd, writes=["gl"])
        S.op("scalar", lambda e: e.activation(out=cond[:], in_=cst[:], func=AF.Silu), reads=["cst"], writes=["cond"])
        pc = [0]
        for v in range(9):
            ad = adas[v % 2]
            S.dma("sync" if v % 2 == 0 else "scalar", ad[:], adawd[v], writes=[ad.name])
            for h2 in range(2):
                Pp = PP[pc[0] % 4]
                pc[0] += 1
                for k in range(8):
                    S.op("tensor", lambda e: e.matmul(Pp[0:1, :], lhsT=cond[:, k:k + 1], rhs=ad[:, k, h2 * 512:(h2 + 1) * 512],
                                                      start=(k == 0), stop=(k == 7)),
                         reads=["cond", ad.name], writes=[Pp.name])
                S.op("vector", lambda e: e.tensor_tensor(out=mod[0:1, v, h2 * 512:(h2 + 1) * 512], in0=Pp[0:1, :],
                                                         in1=adab[0:1, v, h2 * 512:(h2 + 1) * 512], op=ALU.add),
                     reads=[Pp.name, "adab"], writes=["mod%d" % v])
        for s_ in range(3):
            res_w = 1.0 if s_ == 1 else 0.5
            sh, sc, gt = 3 * s_, 3 * s_ + 1, 3 * s_ + 2
            S.op("vector", lambda e: e.tensor_scalar(out=tmp[:], in0=mod[0:1, sc, :], scalar1=1.0, scalar2=None, op0=ALU.add),
                 reads=["mod%d" % sc], writes=["tmp"])
            S.op("vector", lambda e: e.tensor_tensor(out=R[0:1, 3 * s_ + 0, :], in0=tmp[:], in1=gl[0:1, 2 * s_, :], op=ALU.mult),
                 reads=["tmp", "gl"], writes=["R"])
            S.op("vector", lambda e: e.tensor_copy(out=R[0:1, 3 * s_ + 1, :], in_=mod[0:1, sh, :]), reads=["mod%d" % sh, "R"], writes=["R"])
            S.op("vector", lambda e: e.tensor_scalar(out=tmp[:], in0=mod[0:1, gt, :], scalar1=float(res_w), scalar2=None, op0=ALU.mult),
                 reads=["mod%d" % gt, "R"], writes=["tmp"])
            S.op("vector", lambda e: e.tensor_tensor(out=R[0:1, 3 * s_ + 2, :], in0=tmp[:], in1=gl[0:1, 2 * s_ + 1, :], op=ALU.mult),
                 reads=["tmp", "gl", "R"], writes=["R"])
        S.dma("sync", outd.rearrange("(o r) d -> o r d", o=1), R[:], reads=["R"], is_output=True)
        S.close()


NREL = 67


def emit_diff(nc, pfx, binds, lam_init, gathered):
    C = Ctx(nc, pfx, binds)
    nc, es = C.nc, C.es
    hTd = C.din("hT", [D, T], BF16)
    hsrc = hT_source(hTd, gathered)
    odst = None
    wqd = C.din("wq", [128, 8, 256])
    wkd = C.din("wk", [128, 8, 256])
    wvd = C.din("wv", [128, 8, 256])
    based = C.din("base", [128, 2, 512])
    cbd = C.din("cb", [128, 2, NREL])
    dmaskd = C.din("dmask", [128, 4, 512], BF16)
    lamd = C.din("lam", [128, 256])
    subgd = C.din("subg", [128, 128])
    identd = C.din("ident", [128, 128])
    oTd = C.dout("oT", [256, T], BF16)
    odst = o_dest(oTd, gathered)
    with es:
        S = Sched(nc, es, pfx)
        QT = [C.sb("QT%d" % h, [128, T], BF16) for h in range(2)]
        KT = [C.sb("KT%d" % h, [128, T], BF16) for h in range(2)]
        Vaug = C.sb("Vaug", [128, 64, 2, 129], BF16)
        wq = C.sb("wq", [128, 8, 256], BF16)
        wk = C.sb("wk", [128, 8, 256], BF16)
        wv = C.sb("wv", [128, 8, 256], BF16)
        hTs = [C.sb("hTs%d" % i, [128, 8, 512], BF16) for i in range(2)]
        base = C.sb("base", [128, 2, 512], F32)
        cbt = C.sb("cbt", [128, 2, NREL], F32)
        dmask = C.sb("dmask", [128, 4, 512], BF16)
        tb = [C.sb("tb%d" % i, [128, 512], F32) for i in range(2)]
        Pb = [C.sb("Pb%d" % i, [128, 512], BF16) for i in range(7)]
        lam = C.sb("lam", [128, 256], F32)
        subg = C.sb("subg", [128, 128], F32)
        identf = C.sb("identf", [128, 128], F32)
        identb = C.sb("identb", [128, 128], BF16)
        sm = C.sb("sm", [128, 16], F32)
        lamneg = C.sb("lamneg", [128, 1], F32)
        epsb = C.sb("epsb", [128, 1], F32)
        o0s = C.sb("o0s", [128, 128], F32)
        od = C.sb("od", [128, 128], F32)
        junk = C.sb("junk", [128, 256], F32)
        ob = C.sb("ob", [128, 4, 256], BF16)
        oTs = [C.sb("oTs%d" % i, [128, 2, 512], BF16) for i in range(2)]
        QA = [[C.sb("QA%d_%d" % (m, i), [128, 512], BF16) for i in range(2)] for m in range(2)]
        SB_ = [C.ps("Sb%d" % i, [128, 512], F32) for i in range(3)]
        OB = [[C.ps("O%d%d" % (m, p), [128, 512], F32) for p in range(2)] for m in range(2)]
        PT = C.ps("PT", [128, 1024], BF16)
        S.psum_keys.update(C.psum_names)

        S.dma("sync", identf[:], identd[:, :], writes=["identf"])
        S.op("vector", lambda e: e.tensor_copy(out=identb[:], in_=identf[:]), reads=["identf"], writes=["identb"])
        S.dma("gpsimd", wq[:], wqd[:, :, :], writes=["wq"])
        S.dma("gpsimd", wk[:], wkd[:, :, :], writes=["wk"])
        S.dma("gpsimd", wv[:], wvd[:, :, :], writes=["wv"])
        S.dma("sync", base[:], based[:, :, :], writes=["base"])
        S.dma("sync", cbt[:], cbd[:, :, :], writes=["cbt"])
        S.dma("sync", dmask[:], dmaskd[:, :, :], writes=["dmask"])
        S.dma("sync", lam[:], lamd[:, :], writes=["lam"])
        S.dma("sync", subg[:], subgd[:, :], writes=["subg"])
        S.op("vector", lambda e: e.memset(epsb[:], EPS), writes=["epsb"])
        for m_ in range(2):
            for i_ in range(2):
                S.op("vector", lambda e: e.memset(QA[m_][i_][:], 0.0), writes=[QA[m_][i_].name])
        S.op("vector", lambda e: e.memset(Vaug[:, :, :, 128:129], 1.0), writes=["Vones"])
        S.op("vector", lambda e: e.scalar_tensor_tensor(out=junk[:, 0:64], in0=lam[:, 0:64], scalar=1.0, in1=lam[:, 64:128],
                                                        op0=ALU.mult, op1=ALU.mult, accum_out=sm[:, 0:1]),
             reads=["lam"], writes=["junk", "sm0"])
        S.op("vector", lambda e: e.scalar_tensor_tensor(out=junk[:, 0:64], in0=lam[:, 128:192], scalar=1.0, in1=lam[:, 192:256],
                                                        op0=ALU.mult, op1=ALU.mult, accum_out=sm[:, 1:2]),
             reads=["lam", "junk"], writes=["junk", "sm1"])
        S.op("scalar", lambda e: e.activation(out=sm[:, 0:2], in_=sm[:, 0:2], func=AF.Exp), reads=["sm0", "sm1"],
             writes=["sm0", "sm1"])
        S.op("vector", lambda e: e.tensor_tensor(out=sm[:, 2:3], in0=sm[:, 1:2], in1=sm[:, 0:1], op=ALU.subtract),
             reads=["sm0", "sm1"], writes=["sm2"])
        S.op("vector", lambda e: e.tensor_scalar(out=lamneg[:], in0=sm[:, 2:3], scalar1=-float(lam_init), scalar2=None, op0=ALU.add),
             reads=["sm2"], writes=["lamneg"])
        S.op("vector", lambda e: e.tensor_scalar(out=subg[:], in0=subg[:], scalar1=float(1.0 - lam_init), scalar2=None, op0=ALU.mult),
             reads=["subg"], writes=["subg"])

        cp = [0]

        def evac(dst, src, rk, wk_):
            eng = "scalar" if cp[0] % 2 == 0 else "vector"
            cp[0] += 1
            if eng == "scalar":
                S.op("scalar", lambda e: e.activation(out=dst, in_=src, func=AF.Copy), reads=rk, writes=wk_)
            else:
                S.op("vector", lambda e: e.tensor_copy(out=dst, in_=src), reads=rk, writes=wk_)

        bank = [0]
        for tg in range(16):
            hs = hTs[tg % 2]
            hk = "hTs%d" % (tg % 2)
            S.dma("sync", hs[:], hsrc(tg), writes=[hk])
            for (w, wname, dstl, dname) in [(wq, "wq", QT, "QT"), (wk, "wk", KT, "KT")]:
                for h in range(2):
                    Pp = SB_[bank[0] % 3]
                    bank[0] += 1
                    for k in range(8):
                        S.op("tensor", lambda e: e.matmul(Pp[:], lhsT=w[:, k, h * 128:(h + 1) * 128], rhs=hs[:, k, :],
                                                          start=(k == 0), stop=(k == 7)),
                             reads=[wname, hk], writes=[Pp.name])
                    evac(dstl[h][:, tg * 512:(tg + 1) * 512], Pp[:], [Pp.name], ["%s%d_%d" % (dname, h, tg)])
            for tt in range(4):
                Pp = SB_[bank[0] % 3]
                bank[0] += 1
                for k in range(8):
                    S.op("tensor", lambda e: e.matmul(Pp[:, 0:256], lhsT=hs[:, k, tt * 128:(tt + 1) * 128], rhs=wv[:, k, :],
                                                      start=(k == 0), stop=(k == 7)),
                         reads=["wv", hk], writes=[Pp.name])
                blk = tg * 4 + tt
                evac(Vaug[:, blk, :, 0:128], Pp[:, 0:256].rearrange("p (h e) -> p h e", h=2), [Pp.name], ["V_%d" % blk])

        scale = 64 ** -0.5
        ctr = {"s": 0, "t": 0, "p": 0, "o": 0, "qa": 0}
        for g in range(16):
            for h in range(2):
                tiles = [dict(kb=kb, m=m) for kb in range(4 * g + 4) for m in range(2)]
                first_av = {}
                qas = []
                for m in range(2):
                    qa = QA[m][ctr["qa"] % 2]
                    S.op("gpsimd", lambda e: e.tensor_copy(out=qa[m * 64:(m + 1) * 64, :], in_=QT[h][m * 64:(m + 1) * 64, g * 512:(g + 1) * 512]),
                         reads=["QT%d_%d" % (h, g)], writes=[qa.name])
                    qas.append(qa)
                ctr["qa"] += 1

                def emit_s(tl):
                    kb, m = tl["kb"], tl["m"]
                    Sp = SB_[ctr["s"] % 3]
                    ctr["s"] += 1
                    tl["Sp"] = Sp
                    r = kb - 4 * g
                    S.op("tensor", lambda e: e.matmul(Sp[:], lhsT=KT[h][:, kb * 128:(kb + 1) * 128], rhs=qas[m][:], start=True, stop=(r < 0)),
                         reads=["KT%d_%d" % (h, kb // 4), qas[m].name], writes=[Sp.name])
                    if r >= 0:
                        S.op("tensor", lambda e: e.matmul(Sp[:], lhsT=identb[:], rhs=dmask[:, r, :], start=False, stop=True),
                             reads=["identb", "dmask"], writes=[Sp.name])

                def emit_mid(tl):
                    kb, m, Sp = tl["kb"], tl["m"], tl["Sp"]
                    tt_ = tb[ctr["t"] % 2]
                    ctr["t"] += 1
                    Pt = Pb[ctr["p"] % 7]
                    ctr["p"] += 1
                    tl["P"] = Pt
                    S.op("vector", lambda e: e.scalar_tensor_tensor(out=tt_[:], in0=Sp[:], scalar=scale, in1=base[:, h, :],
                                                                    op0=ALU.mult, op1=ALU.add),
                         reads=[Sp.name, "base"], writes=[tt_.name])
                    rel = 4 * g - kb + 3
                    S.op("scalar", lambda e: e.activation(out=Pt[:], in_=tt_[:], func=AF.Exp, bias=cbt[:, h, rel:rel + 1], scale=1.0),
                         reads=[tt_.name, "cbt"], writes=[Pt.name])

                def emit_av(tl):
                    kb, m, Pt = tl["kb"], tl["m"], tl["P"]
                    r = kb - 4 * g
                    for qb in range(4):
                        if qb < r:
                            continue
                        p = qb // 2
                        O = OB[m][p]
                        st_ = (m, p) not in first_av
                        first_av[(m, p)] = True
                        c0 = (qb % 2) * 129
                        S.op("tensor", lambda e: e.matmul(O[:, c0:c0 + 129], lhsT=Pt[:, qb * 128:(qb + 1) * 128], rhs=Vaug[:, kb, h, :],
                                                          start=st_, stop=(kb == 4 * g + qb), skip_group_check=True),
                             reads=[Pt.name, "V_%d" % kb, "Vones"], writes=[O.name])

                flash_pipeline(S, tiles, emit_s, emit_mid, emit_av)
                for qb in range(4):
                    p = qb // 2
                    c0 = (qb % 2) * 129
                    O0, O1 = OB[0][p], OB[1][p]
                    S.op("vector", lambda e: e.reciprocal(out=sm[:, 4:5], in_=O0[:, c0 + 128:c0 + 129]), reads=[O0.name], writes=["sm4"])
                    S.op("vector", lambda e: e.reciprocal(out=sm[:, 5:6], in_=O1[:, c0 + 128:c0 + 129]), reads=[O1.name], writes=["sm5"])
                    S.op("vector", lambda e: e.tensor_tensor(out=sm[:, 5:6], in0=sm[:, 5:6], in1=lamneg[:], op=ALU.mult),
                         reads=["sm5", "lamneg"], writes=["sm5"])
                    S.op("vector", lambda e: e.tensor_scalar(out=o0s[:], in0=O0[:, c0:c0 + 128], scalar1=sm[:, 4:5], scalar2=None, op0=ALU.mult),
                         reads=[O0.name, "sm4"], writes=["o0s"])
                    S.op("vector", lambda e: e.scalar_tensor_tensor(out=od[:], in0=O1[:, c0:c0 + 128], scalar=sm[:, 5:6], in1=o0s[:],
                                                                    op0=ALU.mult, op1=ALU.add),
                         reads=[O1.name, "sm5", "o0s"], writes=["od"])
                    S.op("scalar", lambda e: e.activation(out=junk[:, 0:128], in_=od[:], func=AF.Square, accum_out=sm[:, 6:7]),
                         reads=["od"], writes=["junk", "sm6"])
                    S.op("scalar", lambda e: e.activation(out=sm[:, 6:7], in_=sm[:, 6:7], func=AF.Sqrt, bias=epsb[:], scale=1.0 / 128),
                         reads=["sm6", "epsb"], writes=["sm6"])
                    S.op("vector", lambda e: e.reciprocal(out=sm[:, 6:7], in_=sm[:, 6:7]), reads=["sm6"], writes=["sm6"])
                    S.op("vector", lambda e: e.scalar_tensor_tensor(out=ob[:, qb, h * 128:(h + 1) * 128], in0=od[:], scalar=sm[:, 6:7],
                                                                    in1=subg[:], op0=ALU.mult, op1=ALU.mult),
                         reads=["od", "sm6", "subg"], writes=["ob"])
            ot = oTs[ctr["o"] % 2]
            otk = "oTs%d" % (ctr["o"] % 2)
            ctr["o"] += 1
            for qb in range(4):
                for c in range(2):
                    S.op("tensor", lambda e: e.transpose(out=PT[:, c * 512 + qb * 128:c * 512 + (qb + 1) * 128],
                                                         in_=ob[:, qb, c * 128:(c + 1) * 128], identity=identb[:]),
                         reads=["ob", "identb"], writes=["PT"])
            S.op("vector", lambda e: e.tensor_copy(out=ot[:], in_=PT[:].rearrange("p (c t) -> p c t", c=2)), reads=["PT"], writes=[otk])
            S.dma("sync", odst(g), ot[:], reads=[otk], writes=["oc%d_%d" % (g // 4, g % 4)], is_output=True)
            if binds.get("cc") is not None and g % 4 == 3:
                binds["cc"].ready(S, g // 4, ["oc%d_%d" % (g // 4, q_) for q_ in range(4)])
        S.close()


def alibi_slopes_np(n):
    return np.exp2(-8.0 * np.arange(1, n + 1, dtype=np.float64) / n)


def diag_mask_tiles(strict):
    jj = np.arange(128)[:, None, None]
    r = np.arange(4)[None, :, None]
    q = np.arange(512)[None, None, :]
    d = q - jj - 128 * r
    ok = d >= (1 if strict else 0)
    return np.where(ok, 0.0, -BIGNEG).astype(np.float32)


def base_tile(slope):
    jj = np.arange(128)[:, None]
    q = np.arange(512)[None, :]
    return (-slope * (q - jj)).astype(np.float32)


def cb_table(slope):
    rel = np.arange(NREL) - 3
    return np.ascontiguousarray(np.broadcast_to((-slope * 128.0 * rel)[None, :], (128, NREL))).astype(np.float32)


def arrange_w(wcols):
    n = wcols.shape[1]
    return np.ascontiguousarray(wcols.reshape(8, 128, n).transpose(1, 0, 2))


def diff_inputs(pfx, w_in, lam, subln_g):
    slopes = alibi_slopes_np(8)
    dm = diag_mask_tiles(False).astype(ml_dtypes.bfloat16)
    lamb = np.ascontiguousarray(np.broadcast_to(lam.reshape(1, 256), (128, 256)))
    sgb = np.ascontiguousarray(np.broadcast_to(subln_g.reshape(1, 128), (128, 128)))
    out = []
    for hg in range(4):
        hs = [2 * hg, 2 * hg + 1]
        cols = np.concatenate([np.arange(h * 128, (h + 1) * 128) for h in hs])
        out.append({
            pfx + "wq": arrange_w(w_in[:, cols]),
            pfx + "wk": arrange_w(w_in[:, 1024 + cols]),
            pfx + "wv": arrange_w(w_in[:, 2048 + cols]),
            pfx + "base": np.ascontiguousarray(np.stack([base_tile(slopes[h]) for h in hs], axis=1)),
            pfx + "cb": np.ascontiguousarray(np.stack([cb_table(slopes[h]) for h in hs], axis=1)),
            pfx + "dmask": dm, pfx + "lam": lamb, pfx + "subg": sgb, pfx + "ident": _IDENT,
        })
    return out


def emit_sb(nc, pfx, binds, gathered):
    C = Ctx(nc, pfx, binds)
    nc, es = C.nc, C.es
    hTd = C.din("hT", [D, T], BF16)
    hsrc = hT_source(hTd, gathered)
    odst = None
    wqd = C.din("wq", [128, 8, 256])
    wkd = C.din("wk", [128, 8, 256])
    wvd = C.din("wv", [128, 8, 256])
    m01d = C.din("m01", [128, 4, 512], BF16)
    trid = C.din("tri", [128, 2, 128], BF16)
    identd = C.din("ident", [128, 128])
    oTd = C.dout("oT", [256, T], BF16)
    odst64 = o_dest(oTd, gathered, 64)
    with es:
        S = Sched(nc, es, pfx)
        QT = [C.sb("QT%d" % h, [128, T], BF16) for h in range(2)]
        KT = [C.sb("KT%d" % h, [128, T], BF16) for h in range(2)]
        V = C.sb("V", [128, 64, 256], BF16)
        wq = C.sb("wq", [128, 8, 256], BF16)
        wk = C.sb("wk", [128, 8, 256], BF16)
        wv = C.sb("wv", [128, 8, 256], BF16)
        hTs = [C.sb("hTs%d" % i, [128, 8, 512], BF16) for i in range(2)]
        m01 = C.sb("m01", [128, 4, 512], BF16)
        tri = C.sb("tri", [128, 2, 128], BF16)
        eb = [C.sb("eb%d" % i, [128, 512], F32) for i in range(4)]
        spb = [C.sb("spb%d" % i, [128, 512], BF16) for i in range(4)]
        wb = [C.sb("wb%d" % i, [128, 512], F32) for i in range(2)]
        ab = [C.sb("ab%d" % i, [128, 512], BF16) for i in range(3)]
        identf = C.sb("identf", [128, 128], F32)
        identb = C.sb("identb", [128, 128], BF16)
        obT = [C.sb("obT%d" % i, [64, 4, 512], BF16) for i in range(2)]
        QA = [C.sb("QA%d" % hh_, [128, 512], BF16) for hh_ in range(4)]
        ZB = [C.ps("Zb%d" % i, [128, 512], F32) for i in range(3)]
        XB = [C.ps("Xb%d" % i, [128, 512], F32) for i in range(2)]
        OBk = [C.ps("Ob%d" % i, [128, 512], F32) for i in range(2)]
        PT = C.ps("PT", [128, 1024], BF16)
        S.psum_keys.update(C.psum_names)

        S.dma("sync", identf[:], identd[:, :], writes=["identf"])
        S.op("vector", lambda e: e.tensor_copy(out=identb[:], in_=identf[:]), reads=["identf"], writes=["identb"])
        S.dma("gpsimd", wq[:], wqd[:, :, :], writes=["wq"])
        S.dma("gpsimd", wk[:], wkd[:, :, :], writes=["wk"])
        S.dma("gpsimd", wv[:], wvd[:, :, :], writes=["wv"])
        S.dma("sync", m01[:], m01d[:, :, :], writes=["m01"])
        for hh_ in range(4):
            S.op("vector", lambda e: e.memset(QA[hh_][:], 0.0), writes=[QA[hh_].name])
        S.dma("sync", tri[:], trid[:, :, :], writes=["tri"])

        cp = [0]

        def evac(dst, src, rk, wk_):
            eng = "scalar" if cp[0] % 2 == 0 else "vector"
            cp[0] += 1
            if eng == "scalar":
                S.op("scalar", lambda e: e.activation(out=dst, in_=src, func=AF.Copy), reads=rk, writes=wk_)
            else:
                S.op("vector", lambda e: e.tensor_copy(out=dst, in_=src), reads=rk, writes=wk_)

        bank = [0]
        for tg in range(16):
            hs = hTs[tg % 2]
            hk = "hTs%d" % (tg % 2)
            S.dma("sync", hs[:], hsrc(tg), writes=[hk])
            for (w, wname, dstl, dname) in [(wq, "wq", QT, "QT"), (wk, "wk", KT, "KT")]:
                for h in range(2):
                    Pp = ZB[bank[0] % 3]
                    bank[0] += 1
                    for k in range(8):
                        S.op("tensor", lambda e: e.matmul(Pp[:], lhsT=w[:, k, h * 128:(h + 1) * 128], rhs=hs[:, k, :],
                                                          start=(k == 0), stop=(k == 7)),
                             reads=[wname, hk], writes=[Pp.name])
                    evac(dstl[h][:, tg * 512:(tg + 1) * 512], Pp[:], [Pp.name], ["%s%d_%d" % (dname, h, tg)])
            for tt in range(4):
                Pp = ZB[bank[0] % 3]
                bank[0] += 1
                for k in range(8):
                    S.op("tensor", lambda e: e.matmul(Pp[:, 0:256], lhsT=hs[:, k, tt * 128:(tt + 1) * 128], rhs=wv[:, k, :],
                                                      start=(k == 0), stop=(k == 7)),
                         reads=["wv", hk], writes=[Pp.name])
                blk = tg * 4 + tt
                evac(V[:, blk, :], Pp[:, 0:256], [Pp.name], ["V_%d" % blk])

        scale = 64 ** -0.5
        ctr = {"z": 0, "e": 0, "w": 0, "a": 0, "o": 0, "chain": 0}
        for g in range(16):
            for hh in range(4):
                p_, half_ = hh // 2, hh % 2
                S.op("gpsimd", lambda e: e.tensor_copy(out=QA[hh][half_ * 64:(half_ + 1) * 64, :],
                                                       in_=QT[p_][half_ * 64:(half_ + 1) * 64, g * 512:(g + 1) * 512]),
                     reads=["QT%d_%d" % (p_, g)], writes=[QA[hh].name])
            chains = []
            for hh in range(4):
                ch = ctr["chain"]
                ctr["chain"] += 1
                kbs = list(range(4 * g + 3, -1, -1))
                av0 = [True]
                chains.append([dict(hh=hh, kb=kb, first=(i == 0), last=(i == len(kbs) - 1), X=XB[ch % 2], O=OBk[ch % 2], av0=av0)
                               for i, kb in enumerate(kbs)])
            tiles = []
            for pr in range(2):
                for ta, tb_ in zip(chains[2 * pr], chains[2 * pr + 1]):
                    tiles += [ta, tb_]

            def emit_Z(tl):
                hh, kb = tl["hh"], tl["kb"]
                p, half = hh // 2, hh % 2
                Zp = ZB[ctr["z"] % 3]
                ctr["z"] += 1
                tl["Z"] = Zp
                S.op("tensor", lambda e: e.matmul(Zp[:], lhsT=KT[p][:, kb * 128:(kb + 1) * 128], rhs=QA[hh][:], start=True, stop=True),
                     reads=["KT%d_%d" % (p, kb // 4), QA[hh].name], writes=[Zp.name])

            def emit_esp(tl):
                kb, Zp = tl["kb"], tl["Z"]
                i = ctr["e"] % 4
                ctr["e"] += 1
                tl["e"], tl["sp"] = eb[i], spb[i]
                r = kb - 4 * g
                S.op("scalar", lambda e: e.activation(out=eb[i][:], in_=Zp[:], func=AF.Exp, scale=scale), reads=[Zp.name], writes=[eb[i].name])
                S.op("scalar", lambda e: e.activation(out=spb[i][:], in_=eb[i][:], func=AF.Ln, bias=1.0, scale=1.0),
                     reads=[eb[i].name], writes=[spb[i].name])
                if r >= 0:
                    S.op("gpsimd", lambda e: e.tensor_tensor(out=spb[i][:], in0=spb[i][:], in1=m01[:, r, :], op=ALU.mult),
                         reads=[spb[i].name, "m01"], writes=[spb[i].name])
                    S.op("gpsimd", lambda e: e.tensor_tensor(out=eb[i][:], in0=eb[i][:], in1=m01[:, r, :], op=ALU.mult),
                         reads=[eb[i].name, "m01"], writes=[eb[i].name])

            def emit_L(tl):
                X, sp = tl["X"], tl["sp"]
                S.op("tensor", lambda e: e.matmul(X[:], lhsT=tri[:, 0, :], rhs=sp[:], start=tl["first"], stop=False, skip_group_check=True),
                     reads=["tri", sp.name], writes=[X.name])

            def emit_w(tl):
                X = tl["X"]
                wi = wb[ctr["w"] % 2]
                ctr["w"] += 1
                tl["w"] = wi
                S.op("scalar", lambda e: e.activation(out=wi[:], in_=X[:], func=AF.Exp, scale=-1.0), reads=[X.name], writes=[wi.name])

            def emit_U(tl):
                X, sp = tl["X"], tl["sp"]
                S.op("tensor", lambda e: e.matmul(X[:], lhsT=tri[:, 1, :], rhs=sp[:], start=False, stop=tl["last"], skip_group_check=True),
                     reads=["tri", sp.name], writes=[X.name])

            def emit_a(tl):
                ee, wi = tl["e"], tl["w"]
                ai = ab[ctr["a"] % 3]
                ctr["a"] += 1
                tl["a"] = ai
                S.op("vector", lambda e: e.tensor_tensor(out=ai[:], in0=ee[:], in1=wi[:], op=ALU.mult),
                     reads=[ee.name, wi.name], writes=[ai.name])

            obt = obT[g % 2]

            def stage2(tl):
                hh, kb, O, ai = tl["hh"], tl["kb"], tl["O"], tl["a"]
                st_ = tl["av0"][0]
                tl["av0"][0] = False
                S.op("tensor", lambda e: e.matmul(O[0:64, :], lhsT=V[:, kb, hh * 64:(hh + 1) * 64], rhs=ai[:],
                                                  start=st_, stop=(kb == 0), skip_group_check=True),
                     reads=[ai.name, "V_%d" % kb], writes=[O.name])
                if tl["last"]:
                    S.op("vector", lambda e: e.tensor_copy(out=obt[:, hh, :], in_=O[0:64, :]), reads=[O.name], writes=[obt.name])

            n = len(tiles)
            emit_Z(tiles[0])
            for i in range(n + 2):
                if 1 <= i <= n:
                    emit_L(tiles[i - 1])
                if 2 <= i:
                    emit_U(tiles[i - 2])
                if i + 1 < n:
                    emit_Z(tiles[i + 1])
                if 2 <= i:
                    stage2(tiles[i - 2])
                if i < n:
                    emit_esp(tiles[i])
                if 1 <= i <= n:
                    emit_w(tiles[i - 1])
                    emit_a(tiles[i - 1])
            S.dma("sync", odst64(g), obt[:], reads=[obt.name], writes=["oc%d_%d" % (g // 4, g % 4)], is_output=True)
            if binds.get("cc") is not None and g % 4 == 3:
                binds["cc"].ready(S, g // 4, ["oc%d_%d" % (g // 4, q_) for q_ in range(4)])
        S.close()


def sb_inputs(pfx, w_in):
    jj = np.arange(128)[:, None, None]
    r = np.arange(4)[None, :, None]
    q = np.arange(512)[None, None, :]
    m01 = ((q - jj - 128 * r) >= 1).astype(np.float32).astype(ml_dtypes.bfloat16)
    mm = np.arange(128)[:, None]
    j2 = np.arange(128)[None, :]
    tri = np.ascontiguousarray(np.stack([(mm >= j2), (mm < j2)], axis=1).astype(np.float32).astype(ml_dtypes.bfloat16))
    out = []
    for hg in range(4):
        cols = np.arange(hg * 256, (hg + 1) * 256)
        out.append({
            pfx + "wq": arrange_w(w_in[:, cols]),
            pfx + "wk": arrange_w(w_in[:, 1024 + cols]),
            pfx + "wv": arrange_w(w_in[:, 2048 + cols]),
            pfx + "m01": m01, pfx + "tri": tri, pfx + "ident": _IDENT,
        })
    return out


NSA_FORCE = 1e4
NSA_NEG = -1e30


class _Stop(Exception):
    pass


def emit_nsa(nc, pfx, binds, gathered, dbg=None):
    C = Ctx(nc, pfx, binds)
    nc, es = C.nc, C.es
    hTd = C.din("hT", [D, T], BF16)
    hsrc = hT_source(hTd, gathered)
    odst = None
    wfmd = C.din("wfm", [128, 8, 640])
    wtmd = C.din("wtm", [128, 8, 140])
    cw1d = C.din("cw1", [128, 32, 256])
    cped = C.din("cpe", [128, 32])
    cw2kd = C.din("cw2k", [128, 2, 128])
    cw2vd = C.din("cw2v", [128, 2, 64])
    ovld = C.din("ovl", [128, 4, 128], BF16)
    slpd = C.din("slp", [128, 4])
    cbd = C.din("cb", [128, 4, NREL])
    cbcd = C.din("cbc", [128, 4, 16])
    base0d = C.din("base0", [128, 512])
    basec0d = C.din("basec0", [128, 512])
    cmaskd = C.din("cmask", [128, 5, 512], BF16)
    dmaskd = C.din("dmask", [128, 4, 512], BF16)
    wmaskd = C.din("wmask", [128, 8, 512], BF16)
    indd = C.din("ind", [128, T], BF16)
    adjd = C.din("adj", [64, 128, 128])
    identd = C.din("ident", [128, 128])
    oTd = C.dout("oT", [256, T], BF16)
    odst = o_dest(oTd, gathered)
    with es:
        S = Sched(nc, es, pfx)
        try:
            QT = [C.sb("QT%d" % h, [128, T], BF16) for h in range(2)]
            ksT = C.sb("ksT", [128, T], BF16)
            kwT = C.sb("kwT", [128, T], BF16)
            kcvT = C.sb("kcvT", [128, T], BF16)
            vsA = C.sb("vsA", [128, 64, 65], BF16)
            vwA = C.sb("vwA", [128, 64, 65], BF16)
            gates = C.sb("gates", [128, 64, 12], F32)
            PBUF = C.sb("PBUF", [128, 14464], BF16)
            hTs = [PBUF[:, i * 4096:(i + 1) * 4096].rearrange("p (k t) -> p k t", k=8) for i in range(2)]
            wfm = PBUF[:, 8192:8192 + 5120].rearrange("p (k n) -> p k n", k=8)
            wtm = PBUF[:, 13312:13312 + 1120].rearrange("p (k n) -> p k n", k=8)
            cw1 = PBUF[:, 0:8192].rearrange("p (l f) -> p l f", l=32)
            ind = PBUF[:, 0:8192]
            cpe = C.sb("cpe", [128, 32], BF16)
            cw2k = C.sb("cw2k", [128, 2, 128], BF16)
            cw2v = C.sb("cw2v", [128, 2, 64], BF16)
            slp = C.sb("slp", [128, 4], F32)
            cbt = C.sb("cbt", [128, 4, NREL], F32)
            cbct = C.sb("cbct", [128, 4, 16], F32)
            base0 = C.sb("base0", [128, 512], F32)
            basec0 = C.sb("basec0", [128, 512], F32)
            cmask = C.sb("cmask", [128, 5, 512], BF16)
            dmask = C.sb("dmask", [128, 4, 512], BF16)
            wmask = C.sb("wmask", [128, 8, 512], BF16)
            tb = [C.sb("tb%d" % i, [128, 512], F32) for i in range(3)]
            Pb = [C.sb("Pb%d" % i, [128, 512], BF16) for i in range(7)]
            kcmpT = C.sb("kcmpT", [128, 512], BF16)
            vcA = C.sb("vcA", [128, 4, 193], BF16)
            glb = [C.sb("glb%d" % i, [128, 512], BF16) for i in range(4)]
            peb = C.sb("peb", [128, 4], F32)
            imp = C.sb("imp", [128, 4, 128], F32)
            adjt = [C.sb("adjt%d" % i, [128, 128], F32) for i in range(2)]
            impa = C.sb("impa", [128, 128], F32)
            impb = C.sb("impb", [128, 128], F32)
            m8 = C.sb("m8", [128, 16], F32)
            selb = C.sb("selb", [128, 128], BF16)
            MBT = [C.sb("MBT%d" % i, [128, 512], BF16) for i in range(2)]
            acco = C.sb("acco", [128, 4, 256], F32)
            ob = C.sb("ob", [128, 4, 256], BF16)
            oTs = [C.sb("oTs%d" % i, [128, 2, 512], BF16) for i in range(2)]
            sm = C.sb("sm", [128, 8], F32)
            otf = C.sb("otf", [65, 512], F32)
            QA = [[C.sb("QA%d_%d" % (hf, i), [128, 512], BF16) for i in range(2)] for hf in range(2)]
            identf = C.sb("identf", [128, 128], F32)
            identb = C.sb("identb", [128, 128], BF16)
            SB_ = [C.ps("Sb%d" % i, [128, 512], F32) for i in range(3)]
            AC = [C.ps("Ac%d" % i, [128, 512], F32) for i in range(4)]
            PT = C.ps("PT", [128, 1024], BF16)
            S.psum_keys.update(C.psum_names)

            S.dma("sync", identf[:], identd[:, :], writes=["identf"])
            S.op("vector", lambda e: e.tensor_copy(out=identb[:], in_=identf[:]), reads=["identf"], writes=["identb"])
            S.dma("gpsimd", wfm, wfmd[:, :, :], writes=["wfm"])
            S.dma("gpsimd", wtm, wtmd[:, :, :], writes=["wtm"])
            S.dma("gpsimd", cpe[:], cped[:, :], writes=["cpe"])
            S.dma("gpsimd", cw2k[:], cw2kd[:, :, :], writes=["cw2k"])
            S.dma("gpsimd", cw2v[:], cw2vd[:, :, :], writes=["cw2v"])
            for (dst, src, key) in [(slp, slpd, "slp"), (cbt, cbd, "cbt"), (cbct, cbcd, "cbct"), (base0, base0d, "base0"),
                                    (basec0, basec0d, "basec0"), (cmask, cmaskd, "cmask"), (dmask, dmaskd, "dmask"),
                                    (wmask, wmaskd, "wmask")]:
                S.dma("sync", dst[:], src, writes=[key])
            for hf in range(2):
                for i in range(2):
                    S.op("vector", lambda e: e.memset(QA[hf][i][:], 0.0), writes=[QA[hf][i].name])
            S.op("vector", lambda e: e.memset(vsA[:, :, 64:65], 1.0), writes=["vsones"])
            S.op("vector", lambda e: e.memset(vwA[:, :, 64:65], 1.0), writes=["vwones"])
            S.op("vector", lambda e: e.memset(vcA[:], 0.0), writes=["vcA"])
            S.op("vector", lambda e: e.memset(kcmpT[:], 0.0), writes=["kcmpT"])
            S.op("vector", lambda e: e.memset(vcA[:, :, 64:65], 1.0), reads=["vcA"], writes=["vcA"])
            S.dma("sync", vcA[:, :, 65:193], ovld[:, :, :], reads=["vcA"], writes=["vcA"])

            if dbg == 'const':
                raise _Stop
            cp = [0]

            def evac(dst, src, rk, wk_, scale=None):
                eng = "scalar" if cp[0] % 2 == 0 else "vector"
                cp[0] += 1
                if eng == "scalar" and scale is None:
                    S.op("scalar", lambda e: e.activation(out=dst, in_=src, func=AF.Copy), reads=rk, writes=wk_)
                else:
                    if scale is None:
                        S.op("vector", lambda e: e.tensor_copy(out=dst, in_=src), reads=rk, writes=wk_)
                    else:
                        S.op("vector", lambda e: e.tensor_scalar(out=dst, in0=src, scalar1=float(scale), scalar2=None, op0=ALU.mult),
                             reads=rk, writes=wk_)

            bank = [0]
            fm_dst = [(QT[0], "QT0", 0.125), (QT[1], "QT1", 0.125), (ksT, "ksT", None), (kwT, "kwT", None), (kcvT, "kcvT", None)]
            for tg in range(1 if dbg in ('proj1', 'proj1ns') else 16):
                hs = hTs[tg % 2]
                hk = "hTs%d" % (tg % 2)
                S.dma("sync", hs, hsrc(tg), writes=[hk])
                for fi, (dst, dname, sc) in enumerate(fm_dst):
                    Pp = SB_[bank[0] % 3]
                    bank[0] += 1
                    for k in range(8):
                        S.op("tensor", lambda e: e.matmul(Pp[:], lhsT=wfm[:, k, fi * 128:(fi + 1) * 128], rhs=hs[:, k, :],
                                                          start=(k == 0), stop=(k == 7)),
                             reads=["wfm", hk], writes=[Pp.name])
                    evac(dst[:, tg * 512:(tg + 1) * 512], Pp[:], [Pp.name], ["%s_%d" % (dname, tg)], scale=sc)
                for tt in range(4):
                    Pp = SB_[bank[0] % 3]
                    bank[0] += 1
                    for k in range(8):
                        S.op("tensor", lambda e: e.matmul(Pp[:, 0:140], lhsT=hs[:, k, tt * 128:(tt + 1) * 128], rhs=wtm[:, k, :],
                                                          start=(k == 0), stop=(k == 7)),
                             reads=["wtm", hk], writes=[Pp.name])
                    blk = tg * 4 + tt
                    S.op("vector", lambda e: e.tensor_copy(out=vsA[:, blk, 0:64], in_=Pp[:, 0:64]), reads=[Pp.name], writes=["vs_%d" % blk])
                    S.op("vector", lambda e: e.tensor_copy(out=vwA[:, blk, 0:64], in_=Pp[:, 64:128]), reads=[Pp.name], writes=["vw_%d" % blk])
                    S.op("scalar", lambda e: e.activation(out=gates[:, blk, :], in_=Pp[:, 128:140], func=AF.Exp, scale=-1.0),
                         reads=[Pp.name], writes=["gates_%d" % blk])
                    S.op("vector", lambda e: e.tensor_scalar(out=gates[:, blk, :], in0=gates[:, blk, :], scalar1=1.0, scalar2=None, op0=ALU.add),
                         reads=["gates_%d" % blk], writes=["gates_%d" % blk])
                    S.op("vector", lambda e: e.reciprocal(out=gates[:, blk, :], in_=gates[:, blk, :]),
                         reads=["gates_%d" % blk], writes=["gates_%d" % blk])
            if dbg in ('proj', 'proj1', 'proj1ns'):
                raise _Stop
            S.barrier()

            S.dma("gpsimd", cw1, cw1d[:, :, :], writes=["cw1"])
            kcv = kcvT[:, :].rearrange("p (n s) -> p n s", s=16)
            for j in range(2):
                lo, hi = j * 64, (j + 1) * 64
                for c in range(2):
                    Pp = SB_[bank[0] % 3]
                    bank[0] += 1
                    Pq = AC[0]
                    for l in range(32):
                        S.op("tensor", lambda e: e.matmul(Pq[:, 0:1], lhsT=cw1[lo:hi, l, c * 128:(c + 1) * 128], rhs=cpe[lo:hi, l:l + 1],
                                                          start=(l == 0), stop=(l == 31)),
                             reads=["cw1", "cpe"], writes=[Pq.name])
                    col = j * 2 + c
                    S.op("vector", lambda e: e.tensor_copy(out=peb[:, col:col + 1], in_=Pq[:, 0:1]), reads=[Pq.name], writes=["peb%d" % col])
                    for l in range(32):
                        S.op("tensor", lambda e: e.matmul(Pp[:, 0:511], lhsT=cw1[lo:hi, l, c * 128:(c + 1) * 128],
                                                          rhs=kcv[lo:hi, (l // 16):(l // 16) + 511, l % 16],
                                                          start=(l == 0), stop=(l == 31)),
                             reads=["cw1"] + ["kcvT_%d" % t_ for t_ in range(16)], writes=[Pp.name])
                    xg, x2, ug = tb[0], tb[1], tb[2]
                    S.op("scalar", lambda e: e.activation(out=xg[:, 0:511], in_=Pp[:, 0:511], func=AF.Identity, bias=peb[:, col:col + 1], scale=1.0),
                         reads=[Pp.name, "peb%d" % col], writes=[xg.name])
                    S.op("vector", lambda e: e.tensor_tensor(out=x2[:, 0:511], in0=xg[:, 0:511], in1=xg[:, 0:511], op=ALU.mult),
                         reads=[xg.name], writes=[x2.name])
                    S.op("vector", lambda e: e.tensor_scalar(out=x2[:, 0:511], in0=x2[:, 0:511], scalar1=0.044715, scalar2=1.0,
                                                             op0=ALU.mult, op1=ALU.add), reads=[x2.name], writes=[x2.name])
                    S.op("vector", lambda e: e.tensor_tensor(out=ug[:, 0:511], in0=x2[:, 0:511], in1=xg[:, 0:511], op=ALU.mult),
                         reads=[x2.name, xg.name], writes=[ug.name])
                    S.op("scalar", lambda e: e.activation(out=ug[:, 0:511], in_=ug[:, 0:511], func=AF.Exp, scale=-1.5957691216057308),
                         reads=[ug.name], writes=[ug.name])
                    S.op("vector", lambda e: e.tensor_scalar(out=ug[:, 0:511], in0=ug[:, 0:511], scalar1=1.0, scalar2=None, op0=ALU.add),
                         reads=[ug.name], writes=[ug.name])
                    S.op("vector", lambda e: e.reciprocal(out=ug[:, 0:511], in_=ug[:, 0:511]), reads=[ug.name], writes=[ug.name])
                    gl = glb[j * 2 + c]
                    S.op("vector", lambda e: e.memset(gl[:, 511:512], 0.0), writes=[gl.name])
                    S.op("vector", lambda e: e.tensor_tensor(out=gl[:, 0:511], in0=ug[:, 0:511], in1=xg[:, 0:511], op=ALU.mult),
                         reads=[ug.name, xg.name, gl.name], writes=[gl.name])
            if dbg == 'cmp1':
                raise _Stop
            Pp = SB_[bank[0] % 3]
            bank[0] += 1
            for c in range(2):
                S.op("tensor", lambda e: e.matmul(Pp[:, 0:511], lhsT=cw2k[:, c, :], rhs=glb[c][:, 0:511], start=(c == 0), stop=(c == 1)),
                     reads=["cw2k", glb[c].name], writes=[Pp.name])
            S.op("vector", lambda e: e.tensor_copy(out=kcmpT[:, 0:511], in_=Pp[:, 0:511]), reads=[Pp.name, "kcmpT"], writes=["kcmpT"])
            for nt in range(4):
                nn = 128 if nt < 3 else 127
                Pp = SB_[bank[0] % 3]
                bank[0] += 1
                for c in range(2):
                    S.op("tensor", lambda e: e.matmul(Pp[0:nn, 0:64], lhsT=glb[2 + c][:, nt * 128:nt * 128 + nn], rhs=cw2v[:, c, :],
                                                      start=(c == 0), stop=(c == 1)),
                         reads=["cw2v", glb[2 + c].name], writes=[Pp.name])
                S.op("vector", lambda e: e.tensor_copy(out=vcA[0:nn, nt, 0:64], in_=Pp[0:nn, 0:64]), reads=[Pp.name, "vcA"], writes=["vcA"])
            if dbg == 'cmp2':
                raise _Stop
            S.barrier()
            S.dma("sync", ind, indd[:, :], writes=["ind"])

            ctr = {"s": 0, "t": 0, "p": 0, "o": 0, "ac": 0, "adj": 0, "mbt": 0}

            qactr = [0]

            def run_branch(g, hh, tiles, kT, vA, vkey, ncol, accs, acc_cols, basetile, bkey, cbtab, cbkey, accT=None):
                p, half = hh // 2, hh % 2
                firsts = {}
                qa = QA[half][qactr[0] % 2]
                qactr[0] += 1
                S.op("gpsimd", lambda e: e.tensor_copy(out=qa[half * 64:(half + 1) * 64, :],
                                                       in_=QT[p][half * 64:(half + 1) * 64, g * 512:(g + 1) * 512]),
                     reads=["QT%d_%d" % (p, g)], writes=[qa.name])

                def emit_s(tl):
                    Sp = SB_[ctr["s"] % 3]
                    ctr["s"] += 1
                    tl["Sp"] = Sp
                    kb = tl["kb"]
                    mms = [(kT[:, kb * 128:(kb + 1) * 128], qa[:], tl["kkeys"] + [qa.name])]
                    if tl.get("extra") is not None:
                        mms.append(tl["extra"])
                    if tl.get("mask") is not None:
                        mms.append((identb[:], tl["mask"], ["identb", "cmask", "dmask", "wmask"]))
                    for i, (l_, r_, keys) in enumerate(mms):
                        S.op("tensor", lambda e: e.matmul(Sp[:], lhsT=l_, rhs=r_, start=(i == 0), stop=(i == len(mms) - 1)),
                             reads=keys, writes=[Sp.name])

                def emit_mid(tl):
                    Sp = tl["Sp"]
                    tt_ = tb[ctr["t"] % 3]
                    ctr["t"] += 1
                    Pt = Pb[ctr["p"] % 7]
                    ctr["p"] += 1
                    tl["P"] = Pt
                    S.op("vector", lambda e: e.scalar_tensor_tensor(out=tt_[:], in0=basetile[:], scalar=slp[:, hh:hh + 1], in1=Sp[:],
                                                                    op0=ALU.mult, op1=ALU.add),
                         reads=[Sp.name, bkey, "slp"], writes=[tt_.name])
                    ci = tl["cbi"]
                    S.op("scalar", lambda e: e.activation(out=Pt[:], in_=tt_[:], func=AF.Exp, bias=cbtab[:, hh, ci:ci + 1], scale=1.0),
                         reads=[tt_.name, cbkey], writes=[Pt.name])

                def emit_av(tl):
                    Pt, kb = tl["P"], tl["kb"]
                    if accT is not None:
                        st_ = accT.name not in firsts
                        firsts[accT.name] = True
                        S.op("tensor", lambda e: e.matmul(accT[0:ncol, :], lhsT=vA[:, kb, :], rhs=Pt[:], start=st_, stop=False,
                                                          skip_group_check=True),
                             reads=[Pt.name] + tl["vkeys"], writes=[accT.name])
                        return
                    for qb in tl["qbs"]:
                        acc, c0 = accs[qb], acc_cols[qb]
                        st_ = acc.name not in firsts
                        firsts[acc.name] = True
                        S.op("tensor", lambda e: e.matmul(acc[:, c0:c0 + ncol], lhsT=Pt[:, qb * 128:(qb + 1) * 128], rhs=vA[:, kb, :],
                                                          start=st_, stop=False, skip_group_check=True),
                             reads=[Pt.name] + tl["vkeys"], writes=[acc.name])

                flash_pipeline(S, tiles, emit_s, emit_mid, emit_av)
                if accT is not None:
                    TP = accs[0]
                    S.op("vector", lambda e: e.tensor_copy(out=otf[0:ncol, :], in_=accT[0:ncol, :]), reads=[accT.name], writes=["otf"])
                    for qb in range(4):
                        S.op("tensor", lambda e: e.transpose(out=TP[:, acc_cols[qb]:acc_cols[qb] + ncol], in_=otf[0:ncol, qb * 128:(qb + 1) * 128],
                                                             identity=identf[0:ncol, 0:ncol]),
                             reads=["otf", "identf"], writes=[TP.name])

            for g in range(16):
                ntmax = (512 * g + 480) // 2048
                for hh in range(4):
                    a0 = AC[(ctr["ac"] % 2) * 2]
                    a1 = AC[(ctr["ac"] % 2) * 2 + 1]
                    ctr["ac"] += 1
                    accs = [a0, a0, a1, a1]
                    cols = [0, 193, 0, 193]
                    tiles = []
                    for nt in range(ntmax + 1):
                        rel2 = g - 4 * nt
                        tiles.append(dict(kb=nt, kkeys=["kcmpT"], vkeys=["vcA"], mask=(cmask[:, rel2, :] if rel2 <= 4 else None),
                                          cbi=rel2, qbs=[0, 1, 2, 3]))
                    run_branch(g, hh, tiles, kcmpT, vcA, "vcA", 193, accs, cols, basec0, "basec0", cbct, "cbct")
                    for qb in range(4):
                        acc, c0 = accs[qb], cols[qb]
                        blk = g * 4 + qb
                        S.op("vector", lambda e: e.tensor_scalar(out=sm[:, 0:1], in0=acc[:, c0 + 64:c0 + 65], scalar1=1e-30, scalar2=None, op0=ALU.max),
                             reads=[acc.name], writes=["sm0"])
                        S.op("vector", lambda e: e.reciprocal(out=sm[:, 0:1], in_=sm[:, 0:1]), reads=["sm0"], writes=["sm0"])
                        S.op("vector", lambda e: e.tensor_tensor(out=sm[:, 1:2], in0=sm[:, 0:1], in1=gates[:, blk, hh * 3:hh * 3 + 1], op=ALU.mult),
                             reads=["sm0", "gates_%d" % blk], writes=["sm1"])
                        S.op("vector", lambda e: e.tensor_scalar(out=acco[:, qb, hh * 64:(hh + 1) * 64], in0=acc[:, c0:c0 + 64],
                                                                 scalar1=sm[:, 1:2], scalar2=None, op0=ALU.mult),
                             reads=[acc.name, "sm1"], writes=["acco%d" % qb])
                        if hh == 0:
                            S.op("vector", lambda e: e.tensor_scalar(out=imp[:, qb, :], in0=acc[:, c0 + 65:c0 + 193], scalar1=sm[:, 0:1],
                                                                     scalar2=None, op0=ALU.mult),
                                 reads=[acc.name, "sm0"], writes=["imp%d" % qb])
                        else:
                            S.op("vector", lambda e: e.scalar_tensor_tensor(out=imp[:, qb, :], in0=acc[:, c0 + 65:c0 + 193], scalar=sm[:, 0:1],
                                                                            in1=imp[:, qb, :], op0=ALU.mult, op1=ALU.add),
                                 reads=[acc.name, "sm0", "imp%d" % qb], writes=["imp%d" % qb])
                if dbg == 'g0c':
                    raise _Stop
                mbt = MBT[ctr["mbt"] % 2]
                ctr["mbt"] += 1
                for qb in range(4):
                    blk = g * 4 + qb
                    at = adjt[ctr["adj"] % 2]
                    ctr["adj"] += 1
                    S.dma("sync", at[:], adjd[blk], writes=[at.name])
                    S.op("vector", lambda e: e.tensor_tensor(out=impa[:], in0=imp[:, qb, :], in1=at[:], op=ALU.add),
                         reads=["imp%d" % qb, at.name], writes=["impa"])
                    S.op("vector", lambda e: e.max(out=m8[:, 0:8], in_=impa[:]), reads=["impa"], writes=["m8a"])
                    S.op("vector", lambda e: e.match_replace(out=impb[:], in_to_replace=m8[:, 0:8], in_values=impa[:], imm_value=-3.0e38),
                         reads=["impa", "m8a"], writes=["impb"])
                    S.op("vector", lambda e: e.max(out=m8[:, 8:16], in_=impb[:]), reads=["impb"], writes=["m8b"])
                    S.op("vector", lambda e: e.tensor_scalar(out=selb[:], in0=impa[:], scalar1=m8[:, 15:16], scalar2=1.0,
                                                             op0=ALU.is_ge, op1=ALU.subtract),
                         reads=["impa", "m8b"], writes=["selb"])
                    S.op("tensor", lambda e: e.transpose(out=PT[:, qb * 128:(qb + 1) * 128], in_=selb[:], identity=identb[:]),
                         reads=["selb", "identb"], writes=["PT"])
                S.op("vector", lambda e: e.tensor_copy(out=mbt[:], in_=PT[:, 0:512]), reads=["PT"], writes=[mbt.name])
                if dbg == 'g0k':
                    raise _Stop
                for hh in range(4):
                    a0 = AC[ctr["ac"] % 4]
                    aT = None
                    ctr["ac"] += 1
                    accs = [a0] * 4
                    cols = [0, 65, 130, 195]
                    tiles = []
                    for kb in range(4 * g + 4):
                        r = kb - 4 * g
                        tiles.append(dict(kb=kb, kkeys=["ksT_%d" % (kb // 4)], vkeys=["vs_%d" % kb, "vsones"],
                                          extra=(ind[:, kb * 128:(kb + 1) * 128], mbt[:], ["ind", mbt.name]),
                                          mask=(dmask[:, r, :] if r >= 0 else None), cbi=4 * g - kb + 3,
                                          qbs=[qb for qb in range(4) if qb >= r]))
                    run_branch(g, hh, tiles, ksT, vsA, "vs", 65, accs, cols, base0, "base0", cbt, "cbt", accT=aT)
                    for qb in range(4):
                        c0 = cols[qb]
                        blk = g * 4 + qb
                        S.op("vector", lambda e: e.reciprocal(out=sm[:, 2:3], in_=a0[:, c0 + 64:c0 + 65]), reads=[a0.name], writes=["sm2"])
                        S.op("vector", lambda e: e.tensor_tensor(out=sm[:, 3:4], in0=sm[:, 2:3], in1=gates[:, blk, hh * 3 + 1:hh * 3 + 2], op=ALU.mult),
                             reads=["sm2", "gates_%d" % blk], writes=["sm3"])
                        S.op("vector", lambda e: e.scalar_tensor_tensor(out=acco[:, qb, hh * 64:(hh + 1) * 64], in0=a0[:, c0:c0 + 64],
                                                                        scalar=sm[:, 3:4], in1=acco[:, qb, hh * 64:(hh + 1) * 64],
                                                                        op0=ALU.mult, op1=ALU.add),
                             reads=[a0.name, "sm3", "acco%d" % qb], writes=["acco%d" % qb])
                if dbg == 'g0s':
                    raise _Stop
                for hh in range(4):
                    a0 = AC[ctr["ac"] % 4]
                    aT = None
                    ctr["ac"] += 1
                    accs = [a0] * 4
                    cols = [0, 65, 130, 195]
                    tiles = []
                    for kb in range(max(0, 4 * g - 4), 4 * g + 4):
                        r = kb - 4 * g
                        tiles.append(dict(kb=kb, kkeys=["kwT_%d" % (kb // 4)], vkeys=["vw_%d" % kb, "vwones"],
                                          mask=wmask[:, r + 4, :], cbi=4 * g - kb + 3,
                                          qbs=[qb for qb in range(4) if qb >= r and qb - r < 5]))
                    run_branch(g, hh, tiles, kwT, vwA, "vw", 65, accs, cols, base0, "base0", cbt, "cbt", accT=aT)
                    for qb in range(4):
                        c0 = cols[qb]
                        blk = g * 4 + qb
                        S.op("vector", lambda e: e.reciprocal(out=sm[:, 4:5], in_=a0[:, c0 + 64:c0 + 65]), reads=[a0.name], writes=["sm4"])
                        S.op("vector", lambda e: e.tensor_tensor(out=sm[:, 5:6], in0=sm[:, 4:5], in1=gates[:, blk, hh * 3 + 2:hh * 3 + 3], op=ALU.mult),
                             reads=["sm4", "gates_%d" % blk], writes=["sm5"])
                        S.op("vector", lambda e: e.scalar_tensor_tensor(out=acco[:, qb, hh * 64:(hh + 1) * 64], in0=a0[:, c0:c0 + 64],
                                                                        scalar=sm[:, 5:6], in1=acco[:, qb, hh * 64:(hh + 1) * 64],
                                                                        op0=ALU.mult, op1=ALU.add),
                             reads=[a0.name, "sm5", "acco%d" % qb], writes=["acco%d" % qb])
                if dbg == 'g0w':
                    raise _Stop
                S.op("vector", lambda e: e.tensor_copy(out=ob[:], in_=acco[:]), reads=["acco%d" % q_ for q_ in range(4)], writes=["ob"])
                ot = oTs[ctr["o"] % 2]
                ctr["o"] += 1
                for qb in range(4):
                    for c in range(2):
                        S.op("tensor", lambda e: e.transpose(out=PT[:, c * 512 + qb * 128:c * 512 + (qb + 1) * 128],
                                                             in_=ob[:, qb, c * 128:(c + 1) * 128], identity=identb[:]),
                             reads=["ob", "identb"], writes=["PT"])
                S.op("vector", lambda e: e.tensor_copy(out=ot[:], in_=PT[:].rearrange("p (c t) -> p c t", c=2)), reads=["PT"], writes=[ot.name])
                S.dma("sync", odst(g), ot[:], reads=[ot.name], writes=["oc%d_%d" % (g // 4, g % 4)], is_output=True)
                if binds.get("cc") is not None and g % 4 == 3:
                    binds["cc"].ready(S, g // 4, ["oc%d_%d" % (g // 4, q_) for q_ in range(4)])
                if dbg == 'g0':
                    raise _Stop
        except _Stop:
            pass
        S.close()


def nsa_consts():
    c = {}
    jj = np.arange(128)[:, None]
    q = np.arange(512)[None, :]
    c["base0"] = (q - jj).astype(np.float32) * -1.0
    c["basec0"] = -(q - 16 * jj - 31).astype(np.float32)
    rel2 = np.arange(5)[None, :, None]
    okc = (512 * rel2 + q[:, None, :].transpose(1, 0, 2) * 0 + q[None, :, :] * 1 - 16 * jj[:, :, None] - 31) >= 0
    c["cmask"] = np.where(okc, 0.0, -BIGNEG).astype(np.float32).astype(ml_dtypes.bfloat16)
    c["dmask"] = diag_mask_tiles(False).astype(ml_dtypes.bfloat16)
    r = (np.arange(8) - 4)[None, :, None]
    dd = q[None, :, :] - jj[:, :, None] - 128 * r
    c["wmask"] = np.where((dd >= 0) & (dd < 512), 0.0, -BIGNEG).astype(np.float32).astype(ml_dtypes.bfloat16)
    s_ = np.arange(128)[:, None]
    key = np.arange(T)[None, :]
    c["ind"] = np.where(key // 64 == s_, BIGNEG, 0.0).astype(np.float32).astype(ml_dtypes.bfloat16)
    n = np.arange(512)
    cs = n * 16
    ss = np.arange(128) * 64
    ov = ((cs[:, None] < ss[None, :] + 64) & (cs[:, None] + 32 > ss[None, :])).astype(np.float32)
    ov[511, :] = 0.0
    c["ovl"] = np.ascontiguousarray(ov.reshape(4, 128, 128).transpose(1, 0, 2)).astype(ml_dtypes.bfloat16)
    tt = np.arange(T)
    cur = tt // 64
    sid = np.arange(128)[None, :]
    forced = (sid == 0) | (sid == cur[:, None]) | (sid == cur[:, None] - 1)
    adj = np.where(forced, NSA_FORCE, 0.0)
    adj = np.where(sid <= cur[:, None], adj, NSA_NEG).astype(np.float32)
    c["adj"] = np.ascontiguousarray(adj.reshape(64, 128, 128))
    return c


_NSA_CONSTS = {}


def nsa_inputs(pfx, w_in, cmp_pe, cmp_w1, cmp_w2):
    if not _NSA_CONSTS:
        _NSA_CONSTS.update(nsa_consts())
    cst = _NSA_CONSTS
    slopes = alibi_slopes_np(16)
    cw1 = np.ascontiguousarray(np.concatenate([cmp_w1[j].reshape(32, 64, 256).transpose(1, 0, 2) for j in range(2)], axis=0))
    cpe = np.ascontiguousarray(np.concatenate([cmp_pe[j].T for j in range(2)], axis=0))
    cw2k = np.ascontiguousarray(np.concatenate([cmp_w2[0], cmp_w2[0]], axis=1).reshape(2, 128, 128).transpose(1, 0, 2))
    cw2v = np.ascontiguousarray(cmp_w2[1].reshape(2, 128, 64).transpose(1, 0, 2))
    rel = np.arange(NREL) - 3
    out = []
    for grp in range(4):
        hs = [4 * grp + r_ for r_ in range(4)]
        qc = np.arange(grp * 256, (grp + 1) * 256)
        kc = 1024 + grp * 64 + np.arange(64)
        vc, ks, vs, kw, vw = kc + 256, kc + 512, kc + 768, kc + 1024, kc + 1280
        gc = 2560 + grp * 12 + np.arange(12)
        fm_cols = np.concatenate([qc, ks, ks, kw, kw, kc, vc])
        tm_cols = np.concatenate([vs, vw, gc])
        sl = np.array([slopes[h] for h in hs])
        out.append({
            pfx + "wfm": arrange_w(w_in[:, fm_cols]),
            pfx + "wtm": arrange_w(w_in[:, tm_cols]),
            pfx + "cw1": cw1, pfx + "cpe": cpe, pfx + "cw2k": cw2k, pfx + "cw2v": cw2v,
            pfx + "ovl": cst["ovl"],
            pfx + "slp": np.ascontiguousarray(np.broadcast_to(sl[None, :], (128, 4))).astype(np.float32),
            pfx + "cb": np.ascontiguousarray(np.broadcast_to((-sl[:, None] * 128.0 * rel[None, :])[None], (128, 4, NREL))).astype(np.float32),
            pfx + "cbc": np.ascontiguousarray(np.broadcast_to((-sl[:, None] * 512.0 * np.arange(16)[None, :])[None], (128, 4, 16))).astype(np.float32),
            pfx + "base0": cst["base0"], pfx + "basec0": cst["basec0"], pfx + "cmask": cst["cmask"], pfx + "dmask": cst["dmask"],
            pfx + "wmask": cst["wmask"], pfx + "ind": cst["ind"], pfx + "adj": cst["adj"], pfx + "ident": _IDENT,
        })
    return out


DEPTH = 4
CC_GROUPS = [[0, 1, 2, 3], [4, 5, 6, 7]]


class ChunkAG:
    def __init__(self, nc, name, src, dst):
        self.nc, self.src, self.dst = nc, src, dst
        self.cs = nc.alloc_semaphore(name=name)
        self.n = 0

    def ready(self, S, ch, keys):
        S._deps("gpsimd", keys, [])
        self.nc.gpsimd.collective_compute("AllGather", ALU.bypass, replica_groups=CC_GROUPS,
                                          ins=[self.src[ch]], outs=[self.dst[ch]]).then_inc(self.cs, 1)
        self.n += 1

    def finish(self):
        nc = self.nc
        for eng in (nc.gpsimd, nc.sync, nc.tensor, nc.vector, nc.scalar):
            eng.wait_ge(self.cs, self.n)
        free_sems(nc, [self.cs])


def build_fused(nphase=99):
    nc = bass.Bass("TRN2", target_bir_lowering=False)
    x_in = nc.dram_tensor("x", [TOK, D], F32, kind="ExternalInput").ap()
    out = nc.dram_tensor("out", [TOK, D], F32, kind="ExternalOutput").ap()
    x_scr = nc.dram_tensor("x_scr", [TOK, D], F32, kind="Internal").ap()
    hT_loc = [nc.dram_tensor("hT_loc%d" % i, [4, D, 512], BF16, kind="Internal").ap() for i in range(DEPTH)]
    hT_all = [nc.dram_tensor("hT_all%d" % i, [4, 4 * D, 512], BF16, kind="Internal").ap() for i in range(DEPTH)]
    o_loc = [nc.dram_tensor("o_loc%d" % i, [4, 256, 2048], BF16, kind="Internal").ap() for i in range(DEPTH)]
    o_all = [nc.dram_tensor("o_all%d" % i, [4, 4 * 256, 2048], BF16, kind="Internal").ap() for i in range(DEPTH)]
    rank = nc.sync.partition_id() % 4
    mod_loc = nc.dram_tensor("mod_loc", [9, D], F32, kind="Internal").ap()
    mod_all = nc.dram_tensor("mod_all", [36, D], F32, kind="Internal").ap()

    def allgather(name, src, dst, nch=4):
        cs = nc.alloc_semaphore(name=name)
        for ch in range(nch):
            nc.gpsimd.collective_compute("AllGather", ALU.bypass, replica_groups=CC_GROUPS,
                                         ins=[src[ch] if nch > 1 else src], outs=[dst[ch] if nch > 1 else dst]).then_inc(cs, 1)
        for eng in (nc.gpsimd, nc.sync, nc.tensor, nc.vector, nc.scalar):
            eng.wait_ge(cs, nch)
        free_sems(nc, [cs])

    emit_mods(nc, "pro_", {"rows_out": mod_loc})
    allgather("ccM", mod_loc, mod_all, nch=1)
    ccA = ChunkAG(nc, "ccA0", hT_loc[0], hT_all[0])
    emit_k1(nc, "k0_", False, 1, True, {"x": x_in, "xo": x_scr, "hTo": hT_loc[0], "hTo_chunked": True, "modrows": mod_all, "cc": ccA},
            k1_rows(None, [(0, 0)], 0))
    ccA.finish()
    for i in range(DEPTH):
        ccB = ChunkAG(nc, "ccB%d" % i, o_loc[i], o_all[i])
        binds = {"hT": hT_all[i], "oT": o_loc[i], "cc": ccB}
        kind = i % 3
        if kind == 0:
            emit_nsa(nc, "m%d_" % i, binds, True)
        elif kind == 1:
            emit_sb(nc, "m%d_" % i, binds, True)
        else:
            emit_diff(nc, "m%d_" % i, binds, 0.8 - 0.6 * math.exp(-0.3 * i), True)
        ccB.finish()
        oT_ap = o_all[i][rank]
        if i < DEPTH - 1:
            ccA = ChunkAG(nc, "ccA%d" % (i + 1), hT_loc[i + 1], hT_all[i + 1])
            emit_k1(nc, "k%d_" % (i + 1), True, 2, True,
                    {"x": x_scr, "xo": x_scr, "hTo": hT_loc[i + 1], "hTo_chunked": True, "oT": oT_ap, "modrows": mod_all, "cc": ccA},
                    k1_rows(i, [(i, 1), (i + 1, 0)], i + 1))
            ccA.finish()
        else:
            emit_k1(nc, "k%d_" % (i + 1), True, 1, False, {"x": x_scr, "xo": out, "oT": oT_ap, "modrows": mod_all},
                    k1_rows(i, [(i, 1)], None))
    return nc


_NPHASE = [99]


def kernel(x, c, ada_w, ada_b, norm_g, ffn_w1, ffn_w2, nsa_w_in, nsa_cmp_pe, nsa_cmp_w1, nsa_cmp_w2,
           nsa_w_out, sb_w_in, sb_w_out, diff_w_in, diff_lam, diff_subln_g, diff_w_out):
    f = lambda a: np.asarray(a, dtype=np.float32)
    x, c, ada_w, ada_b, norm_g, ffn_w1, ffn_w2 = map(f, (x, c, ada_w, ada_b, norm_g, ffn_w1, ffn_w2))
    nsa_w_in, nsa_cmp_pe, nsa_cmp_w1, nsa_cmp_w2, nsa_w_out = map(f, (nsa_w_in, nsa_cmp_pe, nsa_cmp_w1, nsa_cmp_w2, nsa_w_out))
    sb_w_in, sb_w_out, diff_w_in, diff_lam, diff_subln_g, diff_w_out = map(
        f, (sb_w_in, sb_w_out, diff_w_in, diff_lam, diff_subln_g, diff_w_out))
    nc = get_nc(("fused", _NPHASE[0]), lambda: build_fused(_NPHASE[0]))
    xt = x.reshape(B * T, D)
    common = {}
    per_b = [dict() for _ in range(B)]
    per_g = [dict() for _ in range(4)]

    def add_k1(pfx, mix, ffns, pre, wout=None):
        common.update(k1_inputs(pfx, ffn_w1, ffn_w2, mix, ffns, wout))

    pr, pb = mods_inputs("pro_", c, ada_w, ada_b, norm_g)
    for b in range(B):
        per_b[b].update(pb[b])
    for g in range(4):
        per_g[g].update(pr[g])

    add_k1("k0_", None, [(0, 0)], 0)
    for i in range(DEPTH):
        kind, j = i % 3, i // 3
        pfx = "m%d_" % i
        if kind == 0:
            pg = nsa_inputs(pfx, nsa_w_in[j], nsa_cmp_pe[j], nsa_cmp_w1[j], nsa_cmp_w2[j])
            wout = nsa_w_out[j]
        elif kind == 1:
            pg = sb_inputs(pfx, sb_w_in[j])
            wout = sb_w_out[j]
        else:
            pg = diff_inputs(pfx, diff_w_in[j], diff_lam[j], diff_subln_g[j])
            wout = diff_w_out[j]
        for g in range(4):
            per_g[g].update(pg[g])
        if i < DEPTH - 1:
            add_k1("k%d_" % (i + 1), i, [(i, 1), (i + 1, 0)], i + 1, wout)
        else:
            add_k1("k%d_" % (i + 1), i, [(i, 1)], None, wout)
    in_maps = []
    for core in range(NCORE):
        b, g = core // 4, core % 4
        m = dict(common)
        m.update(per_b[b])
        m.update(per_g[g])
        m["x"] = np.ascontiguousarray(xt[core * TOK:(core + 1) * TOK])
        in_maps.append(m)
    if _NPHASE[0] < 99:
        npfx = ["k0_"]
        for i in range(DEPTH):
            npfx += [None, "m%d_" % i, None, "k%d_" % (i + 1)]
        keep = set(p for p in npfx[:_NPHASE[0]] if p)
        in_maps = [{k: v for k, v in m.items() if k == "x" or k[:3] in keep or k.startswith("pro_")} for m in in_maps]
    res = run_bass_kernel_spmd(nc, in_maps, core_ids=list(range(NCORE)))
    xo = np.concatenate([res.results[i]["out"] for i in range(NCORE)], axis=0)
    return xo.reshape(B, T, D).astype(np.float32)
```
